# Optimizing a Trainium2 kernel written in Bass

```python
import math
import jax, jax.numpy as jnp
from jax import lax
import numpy as np

D_MODEL = 1024
BATCH = 16
SEQ = 2048
DEPTH = 4

PLE_DIM = 256
EPS = 1e-6
BLOCK = 128
SB_HEADS = 8
SB_DIM = 64
SB_WIDTH = SB_HEADS * SB_DIM
HG_HEADS = 4
HG_DK = 128
HG_DV = 128
HG_QK = HG_HEADS * HG_DK
HG_V = HG_HEADS * HG_DV
HG_CHUNK = 64
AB_SPLITS = [SB_WIDTH] * 3 + [HG_QK] * 2 + [HG_V] * 2
AB_IN = sum(AB_SPLITS)
AB_OUT = SB_WIDTH + HG_V
SW_HEADS = 16
SW_KV_HEADS = 4
SW_DIM = 64
SW_GROUP = SW_HEADS // SW_KV_HEADS
WINDOW = 128
C_IN = (SW_HEADS + 2 * SW_KV_HEADS) * SW_DIM
C_OUT = SW_HEADS * SW_DIM
N_BUCKETS = 32
MAX_DISTANCE = 128
D_FF = 2816
CONV_W = 3
N_EVEN = (DEPTH + 1) // 2
N_ODD = DEPTH // 2

kernel_name = 'hybrid_sb_hgrn2_swa_convffn'


def rmsnorm(x, g):
    xf = x.astype(jnp.float32)
    y = xf * lax.rsqrt(jnp.mean(xf * xf, axis=-1, keepdims=True) + EPS)
    return (y * g.astype(jnp.float32)).astype(x.dtype)


def stick_breaking_attention(q, k, v):
    S = q.shape[1]
    qf = jnp.swapaxes(q, 1, 2).astype(jnp.float32)
    kf = jnp.swapaxes(k, 1, 2).astype(jnp.float32)
    vf = jnp.swapaxes(v, 1, 2).astype(jnp.float32)
    scale = SB_DIM ** -0.5
    outs = []
    for n in range(S // BLOCK):
        t0 = n * BLOCK
        kn = t0 + BLOCK
        z = jnp.einsum('bhtd,bhsd->bhts', qf[:, :, t0:kn], kf[:, :, :kn]) * scale
        t_pos = t0 + jnp.arange(BLOCK)[:, None]
        s_pos = jnp.arange(kn)[None, :]
        mask = s_pos < t_pos
        log_keep = jnp.where(mask, jax.nn.log_sigmoid(-z), 0.0)
        later = lax.cumsum(log_keep, axis=3, reverse=True) - log_keep
        w = jnp.where(mask, jnp.exp(jax.nn.log_sigmoid(z) + later), 0.0)
        outs.append(jnp.einsum('bhts,bhsd->bhtd', w, vf[:, :, :kn]))
    o = jnp.concatenate(outs, axis=2)
    return jnp.swapaxes(o, 1, 2)


def hgrn2(q, f_pre, i, lb):
    B, S = q.shape[:2]
    lb = lb.reshape(HG_HEADS, HG_DK).astype(jnp.float32)
    fp = f_pre.astype(jnp.float32)
    log_f = jnp.log(lb + (1.0 - lb) * jax.nn.sigmoid(fp))
    kk = (1.0 - lb) * jax.nn.sigmoid(-fp)
    qf = jax.nn.silu(q.astype(jnp.float32))
    nc = S // HG_CHUNK

    def to_chunks(a):
        return a.reshape(B, nc, HG_CHUNK, HG_HEADS, a.shape[-1]).transpose(1, 0, 3, 2, 4)

    causal = jnp.tril(jnp.ones((HG_CHUNK, HG_CHUNK), dtype=bool))

    def step(state, xs):
        qc, kc, lfc, ic = xs
        b = jnp.cumsum(lfc, axis=2)
        o_inter = jnp.einsum('bhtk,bhkv->bhtv', qc * jnp.exp(b), state)
        rel = jnp.where(causal[:, :, None], b[:, :, :, None, :] - b[:, :, None, :, :], -jnp.inf)
        scores = jnp.einsum('bhtk,bhsk,bhtsk->bhts', qc, kc, jnp.exp(rel))
        o = o_inter + jnp.einsum('bhts,bhsv->bhtv', scores, ic)
        b_last = b[:, :, -1:, :]
        state = jnp.exp(b_last[:, :, 0, :, None]) * state + jnp.einsum('bhsk,bhsv->bhkv', kc * jnp.exp(b_last - b), ic)
        return state, o

    s0 = jnp.zeros((B, HG_HEADS, HG_DK, HG_DV), jnp.float32)
    _, o = lax.scan(step, s0, (to_chunks(qf), to_chunks(kk), to_chunks(log_f), to_chunks(i.astype(jnp.float32))))
    return o.transpose(1, 0, 3, 2, 4).reshape(B, S, HG_HEADS, HG_DV)


def t5_band_buckets():
    t = np.arange(WINDOW)[:, None]
    s = np.arange(2 * WINDOW)[None, :]
    dist = t + WINDOW - s
    band = (dist >= 0) & (dist < WINDOW)
    max_exact = N_BUCKETS // 2
    large = max_exact + (np.log(np.maximum(dist, max_exact) / max_exact) / math.log(MAX_DISTANCE / max_exact) * (N_BUCKETS - max_exact)).astype(np.int32)
    large = np.minimum(large, N_BUCKETS - 1)
    bucket = np.where(dist < max_exact, np.maximum(dist, 0), large).astype(np.int32)
    return bucket, band


def sliding_window_attention(q, k, v, sinks, rel_bias):
    B, S = q.shape[:2]
    nb = S // WINDOW
    bucket, band = t5_band_buckets()
    bias = rel_bias.astype(jnp.float32)[bucket]
    bias = bias.transpose(2, 0, 1).reshape(SW_KV_HEADS, SW_GROUP, WINDOW, 2 * WINDOW)
    key_pos = np.arange(nb)[:, None] * WINDOW - WINDOW + np.arange(2 * WINDOW)[None, :]
    mask = jnp.asarray(band[None] & (key_pos >= 0)[:, None, :])
    qb = q.astype(jnp.float32).reshape(B, nb, WINDOW, SW_KV_HEADS, SW_GROUP, SW_DIM).transpose(1, 0, 2, 3, 4, 5)

    def band_keys(a):
        ap = jnp.pad(a.astype(jnp.float32), ((0, 0), (WINDOW, 0), (0, 0), (0, 0)))
        ap = ap.reshape(B, nb + 1, WINDOW, SW_KV_HEADS, SW_DIM)
        return jnp.concatenate([ap[:, :-1], ap[:, 1:]], axis=2).transpose(1, 0, 2, 3, 4)

    kb, vb = band_keys(k), band_keys(v)
    sink = sinks.astype(jnp.float32).reshape(SW_KV_HEADS, SW_GROUP, 1, 1)
    scale = SW_DIM ** -0.5

    def block_attn(args):
        qn, kn, vn, mn = args
        logits = jnp.einsum('bqhgd,bkhd->bhgqk', qn, kn) * scale + bias
        logits = jnp.where(mn, logits, -jnp.inf)
        m = jnp.maximum(jnp.max(logits, axis=-1, keepdims=True), sink)
        e = jnp.exp(logits - m)
        w = e / (jnp.sum(e, axis=-1, keepdims=True) + jnp.exp(sink - m))
        return jnp.einsum('bhgqk,bkhd->bqhgd', w, vn)

    o = lax.map(block_attn, (qb, kb, vb, mask))
    return o.transpose(1, 0, 2, 3, 4, 5).reshape(B, S, SW_HEADS, SW_DIM)


def mixer_ab(h, w_in, lb, hg_norm, w_out):
    B, S, _ = h.shape
    proj = h @ w_in
    qa, ka, va, qb, fb, ib, gb = jnp.split(proj, np.cumsum(AB_SPLITS)[:-1].tolist(), axis=-1)
    sb_shape = (B, S, SB_HEADS, SB_DIM)
    o_a = stick_breaking_attention(qa.reshape(sb_shape), ka.reshape(sb_shape), va.reshape(sb_shape))
    o_a = o_a.astype(h.dtype).reshape(B, S, SB_WIDTH)
    o_b = hgrn2(qb.reshape(B, S, HG_HEADS, HG_DK), fb.reshape(B, S, HG_HEADS, HG_DK), ib.reshape(B, S, HG_HEADS, HG_DV), lb)
    o_b = rmsnorm(o_b.astype(h.dtype), hg_norm) * jax.nn.silu(gb.reshape(B, S, HG_HEADS, HG_DV))
    o_b = o_b.reshape(B, S, HG_V)
    return jnp.concatenate([o_a, o_b], axis=-1) @ w_out


def mixer_c(h, w_in, q_norm, k_norm, sinks, rel_bias, w_out):
    B, S, _ = h.shape
    proj = h @ w_in
    q, k, v = jnp.split(proj, [SW_HEADS * SW_DIM, (SW_HEADS + SW_KV_HEADS) * SW_DIM], axis=-1)
    q = rmsnorm(q.reshape(B, S, SW_HEADS, SW_DIM), q_norm)
    k = rmsnorm(k.reshape(B, S, SW_KV_HEADS, SW_DIM), k_norm)
    v = v.reshape(B, S, SW_KV_HEADS, SW_DIM)
    o = sliding_window_attention(q, k, v, sinks, rel_bias).astype(h.dtype)
    return o.reshape(B, S, C_OUT) @ w_out


def conv_glu_ffn(h, w_up, conv_w, conv_b, w_down):
    u = h @ w_up
    u = lax.conv_general_dilated(u, conv_w[:, None, :], window_strides=(1,), padding=[(CONV_W - 1, 0)],
                                 dimension_numbers=('NWC', 'WIO', 'NWC'), feature_group_count=2 * D_FF) + conv_b
    gate, up = jnp.split(u, 2, axis=-1)
    return (jax.nn.silu(gate) * up) @ w_down


def setup_inputs(seed: int = 0) -> dict:
    key = jax.random.key(seed)
    ks = jax.random.split(key, 21)

    def nrm(k, shape):
        return jax.random.normal(k, shape, jnp.float32)

    def w(k, shape, fan_in):
        return nrm(k, shape) * fan_in ** -0.5

    def gain(k, shape):
        return 1.0 + 0.02 * nrm(k, shape)

    F2 = 2 * D_FF
    return {
        'x': nrm(ks[0], (BATCH, SEQ, D_MODEL)),
        'p': nrm(ks[1], (DEPTH, BATCH, SEQ, PLE_DIM)),
        'mix_norm': gain(ks[2], (DEPTH, D_MODEL)),
        'ab_w_in': w(ks[3], (N_EVEN, D_MODEL, AB_IN), D_MODEL),
        'hg_lb_logits': 0.5 * nrm(ks[4], (N_EVEN, HG_QK)),
        'hg_out_norm': gain(ks[5], (N_EVEN, HG_DV)),
        'ab_w_out': w(ks[6], (N_EVEN, AB_OUT, D_MODEL), AB_OUT),
        'c_w_in': w(ks[7], (N_ODD, D_MODEL, C_IN), D_MODEL),
        'q_norm': gain(ks[8], (N_ODD, SW_DIM)),
        'k_norm': gain(ks[9], (N_ODD, SW_DIM)),
        'sinks': 0.5 * nrm(ks[10], (N_ODD, SW_HEADS)),
        'rel_bias': 0.5 * nrm(ks[11], (N_BUCKETS, SW_HEADS)),
        'c_w_out': w(ks[12], (N_ODD, C_OUT, D_MODEL), C_OUT),
        'ffn_norm': gain(ks[13], (DEPTH, D_MODEL)),
        'ffn_up': w(ks[14], (DEPTH, D_MODEL, F2), D_MODEL),
        'ffn_conv': w(ks[15], (DEPTH, CONV_W, F2), CONV_W),
        'ffn_conv_b': 0.02 * nrm(ks[16], (DEPTH, F2)),
        'ffn_down': w(ks[17], (DEPTH, D_FF, D_MODEL), D_FF),
        'ple_norm': gain(ks[18], (DEPTH, D_MODEL)),
        'ple_gate': w(ks[19], (DEPTH, D_MODEL, D_MODEL), D_MODEL),
        'ple_proj': w(ks[20], (DEPTH, PLE_DIM, D_MODEL), PLE_DIM),
    }


def reference(x, p, mix_norm, ab_w_in, hg_lb_logits, hg_out_norm, ab_w_out, c_w_in, q_norm, k_norm,
              sinks, rel_bias, c_w_out, ffn_norm, ffn_up, ffn_conv, ffn_conv_b, ffn_down,
              ple_norm, ple_gate, ple_proj):
    lb_cum = jnp.cumsum(jax.nn.softmax(hg_lb_logits.astype(jnp.float32), axis=0), axis=0)
    lower_bounds = lb_cum - lb_cum[0]
    h = x
    for i in range(DEPTH):
        j = i // 2
        hn = rmsnorm(h, mix_norm[i])
        if i % 2 == 0:
            h = h + mixer_ab(hn, ab_w_in[j], lower_bounds[j], hg_out_norm[j], ab_w_out[j])
        else:
            h = h + mixer_c(hn, c_w_in[j], q_norm[j], k_norm[j], sinks[j], rel_bias, c_w_out[j])
        h = h + conv_glu_ffn(rmsnorm(h, ffn_norm[i]), ffn_up[i], ffn_conv[i], ffn_conv_b[i], ffn_down[i])
        gate = jax.nn.sigmoid(rmsnorm(h, ple_norm[i]) @ ple_gate[i])
        h = h + gate * (p[i] @ ple_proj[i])
    return h
```

```python
import math
from contextlib import ExitStack

import numpy as np

import concourse.bass as bass
import concourse.mybir as mybir
from concourse.bass_utils import run_bass_kernel_spmd

F32 = mybir.dt.float32
BF16 = mybir.dt.bfloat16
AF = mybir.ActivationFunctionType
ALU = mybir.AluOpType
AX = mybir.AxisListType

D = 1024
S = 2048
DEPTH = 4
NCH = 8
TT = 512
NTILE = S // TT
F_FF = 2816
NFC = 22
EPS = 1e-6
NEG = -30000.0
import os as _os
_OPT = _os.environ.get("K_OPT", "normbg,normacc,lock,hggen,preconv").split(",")
OPT_SBEARLY = "sbearly" in _OPT
OPT_NORMACC = "normacc" in _OPT
OPT_NORMBG = "normbg" in _OPT
OPT_LOCK = "lock" in _OPT
OPT_HGPIPE = "hgpipe" in _OPT
OPT_HGGEN = "hggen" in _OPT
OPT_PRECONV = "preconv" in _OPT


class Buf:
    __slots__ = ("w", "r")

    def __init__(self):
        self.w = None
        self.r = {}


class Stream:
    def __init__(self, name, h, pe=False):
        self.name = name
        self.h = h
        self.pe = pe
        self.seq = 0
        self.know = {}
        self.oplist = []
        self.sem = None


class Chan:
    def __init__(self, name):
        self.name = name
        self.cnt = 0
        self.last = None
        self.sem = None


class Op:
    __slots__ = ("st", "fn", "waits", "needed", "seq", "val", "chan")


class FW:
    def __init__(self, nc, es):
        self.nc = nc
        self.es = es
        self.ops = []
        self.pe = self._mk("pe", nc.tensor, True)
        self.act = self._mk("act", nc.scalar)
        self.dve = self._mk("dve", nc.vector)
        self.pool = self._mk("pool", nc.gpsimd)
        self.sp = self._mk("sp", nc.sync)
        self.streams = [self.pe, self.act, self.dve, self.pool, self.sp]
        self.chans = []
        self.sp_ch = [self._ch("spc%d" % i) for i in range(6)]
        self.pl_ch = [self._ch("plc%d" % i) for i in range(4)]
        self._spi = 0
        self._pli = 0

    def _mk(self, name, h, pe=False):
        st = Stream(name, h, pe)
        st.sem = self.es.enter_context(self.nc.semaphore("s_" + name))
        return st

    def _ch(self, name):
        c = Chan(name)
        c.sem = self.es.enter_context(self.nc.semaphore("c_" + name))
        self.chans.append(c)
        return c

    def _need(self, st, waits, ev, raw):
        if ev is None:
            return
        s, n, kn = ev
        if s is st:
            if st.pe:
                return
        if st.know.get(s, 0) >= n:
            return
        waits.append((s, n))
        newk = dict(st.know)
        newk[s] = n
        for a, b in kn.items():
            if newk.get(a, 0) < b:
                newk[a] = b
        st.know = newk

    def op(self, st, fn, reads=(), writes=(), chan=None):
        waits = []
        for b in reads:
            self._need(st, waits, b.w, True)
        for b in writes:
            self._need(st, waits, b.w, False)
            for ev in b.r.values():
                self._need(st, waits, ev, False)
        if chan is not None:
            self._need(st, waits, chan.last, False)
        st.seq += 1
        o = Op()
        o.st = st
        o.fn = fn
        o.waits = waits
        o.needed = False
        o.seq = st.seq
        o.val = 0
        o.chan = chan
        self.ops.append(o)
        st.oplist.append(o)
        if chan is not None:
            chan.cnt += 16 * (len(fn) if isinstance(fn, (list, tuple)) else 1)
            ev = (chan, chan.cnt, st.know)
            chan.last = ev
        else:
            ev = (st, st.seq, st.know)
        for b in reads:
            b.r[ev[0]] = ev
        for b in writes:
            b.w = ev
            b.r = {}
        return ev

    def dma_sp(self, fn, reads=(), writes=()):
        ch = self.sp_ch[self._spi % len(self.sp_ch)]
        self._spi += 1
        return self.op(self.sp, fn, reads, writes, chan=ch)

    def dma_pool(self, fn, reads=(), writes=()):
        ch = self.pl_ch[self._pli % len(self.pl_ch)]
        self._pli += 1
        return self.op(self.pool, fn, reads, writes, chan=ch)

    def finish(self, final_events):
        waits = []
        for ev in final_events:
            self._need(self.sp, waits, ev, True)
        for o in self.ops:
            for (s, n) in o.waits:
                if isinstance(s, Stream):
                    s.oplist[n - 1].needed = True
        for (s, n) in waits:
            if isinstance(s, Stream):
                s.oplist[n - 1].needed = True
        for st in self.streams:
            c = 0
            for o in st.oplist:
                if o.needed:
                    c += 1
                o.val = c

        def emit_wait(st, s, n):
            if isinstance(s, Stream):
                st.h.wait_ge(s.sem, s.oplist[n - 1].val)
            else:
                st.h.wait_ge(s.sem, n)

        nw = 0
        for o in self.ops:
            for (s, n) in o.waits:
                emit_wait(o.st, s, n)
                nw += 1
            if isinstance(o.fn, (list, tuple)):
                for f_ in o.fn:
                    f_().then_inc(o.chan.sem, 16)
                continue
            ins = o.fn()
            if o.chan is not None:
                ins.then_inc(o.chan.sem, 16)
            elif o.needed:
                ins.then_inc(o.st.sem, 1)
        for (s, n) in waits:
            emit_wait(self.sp, s, n)
        self.stats = dict(n_ops=len(self.ops), n_waits=nw,
                          per_stream={st.name: len(st.oplist) for st in self.streams})


class Pool_:
    def __init__(self, tiles):
        self.items = [(t, Buf()) for t in tiles]
        self.free = list(range(len(tiles)))

    def get(self):
        k = self.free.pop(0)
        self.free.append(k)
        return self.items[k]

    def reserve(self):
        k = self.free.pop(0)
        return k, self.items[k]

    def release(self, k):
        self.free.append(k)


def _consts():
    p = np.arange(128)[:, None]
    m = np.arange(128)[None, :]
    c = {}
    c["ident"] = (p == m).astype(np.float32)
    c["ones"] = np.ones((128, 128), np.float32)
    c["bones"] = ((p // 64) == (m // 64)).astype(np.float32)
    c["negtri"] = -(p >= m).astype(np.float32)
    c["sbmask"] = (p < m).astype(np.float32)
    same = (p // 64) == (m // 64)
    tri2 = (same & (p <= m)).astype(np.float32)
    mid = (m // 64) * 64 + 31
    trimid = (same & (p <= mid)).astype(np.float32)
    c["tri2"] = tri2
    c["trid1"] = tri2 - trimid
    c["trisuf"] = (same & (p > m)).astype(np.float32)
    co = np.zeros((128, 128), np.float32)
    co[:, 0] = (np.arange(128) < 64)
    co[:, 1] = (np.arange(128) >= 64)
    c["chunkones"] = co
    mk = np.zeros((128, 256), np.float32)
    cc = np.arange(256)[None, :]
    mk[:, :] = ((p % 64) <= (cc % 64))
    c["maskS"] = mk
    return c


F32_CONSTS = ["tri2", "trid1", "trisuf", "chunkones"]
BF_CONSTS = ["ident", "ones", "bones", "negtri", "sbmask", "maskS"]


def _const_arrays():
    c = _consts()
    f = np.concatenate([c[k] for k in F32_CONSTS], axis=1)
    b = np.concatenate([c[k] for k in BF_CONSTS], axis=1)
    return np.ascontiguousarray(f), np.ascontiguousarray(b)


def _offsets(names, c):
    off = {}
    o = 0
    for k in names:
        off[k] = o
        o += c[k].shape[1]
    return off, o


def _t5_bias_index():
    W = 128
    t = np.arange(W)[None, None, :]
    s = np.arange(W)[:, None, None]
    kb = np.arange(2)[None, :, None]
    dist = t + W - (kb * W + s)
    valid = (dist >= 0) & (dist < W)
    max_exact = 16
    large = max_exact + (np.log(np.maximum(dist, max_exact) / max_exact) / math.log(128 / max_exact) * (32 - max_exact)).astype(np.int32)
    large = np.minimum(large, 31)
    bucket = np.where(dist < max_exact, np.maximum(dist, 0), large).astype(np.int32)
    return bucket, valid


class Builder:
    def __init__(self, nseq, layers, dbg=None):
        self.nseq = nseq
        self.layers = layers
        self.dbg = dbg

    def build(self):
        nc = bass.Bass("TRN2", target_bir_lowering=False)
        self.nc = nc
        nseq = self.nseq
        dt = nc.dram_tensor
        I = {}
        I["xT"] = dt("xT", [nseq, D, S], F32, kind="ExternalInput").ap()
        I["pT"] = dt("pT", [DEPTH, nseq, 256, S], F32, kind="ExternalInput").ap()
        I["ab_w_in"] = dt("ab_w_in", [2, D, 3584], F32, kind="ExternalInput").ap()
        I["ab_w_out"] = dt("ab_w_out", [2, D, D], F32, kind="ExternalInput").ap()
        I["c_w_in"] = dt("c_w_in", [2, D, 1536], F32, kind="ExternalInput").ap()
        I["c_w_out"] = dt("c_w_out", [2, D, D], F32, kind="ExternalInput").ap()
        I["ffn_up"] = dt("ffn_up", [DEPTH, D, 2 * F_FF], F32, kind="ExternalInput").ap()
        I["ffn_down"] = dt("ffn_down", [DEPTH, F_FF, D], F32, kind="ExternalInput").ap()
        I["ple_gate"] = dt("ple_gate", [DEPTH, D, D], F32, kind="ExternalInput").ap()
        I["ple_proj"] = dt("ple_proj", [DEPTH, 256, D], F32, kind="ExternalInput").ap()
        I["norms"] = dt("norms", [128, 3, DEPTH, NCH], F32, kind="ExternalInput").ap()
        I["convw"] = dt("convw", [128, DEPTH, 4, 44], F32, kind="ExternalInput").ap()
        I["lbl"] = dt("lbl", [1, 2 * 512], F32, kind="ExternalInput").ap()
        I["hgn"] = dt("hgn", [1, 2 * 512], F32, kind="ExternalInput").ap()
        I["qkn"] = dt("qkn", [128, 2, 2], F32, kind="ExternalInput").ap()
        I["snk"] = dt("snk", [128, 2, NCH], F32, kind="ExternalInput").ap()
        I["biasT"] = dt("biasT", [128, 2 * 16 * 128], F32, kind="ExternalInput").ap()
        cf, cb = _const_arrays()
        I["cf"] = dt("cf", list(cf.shape), F32, kind="ExternalInput").ap()
        I["cb"] = dt("cb", list(cb.shape), F32, kind="ExternalInput").ap()
        self.I = I
        self.outT = dt("outT", [nseq, D, S], F32, kind="ExternalOutput").ap()
        self.wscr = dt("wscr", [120, 128, 4096], BF16, kind="Internal").ap()
        self.wimg = {}
        self.preconv_done = set()
        if self.dbg:
            self.dbg_out = {k: dt("dbg_" + k, list(shp), F32, kind="ExternalOutput").ap() for k, shp in self.dbg.items()}
        c = _consts()
        self.cf_off, self.cf_n = _offsets(F32_CONSTS, c)
        self.cb_off, self.cb_n = _offsets(BF_CONSTS, c)

        with ExitStack() as es:
            self.es = es
            fw = FW(nc, es)
            self.fw = fw
            sb = lambda name, shape, dtype: es.enter_context(nc.sbuf_tensor(name, shape, dtype))
            self.res = sb("res", [128, NCH, S], F32)
            self.res_b = [[Buf() for _ in range(NTILE)] for _ in range(NCH)]
            self.hn = sb("hn", [128, NCH, TT], BF16)
            self.hn_b = [Buf() for _ in range(NCH)]
            self.ar = sb("arena", [128, NFC, TT], BF16)
            self.ar_b = [Buf() for _ in range(NFC)]
            self.kbuf = sb("kbuf", [128, 4, S], BF16)
            self.kb_b = [[Buf() for _ in range(NTILE)] for _ in range(4)]
            self.vbuf = sb("vbuf", [128, 16, 512], BF16)
            self.vb_b = [Buf() for _ in range(16)]
            self.bias = sb("bias", [128, 2, 16, 128], BF16)
            self.bias_b = Buf()
            self.ws = [sb("ws%d" % i, [128, NCH, 512], BF16) for i in range(3)]
            self.wpool = Pool_(self.ws)
            self.stage = sb("stage", [128, 4, 512], F32)
            self.stage_b = [Buf() for _ in range(4)]
            stf = self.stage[:].rearrange("p a b -> p (a b)")
            self.upool = Pool_([stf[:, i * 520:i * 520 + 516] for i in range(3)])
            t32 = [sb("t32_%d" % i, [128, 516], F32) for i in range(7)]
            self.t32 = Pool_(t32)
            t16 = [sb("t16_%d" % i, [128, 512], BF16) for i in range(7)]
            self.t16 = Pool_(t16)
            self.cf = sb("cf_sb", [128, self.cf_n], F32)
            self.cb = sb("cb_sb", [128, self.cb_n], BF16)
            self.c_b = Buf()
            self.norms = sb("norms_sb", [128, 3, DEPTH, NCH], F32)
            self.convw = sb("convw_sb", [128, DEPTH, 4, 44], F32)
            self.qkn = sb("qkn_sb", [128, 2, 2], F32)
            self.snk = sb("snk_sb", [128, 2, NCH], F32)
            self.lb = sb("lb_sb", [128, 2, 512], F32)
            self.lb1 = sb("lb1_sb", [128, 512], F32)
            self.hgn = sb("hgn_sb", [128, 512], F32)
            self.hgn_b = Buf()
            self.small_b = Buf()
            self.lb_b = Buf()
            self.S32 = sb("S32", [128, 512], F32)
            self.S32_b = Buf()
            self.Sbf = sb("Sbf", [128, 512], BF16)
            self.Sbf_b = Buf()
            self.halo = sb("halo", [128, 2, 44, 2], F32)
            self.halo_b = [Buf(), Buf()]
            self.sm = sb("smalls", [128, 64], F32)
            self.sm_pool = Pool_([self.sm[:, i * 8:(i + 1) * 8] for i in range(8)])
            self.negrow = sb("negrow", [128, 128], BF16)
            self.sbR_b = {0: Buf(), 64: Buf()}
            self.sbrt_b = {0: [Buf(), Buf(), Buf()], 64: [Buf(), Buf(), Buf()]}
            banks = [es.enter_context(nc.psum_tensor("pb%d" % i, [128, 512], F32)) for i in range(7)]
            self.banks = Pool_(banks)
            self.pbt = es.enter_context(nc.psum_tensor("pbt", [128, 1024], BF16))
            self.pbt_b = [Buf(), Buf()]

            self.acc = None
            self.pre_rstd = None
            self.prologue()
            finals = []
            for s in range(nseq):
                self.load_x(s)
                for li in self.layers:
                    self.layer(s, li)
                finals += self.store_out(s)
            if self.dbg:
                finals += self.dbg_events
            fw.finish(finals)
        return nc

    def C(self, name, bf=True, cols=None):
        if bf:
            o = self.cb_off[name]
            n = _consts()[name].shape[1] if cols is None else cols
            return self.cb[:, o:o + n]
        o = self.cf_off[name]
        n = _consts()[name].shape[1] if cols is None else cols
        return self.cf[:, o:o + n]

    def mm(self, out, lhsT, rhs, start, stop, reads, writes, skip=False):
        nc = self.nc
        if skip:
            return self.fw.op(self.fw.pe, lambda: nc.tensor.matmul(out, lhsT, rhs, start=start, stop=stop, skip_group_check=True), reads, writes)
        return self.fw.op(self.fw.pe, lambda: nc.tensor.matmul(out, lhsT, rhs, start=start, stop=stop), reads, writes)

    def tr(self, out, in_, ident, reads, writes):
        nc = self.nc
        return self.fw.op(self.fw.pe, lambda: nc.tensor.transpose(out, in_, ident), reads, writes)

    def actf(self, out, in_, func, reads, writes, bias=0.0, scale=1.0):
        nc = self.nc
        return self.fw.op(self.fw.act, lambda: nc.scalar.activation(out=out, in_=in_, func=func, bias=bias, scale=scale), reads, writes)

    def vtt(self, out, in0, in1, op, reads, writes):
        nc = self.nc
        return self.fw.op(self.fw.dve, lambda: nc.vector.tensor_tensor(out=out, in0=in0, in1=in1, op=op), reads, writes)

    def vts(self, out, in0, s1, s2, op0, op1, reads, writes):
        nc = self.nc
        if op1 is None:
            return self.fw.op(self.fw.dve, lambda: nc.vector.tensor_scalar(out=out, in0=in0, scalar1=s1, scalar2=None, op0=op0), reads, writes)
        return self.fw.op(self.fw.dve, lambda: nc.vector.tensor_scalar(out=out, in0=in0, scalar1=s1, scalar2=s2, op0=op0, op1=op1), reads, writes)

    def vstt(self, out, in0, scalar, in1, op0, op1, reads, writes):
        nc = self.nc
        return self.fw.op(self.fw.dve, lambda: nc.vector.scalar_tensor_tensor(out=out, in0=in0, scalar=scalar, in1=in1, op0=op0, op1=op1), reads, writes)

    def vcopy(self, out, in_, reads, writes):
        nc = self.nc
        return self.fw.op(self.fw.dve, lambda: nc.vector.tensor_copy(out, in_), reads, writes)

    def vrecip(self, out, in_, reads, writes):
        nc = self.nc
        return self.fw.op(self.fw.dve, lambda: nc.vector.reciprocal(out=out, in_=in_), reads, writes)

    def pcopy(self, out, in_, reads, writes):
        nc = self.nc
        return self.fw.op(self.fw.pool, lambda: nc.gpsimd.tensor_copy(out, in_), reads, writes)

    def load_w(self, src_ap, nk=NCH, ncols=512, src2=None, bgq=False):
        nc = self.nc
        key = (src_ap.tensor.name, str(src_ap.offset), tuple(tuple(x) for x in src_ap.ap))
        slot, b = self.wpool.get()
        img = self.wimg.get(key)
        if img is None:
            idx = len(self.wimg)
            ib = Buf()
            self.wimg[key] = (idx, ib)
            if src2 is None:
                dst = slot[:, 0:nk, 0:ncols]
                src = src_ap.rearrange("(c p) n -> p c n", p=128)
                self.fw.dma_pool(lambda: nc.gpsimd.dma_start(out=dst, in_=src), reads=(), writes=[b])
            else:
                h = ncols // 2
                fns = []
                for i_, sa in enumerate((src_ap, src2)):
                    dst = slot[:, 0:nk, i_ * h:(i_ + 1) * h]
                    src = sa.rearrange("(c p) n -> p c n", p=128)
                    fns.append(lambda dst=dst, src=src: nc.gpsimd.dma_start(out=dst, in_=src))
                self.fw.dma_pool(fns, reads=(), writes=[b])
            if ncols == 512:
                simg = self.wscr[idx, :, 0:nk * 512]
                ssrc = slot[:, 0:nk, :].rearrange("p c n -> p (c n)")
                if bgq:
                    self.fw.dma_pool(lambda: nc.gpsimd.dma_start(out=simg, in_=ssrc), reads=[b], writes=[ib])
                else:
                    self.fw.dma_sp(lambda: nc.sync.dma_start(out=simg, in_=ssrc), reads=[b], writes=[ib])
            else:
                self.wimg[key] = None
                del self.wimg[key]
        else:
            idx, ib = img
            simg = self.wscr[idx, :, 0:nk * 512]
            sdst = slot[:, 0:nk, :].rearrange("p c n -> p (c n)")
            self.fw.dma_sp(lambda: nc.sync.dma_start(out=sdst, in_=simg), reads=[ib], writes=[b])
        return slot, b

    def preconvert(self, li):
        if li in self.preconv_done or not OPT_PRECONV:
            return
        self.preconv_done.add(li)
        j = li // 2
        wo = self.I["ab_w_out"][j] if li % 2 == 0 else self.I["c_w_out"][j]
        for half in range(2):
            self.load_w(wo[:, half * 512:(half + 1) * 512], bgq=True)
        wup = self.I["ffn_up"][li]
        for j0 in range(0, NFC, 2):
            self.load_w(wup[:, j0 * 128:(j0 + 2) * 128], ncols=512, src2=wup[:, F_FF + j0 * 128:F_FF + (j0 + 2) * 128], bgq=True)
        wd = self.I["ffn_down"][li]
        for nh in range(2):
            for (j0, nj) in [(0, 8), (8, 8), (16, 6)]:
                self.load_w(wd[j0 * 128:(j0 + nj) * 128, nh * 512:(nh + 1) * 512], nk=nj, bgq=True)
        for half in range(2):
            self.load_w(self.I["ple_gate"][li][:, half * 512:(half + 1) * 512], bgq=True)
            self.load_w(self.I["ple_proj"][li][:, half * 512:(half + 1) * 512], nk=2, bgq=True)

    def dump(self, key, sb_ap, bufs, dst=None):
        if not self.dbg or key not in self.dbg:
            return
        nc = self.nc
        d = self.dbg_out[key] if dst is None else dst
        ev = self.fw.dma_sp(lambda: nc.sync.dma_start(out=d, in_=sb_ap), reads=bufs, writes=())
        self.dbg_events.append(ev)

    def prologue(self):
        nc, fw, I = self.nc, self.fw, self.I
        self.dbg_events = []
        fw.dma_sp(lambda: nc.sync.dma_start(out=self.cf[:], in_=I["cf"]), writes=[self.c_b])
        fw.dma_pool(lambda: nc.gpsimd.dma_start(out=self.cb[:], in_=I["cb"]), writes=[self.c_b])
        fw.dma_sp(lambda: nc.sync.dma_start(out=self.norms[:], in_=I["norms"]), writes=[self.small_b])
        fw.dma_sp(lambda: nc.sync.dma_start(out=self.convw[:], in_=I["convw"]), writes=[self.small_b])
        fw.dma_sp(lambda: nc.sync.dma_start(out=self.qkn[:], in_=I["qkn"]), writes=[self.small_b])
        fw.dma_sp(lambda: nc.sync.dma_start(out=self.snk[:], in_=I["snk"]), writes=[self.small_b])
        l0, l0_b = self.t32.get()
        l1, l1_b = self.t32.get()
        fw.dma_sp(lambda: nc.sync.dma_start(out=l0[:, 0:512], in_=I["lbl"][0:1, 0:512].partition_broadcast(128)), writes=[l0_b])
        fw.dma_sp(lambda: nc.sync.dma_start(out=l1[:, 0:512], in_=I["lbl"][0:1, 512:1024].partition_broadcast(128)), writes=[l1_b])
        self.vtt(l1[:, 0:512], l1[:, 0:512], l0[:, 0:512], ALU.subtract, [l0_b, l1_b], [l1_b])
        self.actf(self.lb1[:], l1[:, 0:512], AF.Sigmoid, [l1_b], [self.small_b])
        fw.dma_pool(lambda: nc.gpsimd.dma_start(out=self.bias[:].rearrange("p a b c -> p (a b c)"), in_=I["biasT"]), writes=[self.bias_b])
        fw.op(fw.dve, lambda: nc.vector.memset(self.negrow[:], -1.0), writes=[self.c_b])
        self.vts(self.qkn[:, :, 0:1], self.qkn[:, :, 0:1], 0.125, None, ALU.mult, None, [self.small_b], [self.small_b])
        self.actf(self.snk[:], self.snk[:], AF.Exp, [self.small_b], [self.small_b])

    def load_x(self, s):
        nc, fw = self.nc, self.fw
        for c in range(NCH):
            src = self.I["xT"][s, c * 128:(c + 1) * 128, :]
            dst = self.res[:, c, :]
            fw.dma_sp(lambda dst=dst, src=src: nc.sync.dma_start(out=dst, in_=src), writes=self.res_b[c])

    def store_out(self, s):
        nc, fw = self.nc, self.fw
        evs = []
        for c in range(NCH):
            dst = self.outT[s, c * 128:(c + 1) * 128, :]
            src = self.res[:, c, :]
            evs.append(fw.dma_sp(lambda dst=dst, src=src: nc.sync.dma_start(out=dst, in_=src), reads=self.res_b[c]))
        return evs

    def rmsnorm(self, T, which, li):
        t0 = T * TT
        if which == 0 and self.pre_rstd is not None and self.pre_rstd[0] == (li, T):
            _, kr, rr, r_b = self.pre_rstd
            self.pre_rstd = None
        elif self.acc is not None and self.acc["T"] == T and self.acc["n"] == NCH:
            kr, rr, r_b = self.stats_finish()
        else:
            self.stats_begin(T)
            for c in range(NCH):
                self.stats_add(c)
            kr, rr, r_b = self.stats_finish()
        for c in range(NCH):
            g = self.norms[:, which, li, c:c + 1]
            self.vstt(self.hn[:, c, :], self.res[:, c, t0:t0 + TT], g, rr, ALU.mult, ALU.mult,
                      [self.res_b[c][T], r_b, self.small_b], [self.hn_b[c]])
        self.t32.release(kr)

    def stats_begin(self, T):
        kb, (ssb, ssb_b) = self.banks.reserve()
        self.acc = dict(T=T, n=0, kb=kb, ssb=ssb, ssb_b=ssb_b, pend=None)

    def _stats_flush(self):
        a = self.acc
        if a["pend"] is not None:
            sq, sq_b, first, last = a["pend"]
            self.mm(a["ssb"][:], self.C("ones"), sq[:], first, last, [sq_b, self.c_b], [a["ssb_b"]])
            a["pend"] = None

    def stats_add(self, c):
        a = self.acc
        T = a["T"]
        t0 = T * TT
        self._stats_flush()
        sq, sq_b = self.t16.get()
        self.actf(sq[:], self.res[:, c, t0:t0 + TT], AF.Square, [self.res_b[c][T]], [sq_b])
        a["pend"] = (sq, sq_b, a["n"] == 0, a["n"] == NCH - 1)
        a["n"] += 1

    def stats_finish(self):
        a = self.acc
        self._stats_flush()
        kr, (r, r_b) = self.t32.reserve()
        rr = r[:, 0:TT]
        self.actf(rr, a["ssb"][:], AF.Ln, [a["ssb_b"]], [r_b], bias=EPS, scale=1.0 / D)
        self.actf(rr, rr, AF.Exp, [r_b], [r_b], scale=-0.5)
        self.banks.release(a["kb"])
        self.acc = None
        return kr, rr, r_b

    def stats_bg(self, key, T):
        self.stats_begin(T)
        acc = self.acc
        self.acc = None
        for c in range(NCH):
            self.acc, sv = acc, self.acc
            self.stats_add(c)
            self.acc = sv
            yield
        self.acc, sv = acc, self.acc
        kr, rr, r_b = self.stats_finish()
        self.acc = sv
        self.pre_rstd = (key, kr, rr, r_b)
        yield

    def proj_fm(self, slot, slot_b, col0, rhs_fn, nk, rhs_bufs, ncols=128):
        bank, bank_b = self.banks.get()
        for k in range(nk):
            rb_ = [rhs_bufs[k]] if len(rhs_bufs) == nk else list(rhs_bufs)
            self.mm(bank[0:ncols, :], slot[:, k, col0:col0 + ncols], rhs_fn(k), k == 0, k == nk - 1,
                    [slot_b] + rb_, [bank_b])
        return bank, bank_b

    def add_to_res(self, bank, bank_b, n, T, stats=False):
        t0 = T * TT
        self.vtt(self.res[:, n, t0:t0 + TT], bank[:], self.res[:, n, t0:t0 + TT], ALU.add,
                 [bank_b, self.res_b[n][T]], [self.res_b[n][T]])
        if stats and OPT_NORMACC:
            self.stats_add(n)

    def out_proj(self, w_ap, T):
        if OPT_NORMACC:
            self.stats_begin(T)
        for half in range(2):
            slot, slot_b = self.load_w(w_ap[:, half * 512:(half + 1) * 512])
            for nq in range(4):
                bank, bank_b = self.proj_fm(slot, slot_b, nq * 128, lambda k: self.hn[:, k, :], NCH, self.hn_b)
                self.add_to_res(bank, bank_b, half * 4 + nq, T, stats=True)

    def layer(self, s, li):
        j = li // 2
        if li % 2 == 0:
            self.even_prep(j)
        import os
        st = os.environ.get("K_STAGES", "norm,mix,outp,ffn,ple,sb,hg").split(",")
        self.st = st
        for T in range(NTILE):
            if "norm" in st:
                self.rmsnorm(T, 0, li)
            if li % 2 == 0:
                if "mix" in st:
                    self.mixer_even(s, j, T)
                if "outp" in st:
                    self.out_proj(self.I["ab_w_out"][j], T)
            else:
                if "mix" in st:
                    self.mixer_odd(s, j, T)
                if "outp" in st:
                    self.out_proj(self.I["c_w_out"][j], T)
            self.dump("res_mix_L%d" % li, self.res[:, :, T * TT:(T + 1) * TT], [self.res_b[c][T] for c in range(NCH)],
                      dst=None if not self.dbg or ("res_mix_L%d" % li) not in self.dbg else self.dbg_out["res_mix_L%d" % li][:, :, T * TT:(T + 1) * TT])
            if "ffn" in st:
                bg = None
                if "norm" in st:
                    if T + 1 < NTILE:
                        nxt = (li, T + 1)
                    else:
                        k_ = self.layers.index(li)
                        nxt = (self.layers[k_ + 1], 0) if k_ + 1 < len(self.layers) else None
                    if nxt is not None and OPT_NORMBG:
                        bg = self.stats_bg(nxt, nxt[1])
                self.ffn(s, li, T, bg)
            if "ple" in st:
                self.ple(s, li, T)
        if s == 0:
            self.dump("res_L%d" % li, self.res[:], [b for c in range(NCH) for b in self.res_b[c]])

    def ffn(self, s, li, T, bg=None):
        nc, fw = self.nc, self.fw
        t0 = T * TT
        self.rmsnorm(T, 1, li)
        cur, nxt = T % 2, (T + 1) % 2
        if T == 0:
            fw.op(fw.dve, lambda: nc.vector.memset(self.halo[:, 0, :, :], 0.0), writes=[self.halo_b[0]])
        wup = self.I["ffn_up"][li]
        groups = [(j0, 2) for j0 in range(0, NFC, 2)]
        cw = self.convw
        for (j0, nj) in groups:
            if bg is not None and j0 >= 2:
                next(bg, None)
            slot, slot_b = self.load_w(wup[:, j0 * 128:(j0 + nj) * 128], ncols=512,
                                       src2=wup[:, F_FF + j0 * 128:F_FF + (j0 + nj) * 128])
            for jj in range(nj):
                jp = j0 + jj
                ys = []
                for (col0, idx) in ((jj * 128, jp), (nj * 128 + jj * 128, NFC + jp)):
                    bank, bank_b = self.proj_fm(slot, slot_b, col0, lambda k: self.hn[:, k, :], NCH, self.hn_b)
                    u, u_b = self.upool.get()
                    y, y_b = self.t32.get()
                    self.actf(u[:, 2:2 + TT], bank[:], AF.Copy, [bank_b], [u_b])
                    self.actf(u[:, 0:2], self.halo[:, cur, idx, :], AF.Copy, [self.halo_b[cur]], [u_b])
                    self.actf(self.halo[:, nxt, idx, :], bank[:, TT - 2:TT], AF.Copy, [bank_b], [self.halo_b[nxt]])
                    self.actf(y[:, 0:TT], bank[:], AF.Identity, [bank_b, self.small_b], [y_b],
                              bias=cw[:, li, 3, idx:idx + 1], scale=cw[:, li, 2, idx:idx + 1])
                    self.vstt(y[:, 0:TT], u[:, 1:1 + TT], cw[:, li, 1, idx:idx + 1], y[:, 0:TT], ALU.mult, ALU.add,
                              [u_b, y_b, self.small_b], [y_b])
                    self.vstt(y[:, 0:TT], u[:, 0:TT], cw[:, li, 0, idx:idx + 1], y[:, 0:TT], ALU.mult, ALU.add,
                              [u_b, y_b, self.small_b], [y_b])
                    ys.append((y, y_b))
                (yg, yg_b), (yu, yu_b) = ys
                self.actf(yg[:, 0:TT], yg[:, 0:TT], AF.Silu, [yg_b], [yg_b])
                self.vtt(self.ar[:, jp, :], yg[:, 0:TT], yu[:, 0:TT], ALU.mult, [yg_b, yu_b], [self.ar_b[jp]])
        if bg is not None:
            for _ in bg:
                pass
        wd = self.I["ffn_down"][li]
        jgs = [(0, 8), (8, 8), (16, 6)]
        if OPT_NORMACC:
            self.stats_begin(T)
        for nh in range(2):
            bks = [self.banks.get() for _ in range(4)]
            for (j0, nj) in jgs:
                slot, slot_b = self.load_w(wd[j0 * 128:(j0 + nj) * 128, nh * 512:(nh + 1) * 512], nk=nj)
                for jj in range(nj):
                    jf = j0 + jj
                    for nq in range(4):
                        self.mm(bks[nq][0][:], slot[:, jj, nq * 128:(nq + 1) * 128], self.ar[:, jf, :], jf == 0, jf == NFC - 1,
                                [slot_b, self.ar_b[jf]], [bks[nq][1]])
            for nq in range(4):
                self.add_to_res(bks[nq][0], bks[nq][1], nh * 4 + nq, T, stats=True)

    def ple(self, s, li, T):
        nc, fw = self.nc, self.fw
        t0 = T * TT
        self.rmsnorm(T, 2, li)
        src = self.I["pT"][li, s, :, t0:t0 + TT].rearrange("(c p) t -> p c t", p=128)
        pbuf = self.ar[:, 20:22, :]
        pbuf_bs = [self.ar_b[20], self.ar_b[21]]
        fw.dma_pool(lambda: nc.gpsimd.dma_start(out=pbuf, in_=src), writes=pbuf_bs)
        for half in range(2):
            sg, sg_b = self.load_w(self.I["ple_gate"][li][:, half * 512:(half + 1) * 512])
            spj, spj_b = self.load_w(self.I["ple_proj"][li][:, half * 512:(half + 1) * 512], nk=2)
            for nq in range(4):
                n = half * 4 + nq
                bg, bg_b = self.proj_fm(sg, sg_b, nq * 128, lambda k: self.hn[:, k, :], NCH, self.hn_b)
                bp, bp_b = self.proj_fm(spj, spj_b, nq * 128, lambda k: self.ar[:, 20 + k, :], 2, pbuf_bs)
                g, g_b = self.t32.get()
                self.actf(g[:, 0:TT], bg[:], AF.Sigmoid, [bg_b], [g_b])
                self.vtt(g[:, 0:TT], g[:, 0:TT], bp[:], ALU.mult, [g_b, bp_b], [g_b])
                self.vtt(self.res[:, n, t0:t0 + TT], g[:, 0:TT], self.res[:, n, t0:t0 + TT], ALU.add,
                         [g_b, self.res_b[n][T]], [self.res_b[n][T]])

    def even_prep(self, j):
        nc, fw = self.nc, self.fw
        if j == 0:
            fw.op(fw.dve, lambda: nc.vector.memset(self.lb[:, 0, :], 0.0), writes=[self.lb_b])
        else:
            self.vcopy(self.lb[:, 0, :], self.lb1[:], [self.small_b], [self.lb_b])
        src = self.I["hgn"][0:1, j * 512:(j + 1) * 512].partition_broadcast(128)
        fw.dma_sp(lambda: nc.sync.dma_start(out=self.hgn[:], in_=src), writes=[self.hgn_b])
        self.vts(self.lb[:, 1, :], self.lb[:, 0, :], -1.0, 1.0, ALU.mult, ALU.add, [self.lb_b], [self.lb_b])
        fw.op(fw.dve, lambda: nc.vector.memset(self.S32[:], 0.0), writes=[self.S32_b])
        fw.op(fw.dve, lambda: nc.vector.memset(self.Sbf[:], 0.0), writes=[self.Sbf_b])

    def mixer_even(self, s, j, T):
        nc, fw = self.nc, self.fw
        t0 = T * TT
        w = self.I["ab_w_in"][j]
        hnf = lambda k: self.hn[:, k, :]
        slot, slot_b = self.load_w(w[:, 0:512])
        for m in range(4):
            bank, bank_b = self.proj_fm(slot, slot_b, m * 128, hnf, NCH, self.hn_b)
            self.actf(self.ar[:, m, :], bank[:], AF.Copy, [bank_b], [self.ar_b[m]], scale=0.125)
        slot, slot_b = self.load_w(w[:, 512:1024])
        for m in range(4):
            bank, bank_b = self.proj_fm(slot, slot_b, m * 128, hnf, NCH, self.hn_b)
            self.actf(self.kbuf[:, m, t0:t0 + TT], bank[:], AF.Copy, [bank_b], [self.kb_b[m][T]])
        def tok_block(col0, evac):
            slot, slot_b = self.load_w(w[:, col0:col0 + 512])
            for sub in range(4):
                bank, bank_b = self.banks.get()
                for k in range(NCH):
                    self.mm(bank[:], self.hn[:, k, sub * 128:(sub + 1) * 128], slot[:, k, :], k == 0, k == NCH - 1,
                            [slot_b, self.hn_b[k]], [bank_b])
                evac(sub, bank, bank_b)
        tok_block(1024, lambda sub, bank, bank_b: self.actf(self.vbuf[:, T * 4 + sub, :], bank[:], AF.Copy, [bank_b], [self.vb_b[T * 4 + sub]]))
        tok_block(1536, lambda sub, bank, bank_b: self.actf(self.ar[:, 8 + sub, :], bank[:], AF.Silu, [bank_b], [self.ar_b[8 + sub]]))
        tok_block(2048, lambda sub, bank, bank_b: self.actf(self.stage[:, sub, :], bank[:], AF.Sigmoid, [bank_b], [self.stage_b[sub]]))
        tok_block(2560, lambda sub, bank, bank_b: self.actf(self.ar[:, 12 + sub, :], bank[:], AF.Copy, [bank_b], [self.ar_b[12 + sub]]))
        tok_block(3072, lambda sub, bank, bank_b: self.actf(self.ar[:, 16 + sub, :], bank[:], AF.Silu, [bank_b], [self.ar_b[16 + sub]]))
        self.preconvert(2 * j)
        if "sb" in self.st:
            for pair in ((0, 1), (2, 3), (4, 5), (6, 7)):
                gens = [self.sb_chain(T, h) for h in pair]
                while gens:
                    for g in list(gens):
                        try:
                            next(g)
                        except StopIteration:
                            gens.remove(g)
        if "hg" in self.st and OPT_HGGEN:
            self.hgrn_tile_gen(j, T)
        elif "hg" in self.st and OPT_HGPIPE:
            fr = {0: self.hgrn_front(j, T, 0), 1: self.hgrn_front(j, T, 1)}
            outs = {}
            outs[0] = self.hgrn_mid(fr.pop(0))
            fr[2] = self.hgrn_front(j, T, 2)
            outs[1] = self.hgrn_mid(fr.pop(1))
            outs.pop(0)()
            fr[3] = self.hgrn_front(j, T, 3)
            outs[2] = self.hgrn_mid(fr.pop(2))
            outs.pop(1)()
            outs[3] = self.hgrn_mid(fr.pop(3))
            outs.pop(2)()
            outs.pop(3)()
        elif "hg" in self.st:
            prev_out = None
            for sub in range(4):
                out = self.hgrn_sub(j, T, sub)
                if prev_out is not None:
                    prev_out()
                prev_out = out
            if prev_out is not None:
                prev_out()

    def sb_chain(self, T, h):
        nc, fw = self.nc, self.fw
        hp = (h % 2) * 64
        pr = 64 - hp
        hc = h // 2
        qT = self.ar[hp:hp + 64, hc, :]
        q_b = self.ar_b[hc]
        negtri = self.C("negtri")
        sbmask = self.C("sbmask")
        onescol = self.C("ones", cols=1)
        negrow = self.negrow[pr:pr + 1, :]
        kpv, (pvb, pvb_b) = self.banks.reserve()
        Rf = self.ar[pr:pr + 1, 4:6, :].rearrange("p a b -> p (a b)").bitcast(F32)
        Rf_b = self.sbR_b[pr]
        rts = [(self.ar[pr:pr + 1, 6, :], self.sbrt_b[pr][0]), (self.ar[pr:pr + 1, 7, :], self.sbrt_b[pr][1]),
               (self.ar[pr:pr + 1, 20, :], self.sbrt_b[pr][2])]
        fw.op(fw.dve, lambda: nc.vector.memset(Rf, 0.0), writes=[Rf_b])
        blocks = [(4 * T + kl, kl * 128, True) for kl in (3, 2, 1, 0)] + [(kb, 0, False) for kb in range(4 * T - 1, -1, -1)]
        nb = len(blocks)

        def stage_a(bi):
            kb, c0, diag = blocks[bi]
            kT = self.kbuf[hp:hp + 64, hc, kb * 128:(kb + 1) * 128]
            k_b = self.kb_b[hc][kb // 4]
            kx, (X, X_b) = self.banks.reserve()
            self.mm(X[:, c0:TT], kT, qT[:, c0:TT], True, True, [k_b, q_b], [X_b])
            if OPT_LOCK:
                yield
            ke, (e, e_b) = self.t32.reserve()
            self.actf(e[:, c0:TT], X[:, c0:TT], AF.Exp, [X_b], [e_b])
            kl, (lp, lp_b) = self.t16.reserve()
            self.actf(lp[:, c0:TT], e[:, c0:TT], AF.Ln, [e_b], [lp_b], bias=1.0)
            self.t32.release(ke)
            if diag:
                self.vtt(lp[:, c0:c0 + 128], lp[:, c0:c0 + 128], sbmask, ALU.mult, [lp_b, self.c_b], [lp_b])
            rnew = None
            if bi < nb - 1 and OPT_SBEARLY:
                self.mm(pvb[pr:pr + 1, c0:TT], onescol, lp[:, c0:TT], True, True, [lp_b, self.c_b], [pvb_b], skip=True)
                self.vtt(Rf[:, c0:TT], pvb[pr:pr + 1, c0:TT], Rf[:, c0:TT], ALU.add, [pvb_b, Rf_b], [Rf_b])
                rt, rt_b = rts[bi % 3]
                self.vcopy(rt[:, c0:TT], Rf[:, c0:TT], [Rf_b], [rt_b])
                rnew = (rt, rt_b)
            if False:
                yield
            return kx, X, X_b, kl, lp, lp_b, kT, k_b, rnew

        pend = {0: (yield from stage_a(0))}
        yield
        rprev = None
        pvq = []

        def flush_pv():
            while pvq:
                (bi_, kb_, c0_, wt_, wt_b_, kw_) = pvq.pop(0)
                self.mm(pvb[hp:hp + 64, c0_:TT], self.vbuf[:, kb_, h * 64:(h + 1) * 64], wt_[:, c0_:TT], bi_ == 0, bi_ == nb - 1,
                        [self.vb_b[kb_], wt_b_], [pvb_b], skip=True)
                self.t16.release(kw_)

        for bi in range(nb):
            if bi + 1 < nb:
                pend[bi + 1] = yield from stage_a(bi + 1)
                yield
            flush_pv()
            if OPT_LOCK:
                yield
            kb, c0, diag = blocks[bi]
            kx, X, X_b, kl, lp, lp_b, kT, k_b, rnew = pend.pop(bi)
            has_r = rprev is not None
            cR = c0 + 128 if diag else 0
            use_r = has_r and cR < TT
            self.mm(X[:, c0:TT], kT, qT[:, c0:TT], True, False, [k_b, q_b], [X_b])
            if OPT_LOCK:
                yield
            self.mm(X[:, c0:TT], negtri, lp[:, c0:TT], False, not use_r, [lp_b, self.c_b], [X_b])
            if OPT_LOCK:
                yield
            if use_r:
                rt, rt_b = rprev
                self.mm(X[:, cR:TT], negrow, rt[:, cR:TT], False, True, [rt_b, self.c_b], [X_b])
            if OPT_LOCK:
                yield
            kw, (wt, wt_b) = self.t16.reserve()
            self.actf(wt[:, c0:TT], X[:, c0:TT], AF.Exp, [X_b], [wt_b])
            self.banks.release(kx)
            if diag:
                self.vtt(wt[:, c0:c0 + 128], wt[:, c0:c0 + 128], sbmask, ALU.mult, [wt_b, self.c_b], [wt_b])
            pvq.append((bi, kb, c0, wt, wt_b, kw))
            if bi < nb - 1 and not OPT_SBEARLY:
                self.mm(pvb[pr:pr + 1, c0:TT], onescol, lp[:, c0:TT], True, True, [lp_b, self.c_b], [pvb_b], skip=True)
                if OPT_LOCK:
                    yield
                self.vtt(Rf[:, c0:TT], pvb[pr:pr + 1, c0:TT], Rf[:, c0:TT], ALU.add, [pvb_b, Rf_b], [Rf_b])
                rt, rt_b = rts[bi % 3]
                self.vcopy(rt[:, c0:TT], Rf[:, c0:TT], [Rf_b], [rt_b])
                rnew = (rt, rt_b)
            rprev = rnew
            self.t16.release(kl)
            yield
        flush_pv()
        self.actf(self.hn[hp:hp + 64, hc, :], pvb[hp:hp + 64, :], AF.Copy, [pvb_b], [self.hn_b[hc]])
        self.banks.release(kpv)

    def hgrn_sub(self, j, T, sub):
        nc, fw = self.nc, self.fw
        qs, qs_b = self.ar[:, 8 + sub, :], self.ar_b[8 + sub]
        ib, ib_b = self.ar[:, 12 + sub, :], self.ar_b[12 + sub]
        gs, gs_b = self.ar[:, 16 + sub, :], self.ar_b[16 + sub]
        sg, sg_b = self.stage[:, sub, :], self.stage_b[sub]
        ident = self.C("ident")
        fA, fA_b = self.t32.get()
        f = fA[:, 0:512]
        self.vtt(f, sg, self.lb[:, 1, :], ALU.mult, [sg_b, self.lb_b], [fA_b])
        self.vtt(f, f, self.lb[:, 0, :], ALU.add, [fA_b, self.lb_b], [fA_b])
        lfB, lfB_b = self.t32.get()
        lf = lfB[:, 0:512]
        self.actf(lf, f, AF.Ln, [fA_b], [lfB_b])
        self.vts(f, f, -1.0, 1.0, ALU.mult, ALU.add, [fA_b], [fA_b])
        bd1, bd1_b = self.banks.get()
        bb, bb_b = self.banks.get()
        bd4, bd4_b = self.banks.get()
        self.mm(bd1[:], self.C("trid1", bf=False), lf, True, True, [lfB_b, self.c_b], [bd1_b])
        self.mm(bb[:], self.C("tri2", bf=False), lf, True, True, [lfB_b, self.c_b], [bb_b])
        self.mm(bd4[:], self.C("trisuf", bf=False), lf, True, True, [lfB_b, self.c_b], [bd4_b])
        bz, bz_b = self.banks.get()
        for h in range(4):
            self.mm(bz[:, 2 * h:2 * h + 2], lf[:, h * 128:(h + 1) * 128], self.C("chunkones", bf=False, cols=2), True, True,
                    [lfB_b, self.c_b], [bz_b])
        el, el_b = self.sm_pool.get()
        self.actf(el, bz[:, 0:8], AF.Exp, [bz_b], [el_b])
        E, E_b = self.t32.get()
        q1, q1_b = self.t16.get()
        self.actf(E[:, 0:512], bd1[:], AF.Exp, [bd1_b], [E_b])
        self.vtt(q1[:], qs, E[:, 0:512], ALU.mult, [qs_b, E_b], [q1_b])
        E2, E2_b = self.t32.get()
        k1, k1_b = self.t16.get()
        self.actf(E2[:, 0:512], bd1[:], AF.Exp, [bd1_b], [E2_b], scale=-1.0)
        self.vtt(k1[:], f, E2[:, 0:512], ALU.mult, [fA_b, E2_b], [k1_b])
        E3, E3_b = self.t32.get()
        q3, q3_b = self.t16.get()
        self.actf(E3[:, 0:512], bb[:], AF.Exp, [bb_b], [E3_b])
        self.vtt(q3[:], qs, E3[:, 0:512], ALU.mult, [qs_b, E3_b], [q3_b])
        E4, E4_b = self.t32.get()
        k4, k4_b = self.t16.get()
        self.actf(E4[:, 0:512], bd4[:], AF.Exp, [bd4_b], [E4_b])
        self.vtt(k4[:], f, E4[:, 0:512], ALU.mult, [fA_b, E4_b], [k4_b])
        import os
        HG = float(os.environ.get("K_HG", "9"))
        if HG < 2:
            return
        pA, pA_b = self.pbt[:, 0:512], self.pbt_b[0]
        pB, pB_b = self.pbt[:, 512:1024], self.pbt_b[0]
        for h in range(4):
            self.tr(pA[:, h * 128:(h + 1) * 128], q1[:, h * 128:(h + 1) * 128], ident, [q1_b, self.c_b], [pA_b])
        for h in range(4):
            self.tr(pB[:, h * 128:(h + 1) * 128], k1[:, h * 128:(h + 1) * 128], ident, [k1_b, self.c_b], [pB_b])
        q1T, q1T_b = self.t16.get()
        k1T, k1T_b = self.t16.get()
        if HG < 2.1:
            return
        self.vcopy(q1T[:], pA, [pA_b], [q1T_b])
        if HG < 2.2:
            return
        self.vcopy(k1T[:], pB, [pB_b], [k1T_b])
        if HG < 2.3:
            return
        for h in range(4):
            self.tr(pA[:, h * 128:(h + 1) * 128], q3[:, h * 128:(h + 1) * 128], ident, [q3_b, self.c_b], [pA_b])
        q3T, q3T_b = self.t16.get()
        self.vcopy(q3T[:], pA, [pA_b], [q3T_b])
        if HG < 3:
            return
        bs, bs_b = self.banks.get()
        for c in range(2):
            for h in range(4):
                self.mm(bs[64 * c:64 * c + 64, h * 64:(h + 1) * 64],
                        k1T[:, h * 128 + 64 * c:h * 128 + 64 * c + 64], q1T[:, h * 128 + 64 * c:h * 128 + 64 * c + 64],
                        True, True, [k1T_b, q1T_b], [bs_b])
        scm, scm_b = self.t16.get()
        self.vtt(scm[:, 0:256], bs[:, 0:256], self.C("maskS"), ALU.mult, [bs_b, self.c_b], [scm_b])
        if HG < 4:
            return
        kbo, (bo, bo_b) = self.banks.reserve()
        for c in range(2):
            pc = 64 * c
            for h in range(4):
                hs = slice(h * 128, (h + 1) * 128)
                self.mm(bo[pc:pc + 64, hs], q3T[:, h * 128 + pc:h * 128 + pc + 64], self.Sbf[:, hs], True, False,
                        [q3T_b, self.Sbf_b], [bo_b])
                self.mm(bo[pc:pc + 64, hs], scm[pc:pc + 64, h * 64:(h + 1) * 64], ib[pc:pc + 64, hs], False, True,
                        [scm_b, ib_b], [bo_b])
            bu, bu_b = self.banks.get()
            for h in range(4):
                hs = slice(h * 128, (h + 1) * 128)
                self.mm(bu[:, hs], k4[pc:pc + 64, hs], ib[pc:pc + 64, hs], True, True, [k4_b, ib_b], [bu_b])
            for h in range(4):
                hs = slice(h * 128, (h + 1) * 128)
                self.vstt(self.S32[:, hs], self.S32[:, hs], el[:, 2 * h + c:2 * h + c + 1], bu[:, hs], ALU.mult, ALU.add,
                          [self.S32_b, el_b, bu_b], [self.S32_b])
            self.actf(self.Sbf[:], self.S32[:], AF.Copy, [self.S32_b], [self.Sbf_b])
        if HG < 5:
            self.banks.release(kbo)
            return
        return lambda: self.hgrn_out(j, sub, kbo, bo, bo_b, gs, gs_b, pB, pB_b, ident)

    def hgrn_front(self, j, T, sub):
        qs, qs_b = self.ar[:, 8 + sub, :], self.ar_b[8 + sub]
        sg, sg_b = self.stage[:, sub, :], self.stage_b[sub]
        ident = self.C("ident")
        fA, fA_b = self.t32.get()
        f = fA[:, 0:512]
        self.vtt(f, sg, self.lb[:, 1, :], ALU.mult, [sg_b, self.lb_b], [fA_b])
        self.vtt(f, f, self.lb[:, 0, :], ALU.add, [fA_b, self.lb_b], [fA_b])
        lfB, lfB_b = self.t32.get()
        lf = lfB[:, 0:512]
        self.actf(lf, f, AF.Ln, [fA_b], [lfB_b])
        self.vts(f, f, -1.0, 1.0, ALU.mult, ALU.add, [fA_b], [fA_b])
        bd1, bd1_b = self.banks.get()
        bb, bb_b = self.banks.get()
        bd4, bd4_b = self.banks.get()
        self.mm(bd1[:], self.C("trid1", bf=False), lf, True, True, [lfB_b, self.c_b], [bd1_b])
        self.mm(bb[:], self.C("tri2", bf=False), lf, True, True, [lfB_b, self.c_b], [bb_b])
        self.mm(bd4[:], self.C("trisuf", bf=False), lf, True, True, [lfB_b, self.c_b], [bd4_b])
        bz, bz_b = self.banks.get()
        for h in range(4):
            self.mm(bz[:, 2 * h:2 * h + 2], lf[:, h * 128:(h + 1) * 128], self.C("chunkones", bf=False, cols=2), True, True,
                    [lfB_b, self.c_b], [bz_b])
        el, el_b = self.sm_pool.get()
        self.actf(el, bz[:, 0:8], AF.Exp, [bz_b], [el_b])
        pA, pA_b = self.pbt[:, 0:512], self.pbt_b[0]
        pB, pB_b = self.pbt[:, 512:1024], self.pbt_b[0]
        E, E_b = self.t32.get()
        kq1, (q1, q1_b) = self.t16.reserve()
        self.actf(E[:, 0:512], bd1[:], AF.Exp, [bd1_b], [E_b])
        self.vtt(q1[:], qs, E[:, 0:512], ALU.mult, [qs_b, E_b], [q1_b])
        E2, E2_b = self.t32.get()
        kk1, (k1, k1_b) = self.t16.reserve()
        self.actf(E2[:, 0:512], bd1[:], AF.Exp, [bd1_b], [E2_b], scale=-1.0)
        self.vtt(k1[:], f, E2[:, 0:512], ALU.mult, [fA_b, E2_b], [k1_b])
        for h in range(4):
            self.tr(pA[:, h * 128:(h + 1) * 128], q1[:, h * 128:(h + 1) * 128], ident, [q1_b, self.c_b], [pA_b])
        for h in range(4):
            self.tr(pB[:, h * 128:(h + 1) * 128], k1[:, h * 128:(h + 1) * 128], ident, [k1_b, self.c_b], [pB_b])
        self.t16.release(kq1)
        self.t16.release(kk1)
        kq1T, (q1T, q1T_b) = self.t16.reserve()
        kk1T, (k1T, k1T_b) = self.t16.reserve()
        self.vcopy(q1T[:], pA, [pA_b], [q1T_b])
        self.vcopy(k1T[:], pB, [pB_b], [k1T_b])
        bs, bs_b = self.banks.get()
        for c in range(2):
            for h in range(4):
                self.mm(bs[64 * c:64 * c + 64, h * 64:(h + 1) * 64],
                        k1T[:, h * 128 + 64 * c:h * 128 + 64 * c + 64], q1T[:, h * 128 + 64 * c:h * 128 + 64 * c + 64],
                        True, True, [k1T_b, q1T_b], [bs_b])
        self.t16.release(kq1T)
        self.t16.release(kk1T)
        kscm, (scm, scm_b) = self.t16.reserve()
        self.vtt(scm[:, 0:256], bs[:, 0:256], self.C("maskS"), ALU.mult, [bs_b, self.c_b], [scm_b])
        E3, E3_b = self.t32.get()
        kq3, (q3, q3_b) = self.t16.reserve()
        self.actf(E3[:, 0:512], bb[:], AF.Exp, [bb_b], [E3_b])
        self.vtt(q3[:], qs, E3[:, 0:512], ALU.mult, [qs_b, E3_b], [q3_b])
        for h in range(4):
            self.tr(pA[:, h * 128:(h + 1) * 128], q3[:, h * 128:(h + 1) * 128], ident, [q3_b, self.c_b], [pA_b])
        self.t16.release(kq3)
        kq3T, (q3T, q3T_b) = self.t16.reserve()
        self.vcopy(q3T[:], pA, [pA_b], [q3T_b])
        E4, E4_b = self.t32.get()
        kk4, (k4, k4_b) = self.t16.reserve()
        self.actf(E4[:, 0:512], bd4[:], AF.Exp, [bd4_b], [E4_b])
        self.vtt(k4[:], f, E4[:, 0:512], ALU.mult, [fA_b, E4_b], [k4_b])
        return dict(j=j, sub=sub, el=el, el_b=el_b, scm=scm, scm_b=scm_b, kscm=kscm, q3T=q3T, q3T_b=q3T_b, kq3T=kq3T,
                    k4=k4, k4_b=k4_b, kk4=kk4)

    def hgrn_mid(self, st):
        sub = st["sub"]
        ib, ib_b = self.ar[:, 12 + sub, :], self.ar_b[12 + sub]
        el, el_b, scm, scm_b, q3T, q3T_b, k4, k4_b = st["el"], st["el_b"], st["scm"], st["scm_b"], st["q3T"], st["q3T_b"], st["k4"], st["k4_b"]
        kbo, (bo, bo_b) = self.banks.reserve()
        for c in range(2):
            pc = 64 * c
            for h in range(4):
                hs = slice(h * 128, (h + 1) * 128)
                self.mm(bo[pc:pc + 64, hs], q3T[:, h * 128 + pc:h * 128 + pc + 64], self.Sbf[:, hs], True, False,
                        [q3T_b, self.Sbf_b], [bo_b])
                self.mm(bo[pc:pc + 64, hs], scm[pc:pc + 64, h * 64:(h + 1) * 64], ib[pc:pc + 64, hs], False, True,
                        [scm_b, ib_b], [bo_b])
            bu, bu_b = self.banks.get()
            for h in range(4):
                hs = slice(h * 128, (h + 1) * 128)
                self.mm(bu[:, hs], k4[pc:pc + 64, hs], ib[pc:pc + 64, hs], True, True, [k4_b, ib_b], [bu_b])
            for h in range(4):
                hs = slice(h * 128, (h + 1) * 128)
                self.vstt(self.S32[:, hs], self.S32[:, hs], el[:, 2 * h + c:2 * h + c + 1], bu[:, hs], ALU.mult, ALU.add,
                          [self.S32_b, el_b, bu_b], [self.S32_b])
            self.actf(self.Sbf[:], self.S32[:], AF.Copy, [self.S32_b], [self.Sbf_b])
        self.t16.release(st["kscm"])
        self.t16.release(st["kq3T"])
        self.t16.release(st["kk4"])
        gs, gs_b = self.ar[:, 16 + sub, :], self.ar_b[16 + sub]
        pB, pB_b = self.pbt[:, 512:1024], self.pbt_b[0]
        ident = self.C("ident")
        j = st["j"]
        return lambda: self.hgrn_out(j, sub, kbo, bo, bo_b, gs, gs_b, pB, pB_b, ident)

    def g_front(self, j, T, sub, st):
        qs, qs_b = self.ar[:, 8 + sub, :], self.ar_b[8 + sub]
        sg, sg_b = self.stage[:, sub, :], self.stage_b[sub]
        ident = self.C("ident")
        pA, pB, p_b = self.pbt[:, 0:512], self.pbt[:, 512:1024], self.pbt_b[0]
        kfA, (fA, fA_b) = self.t32.reserve()
        f = fA[:, 0:512]
        self.vtt(f, sg, self.lb[:, 1, :], ALU.mult, [sg_b, self.lb_b], [fA_b])
        self.vtt(f, f, self.lb[:, 0, :], ALU.add, [fA_b, self.lb_b], [fA_b])
        klf, (lfB, lfB_b) = self.t32.reserve()
        lf = lfB[:, 0:512]
        self.actf(lf, f, AF.Ln, [fA_b], [lfB_b])
        self.vts(f, f, -1.0, 1.0, ALU.mult, ALU.add, [fA_b], [fA_b])
        yield
        k1_, (bd1, bd1_b) = self.banks.reserve()
        k2_, (bb, bb_b) = self.banks.reserve()
        k3_, (bd4, bd4_b) = self.banks.reserve()
        self.mm(bd1[:], self.C("trid1", bf=False), lf, True, True, [lfB_b, self.c_b], [bd1_b])
        self.mm(bb[:], self.C("tri2", bf=False), lf, True, True, [lfB_b, self.c_b], [bb_b])
        self.mm(bd4[:], self.C("trisuf", bf=False), lf, True, True, [lfB_b, self.c_b], [bd4_b])
        k4_, (bz, bz_b) = self.banks.reserve()
        for h in range(4):
            self.mm(bz[:, 2 * h:2 * h + 2], lf[:, h * 128:(h + 1) * 128], self.C("chunkones", bf=False, cols=2), True, True,
                    [lfB_b, self.c_b], [bz_b])
        self.t32.release(klf)
        yield
        el, el_b = self.sm_pool.get()
        self.actf(el, bz[:, 0:8], AF.Exp, [bz_b], [el_b])
        self.banks.release(k4_)
        kE, (E, E_b) = self.t32.reserve()
        kq1, (q1, q1_b) = self.t16.reserve()
        self.actf(E[:, 0:512], bd1[:], AF.Exp, [bd1_b], [E_b])
        self.vtt(q1[:], qs, E[:, 0:512], ALU.mult, [qs_b, E_b], [q1_b])
        kk1, (k1, k1_b) = self.t16.reserve()
        self.actf(E[:, 0:512], bd1[:], AF.Exp, [bd1_b, E_b], [E_b], scale=-1.0)
        self.vtt(k1[:], f, E[:, 0:512], ALU.mult, [fA_b, E_b], [k1_b])
        self.banks.release(k1_)
        yield
        for h in range(4):
            self.tr(pA[:, h * 128:(h + 1) * 128], q1[:, h * 128:(h + 1) * 128], ident, [q1_b, self.c_b], [p_b])
        for h in range(4):
            self.tr(pB[:, h * 128:(h + 1) * 128], k1[:, h * 128:(h + 1) * 128], ident, [k1_b, self.c_b], [p_b])
        self.t16.release(kq1)
        self.t16.release(kk1)
        kq1T, (q1T, q1T_b) = self.t16.reserve()
        kk1T, (k1T, k1T_b) = self.t16.reserve()
        self.vcopy(q1T[:], pA, [p_b], [q1T_b])
        self.vcopy(k1T[:], pB, [p_b], [k1T_b])
        yield
        kbs, (bs, bs_b) = self.banks.reserve()
        for c in range(2):
            for h in range(4):
                self.mm(bs[64 * c:64 * c + 64, h * 64:(h + 1) * 64],
                        k1T[:, h * 128 + 64 * c:h * 128 + 64 * c + 64], q1T[:, h * 128 + 64 * c:h * 128 + 64 * c + 64],
                        True, True, [k1T_b, q1T_b], [bs_b])
        self.t16.release(kq1T)
        self.t16.release(kk1T)
        yield
        kscm, (scm, scm_b) = self.t16.reserve()
        self.vtt(scm[:, 0:256], bs[:, 0:256], self.C("maskS"), ALU.mult, [bs_b, self.c_b], [scm_b])
        self.banks.release(kbs)
        kq3, (q3, q3_b) = self.t16.reserve()
        self.actf(E[:, 0:512], bb[:], AF.Exp, [bb_b, E_b], [E_b])
        self.vtt(q3[:], qs, E[:, 0:512], ALU.mult, [qs_b, E_b], [q3_b])
        self.banks.release(k2_)
        yield
        for h in range(4):
            self.tr(pA[:, h * 128:(h + 1) * 128], q3[:, h * 128:(h + 1) * 128], ident, [q3_b, self.c_b], [p_b])
        self.t16.release(kq3)
        kq3T, (q3T, q3T_b) = self.t16.reserve()
        self.vcopy(q3T[:], pA, [p_b], [q3T_b])
        yield
        kk4, (k4, k4_b) = self.t16.reserve()
        self.actf(E[:, 0:512], bd4[:], AF.Exp, [bd4_b, E_b], [E_b])
        self.vtt(k4[:], f, E[:, 0:512], ALU.mult, [fA_b, E_b], [k4_b])
        self.banks.release(k3_)
        self.t32.release(kE)
        self.t32.release(kfA)
        st.update(dict(j=j, sub=sub, el=el, el_b=el_b, scm=scm, scm_b=scm_b, kscm=kscm, q3T=q3T, q3T_b=q3T_b, kq3T=kq3T,
                       k4=k4, k4_b=k4_b, kk4=kk4, front_done=True))

    def g_mid(self, st, prev):
        while prev is not None and not prev.get("mid_done"):
            yield
        sub = st["sub"]
        ib, ib_b = self.ar[:, 12 + sub, :], self.ar_b[12 + sub]
        el, el_b, scm, scm_b, q3T, q3T_b, k4, k4_b = st["el"], st["el_b"], st["scm"], st["scm_b"], st["q3T"], st["q3T_b"], st["k4"], st["k4_b"]
        kbo, (bo, bo_b) = self.banks.reserve()
        for c in range(2):
            pc = 64 * c
            for h in range(4):
                hs = slice(h * 128, (h + 1) * 128)
                self.mm(bo[pc:pc + 64, hs], q3T[:, h * 128 + pc:h * 128 + pc + 64], self.Sbf[:, hs], True, False,
                        [q3T_b, self.Sbf_b], [bo_b])
                self.mm(bo[pc:pc + 64, hs], scm[pc:pc + 64, h * 64:(h + 1) * 64], ib[pc:pc + 64, hs], False, True,
                        [scm_b, ib_b], [bo_b])
            kbu, (bu, bu_b) = self.banks.reserve()
            for h in range(4):
                hs = slice(h * 128, (h + 1) * 128)
                self.mm(bu[:, hs], k4[pc:pc + 64, hs], ib[pc:pc + 64, hs], True, True, [k4_b, ib_b], [bu_b])
            yield
            for h in range(4):
                hs = slice(h * 128, (h + 1) * 128)
                self.vstt(self.S32[:, hs], self.S32[:, hs], el[:, 2 * h + c:2 * h + c + 1], bu[:, hs], ALU.mult, ALU.add,
                          [self.S32_b, el_b, bu_b], [self.S32_b])
            self.banks.release(kbu)
            self.actf(self.Sbf[:], self.S32[:], AF.Copy, [self.S32_b], [self.Sbf_b])
            yield
        self.t16.release(st["kscm"])
        self.t16.release(st["kq3T"])
        self.t16.release(st["kk4"])
        st["kbo"], st["bo"], st["bo_b"] = kbo, bo, bo_b
        st["mid_done"] = True

    def g_out(self, st):
        nc = self.nc
        j, sub = st["j"], st["sub"]
        kbo, bo, bo_b = st["kbo"], st["bo"], st["bo_b"]
        gs, gs_b = self.ar[:, 16 + sub, :], self.ar_b[16 + sub]
        pB, p_b = self.pbt[:, 512:1024], self.pbt_b[0]
        ident = self.C("ident")
        ko1, (osb, osb_b) = self.t32.reserve()
        ko2, (osq, osq_b) = self.t32.reserve()
        o = osb[:, 0:512]
        self.actf(o, bo[:], AF.Copy, [bo_b], [osb_b])
        self.banks.release(kbo)
        yield
        self.vtt(osq[:, 0:512], o, o, ALU.mult, [osb_b], [osq_b])
        ss, ss_b = self.sm_pool.get()
        self.fw.op(self.fw.dve, lambda: nc.vector.tensor_reduce(out=ss[:, 0:4], in_=osq[:, 0:512].rearrange("p (h v) -> p h v", h=4), axis=AX.X, op=ALU.add),
                   [osq_b], [ss_b])
        self.actf(ss[:, 0:4], ss[:, 0:4], AF.Ln, [ss_b], [ss_b], bias=EPS, scale=1.0 / 128)
        self.actf(ss[:, 0:4], ss[:, 0:4], AF.Exp, [ss_b], [ss_b], scale=-0.5)
        yield
        gg = osq[:, 0:512]
        self.vtt(gg, gs, self.hgn[:], ALU.mult, [gs_b, self.hgn_b, osq_b], [osq_b])
        kyb, (yb, yb_b) = self.t16.reserve()
        for h in range(4):
            hs = slice(h * 128, (h + 1) * 128)
            self.vstt(yb[:, hs], o[:, hs], ss[:, h:h + 1], gg[:, hs], ALU.mult, ALU.mult, [osb_b, ss_b, osq_b], [yb_b])
        yield
        for h in range(4):
            self.tr(pB[:, h * 128:(h + 1) * 128], yb[:, h * 128:(h + 1) * 128], ident, [yb_b, self.c_b], [p_b])
        self.vcopy(self.hn[:, 4:8, sub * 128:(sub + 1) * 128], pB.rearrange("p (h t) -> p h t", h=4), [p_b], [self.hn_b[4 + h] for h in range(4)])
        self.t16.release(kyb)
        self.t32.release(ko1)
        self.t32.release(ko2)

    def hgrn_tile_gen(self, j, T):
        sts = [dict() for _ in range(4)]
        pending_f = list(range(4))
        pending_m = list(range(4))
        pending_o = list(range(4))
        act_f = act_m = act_o = None
        while pending_o or act_o is not None:
            if act_f is None and pending_f:
                s_ = pending_f.pop(0)
                act_f = (s_, self.g_front(j, T, s_, sts[s_]))
            if act_m is None and pending_m and sts[pending_m[0]].get("front_done"):
                s_ = pending_m.pop(0)
                act_m = (s_, self.g_mid(sts[s_], sts[s_ - 1] if s_ > 0 else None))
            if act_o is None and pending_o and sts[pending_o[0]].get("mid_done"):
                s_ = pending_o.pop(0)
                act_o = (s_, self.g_out(sts[s_]))
            for name in ("f", "m", "o"):
                cur = {"f": act_f, "m": act_m, "o": act_o}[name]
                if cur is None:
                    continue
                try:
                    next(cur[1])
                except StopIteration:
                    if name == "f":
                        act_f = None
                    elif name == "m":
                        act_m = None
                    else:
                        act_o = None

    def hgrn_out(self, j, sub, kbo, bo, bo_b, gs, gs_b, pB, pB_b, ident):
        nc = self.nc
        osb, osb_b = self.t32.get()
        o = osb[:, 0:512]
        self.actf(o, bo[:], AF.Copy, [bo_b], [osb_b])
        self.banks.release(kbo)
        osq, osq_b = self.t32.get()
        self.vtt(osq[:, 0:512], o, o, ALU.mult, [osb_b], [osq_b])
        ss, ss_b = self.sm_pool.get()
        self.fw.op(self.fw.dve, lambda: nc.vector.tensor_reduce(out=ss[:, 0:4], in_=osq[:, 0:512].rearrange("p (h v) -> p h v", h=4), axis=AX.X, op=ALU.add),
                   [osq_b], [ss_b])
        self.actf(ss[:, 0:4], ss[:, 0:4], AF.Ln, [ss_b], [ss_b], bias=EPS, scale=1.0 / 128)
        self.actf(ss[:, 0:4], ss[:, 0:4], AF.Exp, [ss_b], [ss_b], scale=-0.5)
        gg = osq[:, 0:512]
        self.vtt(gg, gs, self.hgn[:], ALU.mult, [gs_b, self.hgn_b, osq_b], [osq_b])
        yb, yb_b = self.t16.get()
        for h in range(4):
            hs = slice(h * 128, (h + 1) * 128)
            self.vstt(yb[:, hs], o[:, hs], ss[:, h:h + 1], gg[:, hs], ALU.mult, ALU.mult, [osb_b, ss_b, osq_b], [yb_b])
        for h in range(4):
            self.tr(pB[:, h * 128:(h + 1) * 128], yb[:, h * 128:(h + 1) * 128], ident, [yb_b, self.c_b], [pB_b])
        self.vcopy(self.hn[:, 4:8, sub * 128:(sub + 1) * 128], pB.rearrange("p (h t) -> p h t", h=4), [pB_b], [self.hn_b[4 + h] for h in range(4)])

    def qknorm(self, bank, bank_b, gain_ap, out_ap, out_bufs):
        sq, sq_b = self.t16.get()
        self.actf(sq[:], bank[:], AF.Square, [bank_b], [sq_b])
        b2, b2_b = self.banks.get()
        self.mm(b2[:], self.C("bones"), sq[:], True, True, [sq_b, self.c_b], [b2_b])
        r, r_b = self.t32.get()
        rr = r[:, 0:TT]
        self.actf(rr, b2[:], AF.Ln, [b2_b], [r_b], bias=EPS, scale=1.0 / 64)
        self.actf(rr, rr, AF.Exp, [r_b], [r_b], scale=-0.5)
        self.vstt(out_ap, bank[:], gain_ap, rr, ALU.mult, ALU.mult, [bank_b, r_b, self.small_b], out_bufs)

    def mixer_odd(self, s, j, T):
        nc, fw = self.nc, self.fw
        t0 = T * TT
        w = self.I["c_w_in"][j]
        hnf = lambda k: self.hn[:, k, :]
        gq = self.qkn[:, j, 0:1]
        gk = self.qkn[:, j, 1:2]
        for half in range(2):
            slot, slot_b = self.load_w(w[:, half * 512:(half + 1) * 512])
            for m in range(4):
                bank, bank_b = self.proj_fm(slot, slot_b, m * 128, hnf, NCH, self.hn_b)
                cidx = half * 4 + m
                self.qknorm(bank, bank_b, gq, self.ar[:, cidx, :], [self.ar_b[cidx]])
        slot, slot_b = self.load_w(w[:, 1024:1536])
        for m in range(2):
            bank, bank_b = self.proj_fm(slot, slot_b, m * 128, hnf, NCH, self.hn_b)
            self.qknorm(bank, bank_b, gk, self.kbuf[:, m, t0:t0 + TT], [self.kb_b[m][T]])
            bank, bank_b = self.banks.get()
            for k in range(NCH):
                self.mm(bank[0:64, :], slot[:, k, m * 128 + 64:m * 128 + 128], self.hn[:, k, :], k == 0, k == NCH - 1, [slot_b, self.hn_b[k]], [bank_b])
            for k in range(NCH):
                self.mm(bank[64:128, :], slot[:, k, m * 128:m * 128 + 64], self.hn[:, k, :], k == 0, k == NCH - 1, [slot_b, self.hn_b[k]], [bank_b])
            self.qknorm(bank, bank_b, gk, self.kbuf[:, 2 + m, t0:t0 + TT], [self.kb_b[2 + m][T]])
        for sub in range(4):
            bank, bank_b = self.banks.get()
            for k in range(NCH):
                self.mm(bank[:, 0:256], self.hn[:, k, sub * 128:(sub + 1) * 128], slot[:, k, 256:512], k == 0, k == NCH - 1,
                        [slot_b, self.hn_b[k]], [bank_b])
            self.actf(self.vbuf[:, T * 4 + sub, 0:256], bank[:, 0:256], AF.Copy, [bank_b], [self.vb_b[T * 4 + sub]])
        onesbf = self.C("ones", cols=64)
        import os
        OD = float(os.environ.get("K_ODD", "9"))
        if OD < 2:
            return
        self.preconvert(2 * j + 1)
        HPERM = [0, 2, 1, 3]
        units = [(qb, g) for qb in range(4) for g in range(4)]

        def stage_a(qb, g):
            B = 4 * T + qb
            kbs = [B - 1, B] if B > 0 else [B]
            tmps = [self.t32.get() for _ in kbs]
            for par in range(2):
                bs, bs_b = self.banks.get()
                for ki, kb in enumerate(kbs):
                    for ii in range(2):
                        i = par * 2 + ii
                        h = 4 * g + HPERM[i]
                        hp = (h % 2) * 64
                        var = 0 if (g % 2) == (h % 2) else 2
                        kc = var + g // 2
                        c0_ = ki * 256 + ii * 128
                        self.mm(bs[:, c0_:c0_ + 128], self.kbuf[hp:hp + 64, kc, kb * 128:(kb + 1) * 128],
                                self.ar[hp:hp + 64, h // 2, qb * 128:(qb + 1) * 128], True, True,
                                [self.kb_b[kc][kb // 4], self.ar_b[h // 2]], [bs_b])
                for ki, kb in enumerate(kbs):
                    ksel = 0 if kb == B - 1 else 1
                    tmp, tmp_b = tmps[ki]
                    self.vtt(tmp[:, par * 256:(par + 1) * 256], bs[:, ki * 256:(ki + 1) * 256],
                             self.bias[:, ksel, 4 * g + 2 * par:4 * g + 2 * par + 2, :].rearrange("p a b -> p (a b)"), ALU.add,
                             [bs_b, self.bias_b], [tmp_b])
            es = []
            for ki, kb in enumerate(kbs):
                tmp, tmp_b = tmps[ki]
                ke, (e, e_b) = self.t16.reserve()
                self.actf(e[:], tmp[:, 0:512], AF.Exp, [tmp_b], [e_b])
                es.append((kb, ke, e, e_b))
            return es

        nd_sets = [(self.banks.reserve(), self.banks.reserve()) for _ in range(2)]

        def stage_b(qb, g, es):
            hf = g // 2
            (kN, (nbk, nbk_b)), (kD, (dbk, dbk_b)) = nd_sets[(qb * 2 + hf) % 2]
            for i in range(4):
                h = 4 * g + HPERM[i]
                hp = (h % 2) * 64
                c = h // 2
                cs = slice((c % 4) * 128, (c % 4) * 128 + 128)
                for n_, (kb, ke, e, e_b) in enumerate(es):
                    self.mm(nbk[hp:hp + 64, cs], self.vbuf[:, kb, g * 64:(g + 1) * 64], e[:, i * 128:(i + 1) * 128],
                            n_ == 0, n_ == len(es) - 1, [self.vb_b[kb], e_b], [nbk_b])
                for n_, (kb, ke, e, e_b) in enumerate(es):
                    self.mm(dbk[hp:hp + 64, cs], onesbf, e[:, i * 128:(i + 1) * 128],
                            n_ == 0, n_ == len(es) - 1, [self.c_b, e_b], [dbk_b])
            for (kb, ke, e, e_b) in es:
                self.t16.release(ke)
            if g % 2 == 1:
                r, r_b = self.t32.get()
                for cq in range(4):
                    c = hf * 4 + cq
                    self.actf(r[:, cq * 128:(cq + 1) * 128], dbk[:, cq * 128:(cq + 1) * 128], AF.Ln, [dbk_b, self.small_b], [r_b],
                              bias=self.snk[:, j, c:c + 1])
                self.actf(r[:, 0:512], r[:, 0:512], AF.Exp, [r_b], [r_b], scale=-1.0)
                self.vtt(self.hn[:, hf * 4:(hf + 1) * 4, qb * 128:(qb + 1) * 128],
                         nbk[:].rearrange("p (c t) -> p c t", c=4), r[:, 0:512].rearrange("p (c t) -> p c t", c=4), ALU.mult,
                         [nbk_b, r_b], [self.hn_b[hf * 4 + i] for i in range(4)])

        pend = stage_a(*units[0])
        for ui, (qb, g) in enumerate(units):
            nxt = stage_a(*units[ui + 1]) if ui + 1 < len(units) else None
            stage_b(qb, g, pend)
            pend = nxt
        for (a_, b_) in nd_sets:
            self.banks.release(a_[0])
            self.banks.release(b_[0])


def _host_small(inp):
    f32 = np.float32
    norms = np.stack([inp["mix_norm"], inp["ffn_norm"], inp["ple_norm"]], 0).astype(f32)
    norms = np.ascontiguousarray(norms.reshape(3, DEPTH, NCH, 128).transpose(3, 0, 1, 2))
    cw = np.concatenate([inp["ffn_conv"].astype(f32), inp["ffn_conv_b"].astype(f32)[:, None, :]], 1)
    cw = np.ascontiguousarray(cw.reshape(DEPTH, 4, 44, 128).transpose(3, 0, 1, 2))
    lbl = np.ascontiguousarray(inp["hg_lb_logits"].astype(f32).reshape(1, 1024))
    hgn = np.ascontiguousarray(np.tile(inp["hg_out_norm"].astype(f32), (1, 4)).reshape(1, 1024))
    qkn = np.stack([np.tile(inp["q_norm"].astype(f32), (1, 2)), np.tile(inp["k_norm"].astype(f32), (1, 2))], -1)
    qkn = np.ascontiguousarray(qkn.transpose(1, 0, 2))
    snk = inp["sinks"].astype(f32).reshape(2, NCH, 2)
    snk = np.ascontiguousarray(np.repeat(snk.transpose(2, 0, 1), 64, axis=0))
    bucket, valid = _t5_bias_index()
    rb = inp["rel_bias"].astype(f32)
    bt = rb[bucket]
    bt = np.where(valid[..., None], bt, f32(NEG)).transpose(0, 1, 3, 2)
    hperm = np.array([4 * g + i for g in range(4) for i in (0, 2, 1, 3)])
    bt = bt[:, :, hperm, :]
    biasT = np.ascontiguousarray(bt.reshape(128, 2 * 16 * 128).astype(f32))
    cf, cb = _const_arrays()
    return dict(norms=norms, convw=cw, lbl=lbl, hgn=hgn, qkn=qkn, snk=snk, biasT=biasT, cf=cf, cb=cb)


_CACHE = {}


def _run(inputs, nseq, layers, ncores, dbg=None, trace=False):
    key = (nseq, tuple(layers), tuple(sorted(dbg.items())) if dbg else None)
    f32 = np.float32
    small = _host_small(inputs)
    shared = {k: np.ascontiguousarray(np.asarray(inputs[k], dtype=f32)) for k in
              ["ab_w_in", "ab_w_out", "c_w_in", "c_w_out", "ffn_up", "ffn_down", "ple_gate", "ple_proj"]}
    x = np.asarray(inputs["x"], dtype=f32)
    p = np.asarray(inputs["p"], dtype=f32)
    in_maps = []
    for ci in range(ncores):
        sl = slice(ci * nseq, (ci + 1) * nseq)
        m = dict(shared)
        m.update(small)
        m["xT"] = np.ascontiguousarray(x[sl].transpose(0, 2, 1))
        m["pT"] = np.ascontiguousarray(p[:, sl].transpose(0, 1, 3, 2))
        in_maps.append(m)
    nc = Builder(nseq, layers, dbg).build()
    res = run_bass_kernel_spmd(nc, in_maps, core_ids=list(range(ncores)))
    return res


def kernel(**inputs):
    res = _run(inputs, 2, list(range(DEPTH)), 8)
    outs = [r["outT"] for r in res.results]
    out = np.concatenate(outs, axis=0).transpose(0, 2, 1)
    return np.ascontiguousarray(out.astype(np.float32))
```

```python
import math
from contextlib import ExitStack

import numpy as np

import concourse.bass as bass
import concourse.mybir as mybir
from concourse.bass_utils import run_bass_kernel_spmd

F32 = mybir.dt.float32
BF16 = mybir.dt.bfloat16
AF = mybir.ActivationFunctionType
ALU = mybir.AluOpType
AX = mybir.AxisListType

D = 1024
S = 2048
DEPTH = 4
NCH = 8
TT = 512
NTILE = S // TT
F_FF = 2816
NFC = 22
EPS = 1e-6
NEG = -30000.0
import os as _os
_OPT = _os.environ.get("K_OPT", "normbg,normacc,lock,hggen,preconv").split(",")
OPT_SBEARLY = "sbearly" in _OPT
OPT_NORMACC = "normacc" in _OPT
OPT_NORMBG = "normbg" in _OPT
OPT_LOCK = "lock" in _OPT
OPT_HGPIPE = "hgpipe" in _OPT
OPT_HGGEN = "hggen" in _OPT
OPT_PRECONV = "preconv" in _OPT


class Buf:
    __slots__ = ("w", "r")

    def __init__(self):
        self.w = None
        self.r = {}


class Stream:
    def __init__(self, name, h, pe=False):
        self.name = name
        self.h = h
        self.pe = pe
        self.seq = 0
        self.know = {}
        self.oplist = []
        self.sem = None


class Chan:
    def __init__(self, name):
        self.name = name
        self.cnt = 0
        self.last = None
        self.sem = None


class Op:
    __slots__ = ("st", "fn", "waits", "needed", "seq", "val", "chan")


class FW:
    def __init__(self, nc, es):
        self.nc = nc
        self.es = es
        self.ops = []
        self.pe = self._mk("pe", nc.tensor, True)
        self.act = self._mk("act", nc.scalar)
        self.dve = self._mk("dve", nc.vector)
        self.pool = self._mk("pool", nc.gpsimd)
        self.sp = self._mk("sp", nc.sync)
        self.streams = [self.pe, self.act, self.dve, self.pool, self.sp]
        self.chans = []
        self.sp_ch = [self._ch("spc%d" % i) for i in range(6)]
        self.pl_ch = [self._ch("plc%d" % i) for i in range(4)]
        self._spi = 0
        self._pli = 0

    def _mk(self, name, h, pe=False):
        st = Stream(name, h, pe)
        st.sem = self.es.enter_context(self.nc.semaphore("s_" + name))
        return st

    def _ch(self, name):
        c = Chan(name)
        c.sem = self.es.enter_context(self.nc.semaphore("c_" + name))
        self.chans.append(c)
        return c

    def _need(self, st, waits, ev, raw):
        if ev is None:
            return
        s, n, kn = ev
        if s is st:
            if st.pe:
                return
        if st.know.get(s, 0) >= n:
            return
        waits.append((s, n))
        newk = dict(st.know)
        newk[s] = n
        for a, b in kn.items():
            if newk.get(a, 0) < b:
                newk[a] = b
        st.know = newk

    def op(self, st, fn, reads=(), writes=(), chan=None):
        waits = []
        for b in reads:
            self._need(st, waits, b.w, True)
        for b in writes:
            self._need(st, waits, b.w, False)
            for ev in b.r.values():
                self._need(st, waits, ev, False)
        if chan is not None:
            self._need(st, waits, chan.last, False)
        st.seq += 1
        o = Op()
        o.st = st
        o.fn = fn
        o.waits = waits
        o.needed = False
        o.seq = st.seq
        o.val = 0
        o.chan = chan
        self.ops.append(o)
        st.oplist.append(o)
        if chan is not None:
            chan.cnt += 16 * (len(fn) if isinstance(fn, (list, tuple)) else 1)
            ev = (chan, chan.cnt, st.know)
            chan.last = ev
        else:
            ev = (st, st.seq, st.know)
        for b in reads:
            b.r[ev[0]] = ev
        for b in writes:
            b.w = ev
            b.r = {}
        return ev

    def dma_sp(self, fn, reads=(), writes=()):
        ch = self.sp_ch[self._spi % len(self.sp_ch)]
        self._spi += 1
        return self.op(self.sp, fn, reads, writes, chan=ch)

    def dma_pool(self, fn, reads=(), writes=()):
        ch = self.pl_ch[self._pli % len(self.pl_ch)]
        self._pli += 1
        return self.op(self.pool, fn, reads, writes, chan=ch)

    def finish(self, final_events):
        waits = []
        for ev in final_events:
            self._need(self.sp, waits, ev, True)
        for o in self.ops:
            for (s, n) in o.waits:
                if isinstance(s, Stream):
                    s.oplist[n - 1].needed = True
        for (s, n) in waits:
            if isinstance(s, Stream):
                s.oplist[n - 1].needed = True
        for st in self.streams:
            c = 0
            for o in st.oplist:
                if o.needed:
                    c += 1
                o.val = c

        def emit_wait(st, s, n):
            if isinstance(s, Stream):
                st.h.wait_ge(s.sem, s.oplist[n - 1].val)
            else:
                st.h.wait_ge(s.sem, n)

        nw = 0
        for o in self.ops:
            for (s, n) in o.waits:
                emit_wait(o.st, s, n)
                nw += 1
            if isinstance(o.fn, (list, tuple)):
                for f_ in o.fn:
                    f_().then_inc(o.chan.sem, 16)
                continue
            ins = o.fn()
            if o.chan is not None:
                ins.then_inc(o.chan.sem, 16)
            elif o.needed:
                ins.then_inc(o.st.sem, 1)
        for (s, n) in waits:
            emit_wait(self.sp, s, n)
        self.stats = dict(n_ops=len(self.ops), n_waits=nw,
                          per_stream={st.name: len(st.oplist) for st in self.streams})


class Pool_:
    def __init__(self, tiles):
        self.items = [(t, Buf()) for t in tiles]
        self.free = list(range(len(tiles)))

    def get(self):
        k = self.free.pop(0)
        self.free.append(k)
        return self.items[k]

    def reserve(self):
        k = self.free.pop(0)
        return k, self.items[k]

    def release(self, k):
        self.free.append(k)


def _consts():
    p = np.arange(128)[:, None]
    m = np.arange(128)[None, :]
    c = {}
    c["ident"] = (p == m).astype(np.float32)
    c["ones"] = np.ones((128, 128), np.float32)
    c["bones"] = ((p // 64) == (m // 64)).astype(np.float32)
    c["negtri"] = -(p >= m).astype(np.float32)
    c["sbmask"] = (p < m).astype(np.float32)
    same = (p // 64) == (m // 64)
    tri2 = (same & (p <= m)).astype(np.float32)
    mid = (m // 64) * 64 + 31
    trimid = (same & (p <= mid)).astype(np.float32)
    c["tri2"] = tri2
    c["trid1"] = tri2 - trimid
    c["trisuf"] = (same & (p > m)).astype(np.float32)
    co = np.zeros((128, 128), np.float32)
    co[:, 0] = (np.arange(128) < 64)
    co[:, 1] = (np.arange(128) >= 64)
    c["chunkones"] = co
    mk = np.zeros((128, 256), np.float32)
    cc = np.arange(256)[None, :]
    mk[:, :] = ((p % 64) <= (cc % 64))
    c["maskS"] = mk
    return c


F32_CONSTS = ["tri2", "trid1", "trisuf", "chunkones"]
BF_CONSTS = ["ident", "ones", "bones", "negtri", "sbmask", "maskS"]


def _const_arrays():
    c = _consts()
    f = np.concatenate([c[k] for k in F32_CONSTS], axis=1)
    b = np.concatenate([c[k] for k in BF_CONSTS], axis=1)
    return np.ascontiguousarray(f), np.ascontiguousarray(b)


def _offsets(names, c):
    off = {}
    o = 0
    for k in names:
        off[k] = o
        o += c[k].shape[1]
    return off, o


def _t5_bias_index():
    W = 128
    t = np.arange(W)[None, None, :]
    s = np.arange(W)[:, None, None]
    kb = np.arange(2)[None, :, None]
    dist = t + W - (kb * W + s)
    valid = (dist >= 0) & (dist < W)
    max_exact = 16
    large = max_exact + (np.log(np.maximum(dist, max_exact) / max_exact) / math.log(128 / max_exact) * (32 - max_exact)).astype(np.int32)
    large = np.minimum(large, 31)
    bucket = np.where(dist < max_exact, np.maximum(dist, 0), large).astype(np.int32)
    return bucket, valid


class Builder:
    def __init__(self, nseq, layers, dbg=None):
        self.nseq = nseq
        self.layers = layers
        self.dbg = dbg

    def build(self):
        nc = bass.Bass("TRN2", target_bir_lowering=False)
        self.nc = nc
        nseq = self.nseq
        dt = nc.dram_tensor
        I = {}
        I["xT"] = dt("xT", [nseq, D, S], F32, kind="ExternalInput").ap()
        I["pT"] = dt("pT", [DEPTH, nseq, 256, S], F32, kind="ExternalInput").ap()
        I["ab_w_in"] = dt("ab_w_in", [2, D, 3584], F32, kind="ExternalInput").ap()
        I["ab_w_out"] = dt("ab_w_out", [2, D, D], F32, kind="ExternalInput").ap()
        I["c_w_in"] = dt("c_w_in", [2, D, 1536], F32, kind="ExternalInput").ap()
        I["c_w_out"] = dt("c_w_out", [2, D, D], F32, kind="ExternalInput").ap()
        I["ffn_up"] = dt("ffn_up", [DEPTH, D, 2 * F_FF], F32, kind="ExternalInput").ap()
        I["ffn_down"] = dt("ffn_down", [DEPTH, F_FF, D], F32, kind="ExternalInput").ap()
        I["ple_gate"] = dt("ple_gate", [DEPTH, D, D], F32, kind="ExternalInput").ap()
        I["ple_proj"] = dt("ple_proj", [DEPTH, 256, D], F32, kind="ExternalInput").ap()
        I["norms"] = dt("norms", [128, 3, DEPTH, NCH], F32, kind="ExternalInput").ap()
        I["convw"] = dt("convw", [128, DEPTH, 4, 44], F32, kind="ExternalInput").ap()
        I["lbl"] = dt("lbl", [1, 2 * 512], F32, kind="ExternalInput").ap()
        I["hgn"] = dt("hgn", [1, 2 * 512], F32, kind="ExternalInput").ap()
        I["qkn"] = dt("qkn", [128, 2, 2], F32, kind="ExternalInput").ap()
        I["snk"] = dt("snk", [128, 2, NCH], F32, kind="ExternalInput").ap()
        I["biasT"] = dt("biasT", [128, 2 * 16 * 128], F32, kind="ExternalInput").ap()
        cf, cb = _const_arrays()
        I["cf"] = dt("cf", list(cf.shape), F32, kind="ExternalInput").ap()
        I["cb"] = dt("cb", list(cb.shape), F32, kind="ExternalInput").ap()
        self.I = I
        self.outT = dt("outT", [nseq, D, S], F32, kind="ExternalOutput").ap()
        self.wscr = dt("wscr", [120, 128, 4096], BF16, kind="Internal").ap()
        self.wimg = {}
        self.preconv_done = set()
        if self.dbg:
            self.dbg_out = {k: dt("dbg_" + k, list(shp), F32, kind="ExternalOutput").ap() for k, shp in self.dbg.items()}
        c = _consts()
        self.cf_off, self.cf_n = _offsets(F32_CONSTS, c)
        self.cb_off, self.cb_n = _offsets(BF_CONSTS, c)

        with ExitStack() as es:
            self.es = es
            fw = FW(nc, es)
            self.fw = fw
            sb = lambda name, shape, dtype: es.enter_context(nc.sbuf_tensor(name, shape, dtype))
            self.res = sb("res", [128, NCH, S], F32)
            self.res_b = [[Buf() for _ in range(NTILE)] for _ in range(NCH)]
            self.hn = sb("hn", [128, NCH, TT], BF16)
            self.hn_b = [Buf() for _ in range(NCH)]
            self.ar = sb("arena", [128, NFC, TT], BF16)
            self.ar_b = [Buf() for _ in range(NFC)]
            self.kbuf = sb("kbuf", [128, 4, S], BF16)
            self.kb_b = [[Buf() for _ in range(NTILE)] for _ in range(4)]
            self.vbuf = sb("vbuf", [128, 16, 512], BF16)
            self.vb_b = [Buf() for _ in range(16)]
            self.bias = sb("bias", [128, 2, 16, 128], BF16)
            self.bias_b = Buf()
            self.ws = [sb("ws%d" % i, [128, NCH, 512], BF16) for i in range(3)]
            self.wpool = Pool_(self.ws)
            self.stage = sb("stage", [128, 4, 512], F32)
            self.stage_b = [Buf() for _ in range(4)]
            stf = self.stage[:].rearrange("p a b -> p (a b)")
            self.upool = Pool_([stf[:, i * 520:i * 520 + 516] for i in range(3)])
            t32 = [sb("t32_%d" % i, [128, 516], F32) for i in range(7)]
            self.t32 = Pool_(t32)
            t16 = [sb("t16_%d" % i, [128, 512], BF16) for i in range(7)]
            self.t16 = Pool_(t16)
            self.cf = sb("cf_sb", [128, self.cf_n], F32)
            self.cb = sb("cb_sb", [128, self.cb_n], BF16)
            self.c_b = Buf()
            self.norms = sb("norms_sb", [128, 3, DEPTH, NCH], F32)
            self.convw = sb("convw_sb", [128, DEPTH, 4, 44], F32)
            self.qkn = sb("qkn_sb", [128, 2, 2], F32)
            self.snk = sb("snk_sb", [128, 2, NCH], F32)
            self.lb = sb("lb_sb", [128, 2, 512], F32)
            self.lb1 = sb("lb1_sb", [128, 512], F32)
            self.hgn = sb("hgn_sb", [128, 512], F32)
            self.hgn_b = Buf()
            self.small_b = Buf()
            self.lb_b = Buf()
            self.S32 = sb("S32", [128, 512], F32)
            self.S32_b = Buf()
            self.Sbf = sb("Sbf", [128, 512], BF16)
            self.Sbf_b = Buf()
            self.halo = sb("halo", [128, 2, 44, 2], F32)
            self.halo_b = [Buf(), Buf()]
            self.sm = sb("smalls", [128, 64], F32)
            self.sm_pool = Pool_([self.sm[:, i * 8:(i + 1) * 8] for i in range(8)])
            self.negrow = sb("negrow", [128, 128], BF16)
            self.sbR_b = {0: Buf(), 64: Buf()}
            self.sbrt_b = {0: [Buf(), Buf(), Buf()], 64: [Buf(), Buf(), Buf()]}
            banks = [es.enter_context(nc.psum_tensor("pb%d" % i, [128, 512], F32)) for i in range(7)]
            self.banks = Pool_(banks)
            self.pbt = es.enter_context(nc.psum_tensor("pbt", [128, 1024], BF16))
            self.pbt_b = [Buf(), Buf()]

            self.acc = None
            self.pre_rstd = None
            self.prologue()
            finals = []
            for s in range(nseq):
                self.load_x(s)
                for li in self.layers:
                    self.layer(s, li)
                finals += self.store_out(s)
            if self.dbg:
                finals += self.dbg_events
            fw.finish(finals)
        return nc

    def C(self, name, bf=True, cols=None):
        if bf:
            o = self.cb_off[name]
            n = _consts()[name].shape[1] if cols is None else cols
            return self.cb[:, o:o + n]
        o = self.cf_off[name]
        n = _consts()[name].shape[1] if cols is None else cols
        return self.cf[:, o:o + n]

    def mm(self, out, lhsT, rhs, start, stop, reads, writes, skip=False):
        nc = self.nc
        if skip:
            return self.fw.op(self.fw.pe, lambda: nc.tensor.matmul(out, lhsT, rhs, start=start, stop=stop, skip_group_check=True), reads, writes)
        return self.fw.op(self.fw.pe, lambda: nc.tensor.matmul(out, lhsT, rhs, start=start, stop=stop), reads, writes)

    def tr(self, out, in_, ident, reads, writes):
        nc = self.nc
        return self.fw.op(self.fw.pe, lambda: nc.tensor.transpose(out, in_, ident), reads, writes)

    def actf(self, out, in_, func, reads, writes, bias=0.0, scale=1.0):
        nc = self.nc
        return self.fw.op(self.fw.act, lambda: nc.scalar.activation(out=out, in_=in_, func=func, bias=bias, scale=scale), reads, writes)

    def vtt(self, out, in0, in1, op, reads, writes):
        nc = self.nc
        return self.fw.op(self.fw.dve, lambda: nc.vector.tensor_tensor(out=out, in0=in0, in1=in1, op=op), reads, writes)

    def vts(self, out, in0, s1, s2, op0, op1, reads, writes):
        nc = self.nc
        if op1 is None:
            return self.fw.op(self.fw.dve, lambda: nc.vector.tensor_scalar(out=out, in0=in0, scalar1=s1, scalar2=None, op0=op0), reads, writes)
        return self.fw.op(self.fw.dve, lambda: nc.vector.tensor_scalar(out=out, in0=in0, scalar1=s1, scalar2=s2, op0=op0, op1=op1), reads, writes)

    def vstt(self, out, in0, scalar, in1, op0, op1, reads, writes):
        nc = self.nc
        return self.fw.op(self.fw.dve, lambda: nc.vector.scalar_tensor_tensor(out=out, in0=in0, scalar=scalar, in1=in1, op0=op0, op1=op1), reads, writes)

    def vcopy(self, out, in_, reads, writes):
        nc = self.nc
        return self.fw.op(self.fw.dve, lambda: nc.vector.tensor_copy(out, in_), reads, writes)

    def vrecip(self, out, in_, reads, writes):
        nc = self.nc
        return self.fw.op(self.fw.dve, lambda: nc.vector.reciprocal(out=out, in_=in_), reads, writes)

    def pcopy(self, out, in_, reads, writes):
        nc = self.nc
        return self.fw.op(self.fw.pool, lambda: nc.gpsimd.tensor_copy(out, in_), reads, writes)

    def load_w(self, src_ap, nk=NCH, ncols=512, src2=None, bgq=False):
        nc = self.nc
        key = (src_ap.tensor.name, str(src_ap.offset), tuple(tuple(x) for x in src_ap.ap))
        slot, b = self.wpool.get()
        img = self.wimg.get(key)
        if img is None:
            idx = len(self.wimg)
            ib = Buf()
            self.wimg[key] = (idx, ib)
            if src2 is None:
                dst = slot[:, 0:nk, 0:ncols]
                src = src_ap.rearrange("(c p) n -> p c n", p=128)
                self.fw.dma_pool(lambda: nc.gpsimd.dma_start(out=dst, in_=src), reads=(), writes=[b])
            else:
                h = ncols // 2
                fns = []
                for i_, sa in enumerate((src_ap, src2)):
                    dst = slot[:, 0:nk, i_ * h:(i_ + 1) * h]
                    src = sa.rearrange("(c p) n -> p c n", p=128)
                    fns.append(lambda dst=dst, src=src: nc.gpsimd.dma_start(out=dst, in_=src))
                self.fw.dma_pool(fns, reads=(), writes=[b])
            if ncols == 512:
                simg = self.wscr[idx, :, 0:nk * 512]
                ssrc = slot[:, 0:nk, :].rearrange("p c n -> p (c n)")
                if bgq:
                    self.bg_pending.append((lambda: nc.gpsimd.dma_start(out=simg, in_=ssrc), b, ib))
                    while len(self.bg_pending) > 2:
                        f_, b_, ib_ = self.bg_pending.pop(0)
                        self.fw.dma_pool(f_, reads=[b_], writes=[ib_])
                else:
                    self.fw.dma_sp(lambda: nc.sync.dma_start(out=simg, in_=ssrc), reads=[b], writes=[ib])
            else:
                self.wimg[key] = None
                del self.wimg[key]
        else:
            idx, ib = img
            simg = self.wscr[idx, :, 0:nk * 512]
            sdst = slot[:, 0:nk, :].rearrange("p c n -> p (c n)")
            self.fw.dma_sp(lambda: nc.sync.dma_start(out=sdst, in_=simg), reads=[ib], writes=[b])
        return slot, b

    def preconvert(self, li):
        if li in self.preconv_done or not OPT_PRECONV:
            return
        self.preconv_done.add(li)
        self.bg_pending = []
        j = li // 2
        wo = self.I["ab_w_out"][j] if li % 2 == 0 else self.I["c_w_out"][j]
        for half in range(2):
            self.load_w(wo[:, half * 512:(half + 1) * 512], bgq=True)
        wup = self.I["ffn_up"][li]
        for j0 in range(0, NFC, 2):
            self.load_w(wup[:, j0 * 128:(j0 + 2) * 128], ncols=512, src2=wup[:, F_FF + j0 * 128:F_FF + (j0 + 2) * 128], bgq=True)
        while self.bg_pending:
            f_, b_, ib_ = self.bg_pending.pop(0)
            self.fw.dma_pool(f_, reads=[b_], writes=[ib_])

    def dump(self, key, sb_ap, bufs, dst=None):
        if not self.dbg or key not in self.dbg:
            return
        nc = self.nc
        d = self.dbg_out[key] if dst is None else dst
        ev = self.fw.dma_sp(lambda: nc.sync.dma_start(out=d, in_=sb_ap), reads=bufs, writes=())
        self.dbg_events.append(ev)

    def prologue(self):
        nc, fw, I = self.nc, self.fw, self.I
        self.dbg_events = []
        fw.dma_sp(lambda: nc.sync.dma_start(out=self.cf[:], in_=I["cf"]), writes=[self.c_b])
        fw.dma_pool(lambda: nc.gpsimd.dma_start(out=self.cb[:], in_=I["cb"]), writes=[self.c_b])
        fw.dma_sp(lambda: nc.sync.dma_start(out=self.norms[:], in_=I["norms"]), writes=[self.small_b])
        fw.dma_sp(lambda: nc.sync.dma_start(out=self.convw[:], in_=I["convw"]), writes=[self.small_b])
        fw.dma_sp(lambda: nc.sync.dma_start(out=self.qkn[:], in_=I["qkn"]), writes=[self.small_b])
        fw.dma_sp(lambda: nc.sync.dma_start(out=self.snk[:], in_=I["snk"]), writes=[self.small_b])
        l0, l0_b = self.t32.get()
        l1, l1_b = self.t32.get()
        fw.dma_sp(lambda: nc.sync.dma_start(out=l0[:, 0:512], in_=I["lbl"][0:1, 0:512].partition_broadcast(128)), writes=[l0_b])
        fw.dma_sp(lambda: nc.sync.dma_start(out=l1[:, 0:512], in_=I["lbl"][0:1, 512:1024].partition_broadcast(128)), writes=[l1_b])
        self.vtt(l1[:, 0:512], l1[:, 0:512], l0[:, 0:512], ALU.subtract, [l0_b, l1_b], [l1_b])
        self.actf(self.lb1[:], l1[:, 0:512], AF.Sigmoid, [l1_b], [self.small_b])
        fw.dma_pool(lambda: nc.gpsimd.dma_start(out=self.bias[:].rearrange("p a b c -> p (a b c)"), in_=I["biasT"]), writes=[self.bias_b])
        fw.op(fw.dve, lambda: nc.vector.memset(self.negrow[:], -1.0), writes=[self.c_b])
        self.vts(self.qkn[:, :, 0:1], self.qkn[:, :, 0:1], 0.125, None, ALU.mult, None, [self.small_b], [self.small_b])
        self.actf(self.snk[:], self.snk[:], AF.Exp, [self.small_b], [self.small_b])

    def load_x(self, s):
        nc, fw = self.nc, self.fw
        for c in range(NCH):
            src = self.I["xT"][s, c * 128:(c + 1) * 128, :]
            dst = self.res[:, c, :]
            fw.dma_sp(lambda dst=dst, src=src: nc.sync.dma_start(out=dst, in_=src), writes=self.res_b[c])

    def store_out(self, s):
        nc, fw = self.nc, self.fw
        evs = []
        for c in range(NCH):
            dst = self.outT[s, c * 128:(c + 1) * 128, :]
            src = self.res[:, c, :]
            evs.append(fw.dma_sp(lambda dst=dst, src=src: nc.sync.dma_start(out=dst, in_=src), reads=self.res_b[c]))
        return evs

    def rmsnorm(self, T, which, li):
        t0 = T * TT
        if which == 0 and self.pre_rstd is not None and self.pre_rstd[0] == (li, T):
            _, kr, rr, r_b = self.pre_rstd
            self.pre_rstd = None
        elif self.acc is not None and self.acc["T"] == T and self.acc["n"] == NCH:
            kr, rr, r_b = self.stats_finish()
        else:
            self.stats_begin(T)
            for c in range(NCH):
                self.stats_add(c)
            kr, rr, r_b = self.stats_finish()
        for c in range(NCH):
            g = self.norms[:, which, li, c:c + 1]
            self.vstt(self.hn[:, c, :], self.res[:, c, t0:t0 + TT], g, rr, ALU.mult, ALU.mult,
                      [self.res_b[c][T], r_b, self.small_b], [self.hn_b[c]])
        self.t32.release(kr)

    def stats_begin(self, T):
        kb, (ssb, ssb_b) = self.banks.reserve()
        self.acc = dict(T=T, n=0, kb=kb, ssb=ssb, ssb_b=ssb_b, pend=None)

    def _stats_flush(self):
        a = self.acc
        if a["pend"] is not None:
            sq, sq_b, first, last = a["pend"]
            self.mm(a["ssb"][:], self.C("ones"), sq[:], first, last, [sq_b, self.c_b], [a["ssb_b"]])
            a["pend"] = None

    def stats_add(self, c):
        a = self.acc
        T = a["T"]
        t0 = T * TT
        self._stats_flush()
        sq, sq_b = self.t16.get()
        self.actf(sq[:], self.res[:, c, t0:t0 + TT], AF.Square, [self.res_b[c][T]], [sq_b])
        a["pend"] = (sq, sq_b, a["n"] == 0, a["n"] == NCH - 1)
        a["n"] += 1

    def stats_finish(self):
        a = self.acc
        self._stats_flush()
        kr, (r, r_b) = self.t32.reserve()
        rr = r[:, 0:TT]
        self.actf(rr, a["ssb"][:], AF.Ln, [a["ssb_b"]], [r_b], bias=EPS, scale=1.0 / D)
        self.actf(rr, rr, AF.Exp, [r_b], [r_b], scale=-0.5)
        self.banks.release(a["kb"])
        self.acc = None
        return kr, rr, r_b

    def stats_bg(self, key, T):
        self.stats_begin(T)
        acc = self.acc
        self.acc = None
        for c in range(NCH):
            self.acc, sv = acc, self.acc
            self.stats_add(c)
            self.acc = sv
            yield
        self.acc, sv = acc, self.acc
        kr, rr, r_b = self.stats_finish()
        self.acc = sv
        self.pre_rstd = (key, kr, rr, r_b)
        yield

    def proj_fm(self, slot, slot_b, col0, rhs_fn, nk, rhs_bufs, ncols=128):
        bank, bank_b = self.banks.get()
        for k in range(nk):
            rb_ = [rhs_bufs[k]] if len(rhs_bufs) == nk else list(rhs_bufs)
            self.mm(bank[0:ncols, :], slot[:, k, col0:col0 + ncols], rhs_fn(k), k == 0, k == nk - 1,
                    [slot_b] + rb_, [bank_b])
        return bank, bank_b

    def add_to_res(self, bank, bank_b, n, T, stats=False):
        t0 = T * TT
        self.vtt(self.res[:, n, t0:t0 + TT], bank[:], self.res[:, n, t0:t0 + TT], ALU.add,
                 [bank_b, self.res_b[n][T]], [self.res_b[n][T]])
        if stats and OPT_NORMACC:
            self.stats_add(n)

    def out_proj(self, w_ap, T):
        if OPT_NORMACC:
            self.stats_begin(T)
        for half in range(2):
            slot, slot_b = self.load_w(w_ap[:, half * 512:(half + 1) * 512])
            for nq in range(4):
                bank, bank_b = self.proj_fm(slot, slot_b, nq * 128, lambda k: self.hn[:, k, :], NCH, self.hn_b)
                self.add_to_res(bank, bank_b, half * 4 + nq, T, stats=True)

    def layer(self, s, li):
        j = li // 2
        if li % 2 == 0:
            self.even_prep(j)
        import os
        st = os.environ.get("K_STAGES", "norm,mix,outp,ffn,ple,sb,hg").split(",")
        self.st = st
        for T in range(NTILE):
            if "norm" in st:
                self.rmsnorm(T, 0, li)
            if li % 2 == 0:
                if "mix" in st:
                    self.mixer_even(s, j, T)
                if "outp" in st:
                    self.out_proj(self.I["ab_w_out"][j], T)
            else:
                if "mix" in st:
                    self.mixer_odd(s, j, T)
                if "outp" in st:
                    self.out_proj(self.I["c_w_out"][j], T)
            self.dump("res_mix_L%d" % li, self.res[:, :, T * TT:(T + 1) * TT], [self.res_b[c][T] for c in range(NCH)],
                      dst=None if not self.dbg or ("res_mix_L%d" % li) not in self.dbg else self.dbg_out["res_mix_L%d" % li][:, :, T * TT:(T + 1) * TT])
            if "ffn" in st:
                bg = None
                if "norm" in st:
                    if T + 1 < NTILE:
                        nxt = (li, T + 1)
                    else:
                        k_ = self.layers.index(li)
                        nxt = (self.layers[k_ + 1], 0) if k_ + 1 < len(self.layers) else None
                    if nxt is not None and OPT_NORMBG:
                        bg = self.stats_bg(nxt, nxt[1])
                self.ffn(s, li, T, bg)
            if "ple" in st:
                self.ple(s, li, T)
        if s == 0:
            self.dump("res_L%d" % li, self.res[:], [b for c in range(NCH) for b in self.res_b[c]])

    def ffn(self, s, li, T, bg=None):
        nc, fw = self.nc, self.fw
        t0 = T * TT
        self.rmsnorm(T, 1, li)
        cur, nxt = T % 2, (T + 1) % 2
        if T == 0:
            fw.op(fw.dve, lambda: nc.vector.memset(self.halo[:, 0, :, :], 0.0), writes=[self.halo_b[0]])
        wup = self.I["ffn_up"][li]
        groups = [(j0, 2) for j0 in range(0, NFC, 2)]
        cw = self.convw
        for (j0, nj) in groups:
            if bg is not None and j0 >= 2:
                next(bg, None)
            slot, slot_b = self.load_w(wup[:, j0 * 128:(j0 + nj) * 128], ncols=512,
                                       src2=wup[:, F_FF + j0 * 128:F_FF + (j0 + nj) * 128])
            for jj in range(nj):
                jp = j0 + jj
                ys = []
                for (col0, idx) in ((jj * 128, jp), (nj * 128 + jj * 128, NFC + jp)):
                    bank, bank_b = self.proj_fm(slot, slot_b, col0, lambda k: self.hn[:, k, :], NCH, self.hn_b)
                    u, u_b = self.upool.get()
                    y, y_b = self.t32.get()
                    self.actf(u[:, 2:2 + TT], bank[:], AF.Copy, [bank_b], [u_b])
                    self.actf(u[:, 0:2], self.halo[:, cur, idx, :], AF.Copy, [self.halo_b[cur]], [u_b])
                    self.actf(self.halo[:, nxt, idx, :], bank[:, TT - 2:TT], AF.Copy, [bank_b], [self.halo_b[nxt]])
                    self.actf(y[:, 0:TT], bank[:], AF.Identity, [bank_b, self.small_b], [y_b],
                              bias=cw[:, li, 3, idx:idx + 1], scale=cw[:, li, 2, idx:idx + 1])
                    self.vstt(y[:, 0:TT], u[:, 1:1 + TT], cw[:, li, 1, idx:idx + 1], y[:, 0:TT], ALU.mult, ALU.add,
                              [u_b, y_b, self.small_b], [y_b])
                    self.vstt(y[:, 0:TT], u[:, 0:TT], cw[:, li, 0, idx:idx + 1], y[:, 0:TT], ALU.mult, ALU.add,
                              [u_b, y_b, self.small_b], [y_b])
                    ys.append((y, y_b))
                (yg, yg_b), (yu, yu_b) = ys
                self.actf(yg[:, 0:TT], yg[:, 0:TT], AF.Silu, [yg_b], [yg_b])
                self.vtt(self.ar[:, jp, :], yg[:, 0:TT], yu[:, 0:TT], ALU.mult, [yg_b, yu_b], [self.ar_b[jp]])
        if bg is not None:
            for _ in bg:
                pass
        wd = self.I["ffn_down"][li]
        jgs = [(0, 8), (8, 8), (16, 6)]
        if OPT_NORMACC:
            self.stats_begin(T)
        for nh in range(2):
            bks = [self.banks.get() for _ in range(4)]
            for (j0, nj) in jgs:
                slot, slot_b = self.load_w(wd[j0 * 128:(j0 + nj) * 128, nh * 512:(nh + 1) * 512], nk=nj)
                for jj in range(nj):
                    jf = j0 + jj
                    for nq in range(4):
                        self.mm(bks[nq][0][:], slot[:, jj, nq * 128:(nq + 1) * 128], self.ar[:, jf, :], jf == 0, jf == NFC - 1,
                                [slot_b, self.ar_b[jf]], [bks[nq][1]])
            for nq in range(4):
                self.add_to_res(bks[nq][0], bks[nq][1], nh * 4 + nq, T, stats=True)

    def ple(self, s, li, T):
        nc, fw = self.nc, self.fw
        t0 = T * TT
        self.rmsnorm(T, 2, li)
        src = self.I["pT"][li, s, :, t0:t0 + TT].rearrange("(c p) t -> p c t", p=128)
        pbuf = self.ar[:, 20:22, :]
        pbuf_bs = [self.ar_b[20], self.ar_b[21]]
        fw.dma_pool(lambda: nc.gpsimd.dma_start(out=pbuf, in_=src), writes=pbuf_bs)
        for half in range(2):
            sg, sg_b = self.load_w(self.I["ple_gate"][li][:, half * 512:(half + 1) * 512])
            spj, spj_b = self.load_w(self.I["ple_proj"][li][:, half * 512:(half + 1) * 512], nk=2)
            for nq in range(4):
                n = half * 4 + nq
                bg, bg_b = self.proj_fm(sg, sg_b, nq * 128, lambda k: self.hn[:, k, :], NCH, self.hn_b)
                bp, bp_b = self.proj_fm(spj, spj_b, nq * 128, lambda k: self.ar[:, 20 + k, :], 2, pbuf_bs)
                g, g_b = self.t32.get()
                self.actf(g[:, 0:TT], bg[:], AF.Sigmoid, [bg_b], [g_b])
                self.vtt(g[:, 0:TT], g[:, 0:TT], bp[:], ALU.mult, [g_b, bp_b], [g_b])
                self.vtt(self.res[:, n, t0:t0 + TT], g[:, 0:TT], self.res[:, n, t0:t0 + TT], ALU.add,
                         [g_b, self.res_b[n][T]], [self.res_b[n][T]])

    def even_prep(self, j):
        nc, fw = self.nc, self.fw
        if j == 0:
            fw.op(fw.dve, lambda: nc.vector.memset(self.lb[:, 0, :], 0.0), writes=[self.lb_b])
        else:
            self.vcopy(self.lb[:, 0, :], self.lb1[:], [self.small_b], [self.lb_b])
        src = self.I["hgn"][0:1, j * 512:(j + 1) * 512].partition_broadcast(128)
        fw.dma_sp(lambda: nc.sync.dma_start(out=self.hgn[:], in_=src), writes=[self.hgn_b])
        self.vts(self.lb[:, 1, :], self.lb[:, 0, :], -1.0, 1.0, ALU.mult, ALU.add, [self.lb_b], [self.lb_b])
        fw.op(fw.dve, lambda: nc.vector.memset(self.S32[:], 0.0), writes=[self.S32_b])
        fw.op(fw.dve, lambda: nc.vector.memset(self.Sbf[:], 0.0), writes=[self.Sbf_b])

    def mixer_even(self, s, j, T):
        nc, fw = self.nc, self.fw
        t0 = T * TT
        w = self.I["ab_w_in"][j]
        hnf = lambda k: self.hn[:, k, :]
        slot, slot_b = self.load_w(w[:, 0:512])
        for m in range(4):
            bank, bank_b = self.proj_fm(slot, slot_b, m * 128, hnf, NCH, self.hn_b)
            self.actf(self.ar[:, m, :], bank[:], AF.Copy, [bank_b], [self.ar_b[m]], scale=0.125)
        slot, slot_b = self.load_w(w[:, 512:1024])
        for m in range(4):
            bank, bank_b = self.proj_fm(slot, slot_b, m * 128, hnf, NCH, self.hn_b)
            self.actf(self.kbuf[:, m, t0:t0 + TT], bank[:], AF.Copy, [bank_b], [self.kb_b[m][T]])
        def tok_block(col0, evac):
            slot, slot_b = self.load_w(w[:, col0:col0 + 512])
            for sub in range(4):
                bank, bank_b = self.banks.get()
                for k in range(NCH):
                    self.mm(bank[:], self.hn[:, k, sub * 128:(sub + 1) * 128], slot[:, k, :], k == 0, k == NCH - 1,
                            [slot_b, self.hn_b[k]], [bank_b])
                evac(sub, bank, bank_b)
        tok_block(1024, lambda sub, bank, bank_b: self.actf(self.vbuf[:, T * 4 + sub, :], bank[:], AF.Copy, [bank_b], [self.vb_b[T * 4 + sub]]))
        tok_block(1536, lambda sub, bank, bank_b: self.actf(self.ar[:, 8 + sub, :], bank[:], AF.Silu, [bank_b], [self.ar_b[8 + sub]]))
        tok_block(2048, lambda sub, bank, bank_b: self.actf(self.stage[:, sub, :], bank[:], AF.Sigmoid, [bank_b], [self.stage_b[sub]]))
        tok_block(2560, lambda sub, bank, bank_b: self.actf(self.ar[:, 12 + sub, :], bank[:], AF.Copy, [bank_b], [self.ar_b[12 + sub]]))
        tok_block(3072, lambda sub, bank, bank_b: self.actf(self.ar[:, 16 + sub, :], bank[:], AF.Silu, [bank_b], [self.ar_b[16 + sub]]))
        self.preconvert(2 * j)
        if "sb" in self.st:
            for pair in ((0, 1), (2, 3), (4, 5), (6, 7)):
                gens = [self.sb_chain(T, h) for h in pair]
                while gens:
                    for g in list(gens):
                        try:
                            next(g)
                        except StopIteration:
                            gens.remove(g)
        if "hg" in self.st and OPT_HGGEN:
            self.hgrn_tile_gen(j, T)
        elif "hg" in self.st and OPT_HGPIPE:
            fr = {0: self.hgrn_front(j, T, 0), 1: self.hgrn_front(j, T, 1)}
            outs = {}
            outs[0] = self.hgrn_mid(fr.pop(0))
            fr[2] = self.hgrn_front(j, T, 2)
            outs[1] = self.hgrn_mid(fr.pop(1))
            outs.pop(0)()
            fr[3] = self.hgrn_front(j, T, 3)
            outs[2] = self.hgrn_mid(fr.pop(2))
            outs.pop(1)()
            outs[3] = self.hgrn_mid(fr.pop(3))
            outs.pop(2)()
            outs.pop(3)()
        elif "hg" in self.st:
            prev_out = None
            for sub in range(4):
                out = self.hgrn_sub(j, T, sub)
                if prev_out is not None:
                    prev_out()
                prev_out = out
            if prev_out is not None:
                prev_out()

    def sb_chain(self, T, h):
        nc, fw = self.nc, self.fw
        hp = (h % 2) * 64
        pr = 64 - hp
        hc = h // 2
        qT = self.ar[hp:hp + 64, hc, :]
        q_b = self.ar_b[hc]
        negtri = self.C("negtri")
        sbmask = self.C("sbmask")
        onescol = self.C("ones", cols=1)
        negrow = self.negrow[pr:pr + 1, :]
        kpv, (pvb, pvb_b) = self.banks.reserve()
        Rf = self.ar[pr:pr + 1, 4:6, :].rearrange("p a b -> p (a b)").bitcast(F32)
        Rf_b = self.sbR_b[pr]
        rts = [(self.ar[pr:pr + 1, 6, :], self.sbrt_b[pr][0]), (self.ar[pr:pr + 1, 7, :], self.sbrt_b[pr][1]),
               (self.ar[pr:pr + 1, 20, :], self.sbrt_b[pr][2])]
        fw.op(fw.dve, lambda: nc.vector.memset(Rf, 0.0), writes=[Rf_b])
        blocks = [(4 * T + kl, kl * 128, True) for kl in (3, 2, 1, 0)] + [(kb, 0, False) for kb in range(4 * T - 1, -1, -1)]
        nb = len(blocks)

        def stage_a(bi):
            kb, c0, diag = blocks[bi]
            kT = self.kbuf[hp:hp + 64, hc, kb * 128:(kb + 1) * 128]
            k_b = self.kb_b[hc][kb // 4]
            kx, (X, X_b) = self.banks.reserve()
            self.mm(X[:, c0:TT], kT, qT[:, c0:TT], True, True, [k_b, q_b], [X_b])
            if OPT_LOCK:
                yield
            ke, (e, e_b) = self.t32.reserve()
            self.actf(e[:, c0:TT], X[:, c0:TT], AF.Exp, [X_b], [e_b])
            kl, (lp, lp_b) = self.t16.reserve()
            self.actf(lp[:, c0:TT], e[:, c0:TT], AF.Ln, [e_b], [lp_b], bias=1.0)
            self.t32.release(ke)
            if diag:
                self.vtt(lp[:, c0:c0 + 128], lp[:, c0:c0 + 128], sbmask, ALU.mult, [lp_b, self.c_b], [lp_b])
            rnew = None
            if bi < nb - 1 and OPT_SBEARLY:
                self.mm(pvb[pr:pr + 1, c0:TT], onescol, lp[:, c0:TT], True, True, [lp_b, self.c_b], [pvb_b], skip=True)
                self.vtt(Rf[:, c0:TT], pvb[pr:pr + 1, c0:TT], Rf[:, c0:TT], ALU.add, [pvb_b, Rf_b], [Rf_b])
                rt, rt_b = rts[bi % 3]
                self.vcopy(rt[:, c0:TT], Rf[:, c0:TT], [Rf_b], [rt_b])
                rnew = (rt, rt_b)
            if False:
                yield
            return kx, X, X_b, kl, lp, lp_b, kT, k_b, rnew

        pend = {0: (yield from stage_a(0))}
        yield
        rprev = None
        pvq = []

        def flush_pv():
            while pvq:
                (bi_, kb_, c0_, wt_, wt_b_, kw_) = pvq.pop(0)
                self.mm(pvb[hp:hp + 64, c0_:TT], self.vbuf[:, kb_, h * 64:(h + 1) * 64], wt_[:, c0_:TT], bi_ == 0, bi_ == nb - 1,
                        [self.vb_b[kb_], wt_b_], [pvb_b], skip=True)
                self.t16.release(kw_)

        for bi in range(nb):
            if bi + 1 < nb:
                pend[bi + 1] = yield from stage_a(bi + 1)
                yield
            flush_pv()
            if OPT_LOCK:
                yield
            kb, c0, diag = blocks[bi]
            kx, X, X_b, kl, lp, lp_b, kT, k_b, rnew = pend.pop(bi)
            has_r = rprev is not None
            cR = c0 + 128 if diag else 0
            use_r = has_r and cR < TT
            self.mm(X[:, c0:TT], kT, qT[:, c0:TT], True, False, [k_b, q_b], [X_b])
            if OPT_LOCK:
                yield
            self.mm(X[:, c0:TT], negtri, lp[:, c0:TT], False, not use_r, [lp_b, self.c_b], [X_b])
            if OPT_LOCK:
                yield
            if use_r:
                rt, rt_b = rprev
                self.mm(X[:, cR:TT], negrow, rt[:, cR:TT], False, True, [rt_b, self.c_b], [X_b])
            if OPT_LOCK:
                yield
            kw, (wt, wt_b) = self.t16.reserve()
            self.actf(wt[:, c0:TT], X[:, c0:TT], AF.Exp, [X_b], [wt_b])
            self.banks.release(kx)
            if diag:
                self.vtt(wt[:, c0:c0 + 128], wt[:, c0:c0 + 128], sbmask, ALU.mult, [wt_b, self.c_b], [wt_b])
            pvq.append((bi, kb, c0, wt, wt_b, kw))
            if bi < nb - 1 and not OPT_SBEARLY:
                self.mm(pvb[pr:pr + 1, c0:TT], onescol, lp[:, c0:TT], True, True, [lp_b, self.c_b], [pvb_b], skip=True)
                if OPT_LOCK:
                    yield
                self.vtt(Rf[:, c0:TT], pvb[pr:pr + 1, c0:TT], Rf[:, c0:TT], ALU.add, [pvb_b, Rf_b], [Rf_b])
                rt, rt_b = rts[bi % 3]
                self.vcopy(rt[:, c0:TT], Rf[:, c0:TT], [Rf_b], [rt_b])
                rnew = (rt, rt_b)
            rprev = rnew
            self.t16.release(kl)
            yield
        flush_pv()
        self.actf(self.hn[hp:hp + 64, hc, :], pvb[hp:hp + 64, :], AF.Copy, [pvb_b], [self.hn_b[hc]])
        self.banks.release(kpv)

    def hgrn_sub(self, j, T, sub):
        nc, fw = self.nc, self.fw
        qs, qs_b = self.ar[:, 8 + sub, :], self.ar_b[8 + sub]
        ib, ib_b = self.ar[:, 12 + sub, :], self.ar_b[12 + sub]
        gs, gs_b = self.ar[:, 16 + sub, :], self.ar_b[16 + sub]
        sg, sg_b = self.stage[:, sub, :], self.stage_b[sub]
        ident = self.C("ident")
        fA, fA_b = self.t32.get()
        f = fA[:, 0:512]
        self.vtt(f, sg, self.lb[:, 1, :], ALU.mult, [sg_b, self.lb_b], [fA_b])
        self.vtt(f, f, self.lb[:, 0, :], ALU.add, [fA_b, self.lb_b], [fA_b])
        lfB, lfB_b = self.t32.get()
        lf = lfB[:, 0:512]
        self.actf(lf, f, AF.Ln, [fA_b], [lfB_b])
        self.vts(f, f, -1.0, 1.0, ALU.mult, ALU.add, [fA_b], [fA_b])
        bd1, bd1_b = self.banks.get()
        bb, bb_b = self.banks.get()
        bd4, bd4_b = self.banks.get()
        self.mm(bd1[:], self.C("trid1", bf=False), lf, True, True, [lfB_b, self.c_b], [bd1_b])
        self.mm(bb[:], self.C("tri2", bf=False), lf, True, True, [lfB_b, self.c_b], [bb_b])
        self.mm(bd4[:], self.C("trisuf", bf=False), lf, True, True, [lfB_b, self.c_b], [bd4_b])
        bz, bz_b = self.banks.get()
        for h in range(4):
            self.mm(bz[:, 2 * h:2 * h + 2], lf[:, h * 128:(h + 1) * 128], self.C("chunkones", bf=False, cols=2), True, True,
                    [lfB_b, self.c_b], [bz_b])
        el, el_b = self.sm_pool.get()
        self.actf(el, bz[:, 0:8], AF.Exp, [bz_b], [el_b])
        E, E_b = self.t32.get()
        q1, q1_b = self.t16.get()
        self.actf(E[:, 0:512], bd1[:], AF.Exp, [bd1_b], [E_b])
        self.vtt(q1[:], qs, E[:, 0:512], ALU.mult, [qs_b, E_b], [q1_b])
        E2, E2_b = self.t32.get()
        k1, k1_b = self.t16.get()
        self.actf(E2[:, 0:512], bd1[:], AF.Exp, [bd1_b], [E2_b], scale=-1.0)
        self.vtt(k1[:], f, E2[:, 0:512], ALU.mult, [fA_b, E2_b], [k1_b])
        E3, E3_b = self.t32.get()
        q3, q3_b = self.t16.get()
        self.actf(E3[:, 0:512], bb[:], AF.Exp, [bb_b], [E3_b])
        self.vtt(q3[:], qs, E3[:, 0:512], ALU.mult, [qs_b, E3_b], [q3_b])
        E4, E4_b = self.t32.get()
        k4, k4_b = self.t16.get()
        self.actf(E4[:, 0:512], bd4[:], AF.Exp, [bd4_b], [E4_b])
        self.vtt(k4[:], f, E4[:, 0:512], ALU.mult, [fA_b, E4_b], [k4_b])
        import os
        HG = float(os.environ.get("K_HG", "9"))
        if HG < 2:
            return
        pA, pA_b = self.pbt[:, 0:512], self.pbt_b[0]
        pB, pB_b = self.pbt[:, 512:1024], self.pbt_b[0]
        for h in range(4):
            self.tr(pA[:, h * 128:(h + 1) * 128], q1[:, h * 128:(h + 1) * 128], ident, [q1_b, self.c_b], [pA_b])
        for h in range(4):
            self.tr(pB[:, h * 128:(h + 1) * 128], k1[:, h * 128:(h + 1) * 128], ident, [k1_b, self.c_b], [pB_b])
        q1T, q1T_b = self.t16.get()
        k1T, k1T_b = self.t16.get()
        if HG < 2.1:
            return
        self.vcopy(q1T[:], pA, [pA_b], [q1T_b])
        if HG < 2.2:
            return
        self.vcopy(k1T[:], pB, [pB_b], [k1T_b])
        if HG < 2.3:
            return
        for h in range(4):
            self.tr(pA[:, h * 128:(h + 1) * 128], q3[:, h * 128:(h + 1) * 128], ident, [q3_b, self.c_b], [pA_b])
        q3T, q3T_b = self.t16.get()
        self.vcopy(q3T[:], pA, [pA_b], [q3T_b])
        if HG < 3:
            return
        bs, bs_b = self.banks.get()
        for c in range(2):
            for h in range(4):
                self.mm(bs[64 * c:64 * c + 64, h * 64:(h + 1) * 64],
                        k1T[:, h * 128 + 64 * c:h * 128 + 64 * c + 64], q1T[:, h * 128 + 64 * c:h * 128 + 64 * c + 64],
                        True, True, [k1T_b, q1T_b], [bs_b])
        scm, scm_b = self.t16.get()
        self.vtt(scm[:, 0:256], bs[:, 0:256], self.C("maskS"), ALU.mult, [bs_b, self.c_b], [scm_b])
        if HG < 4:
            return
        kbo, (bo, bo_b) = self.banks.reserve()
        for c in range(2):
            pc = 64 * c
            for h in range(4):
                hs = slice(h * 128, (h + 1) * 128)
                self.mm(bo[pc:pc + 64, hs], q3T[:, h * 128 + pc:h * 128 + pc + 64], self.Sbf[:, hs], True, False,
                        [q3T_b, self.Sbf_b], [bo_b])
                self.mm(bo[pc:pc + 64, hs], scm[pc:pc + 64, h * 64:(h + 1) * 64], ib[pc:pc + 64, hs], False, True,
                        [scm_b, ib_b], [bo_b])
            bu, bu_b = self.banks.get()
            for h in range(4):
                hs = slice(h * 128, (h + 1) * 128)
                self.mm(bu[:, hs], k4[pc:pc + 64, hs], ib[pc:pc + 64, hs], True, True, [k4_b, ib_b], [bu_b])
            for h in range(4):
                hs = slice(h * 128, (h + 1) * 128)
                self.vstt(self.S32[:, hs], self.S32[:, hs], el[:, 2 * h + c:2 * h + c + 1], bu[:, hs], ALU.mult, ALU.add,
                          [self.S32_b, el_b, bu_b], [self.S32_b])
            self.actf(self.Sbf[:], self.S32[:], AF.Copy, [self.S32_b], [self.Sbf_b])
        if HG < 5:
            self.banks.release(kbo)
            return
        return lambda: self.hgrn_out(j, sub, kbo, bo, bo_b, gs, gs_b, pB, pB_b, ident)

    def hgrn_front(self, j, T, sub):
        qs, qs_b = self.ar[:, 8 + sub, :], self.ar_b[8 + sub]
        sg, sg_b = self.stage[:, sub, :], self.stage_b[sub]
        ident = self.C("ident")
        fA, fA_b = self.t32.get()
        f = fA[:, 0:512]
        self.vtt(f, sg, self.lb[:, 1, :], ALU.mult, [sg_b, self.lb_b], [fA_b])
        self.vtt(f, f, self.lb[:, 0, :], ALU.add, [fA_b, self.lb_b], [fA_b])
        lfB, lfB_b = self.t32.get()
        lf = lfB[:, 0:512]
        self.actf(lf, f, AF.Ln, [fA_b], [lfB_b])
        self.vts(f, f, -1.0, 1.0, ALU.mult, ALU.add, [fA_b], [fA_b])
        bd1, bd1_b = self.banks.get()
        bb, bb_b = self.banks.get()
        bd4, bd4_b = self.banks.get()
        self.mm(bd1[:], self.C("trid1", bf=False), lf, True, True, [lfB_b, self.c_b], [bd1_b])
        self.mm(bb[:], self.C("tri2", bf=False), lf, True, True, [lfB_b, self.c_b], [bb_b])
        self.mm(bd4[:], self.C("trisuf", bf=False), lf, True, True, [lfB_b, self.c_b], [bd4_b])
        bz, bz_b = self.banks.get()
        for h in range(4):
            self.mm(bz[:, 2 * h:2 * h + 2], lf[:, h * 128:(h + 1) * 128], self.C("chunkones", bf=False, cols=2), True, True,
                    [lfB_b, self.c_b], [bz_b])
        el, el_b = self.sm_pool.get()
        self.actf(el, bz[:, 0:8], AF.Exp, [bz_b], [el_b])
        pA, pA_b = self.pbt[:, 0:512], self.pbt_b[0]
        pB, pB_b = self.pbt[:, 512:1024], self.pbt_b[0]
        E, E_b = self.t32.get()
        kq1, (q1, q1_b) = self.t16.reserve()
        self.actf(E[:, 0:512], bd1[:], AF.Exp, [bd1_b], [E_b])
        self.vtt(q1[:], qs, E[:, 0:512], ALU.mult, [qs_b, E_b], [q1_b])
        E2, E2_b = self.t32.get()
        kk1, (k1, k1_b) = self.t16.reserve()
        self.actf(E2[:, 0:512], bd1[:], AF.Exp, [bd1_b], [E2_b], scale=-1.0)
        self.vtt(k1[:], f, E2[:, 0:512], ALU.mult, [fA_b, E2_b], [k1_b])
        for h in range(4):
            self.tr(pA[:, h * 128:(h + 1) * 128], q1[:, h * 128:(h + 1) * 128], ident, [q1_b, self.c_b], [pA_b])
        for h in range(4):
            self.tr(pB[:, h * 128:(h + 1) * 128], k1[:, h * 128:(h + 1) * 128], ident, [k1_b, self.c_b], [pB_b])
        self.t16.release(kq1)
        self.t16.release(kk1)
        kq1T, (q1T, q1T_b) = self.t16.reserve()
        kk1T, (k1T, k1T_b) = self.t16.reserve()
        self.vcopy(q1T[:], pA, [pA_b], [q1T_b])
        self.vcopy(k1T[:], pB, [pB_b], [k1T_b])
        bs, bs_b = self.banks.get()
        for c in range(2):
            for h in range(4):
                self.mm(bs[64 * c:64 * c + 64, h * 64:(h + 1) * 64],
                        k1T[:, h * 128 + 64 * c:h * 128 + 64 * c + 64], q1T[:, h * 128 + 64 * c:h * 128 + 64 * c + 64],
                        True, True, [k1T_b, q1T_b], [bs_b])
        self.t16.release(kq1T)
        self.t16.release(kk1T)
        kscm, (scm, scm_b) = self.t16.reserve()
        self.vtt(scm[:, 0:256], bs[:, 0:256], self.C("maskS"), ALU.mult, [bs_b, self.c_b], [scm_b])
        E3, E3_b = self.t32.get()
        kq3, (q3, q3_b) = self.t16.reserve()
        self.actf(E3[:, 0:512], bb[:], AF.Exp, [bb_b], [E3_b])
        self.vtt(q3[:], qs, E3[:, 0:512], ALU.mult, [qs_b, E3_b], [q3_b])
        for h in range(4):
            self.tr(pA[:, h * 128:(h + 1) * 128], q3[:, h * 128:(h + 1) * 128], ident, [q3_b, self.c_b], [pA_b])
        self.t16.release(kq3)
        kq3T, (q3T, q3T_b) = self.t16.reserve()
        self.vcopy(q3T[:], pA, [pA_b], [q3T_b])
        E4, E4_b = self.t32.get()
        kk4, (k4, k4_b) = self.t16.reserve()
        self.actf(E4[:, 0:512], bd4[:], AF.Exp, [bd4_b], [E4_b])
        self.vtt(k4[:], f, E4[:, 0:512], ALU.mult, [fA_b, E4_b], [k4_b])
        return dict(j=j, sub=sub, el=el, el_b=el_b, scm=scm, scm_b=scm_b, kscm=kscm, q3T=q3T, q3T_b=q3T_b, kq3T=kq3T,
                    k4=k4, k4_b=k4_b, kk4=kk4)

    def hgrn_mid(self, st):
        sub = st["sub"]
        ib, ib_b = self.ar[:, 12 + sub, :], self.ar_b[12 + sub]
        el, el_b, scm, scm_b, q3T, q3T_b, k4, k4_b = st["el"], st["el_b"], st["scm"], st["scm_b"], st["q3T"], st["q3T_b"], st["k4"], st["k4_b"]
        kbo, (bo, bo_b) = self.banks.reserve()
        for c in range(2):
            pc = 64 * c
            for h in range(4):
                hs = slice(h * 128, (h + 1) * 128)
                self.mm(bo[pc:pc + 64, hs], q3T[:, h * 128 + pc:h * 128 + pc + 64], self.Sbf[:, hs], True, False,
                        [q3T_b, self.Sbf_b], [bo_b])
                self.mm(bo[pc:pc + 64, hs], scm[pc:pc + 64, h * 64:(h + 1) * 64], ib[pc:pc + 64, hs], False, True,
                        [scm_b, ib_b], [bo_b])
            bu, bu_b = self.banks.get()
            for h in range(4):
                hs = slice(h * 128, (h + 1) * 128)
                self.mm(bu[:, hs], k4[pc:pc + 64, hs], ib[pc:pc + 64, hs], True, True, [k4_b, ib_b], [bu_b])
            for h in range(4):
                hs = slice(h * 128, (h + 1) * 128)
                self.vstt(self.S32[:, hs], self.S32[:, hs], el[:, 2 * h + c:2 * h + c + 1], bu[:, hs], ALU.mult, ALU.add,
                          [self.S32_b, el_b, bu_b], [self.S32_b])
            self.actf(self.Sbf[:], self.S32[:], AF.Copy, [self.S32_b], [self.Sbf_b])
        self.t16.release(st["kscm"])
        self.t16.release(st["kq3T"])
        self.t16.release(st["kk4"])
        gs, gs_b = self.ar[:, 16 + sub, :], self.ar_b[16 + sub]
        pB, pB_b = self.pbt[:, 512:1024], self.pbt_b[0]
        ident = self.C("ident")
        j = st["j"]
        return lambda: self.hgrn_out(j, sub, kbo, bo, bo_b, gs, gs_b, pB, pB_b, ident)

    def g_front(self, j, T, sub, st):
        qs, qs_b = self.ar[:, 8 + sub, :], self.ar_b[8 + sub]
        sg, sg_b = self.stage[:, sub, :], self.stage_b[sub]
        ident = self.C("ident")
        pA, pB, p_b = self.pbt[:, 0:512], self.pbt[:, 512:1024], self.pbt_b[0]
        kfA, (fA, fA_b) = self.t32.reserve()
        f = fA[:, 0:512]
        self.vtt(f, sg, self.lb[:, 1, :], ALU.mult, [sg_b, self.lb_b], [fA_b])
        self.vtt(f, f, self.lb[:, 0, :], ALU.add, [fA_b, self.lb_b], [fA_b])
        klf, (lfB, lfB_b) = self.t32.reserve()
        lf = lfB[:, 0:512]
        self.actf(lf, f, AF.Ln, [fA_b], [lfB_b])
        self.vts(f, f, -1.0, 1.0, ALU.mult, ALU.add, [fA_b], [fA_b])
        yield
        k1_, (bd1, bd1_b) = self.banks.reserve()
        k2_, (bb, bb_b) = self.banks.reserve()
        k3_, (bd4, bd4_b) = self.banks.reserve()
        self.mm(bd1[:], self.C("trid1", bf=False), lf, True, True, [lfB_b, self.c_b], [bd1_b])
        self.mm(bb[:], self.C("tri2", bf=False), lf, True, True, [lfB_b, self.c_b], [bb_b])
        self.mm(bd4[:], self.C("trisuf", bf=False), lf, True, True, [lfB_b, self.c_b], [bd4_b])
        k4_, (bz, bz_b) = self.banks.reserve()
        for h in range(4):
            self.mm(bz[:, 2 * h:2 * h + 2], lf[:, h * 128:(h + 1) * 128], self.C("chunkones", bf=False, cols=2), True, True,
                    [lfB_b, self.c_b], [bz_b])
        self.t32.release(klf)
        yield
        el, el_b = self.sm_pool.get()
        self.actf(el, bz[:, 0:8], AF.Exp, [bz_b], [el_b])
        self.banks.release(k4_)
        kE, (E, E_b) = self.t32.reserve()
        kq1, (q1, q1_b) = self.t16.reserve()
        self.actf(E[:, 0:512], bd1[:], AF.Exp, [bd1_b], [E_b])
        self.vtt(q1[:], qs, E[:, 0:512], ALU.mult, [qs_b, E_b], [q1_b])
        kk1, (k1, k1_b) = self.t16.reserve()
        self.actf(E[:, 0:512], bd1[:], AF.Exp, [bd1_b, E_b], [E_b], scale=-1.0)
        self.vtt(k1[:], f, E[:, 0:512], ALU.mult, [fA_b, E_b], [k1_b])
        self.banks.release(k1_)
        yield
        for h in range(4):
            self.tr(pA[:, h * 128:(h + 1) * 128], q1[:, h * 128:(h + 1) * 128], ident, [q1_b, self.c_b], [p_b])
        for h in range(4):
            self.tr(pB[:, h * 128:(h + 1) * 128], k1[:, h * 128:(h + 1) * 128], ident, [k1_b, self.c_b], [p_b])
        self.t16.release(kq1)
        self.t16.release(kk1)
        kq1T, (q1T, q1T_b) = self.t16.reserve()
        kk1T, (k1T, k1T_b) = self.t16.reserve()
        self.vcopy(q1T[:], pA, [p_b], [q1T_b])
        self.vcopy(k1T[:], pB, [p_b], [k1T_b])
        yield
        kbs, (bs, bs_b) = self.banks.reserve()
        for c in range(2):
            for h in range(4):
                self.mm(bs[64 * c:64 * c + 64, h * 64:(h + 1) * 64],
                        k1T[:, h * 128 + 64 * c:h * 128 + 64 * c + 64], q1T[:, h * 128 + 64 * c:h * 128 + 64 * c + 64],
                        True, True, [k1T_b, q1T_b], [bs_b])
        self.t16.release(kq1T)
        self.t16.release(kk1T)
        yield
        kscm, (scm, scm_b) = self.t16.reserve()
        self.vtt(scm[:, 0:256], bs[:, 0:256], self.C("maskS"), ALU.mult, [bs_b, self.c_b], [scm_b])
        self.banks.release(kbs)
        kq3, (q3, q3_b) = self.t16.reserve()
        self.actf(E[:, 0:512], bb[:], AF.Exp, [bb_b, E_b], [E_b])
        self.vtt(q3[:], qs, E[:, 0:512], ALU.mult, [qs_b, E_b], [q3_b])
        self.banks.release(k2_)
        yield
        for h in range(4):
            self.tr(pA[:, h * 128:(h + 1) * 128], q3[:, h * 128:(h + 1) * 128], ident, [q3_b, self.c_b], [p_b])
        self.t16.release(kq3)
        kq3T, (q3T, q3T_b) = self.t16.reserve()
        self.vcopy(q3T[:], pA, [p_b], [q3T_b])
        yield
        kk4, (k4, k4_b) = self.t16.reserve()
        self.actf(E[:, 0:512], bd4[:], AF.Exp, [bd4_b, E_b], [E_b])
        self.vtt(k4[:], f, E[:, 0:512], ALU.mult, [fA_b, E_b], [k4_b])
        self.banks.release(k3_)
        self.t32.release(kE)
        self.t32.release(kfA)
        st.update(dict(j=j, sub=sub, el=el, el_b=el_b, scm=scm, scm_b=scm_b, kscm=kscm, q3T=q3T, q3T_b=q3T_b, kq3T=kq3T,
                       k4=k4, k4_b=k4_b, kk4=kk4, front_done=True))

    def g_mid(self, st, prev):
        while prev is not None and not prev.get("mid_done"):
            yield
        sub = st["sub"]
        ib, ib_b = self.ar[:, 12 + sub, :], self.ar_b[12 + sub]
        el, el_b, scm, scm_b, q3T, q3T_b, k4, k4_b = st["el"], st["el_b"], st["scm"], st["scm_b"], st["q3T"], st["q3T_b"], st["k4"], st["k4_b"]
        kbo, (bo, bo_b) = self.banks.reserve()
        for c in range(2):
            pc = 64 * c
            for h in range(4):
                hs = slice(h * 128, (h + 1) * 128)
                self.mm(bo[pc:pc + 64, hs], q3T[:, h * 128 + pc:h * 128 + pc + 64], self.Sbf[:, hs], True, False,
                        [q3T_b, self.Sbf_b], [bo_b])
                self.mm(bo[pc:pc + 64, hs], scm[pc:pc + 64, h * 64:(h + 1) * 64], ib[pc:pc + 64, hs], False, True,
                        [scm_b, ib_b], [bo_b])
            kbu, (bu, bu_b) = self.banks.reserve()
            for h in range(4):
                hs = slice(h * 128, (h + 1) * 128)
                self.mm(bu[:, hs], k4[pc:pc + 64, hs], ib[pc:pc + 64, hs], True, True, [k4_b, ib_b], [bu_b])
            yield
            for h in range(4):
                hs = slice(h * 128, (h + 1) * 128)
                self.vstt(self.S32[:, hs], self.S32[:, hs], el[:, 2 * h + c:2 * h + c + 1], bu[:, hs], ALU.mult, ALU.add,
                          [self.S32_b, el_b, bu_b], [self.S32_b])
            self.banks.release(kbu)
            self.actf(self.Sbf[:], self.S32[:], AF.Copy, [self.S32_b], [self.Sbf_b])
            yield
        self.t16.release(st["kscm"])
        self.t16.release(st["kq3T"])
        self.t16.release(st["kk4"])
        st["kbo"], st["bo"], st["bo_b"] = kbo, bo, bo_b
        st["mid_done"] = True

    def g_out(self, st):
        nc = self.nc
        j, sub = st["j"], st["sub"]
        kbo, bo, bo_b = st["kbo"], st["bo"], st["bo_b"]
        gs, gs_b = self.ar[:, 16 + sub, :], self.ar_b[16 + sub]
        pB, p_b = self.pbt[:, 512:1024], self.pbt_b[0]
        ident = self.C("ident")
        ko1, (osb, osb_b) = self.t32.reserve()
        ko2, (osq, osq_b) = self.t32.reserve()
        o = osb[:, 0:512]
        self.actf(o, bo[:], AF.Copy, [bo_b], [osb_b])
        self.banks.release(kbo)
        yield
        self.vtt(osq[:, 0:512], o, o, ALU.mult, [osb_b], [osq_b])
        ss, ss_b = self.sm_pool.get()
        self.fw.op(self.fw.dve, lambda: nc.vector.tensor_reduce(out=ss[:, 0:4], in_=osq[:, 0:512].rearrange("p (h v) -> p h v", h=4), axis=AX.X, op=ALU.add),
                   [osq_b], [ss_b])
        self.actf(ss[:, 0:4], ss[:, 0:4], AF.Ln, [ss_b], [ss_b], bias=EPS, scale=1.0 / 128)
        self.actf(ss[:, 0:4], ss[:, 0:4], AF.Exp, [ss_b], [ss_b], scale=-0.5)
        yield
        gg = osq[:, 0:512]
        self.vtt(gg, gs, self.hgn[:], ALU.mult, [gs_b, self.hgn_b, osq_b], [osq_b])
        kyb, (yb, yb_b) = self.t16.reserve()
        for h in range(4):
            hs = slice(h * 128, (h + 1) * 128)
            self.vstt(yb[:, hs], o[:, hs], ss[:, h:h + 1], gg[:, hs], ALU.mult, ALU.mult, [osb_b, ss_b, osq_b], [yb_b])
        yield
        for h in range(4):
            self.tr(pB[:, h * 128:(h + 1) * 128], yb[:, h * 128:(h + 1) * 128], ident, [yb_b, self.c_b], [p_b])
        self.vcopy(self.hn[:, 4:8, sub * 128:(sub + 1) * 128], pB.rearrange("p (h t) -> p h t", h=4), [p_b], [self.hn_b[4 + h] for h in range(4)])
        self.t16.release(kyb)
        self.t32.release(ko1)
        self.t32.release(ko2)

    def hgrn_tile_gen(self, j, T):
        sts = [dict() for _ in range(4)]
        pending_f = list(range(4))
        pending_m = list(range(4))
        pending_o = list(range(4))
        act_f = act_m = act_o = None
        while pending_o or act_o is not None:
            if act_f is None and pending_f:
                s_ = pending_f.pop(0)
                act_f = (s_, self.g_front(j, T, s_, sts[s_]))
            if act_m is None and pending_m and sts[pending_m[0]].get("front_done"):
                s_ = pending_m.pop(0)
                act_m = (s_, self.g_mid(sts[s_], sts[s_ - 1] if s_ > 0 else None))
            if act_o is None and pending_o and sts[pending_o[0]].get("mid_done"):
                s_ = pending_o.pop(0)
                act_o = (s_, self.g_out(sts[s_]))
            for name in ("f", "m", "o"):
                cur = {"f": act_f, "m": act_m, "o": act_o}[name]
                if cur is None:
                    continue
                try:
                    next(cur[1])
                except StopIteration:
                    if name == "f":
                        act_f = None
                    elif name == "m":
                        act_m = None
                    else:
                        act_o = None

    def hgrn_out(self, j, sub, kbo, bo, bo_b, gs, gs_b, pB, pB_b, ident):
        nc = self.nc
        osb, osb_b = self.t32.get()
        o = osb[:, 0:512]
        self.actf(o, bo[:], AF.Copy, [bo_b], [osb_b])
        self.banks.release(kbo)
        osq, osq_b = self.t32.get()
        self.vtt(osq[:, 0:512], o, o, ALU.mult, [osb_b], [osq_b])
        ss, ss_b = self.sm_pool.get()
        self.fw.op(self.fw.dve, lambda: nc.vector.tensor_reduce(out=ss[:, 0:4], in_=osq[:, 0:512].rearrange("p (h v) -> p h v", h=4), axis=AX.X, op=ALU.add),
                   [osq_b], [ss_b])
        self.actf(ss[:, 0:4], ss[:, 0:4], AF.Ln, [ss_b], [ss_b], bias=EPS, scale=1.0 / 128)
        self.actf(ss[:, 0:4], ss[:, 0:4], AF.Exp, [ss_b], [ss_b], scale=-0.5)
        gg = osq[:, 0:512]
        self.vtt(gg, gs, self.hgn[:], ALU.mult, [gs_b, self.hgn_b, osq_b], [osq_b])
        yb, yb_b = self.t16.get()
        for h in range(4):
            hs = slice(h * 128, (h + 1) * 128)
            self.vstt(yb[:, hs], o[:, hs], ss[:, h:h + 1], gg[:, hs], ALU.mult, ALU.mult, [osb_b, ss_b, osq_b], [yb_b])
        for h in range(4):
            self.tr(pB[:, h * 128:(h + 1) * 128], yb[:, h * 128:(h + 1) * 128], ident, [yb_b, self.c_b], [pB_b])
        self.vcopy(self.hn[:, 4:8, sub * 128:(sub + 1) * 128], pB.rearrange("p (h t) -> p h t", h=4), [pB_b], [self.hn_b[4 + h] for h in range(4)])

    def qknorm(self, bank, bank_b, gain_ap, out_ap, out_bufs):
        sq, sq_b = self.t16.get()
        self.actf(sq[:], bank[:], AF.Square, [bank_b], [sq_b])
        b2, b2_b = self.banks.get()
        self.mm(b2[:], self.C("bones"), sq[:], True, True, [sq_b, self.c_b], [b2_b])
        r, r_b = self.t32.get()
        rr = r[:, 0:TT]
        self.actf(rr, b2[:], AF.Ln, [b2_b], [r_b], bias=EPS, scale=1.0 / 64)
        self.actf(rr, rr, AF.Exp, [r_b], [r_b], scale=-0.5)
        self.vstt(out_ap, bank[:], gain_ap, rr, ALU.mult, ALU.mult, [bank_b, r_b, self.small_b], out_bufs)

    def mixer_odd(self, s, j, T):
        nc, fw = self.nc, self.fw
        t0 = T * TT
        w = self.I["c_w_in"][j]
        hnf = lambda k: self.hn[:, k, :]
        gq = self.qkn[:, j, 0:1]
        gk = self.qkn[:, j, 1:2]
        for half in range(2):
            slot, slot_b = self.load_w(w[:, half * 512:(half + 1) * 512])
            for m in range(4):
                bank, bank_b = self.proj_fm(slot, slot_b, m * 128, hnf, NCH, self.hn_b)
                cidx = half * 4 + m
                self.qknorm(bank, bank_b, gq, self.ar[:, cidx, :], [self.ar_b[cidx]])
        slot, slot_b = self.load_w(w[:, 1024:1536])
        for m in range(2):
            bank, bank_b = self.proj_fm(slot, slot_b, m * 128, hnf, NCH, self.hn_b)
            self.qknorm(bank, bank_b, gk, self.kbuf[:, m, t0:t0 + TT], [self.kb_b[m][T]])
            bank, bank_b = self.banks.get()
            for k in range(NCH):
                self.mm(bank[0:64, :], slot[:, k, m * 128 + 64:m * 128 + 128], self.hn[:, k, :], k == 0, k == NCH - 1, [slot_b, self.hn_b[k]], [bank_b])
            for k in range(NCH):
                self.mm(bank[64:128, :], slot[:, k, m * 128:m * 128 + 64], self.hn[:, k, :], k == 0, k == NCH - 1, [slot_b, self.hn_b[k]], [bank_b])
            self.qknorm(bank, bank_b, gk, self.kbuf[:, 2 + m, t0:t0 + TT], [self.kb_b[2 + m][T]])
        for sub in range(4):
            bank, bank_b = self.banks.get()
            for k in range(NCH):
                self.mm(bank[:, 0:256], self.hn[:, k, sub * 128:(sub + 1) * 128], slot[:, k, 256:512], k == 0, k == NCH - 1,
                        [slot_b, self.hn_b[k]], [bank_b])
            self.actf(self.vbuf[:, T * 4 + sub, 0:256], bank[:, 0:256], AF.Copy, [bank_b], [self.vb_b[T * 4 + sub]])
        onesbf = self.C("ones", cols=64)
        import os
        OD = float(os.environ.get("K_ODD", "9"))
        if OD < 2:
            return
        self.preconvert(2 * j + 1)
        HPERM = [0, 2, 1, 3]
        units = [(qb, g) for qb in range(4) for g in range(4)]

        def stage_a(qb, g):
            B = 4 * T + qb
            kbs = [B - 1, B] if B > 0 else [B]
            tmps = [self.t32.get() for _ in kbs]
            for par in range(2):
                bs, bs_b = self.banks.get()
                for ki, kb in enumerate(kbs):
                    for ii in range(2):
                        i = par * 2 + ii
                        h = 4 * g + HPERM[i]
                        hp = (h % 2) * 64
                        var = 0 if (g % 2) == (h % 2) else 2
                        kc = var + g // 2
                        c0_ = ki * 256 + ii * 128
                        self.mm(bs[:, c0_:c0_ + 128], self.kbuf[hp:hp + 64, kc, kb * 128:(kb + 1) * 128],
                                self.ar[hp:hp + 64, h // 2, qb * 128:(qb + 1) * 128], True, True,
                                [self.kb_b[kc][kb // 4], self.ar_b[h // 2]], [bs_b])
                for ki, kb in enumerate(kbs):
                    ksel = 0 if kb == B - 1 else 1
                    tmp, tmp_b = tmps[ki]
                    self.vtt(tmp[:, par * 256:(par + 1) * 256], bs[:, ki * 256:(ki + 1) * 256],
                             self.bias[:, ksel, 4 * g + 2 * par:4 * g + 2 * par + 2, :].rearrange("p a b -> p (a b)"), ALU.add,
                             [bs_b, self.bias_b], [tmp_b])
            es = []
            for ki, kb in enumerate(kbs):
                tmp, tmp_b = tmps[ki]
                ke, (e, e_b) = self.t16.reserve()
                self.actf(e[:], tmp[:, 0:512], AF.Exp, [tmp_b], [e_b])
                es.append((kb, ke, e, e_b))
            return es

        nd_sets = [(self.banks.reserve(), self.banks.reserve()) for _ in range(2)]

        def stage_b(qb, g, es):
            hf = g // 2
            (kN, (nbk, nbk_b)), (kD, (dbk, dbk_b)) = nd_sets[(qb * 2 + hf) % 2]
            for i in range(4):
                h = 4 * g + HPERM[i]
                hp = (h % 2) * 64
                c = h // 2
                cs = slice((c % 4) * 128, (c % 4) * 128 + 128)
                for n_, (kb, ke, e, e_b) in enumerate(es):
                    self.mm(nbk[hp:hp + 64, cs], self.vbuf[:, kb, g * 64:(g + 1) * 64], e[:, i * 128:(i + 1) * 128],
                            n_ == 0, n_ == len(es) - 1, [self.vb_b[kb], e_b], [nbk_b])
                for n_, (kb, ke, e, e_b) in enumerate(es):
                    self.mm(dbk[hp:hp + 64, cs], onesbf, e[:, i * 128:(i + 1) * 128],
                            n_ == 0, n_ == len(es) - 1, [self.c_b, e_b], [dbk_b])
            for (kb, ke, e, e_b) in es:
                self.t16.release(ke)
            if g % 2 == 1:
                r, r_b = self.t32.get()
                for cq in range(4):
                    c = hf * 4 + cq
                    self.actf(r[:, cq * 128:(cq + 1) * 128], dbk[:, cq * 128:(cq + 1) * 128], AF.Ln, [dbk_b, self.small_b], [r_b],
                              bias=self.snk[:, j, c:c + 1])
                self.actf(r[:, 0:512], r[:, 0:512], AF.Exp, [r_b], [r_b], scale=-1.0)
                self.vtt(self.hn[:, hf * 4:(hf + 1) * 4, qb * 128:(qb + 1) * 128],
                         nbk[:].rearrange("p (c t) -> p c t", c=4), r[:, 0:512].rearrange("p (c t) -> p c t", c=4), ALU.mult,
                         [nbk_b, r_b], [self.hn_b[hf * 4 + i] for i in range(4)])

        pend = stage_a(*units[0])
        for ui, (qb, g) in enumerate(units):
            nxt = stage_a(*units[ui + 1]) if ui + 1 < len(units) else None
            stage_b(qb, g, pend)
            pend = nxt
        for (a_, b_) in nd_sets:
            self.banks.release(a_[0])
            self.banks.release(b_[0])


def _host_small(inp):
    f32 = np.float32
    norms = np.stack([inp["mix_norm"], inp["ffn_norm"], inp["ple_norm"]], 0).astype(f32)
    norms = np.ascontiguousarray(norms.reshape(3, DEPTH, NCH, 128).transpose(3, 0, 1, 2))
    cw = np.concatenate([inp["ffn_conv"].astype(f32), inp["ffn_conv_b"].astype(f32)[:, None, :]], 1)
    cw = np.ascontiguousarray(cw.reshape(DEPTH, 4, 44, 128).transpose(3, 0, 1, 2))
    lbl = np.ascontiguousarray(inp["hg_lb_logits"].astype(f32).reshape(1, 1024))
    hgn = np.ascontiguousarray(np.tile(inp["hg_out_norm"].astype(f32), (1, 4)).reshape(1, 1024))
    qkn = np.stack([np.tile(inp["q_norm"].astype(f32), (1, 2)), np.tile(inp["k_norm"].astype(f32), (1, 2))], -1)
    qkn = np.ascontiguousarray(qkn.transpose(1, 0, 2))
    snk = inp["sinks"].astype(f32).reshape(2, NCH, 2)
    snk = np.ascontiguousarray(np.repeat(snk.transpose(2, 0, 1), 64, axis=0))
    bucket, valid = _t5_bias_index()
    rb = inp["rel_bias"].astype(f32)
    bt = rb[bucket]
    bt = np.where(valid[..., None], bt, f32(NEG)).transpose(0, 1, 3, 2)
    hperm = np.array([4 * g + i for g in range(4) for i in (0, 2, 1, 3)])
    bt = bt[:, :, hperm, :]
    biasT = np.ascontiguousarray(bt.reshape(128, 2 * 16 * 128).astype(f32))
    cf, cb = _const_arrays()
    return dict(norms=norms, convw=cw, lbl=lbl, hgn=hgn, qkn=qkn, snk=snk, biasT=biasT, cf=cf, cb=cb)


_CACHE = {}


def _run(inputs, nseq, layers, ncores, dbg=None, trace=False):
    key = (nseq, tuple(layers), tuple(sorted(dbg.items())) if dbg else None)
    f32 = np.float32
    small = _host_small(inputs)
    shared = {k: np.ascontiguousarray(np.asarray(inputs[k], dtype=f32)) for k in
              ["ab_w_in", "ab_w_out", "c_w_in", "c_w_out", "ffn_up", "ffn_down", "ple_gate", "ple_proj"]}
    x = np.asarray(inputs["x"], dtype=f32)
    p = np.asarray(inputs["p"], dtype=f32)
    in_maps = []
    for ci in range(ncores):
        sl = slice(ci * nseq, (ci + 1) * nseq)
        m = dict(shared)
        m.update(small)
        m["xT"] = np.ascontiguousarray(x[sl].transpose(0, 2, 1))
        m["pT"] = np.ascontiguousarray(p[:, sl].transpose(0, 1, 3, 2))
        in_maps.append(m)
    nc = Builder(nseq, layers, dbg).build()
    res = run_bass_kernel_spmd(nc, in_maps, core_ids=list(range(ncores)))
    return res


def kernel(**inputs):
    res = _run(inputs, 2, list(range(DEPTH)), 8)
    outs = [r["outT"] for r in res.results]
    out = np.concatenate(outs, axis=0).transpose(0, 2, 1)
    return np.ascontiguousarray(out.astype(np.float32))
```

```python
import math
from contextlib import ExitStack

import numpy as np

import concourse.bass as bass
import concourse.mybir as mybir
from concourse.bass_utils import run_bass_kernel_spmd

F32 = mybir.dt.float32
BF16 = mybir.dt.bfloat16
AF = mybir.ActivationFunctionType
ALU = mybir.AluOpType
AX = mybir.AxisListType

D = 1024
S = 2048
DEPTH = 4
NCH = 8
TT = 512
NTILE = S // TT
F_FF = 2816
NFC = 22
EPS = 1e-6
NEG = -30000.0
import os as _os
_OPT = _os.environ.get("K_OPT", "normbg,normacc,lock,hggen,preconv").split(",")
OPT_SBEARLY = "sbearly" in _OPT
OPT_NORMACC = "normacc" in _OPT
OPT_NORMBG = "normbg" in _OPT
OPT_LOCK = "lock" in _OPT
OPT_HGPIPE = "hgpipe" in _OPT
OPT_HGGEN = "hggen" in _OPT
OPT_PRECONV = "preconv" in _OPT


class Buf:
    __slots__ = ("w", "r")

    def __init__(self):
        self.w = None
        self.r = {}


class Stream:
    def __init__(self, name, h, pe=False):
        self.name = name
        self.h = h
        self.pe = pe
        self.seq = 0
        self.know = {}
        self.oplist = []
        self.sem = None


class Chan:
    def __init__(self, name):
        self.name = name
        self.cnt = 0
        self.last = None
        self.sem = None


class Op:
    __slots__ = ("st", "fn", "waits", "needed", "seq", "val", "chan")


class FW:
    def __init__(self, nc, es):
        self.nc = nc
        self.es = es
        self.ops = []
        self.pe = self._mk("pe", nc.tensor, True)
        self.act = self._mk("act", nc.scalar)
        self.dve = self._mk("dve", nc.vector)
        self.pool = self._mk("pool", nc.gpsimd)
        self.sp = self._mk("sp", nc.sync)
        self.streams = [self.pe, self.act, self.dve, self.pool, self.sp]
        self.chans = []
        self.sp_ch = [self._ch("spc%d" % i) for i in range(6)]
        self.pl_ch = [self._ch("plc%d" % i) for i in range(4)]
        self._spi = 0
        self._pli = 0

    def _mk(self, name, h, pe=False):
        st = Stream(name, h, pe)
        st.sem = self.es.enter_context(self.nc.semaphore("s_" + name))
        return st

    def _ch(self, name):
        c = Chan(name)
        c.sem = self.es.enter_context(self.nc.semaphore("c_" + name))
        self.chans.append(c)
        return c

    def _need(self, st, waits, ev, raw):
        if ev is None:
            return
        s, n, kn = ev
        if s is st:
            if st.pe:
                return
        if st.know.get(s, 0) >= n:
            return
        waits.append((s, n))
        newk = dict(st.know)
        newk[s] = n
        for a, b in kn.items():
            if newk.get(a, 0) < b:
                newk[a] = b
        st.know = newk

    def op(self, st, fn, reads=(), writes=(), chan=None):
        waits = []
        for b in reads:
            self._need(st, waits, b.w, True)
        for b in writes:
            self._need(st, waits, b.w, False)
            for ev in b.r.values():
                self._need(st, waits, ev, False)
        if chan is not None:
            self._need(st, waits, chan.last, False)
        st.seq += 1
        o = Op()
        o.st = st
        o.fn = fn
        o.waits = waits
        o.needed = False
        o.seq = st.seq
        o.val = 0
        o.chan = chan
        self.ops.append(o)
        st.oplist.append(o)
        if chan is not None:
            chan.cnt += 16 * (len(fn) if isinstance(fn, (list, tuple)) else 1)
            ev = (chan, chan.cnt, st.know)
            chan.last = ev
        else:
            ev = (st, st.seq, st.know)
        for b in reads:
            b.r[ev[0]] = ev
        for b in writes:
            b.w = ev
            b.r = {}
        return ev

    def dma_sp(self, fn, reads=(), writes=()):
        ch = self.sp_ch[self._spi % len(self.sp_ch)]
        self._spi += 1
        return self.op(self.sp, fn, reads, writes, chan=ch)

    def dma_pool(self, fn, reads=(), writes=()):
        ch = self.pl_ch[self._pli % len(self.pl_ch)]
        self._pli += 1
        return self.op(self.pool, fn, reads, writes, chan=ch)

    def finish(self, final_events):
        waits = []
        for ev in final_events:
            self._need(self.sp, waits, ev, True)
        for o in self.ops:
            for (s, n) in o.waits:
                if isinstance(s, Stream):
                    s.oplist[n - 1].needed = True
        for (s, n) in waits:
            if isinstance(s, Stream):
                s.oplist[n - 1].needed = True
        for st in self.streams:
            c = 0
            for o in st.oplist:
                if o.needed:
                    c += 1
                o.val = c

        def emit_wait(st, s, n):
            if isinstance(s, Stream):
                st.h.wait_ge(s.sem, s.oplist[n - 1].val)
            else:
                st.h.wait_ge(s.sem, n)

        nw = 0
        for o in self.ops:
            for (s, n) in o.waits:
                emit_wait(o.st, s, n)
                nw += 1
            if isinstance(o.fn, (list, tuple)):
                for f_ in o.fn:
                    f_().then_inc(o.chan.sem, 16)
                continue
            ins = o.fn()
            if o.chan is not None:
                ins.then_inc(o.chan.sem, 16)
            elif o.needed:
                ins.then_inc(o.st.sem, 1)
        for (s, n) in waits:
            emit_wait(self.sp, s, n)
        self.stats = dict(n_ops=len(self.ops), n_waits=nw,
                          per_stream={st.name: len(st.oplist) for st in self.streams})


class Pool_:
    def __init__(self, tiles):
        self.items = [(t, Buf()) for t in tiles]
        self.free = list(range(len(tiles)))

    def get(self):
        k = self.free.pop(0)
        self.free.append(k)
        return self.items[k]

    def reserve(self):
        k = self.free.pop(0)
        return k, self.items[k]

    def release(self, k):
        self.free.append(k)


def _consts():
    p = np.arange(128)[:, None]
    m = np.arange(128)[None, :]
    c = {}
    c["ident"] = (p == m).astype(np.float32)
    c["ones"] = np.ones((128, 128), np.float32)
    c["bones"] = ((p // 64) == (m // 64)).astype(np.float32)
    c["negtri"] = -(p >= m).astype(np.float32)
    c["sbmask"] = (p < m).astype(np.float32)
    same = (p // 64) == (m // 64)
    tri2 = (same & (p <= m)).astype(np.float32)
    mid = (m // 64) * 64 + 31
    trimid = (same & (p <= mid)).astype(np.float32)
    c["tri2"] = tri2
    c["trid1"] = tri2 - trimid
    c["trisuf"] = (same & (p > m)).astype(np.float32)
    co = np.zeros((128, 128), np.float32)
    co[:, 0] = (np.arange(128) < 64)
    co[:, 1] = (np.arange(128) >= 64)
    c["chunkones"] = co
    mk = np.zeros((128, 256), np.float32)
    cc = np.arange(256)[None, :]
    mk[:, :] = ((p % 64) <= (cc % 64))
    c["maskS"] = mk
    return c


F32_CONSTS = ["tri2", "trid1", "trisuf", "chunkones"]
BF_CONSTS = ["ident", "ones", "bones", "negtri", "sbmask", "maskS"]


def _const_arrays():
    c = _consts()
    f = np.concatenate([c[k] for k in F32_CONSTS], axis=1)
    b = np.concatenate([c[k] for k in BF_CONSTS], axis=1)
    return np.ascontiguousarray(f), np.ascontiguousarray(b)


def _offsets(names, c):
    off = {}
    o = 0
    for k in names:
        off[k] = o
        o += c[k].shape[1]
    return off, o


def _t5_bias_index():
    W = 128
    t = np.arange(W)[None, None, :]
    s = np.arange(W)[:, None, None]
    kb = np.arange(2)[None, :, None]
    dist = t + W - (kb * W + s)
    valid = (dist >= 0) & (dist < W)
    max_exact = 16
    large = max_exact + (np.log(np.maximum(dist, max_exact) / max_exact) / math.log(128 / max_exact) * (32 - max_exact)).astype(np.int32)
    large = np.minimum(large, 31)
    bucket = np.where(dist < max_exact, np.maximum(dist, 0), large).astype(np.int32)
    return bucket, valid


class Builder:
    def __init__(self, nseq, layers, dbg=None):
        self.nseq = nseq
        self.layers = layers
        self.dbg = dbg

    def build(self):
        nc = bass.Bass("TRN2", target_bir_lowering=False)
        self.nc = nc
        nseq = self.nseq
        dt = nc.dram_tensor
        I = {}
        I["xT"] = dt("xT", [nseq, D, S], F32, kind="ExternalInput").ap()
        I["pT"] = dt("pT", [DEPTH, nseq, 256, S], F32, kind="ExternalInput").ap()
        I["ab_w_in"] = dt("ab_w_in", [2, D, 3584], F32, kind="ExternalInput").ap()
        I["ab_w_out"] = dt("ab_w_out", [2, D, D], F32, kind="ExternalInput").ap()
        I["c_w_in"] = dt("c_w_in", [2, D, 1536], F32, kind="ExternalInput").ap()
        I["c_w_out"] = dt("c_w_out", [2, D, D], F32, kind="ExternalInput").ap()
        I["ffn_up"] = dt("ffn_up", [DEPTH, D, 2 * F_FF], F32, kind="ExternalInput").ap()
        I["ffn_down"] = dt("ffn_down", [DEPTH, F_FF, D], F32, kind="ExternalInput").ap()
        I["ple_gate"] = dt("ple_gate", [DEPTH, D, D], F32, kind="ExternalInput").ap()
        I["ple_proj"] = dt("ple_proj", [DEPTH, 256, D], F32, kind="ExternalInput").ap()
        I["norms"] = dt("norms", [128, 3, DEPTH, NCH], F32, kind="ExternalInput").ap()
        I["convw"] = dt("convw", [128, DEPTH, 4, 44], F32, kind="ExternalInput").ap()
        I["lbl"] = dt("lbl", [1, 2 * 512], F32, kind="ExternalInput").ap()
        I["hgn"] = dt("hgn", [1, 2 * 512], F32, kind="ExternalInput").ap()
        I["qkn"] = dt("qkn", [128, 2, 2], F32, kind="ExternalInput").ap()
        I["snk"] = dt("snk", [128, 2, NCH], F32, kind="ExternalInput").ap()
        I["biasT"] = dt("biasT", [128, 2 * 16 * 128], F32, kind="ExternalInput").ap()
        cf, cb = _const_arrays()
        I["cf"] = dt("cf", list(cf.shape), F32, kind="ExternalInput").ap()
        I["cb"] = dt("cb", list(cb.shape), F32, kind="ExternalInput").ap()
        self.I = I
        self.outT = dt("outT", [nseq, D, S], F32, kind="ExternalOutput").ap()
        self.wscr = dt("wscr", [120, 128, 4096], BF16, kind="Internal").ap()
        self.wimg = {}
        self.preconv_done = set()
        if self.dbg:
            self.dbg_out = {k: dt("dbg_" + k, list(shp), F32, kind="ExternalOutput").ap() for k, shp in self.dbg.items()}
        c = _consts()
        self.cf_off, self.cf_n = _offsets(F32_CONSTS, c)
        self.cb_off, self.cb_n = _offsets(BF_CONSTS, c)

        with ExitStack() as es:
            self.es = es
            fw = FW(nc, es)
            self.fw = fw
            sb = lambda name, shape, dtype: es.enter_context(nc.sbuf_tensor(name, shape, dtype))
            self.res = sb("res", [128, NCH, S], F32)
            self.res_b = [[Buf() for _ in range(NTILE)] for _ in range(NCH)]
            self.hn = sb("hn", [128, NCH, TT], BF16)
            self.hn_b = [Buf() for _ in range(NCH)]
            self.ar = sb("arena", [128, NFC, TT], BF16)
            self.ar_b = [Buf() for _ in range(NFC)]
            self.kbuf = sb("kbuf", [128, 4, S], BF16)
            self.kb_b = [[Buf() for _ in range(NTILE)] for _ in range(4)]
            self.vbuf = sb("vbuf", [128, 16, 512], BF16)
            self.vb_b = [Buf() for _ in range(16)]
            self.bias = sb("bias", [128, 2, 16, 128], BF16)
            self.bias_b = Buf()
            self.ws = [sb("ws%d" % i, [128, NCH, 512], BF16) for i in range(3)]
            self.wpool = Pool_(self.ws)
            self.stage = sb("stage", [128, 4, 512], F32)
            self.stage_b = [Buf() for _ in range(4)]
            stf = self.stage[:].rearrange("p a b -> p (a b)")
            self.upool = Pool_([stf[:, i * 520:i * 520 + 516] for i in range(3)])
            t32 = [sb("t32_%d" % i, [128, 516], F32) for i in range(7)]
            self.t32 = Pool_(t32)
            t16 = [sb("t16_%d" % i, [128, 512], BF16) for i in range(7)]
            self.t16 = Pool_(t16)
            self.cf = sb("cf_sb", [128, self.cf_n], F32)
            self.cb = sb("cb_sb", [128, self.cb_n], BF16)
            self.c_b = Buf()
            self.norms = sb("norms_sb", [128, 3, DEPTH, NCH], F32)
            self.convw = sb("convw_sb", [128, DEPTH, 4, 44], F32)
            self.qkn = sb("qkn_sb", [128, 2, 2], F32)
            self.snk = sb("snk_sb", [128, 2, NCH], F32)
            self.lb = sb("lb_sb", [128, 2, 512], F32)
            self.lb1 = sb("lb1_sb", [128, 512], F32)
            self.hgn = sb("hgn_sb", [128, 512], F32)
            self.hgn_b = Buf()
            self.small_b = Buf()
            self.lb_b = Buf()
            self.S32 = sb("S32", [128, 512], F32)
            self.S32_b = Buf()
            self.Sbf = sb("Sbf", [128, 512], BF16)
            self.Sbf_b = Buf()
            self.halo = sb("halo", [128, 2, 44, 2], F32)
            self.halo_b = [Buf(), Buf()]
            self.sm = sb("smalls", [128, 64], F32)
            self.sm_pool = Pool_([self.sm[:, i * 8:(i + 1) * 8] for i in range(8)])
            self.negrow = sb("negrow", [128, 128], BF16)
            self.sbR_b = {0: Buf(), 64: Buf()}
            self.sbrt_b = {0: [Buf(), Buf(), Buf()], 64: [Buf(), Buf(), Buf()]}
            banks = [es.enter_context(nc.psum_tensor("pb%d" % i, [128, 512], F32)) for i in range(7)]
            self.banks = Pool_(banks)
            self.pbt = es.enter_context(nc.psum_tensor("pbt", [128, 1024], BF16))
            self.pbt_b = [Buf(), Buf()]

            self.acc = None
            self.pre_rstd = None
            self.prologue()
            finals = []
            for s in range(nseq):
                self.load_x(s)
                for li in self.layers:
                    self.layer(s, li)
                finals += self.store_out(s)
            if self.dbg:
                finals += self.dbg_events
            fw.finish(finals)
        return nc

    def C(self, name, bf=True, cols=None):
        if bf:
            o = self.cb_off[name]
            n = _consts()[name].shape[1] if cols is None else cols
            return self.cb[:, o:o + n]
        o = self.cf_off[name]
        n = _consts()[name].shape[1] if cols is None else cols
        return self.cf[:, o:o + n]

    def mm(self, out, lhsT, rhs, start, stop, reads, writes, skip=False):
        nc = self.nc
        if skip:
            return self.fw.op(self.fw.pe, lambda: nc.tensor.matmul(out, lhsT, rhs, start=start, stop=stop, skip_group_check=True), reads, writes)
        return self.fw.op(self.fw.pe, lambda: nc.tensor.matmul(out, lhsT, rhs, start=start, stop=stop), reads, writes)

    def tr(self, out, in_, ident, reads, writes):
        nc = self.nc
        return self.fw.op(self.fw.pe, lambda: nc.tensor.transpose(out, in_, ident), reads, writes)

    def actf(self, out, in_, func, reads, writes, bias=0.0, scale=1.0):
        nc = self.nc
        return self.fw.op(self.fw.act, lambda: nc.scalar.activation(out=out, in_=in_, func=func, bias=bias, scale=scale), reads, writes)

    def vtt(self, out, in0, in1, op, reads, writes):
        nc = self.nc
        return self.fw.op(self.fw.dve, lambda: nc.vector.tensor_tensor(out=out, in0=in0, in1=in1, op=op), reads, writes)

    def vts(self, out, in0, s1, s2, op0, op1, reads, writes):
        nc = self.nc
        if op1 is None:
            return self.fw.op(self.fw.dve, lambda: nc.vector.tensor_scalar(out=out, in0=in0, scalar1=s1, scalar2=None, op0=op0), reads, writes)
        return self.fw.op(self.fw.dve, lambda: nc.vector.tensor_scalar(out=out, in0=in0, scalar1=s1, scalar2=s2, op0=op0, op1=op1), reads, writes)

    def vstt(self, out, in0, scalar, in1, op0, op1, reads, writes):
        nc = self.nc
        return self.fw.op(self.fw.dve, lambda: nc.vector.scalar_tensor_tensor(out=out, in0=in0, scalar=scalar, in1=in1, op0=op0, op1=op1), reads, writes)

    def vcopy(self, out, in_, reads, writes):
        nc = self.nc
        return self.fw.op(self.fw.dve, lambda: nc.vector.tensor_copy(out, in_), reads, writes)

    def vrecip(self, out, in_, reads, writes):
        nc = self.nc
        return self.fw.op(self.fw.dve, lambda: nc.vector.reciprocal(out=out, in_=in_), reads, writes)

    def pcopy(self, out, in_, reads, writes):
        nc = self.nc
        return self.fw.op(self.fw.pool, lambda: nc.gpsimd.tensor_copy(out, in_), reads, writes)

    def load_w(self, src_ap, nk=NCH, ncols=512, src2=None, bgq=False):
        nc = self.nc
        key = (src_ap.tensor.name, str(src_ap.offset), tuple(tuple(x) for x in src_ap.ap))
        slot, b = self.wpool.get()
        img = self.wimg.get(key)
        if img is None:
            idx = len(self.wimg)
            ib = Buf()
            self.wimg[key] = (idx, ib)
            if src2 is None:
                dst = slot[:, 0:nk, 0:ncols]
                src = src_ap.rearrange("(c p) n -> p c n", p=128)
                self.fw.dma_pool(lambda: nc.gpsimd.dma_start(out=dst, in_=src), reads=(), writes=[b])
            else:
                h = ncols // 2
                fns = []
                for i_, sa in enumerate((src_ap, src2)):
                    dst = slot[:, 0:nk, i_ * h:(i_ + 1) * h]
                    src = sa.rearrange("(c p) n -> p c n", p=128)
                    fns.append(lambda dst=dst, src=src: nc.gpsimd.dma_start(out=dst, in_=src))
                self.fw.dma_pool(fns, reads=(), writes=[b])
            if ncols == 512:
                simg = self.wscr[idx, :, 0:nk * 512]
                ssrc = slot[:, 0:nk, :].rearrange("p c n -> p (c n)")
                if bgq:
                    self.bg_pending.append((lambda: nc.gpsimd.dma_start(out=simg, in_=ssrc), b, ib))
                    while len(self.bg_pending) > 2:
                        f_, b_, ib_ = self.bg_pending.pop(0)
                        self.fw.dma_pool(f_, reads=[b_], writes=[ib_])
                else:
                    self.fw.dma_sp(lambda: nc.sync.dma_start(out=simg, in_=ssrc), reads=[b], writes=[ib])
            else:
                self.wimg[key] = None
                del self.wimg[key]
        else:
            idx, ib = img
            simg = self.wscr[idx, :, 0:nk * 512]
            sdst = slot[:, 0:nk, :].rearrange("p c n -> p (c n)")
            self.fw.dma_sp(lambda: nc.sync.dma_start(out=sdst, in_=simg), reads=[ib], writes=[b])
        return slot, b

    def preconvert(self, li):
        if li in self.preconv_done or not OPT_PRECONV:
            return
        self.preconv_done.add(li)
        self.bg_pending = []
        j = li // 2
        wo = self.I["ab_w_out"][j] if li % 2 == 0 else self.I["c_w_out"][j]
        for half in range(2):
            self.load_w(wo[:, half * 512:(half + 1) * 512], bgq=True)
        wup = self.I["ffn_up"][li]
        for j0 in range(0, NFC, 2):
            self.load_w(wup[:, j0 * 128:(j0 + 2) * 128], ncols=512, src2=wup[:, F_FF + j0 * 128:F_FF + (j0 + 2) * 128], bgq=True)
        while self.bg_pending:
            f_, b_, ib_ = self.bg_pending.pop(0)
            self.fw.dma_pool(f_, reads=[b_], writes=[ib_])

    def dump(self, key, sb_ap, bufs, dst=None):
        if not self.dbg or key not in self.dbg:
            return
        nc = self.nc
        d = self.dbg_out[key] if dst is None else dst
        ev = self.fw.dma_sp(lambda: nc.sync.dma_start(out=d, in_=sb_ap), reads=bufs, writes=())
        self.dbg_events.append(ev)

    def prologue(self):
        nc, fw, I = self.nc, self.fw, self.I
        self.dbg_events = []
        fw.dma_sp(lambda: nc.sync.dma_start(out=self.cf[:], in_=I["cf"]), writes=[self.c_b])
        fw.dma_pool(lambda: nc.gpsimd.dma_start(out=self.cb[:], in_=I["cb"]), writes=[self.c_b])
        fw.dma_sp(lambda: nc.sync.dma_start(out=self.norms[:], in_=I["norms"]), writes=[self.small_b])
        fw.dma_sp(lambda: nc.sync.dma_start(out=self.convw[:], in_=I["convw"]), writes=[self.small_b])
        fw.dma_sp(lambda: nc.sync.dma_start(out=self.qkn[:], in_=I["qkn"]), writes=[self.small_b])
        fw.dma_sp(lambda: nc.sync.dma_start(out=self.snk[:], in_=I["snk"]), writes=[self.small_b])
        l0, l0_b = self.t32.get()
        l1, l1_b = self.t32.get()
        fw.dma_sp(lambda: nc.sync.dma_start(out=l0[:, 0:512], in_=I["lbl"][0:1, 0:512].partition_broadcast(128)), writes=[l0_b])
        fw.dma_sp(lambda: nc.sync.dma_start(out=l1[:, 0:512], in_=I["lbl"][0:1, 512:1024].partition_broadcast(128)), writes=[l1_b])
        self.vtt(l1[:, 0:512], l1[:, 0:512], l0[:, 0:512], ALU.subtract, [l0_b, l1_b], [l1_b])
        self.actf(self.lb1[:], l1[:, 0:512], AF.Sigmoid, [l1_b], [self.small_b])
        fw.dma_pool(lambda: nc.gpsimd.dma_start(out=self.bias[:].rearrange("p a b c -> p (a b c)"), in_=I["biasT"]), writes=[self.bias_b])
        fw.op(fw.dve, lambda: nc.vector.memset(self.negrow[:], -1.0), writes=[self.c_b])
        self.vts(self.qkn[:, :, 0:1], self.qkn[:, :, 0:1], 0.125, None, ALU.mult, None, [self.small_b], [self.small_b])
        self.actf(self.snk[:], self.snk[:], AF.Exp, [self.small_b], [self.small_b])

    def load_x(self, s):
        nc, fw = self.nc, self.fw
        for c in range(NCH):
            src = self.I["xT"][s, c * 128:(c + 1) * 128, :]
            dst = self.res[:, c, :]
            fw.dma_sp(lambda dst=dst, src=src: nc.sync.dma_start(out=dst, in_=src), writes=self.res_b[c])

    def store_out(self, s):
        nc, fw = self.nc, self.fw
        evs = []
        for c in range(NCH):
            dst = self.outT[s, c * 128:(c + 1) * 128, :]
            src = self.res[:, c, :]
            evs.append(fw.dma_sp(lambda dst=dst, src=src: nc.sync.dma_start(out=dst, in_=src), reads=self.res_b[c]))
        return evs

    def rmsnorm(self, T, which, li):
        t0 = T * TT
        if which == 0 and self.pre_rstd is not None and self.pre_rstd[0] == (li, T):
            _, kr, rr, r_b = self.pre_rstd
            self.pre_rstd = None
        elif self.acc is not None and self.acc["T"] == T and self.acc["n"] == NCH:
            kr, rr, r_b = self.stats_finish()
        else:
            self.stats_begin(T)
            for c in range(NCH):
                self.stats_add(c)
            kr, rr, r_b = self.stats_finish()
        for c in range(NCH):
            g = self.norms[:, which, li, c:c + 1]
            self.vstt(self.hn[:, c, :], self.res[:, c, t0:t0 + TT], g, rr, ALU.mult, ALU.mult,
                      [self.res_b[c][T], r_b, self.small_b], [self.hn_b[c]])
        self.t32.release(kr)

    def stats_begin(self, T):
        kb, (ssb, ssb_b) = self.banks.reserve()
        self.acc = dict(T=T, n=0, kb=kb, ssb=ssb, ssb_b=ssb_b, pend=None)

    def _stats_flush(self):
        a = self.acc
        if a["pend"] is not None:
            sq, sq_b, first, last = a["pend"]
            self.mm(a["ssb"][:], self.C("ones"), sq[:], first, last, [sq_b, self.c_b], [a["ssb_b"]])
            a["pend"] = None

    def stats_add(self, c):
        a = self.acc
        T = a["T"]
        t0 = T * TT
        self._stats_flush()
        sq, sq_b = self.t16.get()
        self.actf(sq[:], self.res[:, c, t0:t0 + TT], AF.Square, [self.res_b[c][T]], [sq_b])
        a["pend"] = (sq, sq_b, a["n"] == 0, a["n"] == NCH - 1)
        a["n"] += 1

    def stats_finish(self):
        a = self.acc
        self._stats_flush()
        kr, (r, r_b) = self.t32.reserve()
        rr = r[:, 0:TT]
        self.actf(rr, a["ssb"][:], AF.Ln, [a["ssb_b"]], [r_b], bias=EPS, scale=1.0 / D)
        self.actf(rr, rr, AF.Exp, [r_b], [r_b], scale=-0.5)
        self.banks.release(a["kb"])
        self.acc = None
        return kr, rr, r_b

    def stats_bg(self, key, T):
        self.stats_begin(T)
        acc = self.acc
        self.acc = None
        for c in range(NCH):
            self.acc, sv = acc, self.acc
            self.stats_add(c)
            self.acc = sv
            yield
        self.acc, sv = acc, self.acc
        kr, rr, r_b = self.stats_finish()
        self.acc = sv
        self.pre_rstd = (key, kr, rr, r_b)
        yield

    def proj_fm(self, slot, slot_b, col0, rhs_fn, nk, rhs_bufs, ncols=128):
        bank, bank_b = self.banks.get()
        for k in range(nk):
            rb_ = [rhs_bufs[k]] if len(rhs_bufs) == nk else list(rhs_bufs)
            self.mm(bank[0:ncols, :], slot[:, k, col0:col0 + ncols], rhs_fn(k), k == 0, k == nk - 1,
                    [slot_b] + rb_, [bank_b])
        return bank, bank_b

    def proj_fm_pair(self, slot, slot_b, cols, rhs_fn, nk, rhs_bufs):
        bks = [self.banks.get() for _ in cols]
        for k in range(nk):
            rb_ = [rhs_bufs[k]] if len(rhs_bufs) == nk else list(rhs_bufs)
            for (bank, bank_b), col0 in zip(bks, cols):
                self.mm(bank[:, :], slot[:, k, col0:col0 + 128], rhs_fn(k), k == 0, k == nk - 1, [slot_b] + rb_, [bank_b])
        return bks

    def add_to_res(self, bank, bank_b, n, T, stats=False):
        t0 = T * TT
        self.vtt(self.res[:, n, t0:t0 + TT], bank[:], self.res[:, n, t0:t0 + TT], ALU.add,
                 [bank_b, self.res_b[n][T]], [self.res_b[n][T]])
        if stats and OPT_NORMACC:
            self.stats_add(n)

    def out_proj(self, w_ap, T):
        if OPT_NORMACC:
            self.stats_begin(T)
        for half in range(2):
            slot, slot_b = self.load_w(w_ap[:, half * 512:(half + 1) * 512])
            for nq in range(4):
                bank, bank_b = self.proj_fm(slot, slot_b, nq * 128, lambda k: self.hn[:, k, :], NCH, self.hn_b)
                self.add_to_res(bank, bank_b, half * 4 + nq, T, stats=True)

    def layer(self, s, li):
        j = li // 2
        if li % 2 == 0:
            self.even_prep(j)
        import os
        st = os.environ.get("K_STAGES", "norm,mix,outp,ffn,ple,sb,hg").split(",")
        self.st = st
        for T in range(NTILE):
            if "norm" in st:
                self.rmsnorm(T, 0, li)
            if li % 2 == 0:
                if "mix" in st:
                    self.mixer_even(s, j, T)
                if "outp" in st:
                    self.out_proj(self.I["ab_w_out"][j], T)
            else:
                if "mix" in st:
                    self.mixer_odd(s, j, T)
                if "outp" in st:
                    self.out_proj(self.I["c_w_out"][j], T)
            self.dump("res_mix_L%d" % li, self.res[:, :, T * TT:(T + 1) * TT], [self.res_b[c][T] for c in range(NCH)],
                      dst=None if not self.dbg or ("res_mix_L%d" % li) not in self.dbg else self.dbg_out["res_mix_L%d" % li][:, :, T * TT:(T + 1) * TT])
            if "ffn" in st:
                bg = None
                if "norm" in st:
                    if T + 1 < NTILE:
                        nxt = (li, T + 1)
                    else:
                        k_ = self.layers.index(li)
                        nxt = (self.layers[k_ + 1], 0) if k_ + 1 < len(self.layers) else None
                    if nxt is not None and OPT_NORMBG:
                        bg = self.stats_bg(nxt, nxt[1])
                self.ffn(s, li, T, bg)
            if "ple" in st:
                self.ple(s, li, T)
        if s == 0:
            self.dump("res_L%d" % li, self.res[:], [b for c in range(NCH) for b in self.res_b[c]])

    def ffn(self, s, li, T, bg=None):
        nc, fw = self.nc, self.fw
        t0 = T * TT
        self.rmsnorm(T, 1, li)
        cur, nxt = T % 2, (T + 1) % 2
        if T == 0:
            fw.op(fw.dve, lambda: nc.vector.memset(self.halo[:, 0, :, :], 0.0), writes=[self.halo_b[0]])
        wup = self.I["ffn_up"][li]
        groups = [(j0, 2) for j0 in range(0, NFC, 2)]
        cw = self.convw
        for (j0, nj) in groups:
            if bg is not None and j0 >= 2:
                next(bg, None)
            slot, slot_b = self.load_w(wup[:, j0 * 128:(j0 + nj) * 128], ncols=512,
                                       src2=wup[:, F_FF + j0 * 128:F_FF + (j0 + nj) * 128])
            for jj in range(nj):
                jp = j0 + jj
                ys = []
                pair_bks = self.proj_fm_pair(slot, slot_b, (jj * 128, nj * 128 + jj * 128), lambda k: self.hn[:, k, :], NCH, self.hn_b)
                for (bank, bank_b), idx in zip(pair_bks, (jp, NFC + jp)):
                    u, u_b = self.upool.get()
                    y, y_b = self.t32.get()
                    self.actf(u[:, 2:2 + TT], bank[:], AF.Copy, [bank_b], [u_b])
                    self.actf(u[:, 0:2], self.halo[:, cur, idx, :], AF.Copy, [self.halo_b[cur]], [u_b])
                    self.actf(self.halo[:, nxt, idx, :], bank[:, TT - 2:TT], AF.Copy, [bank_b], [self.halo_b[nxt]])
                    self.actf(y[:, 0:TT], bank[:], AF.Identity, [bank_b, self.small_b], [y_b],
                              bias=cw[:, li, 3, idx:idx + 1], scale=cw[:, li, 2, idx:idx + 1])
                    self.vstt(y[:, 0:TT], u[:, 1:1 + TT], cw[:, li, 1, idx:idx + 1], y[:, 0:TT], ALU.mult, ALU.add,
                              [u_b, y_b, self.small_b], [y_b])
                    self.vstt(y[:, 0:TT], u[:, 0:TT], cw[:, li, 0, idx:idx + 1], y[:, 0:TT], ALU.mult, ALU.add,
                              [u_b, y_b, self.small_b], [y_b])
                    ys.append((y, y_b))
                (yg, yg_b), (yu, yu_b) = ys
                self.actf(yg[:, 0:TT], yg[:, 0:TT], AF.Silu, [yg_b], [yg_b])
                self.vtt(self.ar[:, jp, :], yg[:, 0:TT], yu[:, 0:TT], ALU.mult, [yg_b, yu_b], [self.ar_b[jp]])
        if bg is not None:
            for _ in bg:
                pass
        wd = self.I["ffn_down"][li]
        jgs = [(0, 8), (8, 8), (16, 6)]
        if OPT_NORMACC:
            self.stats_begin(T)
        for nh in range(2):
            bks = [self.banks.get() for _ in range(4)]
            for (j0, nj) in jgs:
                slot, slot_b = self.load_w(wd[j0 * 128:(j0 + nj) * 128, nh * 512:(nh + 1) * 512], nk=nj)
                for jj in range(nj):
                    jf = j0 + jj
                    for nq in range(4):
                        self.mm(bks[nq][0][:], slot[:, jj, nq * 128:(nq + 1) * 128], self.ar[:, jf, :], jf == 0, jf == NFC - 1,
                                [slot_b, self.ar_b[jf]], [bks[nq][1]])
            for nq in range(4):
                self.add_to_res(bks[nq][0], bks[nq][1], nh * 4 + nq, T, stats=True)

    def ple(self, s, li, T):
        nc, fw = self.nc, self.fw
        t0 = T * TT
        self.rmsnorm(T, 2, li)
        src = self.I["pT"][li, s, :, t0:t0 + TT].rearrange("(c p) t -> p c t", p=128)
        pbuf = self.ar[:, 20:22, :]
        pbuf_bs = [self.ar_b[20], self.ar_b[21]]
        fw.dma_pool(lambda: nc.gpsimd.dma_start(out=pbuf, in_=src), writes=pbuf_bs)
        for half in range(2):
            sg, sg_b = self.load_w(self.I["ple_gate"][li][:, half * 512:(half + 1) * 512])
            spj, spj_b = self.load_w(self.I["ple_proj"][li][:, half * 512:(half + 1) * 512], nk=2)
            for nq in range(4):
                n = half * 4 + nq
                bg, bg_b = self.proj_fm(sg, sg_b, nq * 128, lambda k: self.hn[:, k, :], NCH, self.hn_b)
                bp, bp_b = self.proj_fm(spj, spj_b, nq * 128, lambda k: self.ar[:, 20 + k, :], 2, pbuf_bs)
                g, g_b = self.t32.get()
                self.actf(g[:, 0:TT], bg[:], AF.Sigmoid, [bg_b], [g_b])
                self.vtt(g[:, 0:TT], g[:, 0:TT], bp[:], ALU.mult, [g_b, bp_b], [g_b])
                self.vtt(self.res[:, n, t0:t0 + TT], g[:, 0:TT], self.res[:, n, t0:t0 + TT], ALU.add,
                         [g_b, self.res_b[n][T]], [self.res_b[n][T]])

    def even_prep(self, j):
        nc, fw = self.nc, self.fw
        if j == 0:
            fw.op(fw.dve, lambda: nc.vector.memset(self.lb[:, 0, :], 0.0), writes=[self.lb_b])
        else:
            self.vcopy(self.lb[:, 0, :], self.lb1[:], [self.small_b], [self.lb_b])
        src = self.I["hgn"][0:1, j * 512:(j + 1) * 512].partition_broadcast(128)
        fw.dma_sp(lambda: nc.sync.dma_start(out=self.hgn[:], in_=src), writes=[self.hgn_b])
        self.vts(self.lb[:, 1, :], self.lb[:, 0, :], -1.0, 1.0, ALU.mult, ALU.add, [self.lb_b], [self.lb_b])
        fw.op(fw.dve, lambda: nc.vector.memset(self.S32[:], 0.0), writes=[self.S32_b])
        fw.op(fw.dve, lambda: nc.vector.memset(self.Sbf[:], 0.0), writes=[self.Sbf_b])

    def mixer_even(self, s, j, T):
        nc, fw = self.nc, self.fw
        t0 = T * TT
        w = self.I["ab_w_in"][j]
        hnf = lambda k: self.hn[:, k, :]
        slot, slot_b = self.load_w(w[:, 0:512])
        for m in range(4):
            bank, bank_b = self.proj_fm(slot, slot_b, m * 128, hnf, NCH, self.hn_b)
            self.actf(self.ar[:, m, :], bank[:], AF.Copy, [bank_b], [self.ar_b[m]], scale=0.125)
        slot, slot_b = self.load_w(w[:, 512:1024])
        for m in range(4):
            bank, bank_b = self.proj_fm(slot, slot_b, m * 128, hnf, NCH, self.hn_b)
            self.actf(self.kbuf[:, m, t0:t0 + TT], bank[:], AF.Copy, [bank_b], [self.kb_b[m][T]])
        def tok_block(col0, evac):
            slot, slot_b = self.load_w(w[:, col0:col0 + 512])
            for sub in range(4):
                bank, bank_b = self.banks.get()
                for k in range(NCH):
                    self.mm(bank[:], self.hn[:, k, sub * 128:(sub + 1) * 128], slot[:, k, :], k == 0, k == NCH - 1,
                            [slot_b, self.hn_b[k]], [bank_b])
                evac(sub, bank, bank_b)
        tok_block(1024, lambda sub, bank, bank_b: self.actf(self.vbuf[:, T * 4 + sub, :], bank[:], AF.Copy, [bank_b], [self.vb_b[T * 4 + sub]]))
        tok_block(1536, lambda sub, bank, bank_b: self.actf(self.ar[:, 8 + sub, :], bank[:], AF.Silu, [bank_b], [self.ar_b[8 + sub]]))
        tok_block(2048, lambda sub, bank, bank_b: self.actf(self.stage[:, sub, :], bank[:], AF.Sigmoid, [bank_b], [self.stage_b[sub]]))
        tok_block(2560, lambda sub, bank, bank_b: self.actf(self.ar[:, 12 + sub, :], bank[:], AF.Copy, [bank_b], [self.ar_b[12 + sub]]))
        tok_block(3072, lambda sub, bank, bank_b: self.actf(self.ar[:, 16 + sub, :], bank[:], AF.Silu, [bank_b], [self.ar_b[16 + sub]]))
        self.preconvert(2 * j)
        if "sb" in self.st:
            for pair in ((0, 1), (2, 3), (4, 5), (6, 7)):
                gens = [self.sb_chain(T, h) for h in pair]
                while gens:
                    for g in list(gens):
                        try:
                            next(g)
                        except StopIteration:
                            gens.remove(g)
        if "hg" in self.st and OPT_HGGEN:
            self.hgrn_tile_gen(j, T)
        elif "hg" in self.st and OPT_HGPIPE:
            fr = {0: self.hgrn_front(j, T, 0), 1: self.hgrn_front(j, T, 1)}
            outs = {}
            outs[0] = self.hgrn_mid(fr.pop(0))
            fr[2] = self.hgrn_front(j, T, 2)
            outs[1] = self.hgrn_mid(fr.pop(1))
            outs.pop(0)()
            fr[3] = self.hgrn_front(j, T, 3)
            outs[2] = self.hgrn_mid(fr.pop(2))
            outs.pop(1)()
            outs[3] = self.hgrn_mid(fr.pop(3))
            outs.pop(2)()
            outs.pop(3)()
        elif "hg" in self.st:
            prev_out = None
            for sub in range(4):
                out = self.hgrn_sub(j, T, sub)
                if prev_out is not None:
                    prev_out()
                prev_out = out
            if prev_out is not None:
                prev_out()

    def sb_chain(self, T, h):
        nc, fw = self.nc, self.fw
        hp = (h % 2) * 64
        pr = 64 - hp
        hc = h // 2
        qT = self.ar[hp:hp + 64, hc, :]
        q_b = self.ar_b[hc]
        negtri = self.C("negtri")
        sbmask = self.C("sbmask")
        onescol = self.C("ones", cols=1)
        negrow = self.negrow[pr:pr + 1, :]
        kpv, (pvb, pvb_b) = self.banks.reserve()
        Rf = self.ar[pr:pr + 1, 4:6, :].rearrange("p a b -> p (a b)").bitcast(F32)
        Rf_b = self.sbR_b[pr]
        rts = [(self.ar[pr:pr + 1, 6, :], self.sbrt_b[pr][0]), (self.ar[pr:pr + 1, 7, :], self.sbrt_b[pr][1]),
               (self.ar[pr:pr + 1, 20, :], self.sbrt_b[pr][2])]
        fw.op(fw.dve, lambda: nc.vector.memset(Rf, 0.0), writes=[Rf_b])
        blocks = [(4 * T + kl, kl * 128, True) for kl in (3, 2, 1, 0)] + [(kb, 0, False) for kb in range(4 * T - 1, -1, -1)]
        nb = len(blocks)

        def stage_a(bi):
            kb, c0, diag = blocks[bi]
            kT = self.kbuf[hp:hp + 64, hc, kb * 128:(kb + 1) * 128]
            k_b = self.kb_b[hc][kb // 4]
            kx, (X, X_b) = self.banks.reserve()
            self.mm(X[:, c0:TT], kT, qT[:, c0:TT], True, True, [k_b, q_b], [X_b])
            if OPT_LOCK:
                yield
            ke, (e, e_b) = self.t32.reserve()
            self.actf(e[:, c0:TT], X[:, c0:TT], AF.Exp, [X_b], [e_b])
            kl, (lp, lp_b) = self.t16.reserve()
            self.actf(lp[:, c0:TT], e[:, c0:TT], AF.Ln, [e_b], [lp_b], bias=1.0)
            self.t32.release(ke)
            if diag:
                self.vtt(lp[:, c0:c0 + 128], lp[:, c0:c0 + 128], sbmask, ALU.mult, [lp_b, self.c_b], [lp_b])
            rnew = None
            if bi < nb - 1 and OPT_SBEARLY:
                self.mm(pvb[pr:pr + 1, c0:TT], onescol, lp[:, c0:TT], True, True, [lp_b, self.c_b], [pvb_b], skip=True)
                self.vtt(Rf[:, c0:TT], pvb[pr:pr + 1, c0:TT], Rf[:, c0:TT], ALU.add, [pvb_b, Rf_b], [Rf_b])
                rt, rt_b = rts[bi % 3]
                self.vcopy(rt[:, c0:TT], Rf[:, c0:TT], [Rf_b], [rt_b])
                rnew = (rt, rt_b)
            if False:
                yield
            return kx, X, X_b, kl, lp, lp_b, kT, k_b, rnew

        pend = {0: (yield from stage_a(0))}
        yield
        rprev = None
        pvq = []

        def flush_pv():
            while pvq:
                (bi_, kb_, c0_, wt_, wt_b_, kw_) = pvq.pop(0)
                self.mm(pvb[hp:hp + 64, c0_:TT], self.vbuf[:, kb_, h * 64:(h + 1) * 64], wt_[:, c0_:TT], bi_ == 0, bi_ == nb - 1,
                        [self.vb_b[kb_], wt_b_], [pvb_b], skip=True)
                self.t16.release(kw_)

        for bi in range(nb):
            if bi + 1 < nb:
                pend[bi + 1] = yield from stage_a(bi + 1)
                yield
            flush_pv()
            if OPT_LOCK:
                yield
            kb, c0, diag = blocks[bi]
            kx, X, X_b, kl, lp, lp_b, kT, k_b, rnew = pend.pop(bi)
            has_r = rprev is not None
            cR = c0 + 128 if diag else 0
            use_r = has_r and cR < TT
            self.mm(X[:, c0:TT], kT, qT[:, c0:TT], True, False, [k_b, q_b], [X_b])
            if OPT_LOCK:
                yield
            self.mm(X[:, c0:TT], negtri, lp[:, c0:TT], False, not use_r, [lp_b, self.c_b], [X_b])
            if OPT_LOCK:
                yield
            if use_r:
                rt, rt_b = rprev
                self.mm(X[:, cR:TT], negrow, rt[:, cR:TT], False, True, [rt_b, self.c_b], [X_b])
            if OPT_LOCK:
                yield
            kw, (wt, wt_b) = self.t16.reserve()
            self.actf(wt[:, c0:TT], X[:, c0:TT], AF.Exp, [X_b], [wt_b])
            self.banks.release(kx)
            if diag:
                self.vtt(wt[:, c0:c0 + 128], wt[:, c0:c0 + 128], sbmask, ALU.mult, [wt_b, self.c_b], [wt_b])
            pvq.append((bi, kb, c0, wt, wt_b, kw))
            if bi < nb - 1 and not OPT_SBEARLY:
                self.mm(pvb[pr:pr + 1, c0:TT], onescol, lp[:, c0:TT], True, True, [lp_b, self.c_b], [pvb_b], skip=True)
                if OPT_LOCK:
                    yield
                self.vtt(Rf[:, c0:TT], pvb[pr:pr + 1, c0:TT], Rf[:, c0:TT], ALU.add, [pvb_b, Rf_b], [Rf_b])
                rt, rt_b = rts[bi % 3]
                self.vcopy(rt[:, c0:TT], Rf[:, c0:TT], [Rf_b], [rt_b])
                rnew = (rt, rt_b)
            rprev = rnew
            self.t16.release(kl)
            yield
        flush_pv()
        self.actf(self.hn[hp:hp + 64, hc, :], pvb[hp:hp + 64, :], AF.Copy, [pvb_b], [self.hn_b[hc]])
        self.banks.release(kpv)

    def hgrn_sub(self, j, T, sub):
        nc, fw = self.nc, self.fw
        qs, qs_b = self.ar[:, 8 + sub, :], self.ar_b[8 + sub]
        ib, ib_b = self.ar[:, 12 + sub, :], self.ar_b[12 + sub]
        gs, gs_b = self.ar[:, 16 + sub, :], self.ar_b[16 + sub]
        sg, sg_b = self.stage[:, sub, :], self.stage_b[sub]
        ident = self.C("ident")
        fA, fA_b = self.t32.get()
        f = fA[:, 0:512]
        self.vtt(f, sg, self.lb[:, 1, :], ALU.mult, [sg_b, self.lb_b], [fA_b])
        self.vtt(f, f, self.lb[:, 0, :], ALU.add, [fA_b, self.lb_b], [fA_b])
        lfB, lfB_b = self.t32.get()
        lf = lfB[:, 0:512]
        self.actf(lf, f, AF.Ln, [fA_b], [lfB_b])
        self.vts(f, f, -1.0, 1.0, ALU.mult, ALU.add, [fA_b], [fA_b])
        bd1, bd1_b = self.banks.get()
        bb, bb_b = self.banks.get()
        bd4, bd4_b = self.banks.get()
        self.mm(bd1[:], self.C("trid1", bf=False), lf, True, True, [lfB_b, self.c_b], [bd1_b])
        self.mm(bb[:], self.C("tri2", bf=False), lf, True, True, [lfB_b, self.c_b], [bb_b])
        self.mm(bd4[:], self.C("trisuf", bf=False), lf, True, True, [lfB_b, self.c_b], [bd4_b])
        bz, bz_b = self.banks.get()
        for h in range(4):
            self.mm(bz[:, 2 * h:2 * h + 2], lf[:, h * 128:(h + 1) * 128], self.C("chunkones", bf=False, cols=2), True, True,
                    [lfB_b, self.c_b], [bz_b])
        el, el_b = self.sm_pool.get()
        self.actf(el, bz[:, 0:8], AF.Exp, [bz_b], [el_b])
        E, E_b = self.t32.get()
        q1, q1_b = self.t16.get()
        self.actf(E[:, 0:512], bd1[:], AF.Exp, [bd1_b], [E_b])
        self.vtt(q1[:], qs, E[:, 0:512], ALU.mult, [qs_b, E_b], [q1_b])
        E2, E2_b = self.t32.get()
        k1, k1_b = self.t16.get()
        self.actf(E2[:, 0:512], bd1[:], AF.Exp, [bd1_b], [E2_b], scale=-1.0)
        self.vtt(k1[:], f, E2[:, 0:512], ALU.mult, [fA_b, E2_b], [k1_b])
        E3, E3_b = self.t32.get()
        q3, q3_b = self.t16.get()
        self.actf(E3[:, 0:512], bb[:], AF.Exp, [bb_b], [E3_b])
        self.vtt(q3[:], qs, E3[:, 0:512], ALU.mult, [qs_b, E3_b], [q3_b])
        E4, E4_b = self.t32.get()
        k4, k4_b = self.t16.get()
        self.actf(E4[:, 0:512], bd4[:], AF.Exp, [bd4_b], [E4_b])
        self.vtt(k4[:], f, E4[:, 0:512], ALU.mult, [fA_b, E4_b], [k4_b])
        import os
        HG = float(os.environ.get("K_HG", "9"))
        if HG < 2:
            return
        pA, pA_b = self.pbt[:, 0:512], self.pbt_b[0]
        pB, pB_b = self.pbt[:, 512:1024], self.pbt_b[0]
        for h in range(4):
            self.tr(pA[:, h * 128:(h + 1) * 128], q1[:, h * 128:(h + 1) * 128], ident, [q1_b, self.c_b], [pA_b])
        for h in range(4):
            self.tr(pB[:, h * 128:(h + 1) * 128], k1[:, h * 128:(h + 1) * 128], ident, [k1_b, self.c_b], [pB_b])
        q1T, q1T_b = self.t16.get()
        k1T, k1T_b = self.t16.get()
        if HG < 2.1:
            return
        self.vcopy(q1T[:], pA, [pA_b], [q1T_b])
        if HG < 2.2:
            return
        self.vcopy(k1T[:], pB, [pB_b], [k1T_b])
        if HG < 2.3:
            return
        for h in range(4):
            self.tr(pA[:, h * 128:(h + 1) * 128], q3[:, h * 128:(h + 1) * 128], ident, [q3_b, self.c_b], [pA_b])
        q3T, q3T_b = self.t16.get()
        self.vcopy(q3T[:], pA, [pA_b], [q3T_b])
        if HG < 3:
            return
        bs, bs_b = self.banks.get()
        for c in range(2):
            for h in range(4):
                self.mm(bs[64 * c:64 * c + 64, h * 64:(h + 1) * 64],
                        k1T[:, h * 128 + 64 * c:h * 128 + 64 * c + 64], q1T[:, h * 128 + 64 * c:h * 128 + 64 * c + 64],
                        True, True, [k1T_b, q1T_b], [bs_b])
        scm, scm_b = self.t16.get()
        self.vtt(scm[:, 0:256], bs[:, 0:256], self.C("maskS"), ALU.mult, [bs_b, self.c_b], [scm_b])
        if HG < 4:
            return
        kbo, (bo, bo_b) = self.banks.reserve()
        for c in range(2):
            pc = 64 * c
            for h in range(4):
                hs = slice(h * 128, (h + 1) * 128)
                self.mm(bo[pc:pc + 64, hs], q3T[:, h * 128 + pc:h * 128 + pc + 64], self.Sbf[:, hs], True, False,
                        [q3T_b, self.Sbf_b], [bo_b])
                self.mm(bo[pc:pc + 64, hs], scm[pc:pc + 64, h * 64:(h + 1) * 64], ib[pc:pc + 64, hs], False, True,
                        [scm_b, ib_b], [bo_b])
            bu, bu_b = self.banks.get()
            for h in range(4):
                hs = slice(h * 128, (h + 1) * 128)
                self.mm(bu[:, hs], k4[pc:pc + 64, hs], ib[pc:pc + 64, hs], True, True, [k4_b, ib_b], [bu_b])
            for h in range(4):
                hs = slice(h * 128, (h + 1) * 128)
                self.vstt(self.S32[:, hs], self.S32[:, hs], el[:, 2 * h + c:2 * h + c + 1], bu[:, hs], ALU.mult, ALU.add,
                          [self.S32_b, el_b, bu_b], [self.S32_b])
            self.actf(self.Sbf[:], self.S32[:], AF.Copy, [self.S32_b], [self.Sbf_b])
        if HG < 5:
            self.banks.release(kbo)
            return
        return lambda: self.hgrn_out(j, sub, kbo, bo, bo_b, gs, gs_b, pB, pB_b, ident)

    def hgrn_front(self, j, T, sub):
        qs, qs_b = self.ar[:, 8 + sub, :], self.ar_b[8 + sub]
        sg, sg_b = self.stage[:, sub, :], self.stage_b[sub]
        ident = self.C("ident")
        fA, fA_b = self.t32.get()
        f = fA[:, 0:512]
        self.vtt(f, sg, self.lb[:, 1, :], ALU.mult, [sg_b, self.lb_b], [fA_b])
        self.vtt(f, f, self.lb[:, 0, :], ALU.add, [fA_b, self.lb_b], [fA_b])
        lfB, lfB_b = self.t32.get()
        lf = lfB[:, 0:512]
        self.actf(lf, f, AF.Ln, [fA_b], [lfB_b])
        self.vts(f, f, -1.0, 1.0, ALU.mult, ALU.add, [fA_b], [fA_b])
        bd1, bd1_b = self.banks.get()
        bb, bb_b = self.banks.get()
        bd4, bd4_b = self.banks.get()
        self.mm(bd1[:], self.C("trid1", bf=False), lf, True, True, [lfB_b, self.c_b], [bd1_b])
        self.mm(bb[:], self.C("tri2", bf=False), lf, True, True, [lfB_b, self.c_b], [bb_b])
        self.mm(bd4[:], self.C("trisuf", bf=False), lf, True, True, [lfB_b, self.c_b], [bd4_b])
        bz, bz_b = self.banks.get()
        for h in range(4):
            self.mm(bz[:, 2 * h:2 * h + 2], lf[:, h * 128:(h + 1) * 128], self.C("chunkones", bf=False, cols=2), True, True,
                    [lfB_b, self.c_b], [bz_b])
        el, el_b = self.sm_pool.get()
        self.actf(el, bz[:, 0:8], AF.Exp, [bz_b], [el_b])
        pA, pA_b = self.pbt[:, 0:512], self.pbt_b[0]
        pB, pB_b = self.pbt[:, 512:1024], self.pbt_b[0]
        E, E_b = self.t32.get()
        kq1, (q1, q1_b) = self.t16.reserve()
        self.actf(E[:, 0:512], bd1[:], AF.Exp, [bd1_b], [E_b])
        self.vtt(q1[:], qs, E[:, 0:512], ALU.mult, [qs_b, E_b], [q1_b])
        E2, E2_b = self.t32.get()
        kk1, (k1, k1_b) = self.t16.reserve()
        self.actf(E2[:, 0:512], bd1[:], AF.Exp, [bd1_b], [E2_b], scale=-1.0)
        self.vtt(k1[:], f, E2[:, 0:512], ALU.mult, [fA_b, E2_b], [k1_b])
        for h in range(4):
            self.tr(pA[:, h * 128:(h + 1) * 128], q1[:, h * 128:(h + 1) * 128], ident, [q1_b, self.c_b], [pA_b])
        for h in range(4):
            self.tr(pB[:, h * 128:(h + 1) * 128], k1[:, h * 128:(h + 1) * 128], ident, [k1_b, self.c_b], [pB_b])
        self.t16.release(kq1)
        self.t16.release(kk1)
        kq1T, (q1T, q1T_b) = self.t16.reserve()
        kk1T, (k1T, k1T_b) = self.t16.reserve()
        self.vcopy(q1T[:], pA, [pA_b], [q1T_b])
        self.vcopy(k1T[:], pB, [pB_b], [k1T_b])
        bs, bs_b = self.banks.get()
        for c in range(2):
            for h in range(4):
                self.mm(bs[64 * c:64 * c + 64, h * 64:(h + 1) * 64],
                        k1T[:, h * 128 + 64 * c:h * 128 + 64 * c + 64], q1T[:, h * 128 + 64 * c:h * 128 + 64 * c + 64],
                        True, True, [k1T_b, q1T_b], [bs_b])
        self.t16.release(kq1T)
        self.t16.release(kk1T)
        kscm, (scm, scm_b) = self.t16.reserve()
        self.vtt(scm[:, 0:256], bs[:, 0:256], self.C("maskS"), ALU.mult, [bs_b, self.c_b], [scm_b])
        E3, E3_b = self.t32.get()
        kq3, (q3, q3_b) = self.t16.reserve()
        self.actf(E3[:, 0:512], bb[:], AF.Exp, [bb_b], [E3_b])
        self.vtt(q3[:], qs, E3[:, 0:512], ALU.mult, [qs_b, E3_b], [q3_b])
        for h in range(4):
            self.tr(pA[:, h * 128:(h + 1) * 128], q3[:, h * 128:(h + 1) * 128], ident, [q3_b, self.c_b], [pA_b])
        self.t16.release(kq3)
        kq3T, (q3T, q3T_b) = self.t16.reserve()
        self.vcopy(q3T[:], pA, [pA_b], [q3T_b])
        E4, E4_b = self.t32.get()
        kk4, (k4, k4_b) = self.t16.reserve()
        self.actf(E4[:, 0:512], bd4[:], AF.Exp, [bd4_b], [E4_b])
        self.vtt(k4[:], f, E4[:, 0:512], ALU.mult, [fA_b, E4_b], [k4_b])
        return dict(j=j, sub=sub, el=el, el_b=el_b, scm=scm, scm_b=scm_b, kscm=kscm, q3T=q3T, q3T_b=q3T_b, kq3T=kq3T,
                    k4=k4, k4_b=k4_b, kk4=kk4)

    def hgrn_mid(self, st):
        sub = st["sub"]
        ib, ib_b = self.ar[:, 12 + sub, :], self.ar_b[12 + sub]
        el, el_b, scm, scm_b, q3T, q3T_b, k4, k4_b = st["el"], st["el_b"], st["scm"], st["scm_b"], st["q3T"], st["q3T_b"], st["k4"], st["k4_b"]
        kbo, (bo, bo_b) = self.banks.reserve()
        for c in range(2):
            pc = 64 * c
            for h in range(4):
                hs = slice(h * 128, (h + 1) * 128)
                self.mm(bo[pc:pc + 64, hs], q3T[:, h * 128 + pc:h * 128 + pc + 64], self.Sbf[:, hs], True, False,
                        [q3T_b, self.Sbf_b], [bo_b])
                self.mm(bo[pc:pc + 64, hs], scm[pc:pc + 64, h * 64:(h + 1) * 64], ib[pc:pc + 64, hs], False, True,
                        [scm_b, ib_b], [bo_b])
            bu, bu_b = self.banks.get()
            for h in range(4):
                hs = slice(h * 128, (h + 1) * 128)
                self.mm(bu[:, hs], k4[pc:pc + 64, hs], ib[pc:pc + 64, hs], True, True, [k4_b, ib_b], [bu_b])
            for h in range(4):
                hs = slice(h * 128, (h + 1) * 128)
                self.vstt(self.S32[:, hs], self.S32[:, hs], el[:, 2 * h + c:2 * h + c + 1], bu[:, hs], ALU.mult, ALU.add,
                          [self.S32_b, el_b, bu_b], [self.S32_b])
            self.actf(self.Sbf[:], self.S32[:], AF.Copy, [self.S32_b], [self.Sbf_b])
        self.t16.release(st["kscm"])
        self.t16.release(st["kq3T"])
        self.t16.release(st["kk4"])
        gs, gs_b = self.ar[:, 16 + sub, :], self.ar_b[16 + sub]
        pB, pB_b = self.pbt[:, 512:1024], self.pbt_b[0]
        ident = self.C("ident")
        j = st["j"]
        return lambda: self.hgrn_out(j, sub, kbo, bo, bo_b, gs, gs_b, pB, pB_b, ident)

    def g_front(self, j, T, sub, st):
        qs, qs_b = self.ar[:, 8 + sub, :], self.ar_b[8 + sub]
        sg, sg_b = self.stage[:, sub, :], self.stage_b[sub]
        ident = self.C("ident")
        pA, pB, p_b = self.pbt[:, 0:512], self.pbt[:, 512:1024], self.pbt_b[0]
        kfA, (fA, fA_b) = self.t32.reserve()
        f = fA[:, 0:512]
        self.vtt(f, sg, self.lb[:, 1, :], ALU.mult, [sg_b, self.lb_b], [fA_b])
        self.vtt(f, f, self.lb[:, 0, :], ALU.add, [fA_b, self.lb_b], [fA_b])
        klf, (lfB, lfB_b) = self.t32.reserve()
        lf = lfB[:, 0:512]
        self.actf(lf, f, AF.Ln, [fA_b], [lfB_b])
        self.vts(f, f, -1.0, 1.0, ALU.mult, ALU.add, [fA_b], [fA_b])
        yield
        k1_, (bd1, bd1_b) = self.banks.reserve()
        k2_, (bb, bb_b) = self.banks.reserve()
        k3_, (bd4, bd4_b) = self.banks.reserve()
        self.mm(bd1[:], self.C("trid1", bf=False), lf, True, True, [lfB_b, self.c_b], [bd1_b])
        self.mm(bb[:], self.C("tri2", bf=False), lf, True, True, [lfB_b, self.c_b], [bb_b])
        self.mm(bd4[:], self.C("trisuf", bf=False), lf, True, True, [lfB_b, self.c_b], [bd4_b])
        k4_, (bz, bz_b) = self.banks.reserve()
        for h in range(4):
            self.mm(bz[:, 2 * h:2 * h + 2], lf[:, h * 128:(h + 1) * 128], self.C("chunkones", bf=False, cols=2), True, True,
                    [lfB_b, self.c_b], [bz_b])
        self.t32.release(klf)
        yield
        el, el_b = self.sm_pool.get()
        self.actf(el, bz[:, 0:8], AF.Exp, [bz_b], [el_b])
        self.banks.release(k4_)
        kE, (E, E_b) = self.t32.reserve()
        kq1, (q1, q1_b) = self.t16.reserve()
        self.actf(E[:, 0:512], bd1[:], AF.Exp, [bd1_b], [E_b])
        self.vtt(q1[:], qs, E[:, 0:512], ALU.mult, [qs_b, E_b], [q1_b])
        kk1, (k1, k1_b) = self.t16.reserve()
        self.actf(E[:, 0:512], bd1[:], AF.Exp, [bd1_b, E_b], [E_b], scale=-1.0)
        self.vtt(k1[:], f, E[:, 0:512], ALU.mult, [fA_b, E_b], [k1_b])
        self.banks.release(k1_)
        yield
        for h in range(4):
            self.tr(pA[:, h * 128:(h + 1) * 128], q1[:, h * 128:(h + 1) * 128], ident, [q1_b, self.c_b], [p_b])
        for h in range(4):
            self.tr(pB[:, h * 128:(h + 1) * 128], k1[:, h * 128:(h + 1) * 128], ident, [k1_b, self.c_b], [p_b])
        self.t16.release(kq1)
        self.t16.release(kk1)
        kq1T, (q1T, q1T_b) = self.t16.reserve()
        kk1T, (k1T, k1T_b) = self.t16.reserve()
        self.vcopy(q1T[:], pA, [p_b], [q1T_b])
        self.vcopy(k1T[:], pB, [p_b], [k1T_b])
        yield
        kbs, (bs, bs_b) = self.banks.reserve()
        for c in range(2):
            for h in range(4):
                self.mm(bs[64 * c:64 * c + 64, h * 64:(h + 1) * 64],
                        k1T[:, h * 128 + 64 * c:h * 128 + 64 * c + 64], q1T[:, h * 128 + 64 * c:h * 128 + 64 * c + 64],
                        True, True, [k1T_b, q1T_b], [bs_b])
        self.t16.release(kq1T)
        self.t16.release(kk1T)
        yield
        kscm, (scm, scm_b) = self.t16.reserve()
        self.vtt(scm[:, 0:256], bs[:, 0:256], self.C("maskS"), ALU.mult, [bs_b, self.c_b], [scm_b])
        self.banks.release(kbs)
        kq3, (q3, q3_b) = self.t16.reserve()
        self.actf(E[:, 0:512], bb[:], AF.Exp, [bb_b, E_b], [E_b])
        self.vtt(q3[:], qs, E[:, 0:512], ALU.mult, [qs_b, E_b], [q3_b])
        self.banks.release(k2_)
        yield
        for h in range(4):
            self.tr(pA[:, h * 128:(h + 1) * 128], q3[:, h * 128:(h + 1) * 128], ident, [q3_b, self.c_b], [p_b])
        self.t16.release(kq3)
        kq3T, (q3T, q3T_b) = self.t16.reserve()
        self.vcopy(q3T[:], pA, [p_b], [q3T_b])
        yield
        kk4, (k4, k4_b) = self.t16.reserve()
        self.actf(E[:, 0:512], bd4[:], AF.Exp, [bd4_b, E_b], [E_b])
        self.vtt(k4[:], f, E[:, 0:512], ALU.mult, [fA_b, E_b], [k4_b])
        self.banks.release(k3_)
        self.t32.release(kE)
        self.t32.release(kfA)
        st.update(dict(j=j, sub=sub, el=el, el_b=el_b, scm=scm, scm_b=scm_b, kscm=kscm, q3T=q3T, q3T_b=q3T_b, kq3T=kq3T,
                       k4=k4, k4_b=k4_b, kk4=kk4, front_done=True))

    def g_mid(self, st, prev):
        while prev is not None and not prev.get("mid_done"):
            yield
        sub = st["sub"]
        ib, ib_b = self.ar[:, 12 + sub, :], self.ar_b[12 + sub]
        el, el_b, scm, scm_b, q3T, q3T_b, k4, k4_b = st["el"], st["el_b"], st["scm"], st["scm_b"], st["q3T"], st["q3T_b"], st["k4"], st["k4_b"]
        kbo, (bo, bo_b) = self.banks.reserve()
        for c in range(2):
            pc = 64 * c
            for h in range(4):
                hs = slice(h * 128, (h + 1) * 128)
                self.mm(bo[pc:pc + 64, hs], q3T[:, h * 128 + pc:h * 128 + pc + 64], self.Sbf[:, hs], True, False,
                        [q3T_b, self.Sbf_b], [bo_b])
                self.mm(bo[pc:pc + 64, hs], scm[pc:pc + 64, h * 64:(h + 1) * 64], ib[pc:pc + 64, hs], False, True,
                        [scm_b, ib_b], [bo_b])
            kbu, (bu, bu_b) = self.banks.reserve()
            for h in range(4):
                hs = slice(h * 128, (h + 1) * 128)
                self.mm(bu[:, hs], k4[pc:pc + 64, hs], ib[pc:pc + 64, hs], True, True, [k4_b, ib_b], [bu_b])
            yield
            for h in range(4):
                hs = slice(h * 128, (h + 1) * 128)
                self.vstt(self.S32[:, hs], self.S32[:, hs], el[:, 2 * h + c:2 * h + c + 1], bu[:, hs], ALU.mult, ALU.add,
                          [self.S32_b, el_b, bu_b], [self.S32_b])
            self.banks.release(kbu)
            self.actf(self.Sbf[:], self.S32[:], AF.Copy, [self.S32_b], [self.Sbf_b])
            yield
        self.t16.release(st["kscm"])
        self.t16.release(st["kq3T"])
        self.t16.release(st["kk4"])
        st["kbo"], st["bo"], st["bo_b"] = kbo, bo, bo_b
        st["mid_done"] = True

    def g_out(self, st):
        nc = self.nc
        j, sub = st["j"], st["sub"]
        kbo, bo, bo_b = st["kbo"], st["bo"], st["bo_b"]
        gs, gs_b = self.ar[:, 16 + sub, :], self.ar_b[16 + sub]
        pB, p_b = self.pbt[:, 512:1024], self.pbt_b[0]
        ident = self.C("ident")
        ko1, (osb, osb_b) = self.t32.reserve()
        ko2, (osq, osq_b) = self.t32.reserve()
        o = osb[:, 0:512]
        self.actf(o, bo[:], AF.Copy, [bo_b], [osb_b])
        self.banks.release(kbo)
        yield
        self.vtt(osq[:, 0:512], o, o, ALU.mult, [osb_b], [osq_b])
        ss, ss_b = self.sm_pool.get()
        self.fw.op(self.fw.dve, lambda: nc.vector.tensor_reduce(out=ss[:, 0:4], in_=osq[:, 0:512].rearrange("p (h v) -> p h v", h=4), axis=AX.X, op=ALU.add),
                   [osq_b], [ss_b])
        self.actf(ss[:, 0:4], ss[:, 0:4], AF.Ln, [ss_b], [ss_b], bias=EPS, scale=1.0 / 128)
        self.actf(ss[:, 0:4], ss[:, 0:4], AF.Exp, [ss_b], [ss_b], scale=-0.5)
        yield
        gg = osq[:, 0:512]
        self.vtt(gg, gs, self.hgn[:], ALU.mult, [gs_b, self.hgn_b, osq_b], [osq_b])
        kyb, (yb, yb_b) = self.t16.reserve()
        for h in range(4):
            hs = slice(h * 128, (h + 1) * 128)
            self.vstt(yb[:, hs], o[:, hs], ss[:, h:h + 1], gg[:, hs], ALU.mult, ALU.mult, [osb_b, ss_b, osq_b], [yb_b])
        yield
        for h in range(4):
            self.tr(pB[:, h * 128:(h + 1) * 128], yb[:, h * 128:(h + 1) * 128], ident, [yb_b, self.c_b], [p_b])
        self.vcopy(self.hn[:, 4:8, sub * 128:(sub + 1) * 128], pB.rearrange("p (h t) -> p h t", h=4), [p_b], [self.hn_b[4 + h] for h in range(4)])
        self.t16.release(kyb)
        self.t32.release(ko1)
        self.t32.release(ko2)

    def hgrn_tile_gen(self, j, T):
        sts = [dict() for _ in range(4)]
        pending_f = list(range(4))
        pending_m = list(range(4))
        pending_o = list(range(4))
        act_f = act_m = act_o = None
        while pending_o or act_o is not None:
            if act_f is None and pending_f:
                s_ = pending_f.pop(0)
                act_f = (s_, self.g_front(j, T, s_, sts[s_]))
            if act_m is None and pending_m and sts[pending_m[0]].get("front_done"):
                s_ = pending_m.pop(0)
                act_m = (s_, self.g_mid(sts[s_], sts[s_ - 1] if s_ > 0 else None))
            if act_o is None and pending_o and sts[pending_o[0]].get("mid_done"):
                s_ = pending_o.pop(0)
                act_o = (s_, self.g_out(sts[s_]))
            for name in ("f", "m", "o"):
                cur = {"f": act_f, "m": act_m, "o": act_o}[name]
                if cur is None:
                    continue
                try:
                    next(cur[1])
                except StopIteration:
                    if name == "f":
                        act_f = None
                    elif name == "m":
                        act_m = None
                    else:
                        act_o = None

    def hgrn_out(self, j, sub, kbo, bo, bo_b, gs, gs_b, pB, pB_b, ident):
        nc = self.nc
        osb, osb_b = self.t32.get()
        o = osb[:, 0:512]
        self.actf(o, bo[:], AF.Copy, [bo_b], [osb_b])
        self.banks.release(kbo)
        osq, osq_b = self.t32.get()
        self.vtt(osq[:, 0:512], o, o, ALU.mult, [osb_b], [osq_b])
        ss, ss_b = self.sm_pool.get()
        self.fw.op(self.fw.dve, lambda: nc.vector.tensor_reduce(out=ss[:, 0:4], in_=osq[:, 0:512].rearrange("p (h v) -> p h v", h=4), axis=AX.X, op=ALU.add),
                   [osq_b], [ss_b])
        self.actf(ss[:, 0:4], ss[:, 0:4], AF.Ln, [ss_b], [ss_b], bias=EPS, scale=1.0 / 128)
        self.actf(ss[:, 0:4], ss[:, 0:4], AF.Exp, [ss_b], [ss_b], scale=-0.5)
        gg = osq[:, 0:512]
        self.vtt(gg, gs, self.hgn[:], ALU.mult, [gs_b, self.hgn_b, osq_b], [osq_b])
        yb, yb_b = self.t16.get()
        for h in range(4):
            hs = slice(h * 128, (h + 1) * 128)
            self.vstt(yb[:, hs], o[:, hs], ss[:, h:h + 1], gg[:, hs], ALU.mult, ALU.mult, [osb_b, ss_b, osq_b], [yb_b])
        for h in range(4):
            self.tr(pB[:, h * 128:(h + 1) * 128], yb[:, h * 128:(h + 1) * 128], ident, [yb_b, self.c_b], [pB_b])
        self.vcopy(self.hn[:, 4:8, sub * 128:(sub + 1) * 128], pB.rearrange("p (h t) -> p h t", h=4), [pB_b], [self.hn_b[4 + h] for h in range(4)])

    def qknorm(self, bank, bank_b, gain_ap, out_ap, out_bufs):
        sq, sq_b = self.t16.get()
        self.actf(sq[:], bank[:], AF.Square, [bank_b], [sq_b])
        b2, b2_b = self.banks.get()
        self.mm(b2[:], self.C("bones"), sq[:], True, True, [sq_b, self.c_b], [b2_b])
        r, r_b = self.t32.get()
        rr = r[:, 0:TT]
        self.actf(rr, b2[:], AF.Ln, [b2_b], [r_b], bias=EPS, scale=1.0 / 64)
        self.actf(rr, rr, AF.Exp, [r_b], [r_b], scale=-0.5)
        self.vstt(out_ap, bank[:], gain_ap, rr, ALU.mult, ALU.mult, [bank_b, r_b, self.small_b], out_bufs)

    def mixer_odd(self, s, j, T):
        nc, fw = self.nc, self.fw
        t0 = T * TT
        w = self.I["c_w_in"][j]
        hnf = lambda k: self.hn[:, k, :]
        gq = self.qkn[:, j, 0:1]
        gk = self.qkn[:, j, 1:2]
        for half in range(2):
            slot, slot_b = self.load_w(w[:, half * 512:(half + 1) * 512])
            for m in range(4):
                bank, bank_b = self.proj_fm(slot, slot_b, m * 128, hnf, NCH, self.hn_b)
                cidx = half * 4 + m
                self.qknorm(bank, bank_b, gq, self.ar[:, cidx, :], [self.ar_b[cidx]])
        slot, slot_b = self.load_w(w[:, 1024:1536])
        for m in range(2):
            bank, bank_b = self.proj_fm(slot, slot_b, m * 128, hnf, NCH, self.hn_b)
            self.qknorm(bank, bank_b, gk, self.kbuf[:, m, t0:t0 + TT], [self.kb_b[m][T]])
            bank, bank_b = self.banks.get()
            for k in range(NCH):
                self.mm(bank[0:64, :], slot[:, k, m * 128 + 64:m * 128 + 128], self.hn[:, k, :], k == 0, k == NCH - 1, [slot_b, self.hn_b[k]], [bank_b])
            for k in range(NCH):
                self.mm(bank[64:128, :], slot[:, k, m * 128:m * 128 + 64], self.hn[:, k, :], k == 0, k == NCH - 1, [slot_b, self.hn_b[k]], [bank_b])
            self.qknorm(bank, bank_b, gk, self.kbuf[:, 2 + m, t0:t0 + TT], [self.kb_b[2 + m][T]])
        for sub in range(4):
            bank, bank_b = self.banks.get()
            for k in range(NCH):
                self.mm(bank[:, 0:256], self.hn[:, k, sub * 128:(sub + 1) * 128], slot[:, k, 256:512], k == 0, k == NCH - 1,
                        [slot_b, self.hn_b[k]], [bank_b])
            self.actf(self.vbuf[:, T * 4 + sub, 0:256], bank[:, 0:256], AF.Copy, [bank_b], [self.vb_b[T * 4 + sub]])
        onesbf = self.C("ones", cols=64)
        import os
        OD = float(os.environ.get("K_ODD", "9"))
        if OD < 2:
            return
        self.preconvert(2 * j + 1)
        HPERM = [0, 2, 1, 3]
        units = [(qb, g) for qb in range(4) for g in range(4)]

        def stage_a(qb, g):
            B = 4 * T + qb
            kbs = [B - 1, B] if B > 0 else [B]
            tmps = [self.t32.get() for _ in kbs]
            for par in range(2):
                bs, bs_b = self.banks.get()
                for ki, kb in enumerate(kbs):
                    for ii in range(2):
                        i = par * 2 + ii
                        h = 4 * g + HPERM[i]
                        hp = (h % 2) * 64
                        var = 0 if (g % 2) == (h % 2) else 2
                        kc = var + g // 2
                        c0_ = ki * 256 + ii * 128
                        self.mm(bs[:, c0_:c0_ + 128], self.kbuf[hp:hp + 64, kc, kb * 128:(kb + 1) * 128],
                                self.ar[hp:hp + 64, h // 2, qb * 128:(qb + 1) * 128], True, True,
                                [self.kb_b[kc][kb // 4], self.ar_b[h // 2]], [bs_b])
                for ki, kb in enumerate(kbs):
                    ksel = 0 if kb == B - 1 else 1
                    tmp, tmp_b = tmps[ki]
                    self.vtt(tmp[:, par * 256:(par + 1) * 256], bs[:, ki * 256:(ki + 1) * 256],
                             self.bias[:, ksel, 4 * g + 2 * par:4 * g + 2 * par + 2, :].rearrange("p a b -> p (a b)"), ALU.add,
                             [bs_b, self.bias_b], [tmp_b])
            es = []
            for ki, kb in enumerate(kbs):
                tmp, tmp_b = tmps[ki]
                ke, (e, e_b) = self.t16.reserve()
                self.actf(e[:], tmp[:, 0:512], AF.Exp, [tmp_b], [e_b])
                es.append((kb, ke, e, e_b))
            return es

        nd_sets = [(self.banks.reserve(), self.banks.reserve()) for _ in range(2)]

        def stage_b(qb, g, es):
            hf = g // 2
            (kN, (nbk, nbk_b)), (kD, (dbk, dbk_b)) = nd_sets[(qb * 2 + hf) % 2]
            for i in range(4):
                h = 4 * g + HPERM[i]
                hp = (h % 2) * 64
                c = h // 2
                cs = slice((c % 4) * 128, (c % 4) * 128 + 128)
                for n_, (kb, ke, e, e_b) in enumerate(es):
                    self.mm(nbk[hp:hp + 64, cs], self.vbuf[:, kb, g * 64:(g + 1) * 64], e[:, i * 128:(i + 1) * 128],
                            n_ == 0, n_ == len(es) - 1, [self.vb_b[kb], e_b], [nbk_b])
                for n_, (kb, ke, e, e_b) in enumerate(es):
                    self.mm(dbk[hp:hp + 64, cs], onesbf, e[:, i * 128:(i + 1) * 128],
                            n_ == 0, n_ == len(es) - 1, [self.c_b, e_b], [dbk_b])
            for (kb, ke, e, e_b) in es:
                self.t16.release(ke)
            if g % 2 == 1:
                r, r_b = self.t32.get()
                for cq in range(4):
                    c = hf * 4 + cq
                    self.actf(r[:, cq * 128:(cq + 1) * 128], dbk[:, cq * 128:(cq + 1) * 128], AF.Ln, [dbk_b, self.small_b], [r_b],
                              bias=self.snk[:, j, c:c + 1])
                self.actf(r[:, 0:512], r[:, 0:512], AF.Exp, [r_b], [r_b], scale=-1.0)
                self.vtt(self.hn[:, hf * 4:(hf + 1) * 4, qb * 128:(qb + 1) * 128],
                         nbk[:].rearrange("p (c t) -> p c t", c=4), r[:, 0:512].rearrange("p (c t) -> p c t", c=4), ALU.mult,
                         [nbk_b, r_b], [self.hn_b[hf * 4 + i] for i in range(4)])

        pend = stage_a(*units[0])
        for ui, (qb, g) in enumerate(units):
            nxt = stage_a(*units[ui + 1]) if ui + 1 < len(units) else None
            stage_b(qb, g, pend)
            pend = nxt
        for (a_, b_) in nd_sets:
            self.banks.release(a_[0])
            self.banks.release(b_[0])


def _host_small(inp):
    f32 = np.float32
    norms = np.stack([inp["mix_norm"], inp["ffn_norm"], inp["ple_norm"]], 0).astype(f32)
    norms = np.ascontiguousarray(norms.reshape(3, DEPTH, NCH, 128).transpose(3, 0, 1, 2))
    cw = np.concatenate([inp["ffn_conv"].astype(f32), inp["ffn_conv_b"].astype(f32)[:, None, :]], 1)
    cw = np.ascontiguousarray(cw.reshape(DEPTH, 4, 44, 128).transpose(3, 0, 1, 2))
    lbl = np.ascontiguousarray(inp["hg_lb_logits"].astype(f32).reshape(1, 1024))
    hgn = np.ascontiguousarray(np.tile(inp["hg_out_norm"].astype(f32), (1, 4)).reshape(1, 1024))
    qkn = np.stack([np.tile(inp["q_norm"].astype(f32), (1, 2)), np.tile(inp["k_norm"].astype(f32), (1, 2))], -1)
    qkn = np.ascontiguousarray(qkn.transpose(1, 0, 2))
    snk = inp["sinks"].astype(f32).reshape(2, NCH, 2)
    snk = np.ascontiguousarray(np.repeat(snk.transpose(2, 0, 1), 64, axis=0))
    bucket, valid = _t5_bias_index()
    rb = inp["rel_bias"].astype(f32)
    bt = rb[bucket]
    bt = np.where(valid[..., None], bt, f32(NEG)).transpose(0, 1, 3, 2)
    hperm = np.array([4 * g + i for g in range(4) for i in (0, 2, 1, 3)])
    bt = bt[:, :, hperm, :]
    biasT = np.ascontiguousarray(bt.reshape(128, 2 * 16 * 128).astype(f32))
    cf, cb = _const_arrays()
    return dict(norms=norms, convw=cw, lbl=lbl, hgn=hgn, qkn=qkn, snk=snk, biasT=biasT, cf=cf, cb=cb)


_CACHE = {}


def _run(inputs, nseq, layers, ncores, dbg=None, trace=False):
    key = (nseq, tuple(layers), tuple(sorted(dbg.items())) if dbg else None)
    f32 = np.float32
    small = _host_small(inputs)
    shared = {k: np.ascontiguousarray(np.asarray(inputs[k], dtype=f32)) for k in
              ["ab_w_in", "ab_w_out", "c_w_in", "c_w_out", "ffn_up", "ffn_down", "ple_gate", "ple_proj"]}
    x = np.asarray(inputs["x"], dtype=f32)
    p = np.asarray(inputs["p"], dtype=f32)
    in_maps = []
    for ci in range(ncores):
        sl = slice(ci * nseq, (ci + 1) * nseq)
        m = dict(shared)
        m.update(small)
        m["xT"] = np.ascontiguousarray(x[sl].transpose(0, 2, 1))
        m["pT"] = np.ascontiguousarray(p[:, sl].transpose(0, 1, 3, 2))
        in_maps.append(m)
    nc = Builder(nseq, layers, dbg).build()
    res = run_bass_kernel_spmd(nc, in_maps, core_ids=list(range(ncores)))
    return res


def kernel(**inputs):
    res = _run(inputs, 2, list(range(DEPTH)), 8)
    outs = [r["outT"] for r in res.results]
    out = np.concatenate(outs, axis=0).transpose(0, 2, 1)
    return np.ascontiguousarray(out.astype(np.float32))
```

```python
import math
from contextlib import ExitStack

import numpy as np

import concourse.bass as bass
import concourse.mybir as mybir
from concourse.bass_utils import run_bass_kernel_spmd

F32 = mybir.dt.float32
BF16 = mybir.dt.bfloat16
AF = mybir.ActivationFunctionType
ALU = mybir.AluOpType
AX = mybir.AxisListType

D = 1024
S = 2048
DEPTH = 4
NCH = 8
TT = 512
NTILE = S // TT
F_FF = 2816
NFC = 22
EPS = 1e-6
NEG = -30000.0
import os as _os
_OPT = _os.environ.get("K_OPT", "normbg,normacc,lock,hggen,preconv").split(",")
OPT_SBEARLY = "sbearly" in _OPT
OPT_NORMACC = "normacc" in _OPT
OPT_NORMBG = "normbg" in _OPT
OPT_LOCK = "lock" in _OPT
OPT_HGPIPE = "hgpipe" in _OPT
OPT_HGGEN = "hggen" in _OPT
OPT_PRECONV = "preconv" in _OPT


class Buf:
    __slots__ = ("w", "r")

    def __init__(self):
        self.w = None
        self.r = {}


class Stream:
    def __init__(self, name, h, pe=False):
        self.name = name
        self.h = h
        self.pe = pe
        self.seq = 0
        self.know = {}
        self.oplist = []
        self.sem = None


class Chan:
    def __init__(self, name):
        self.name = name
        self.cnt = 0
        self.last = None
        self.sem = None


class Op:
    __slots__ = ("st", "fn", "waits", "needed", "seq", "val", "chan")


class FW:
    def __init__(self, nc, es):
        self.nc = nc
        self.es = es
        self.ops = []
        self.pe = self._mk("pe", nc.tensor, True)
        self.act = self._mk("act", nc.scalar)
        self.dve = self._mk("dve", nc.vector)
        self.pool = self._mk("pool", nc.gpsimd)
        self.sp = self._mk("sp", nc.sync)
        self.streams = [self.pe, self.act, self.dve, self.pool, self.sp]
        self.chans = []
        self.sp_ch = [self._ch("spc%d" % i) for i in range(6)]
        self.pl_ch = [self._ch("plc%d" % i) for i in range(4)]
        self._spi = 0
        self._pli = 0

    def _mk(self, name, h, pe=False):
        st = Stream(name, h, pe)
        st.sem = self.es.enter_context(self.nc.semaphore("s_" + name))
        return st

    def _ch(self, name):
        c = Chan(name)
        c.sem = self.es.enter_context(self.nc.semaphore("c_" + name))
        self.chans.append(c)
        return c

    def _need(self, st, waits, ev, raw):
        if ev is None:
            return
        s, n, kn = ev
        if s is st:
            if st.pe:
                return
        if st.know.get(s, 0) >= n:
            return
        waits.append((s, n))
        newk = dict(st.know)
        newk[s] = n
        for a, b in kn.items():
            if newk.get(a, 0) < b:
                newk[a] = b
        st.know = newk

    def op(self, st, fn, reads=(), writes=(), chan=None):
        waits = []
        for b in reads:
            self._need(st, waits, b.w, True)
        for b in writes:
            self._need(st, waits, b.w, False)
            for ev in b.r.values():
                self._need(st, waits, ev, False)
        if chan is not None:
            self._need(st, waits, chan.last, False)
        st.seq += 1
        o = Op()
        o.st = st
        o.fn = fn
        o.waits = waits
        o.needed = False
        o.seq = st.seq
        o.val = 0
        o.chan = chan
        self.ops.append(o)
        st.oplist.append(o)
        if chan is not None:
            chan.cnt += 16 * (len(fn) if isinstance(fn, (list, tuple)) else 1)
            ev = (chan, chan.cnt, st.know)
            chan.last = ev
        else:
            ev = (st, st.seq, st.know)
        for b in reads:
            b.r[ev[0]] = ev
        for b in writes:
            b.w = ev
            b.r = {}
        return ev

    def dma_sp(self, fn, reads=(), writes=()):
        ch = self.sp_ch[self._spi % len(self.sp_ch)]
        self._spi += 1
        return self.op(self.sp, fn, reads, writes, chan=ch)

    def dma_pool(self, fn, reads=(), writes=()):
        ch = self.pl_ch[self._pli % len(self.pl_ch)]
        self._pli += 1
        return self.op(self.pool, fn, reads, writes, chan=ch)

    def finish(self, final_events):
        waits = []
        for ev in final_events:
            self._need(self.sp, waits, ev, True)
        for o in self.ops:
            for (s, n) in o.waits:
                if isinstance(s, Stream):
                    s.oplist[n - 1].needed = True
        for (s, n) in waits:
            if isinstance(s, Stream):
                s.oplist[n - 1].needed = True
        for st in self.streams:
            c = 0
            for o in st.oplist:
                if o.needed:
                    c += 1
                o.val = c

        def emit_wait(st, s, n):
            if isinstance(s, Stream):
                st.h.wait_ge(s.sem, s.oplist[n - 1].val)
            else:
                st.h.wait_ge(s.sem, n)

        nw = 0
        for o in self.ops:
            for (s, n) in o.waits:
                emit_wait(o.st, s, n)
                nw += 1
            if isinstance(o.fn, (list, tuple)):
                for f_ in o.fn:
                    f_().then_inc(o.chan.sem, 16)
                continue
            ins = o.fn()
            if o.chan is not None:
                ins.then_inc(o.chan.sem, 16)
            elif o.needed:
                ins.then_inc(o.st.sem, 1)
        for (s, n) in waits:
            emit_wait(self.sp, s, n)
        self.stats = dict(n_ops=len(self.ops), n_waits=nw,
                          per_stream={st.name: len(st.oplist) for st in self.streams})


class Pool_:
    def __init__(self, tiles):
        self.items = [(t, Buf()) for t in tiles]
        self.free = list(range(len(tiles)))

    def get(self):
        k = self.free.pop(0)
        self.free.append(k)
        return self.items[k]

    def reserve(self):
        k = self.free.pop(0)
        return k, self.items[k]

    def release(self, k):
        self.free.append(k)


def _consts():
    p = np.arange(128)[:, None]
    m = np.arange(128)[None, :]
    c = {}
    c["ident"] = (p == m).astype(np.float32)
    c["ones"] = np.ones((128, 128), np.float32)
    c["bones"] = ((p // 64) == (m // 64)).astype(np.float32)
    c["negtri"] = -(p >= m).astype(np.float32)
    c["sbmask"] = (p < m).astype(np.float32)
    same = (p // 64) == (m // 64)
    tri2 = (same & (p <= m)).astype(np.float32)
    mid = (m // 64) * 64 + 31
    trimid = (same & (p <= mid)).astype(np.float32)
    c["tri2"] = tri2
    c["trid1"] = tri2 - trimid
    c["trisuf"] = (same & (p > m)).astype(np.float32)
    co = np.zeros((128, 128), np.float32)
    co[:, 0] = (np.arange(128) < 64)
    co[:, 1] = (np.arange(128) >= 64)
    c["chunkones"] = co
    mk = np.zeros((128, 256), np.float32)
    cc = np.arange(256)[None, :]
    mk[:, :] = ((p % 64) <= (cc % 64))
    c["maskS"] = mk
    return c


F32_CONSTS = ["tri2", "trid1", "trisuf", "chunkones"]
BF_CONSTS = ["ident", "ones", "bones", "negtri", "sbmask", "maskS"]


def _const_arrays():
    c = _consts()
    f = np.concatenate([c[k] for k in F32_CONSTS], axis=1)
    b = np.concatenate([c[k] for k in BF_CONSTS], axis=1)
    return np.ascontiguousarray(f), np.ascontiguousarray(b)


def _offsets(names, c):
    off = {}
    o = 0
    for k in names:
        off[k] = o
        o += c[k].shape[1]
    return off, o


def _t5_bias_index():
    W = 128
    t = np.arange(W)[None, None, :]
    s = np.arange(W)[:, None, None]
    kb = np.arange(2)[None, :, None]
    dist = t + W - (kb * W + s)
    valid = (dist >= 0) & (dist < W)
    max_exact = 16
    large = max_exact + (np.log(np.maximum(dist, max_exact) / max_exact) / math.log(128 / max_exact) * (32 - max_exact)).astype(np.int32)
    large = np.minimum(large, 31)
    bucket = np.where(dist < max_exact, np.maximum(dist, 0), large).astype(np.int32)
    return bucket, valid


class Builder:
    def __init__(self, nseq, layers, dbg=None):
        self.nseq = nseq
        self.layers = layers
        self.dbg = dbg

    def build(self):
        nc = bass.Bass("TRN2", target_bir_lowering=False)
        self.nc = nc
        nseq = self.nseq
        dt = nc.dram_tensor
        I = {}
        I["xT"] = dt("xT", [nseq, D, S], F32, kind="ExternalInput").ap()
        I["pT"] = dt("pT", [DEPTH, nseq, 256, S], F32, kind="ExternalInput").ap()
        I["ab_w_in"] = dt("ab_w_in", [2, D, 3584], F32, kind="ExternalInput").ap()
        I["ab_w_out"] = dt("ab_w_out", [2, D, D], F32, kind="ExternalInput").ap()
        I["c_w_in"] = dt("c_w_in", [2, D, 1536], F32, kind="ExternalInput").ap()
        I["c_w_out"] = dt("c_w_out", [2, D, D], F32, kind="ExternalInput").ap()
        I["ffn_up"] = dt("ffn_up", [DEPTH, D, 2 * F_FF], F32, kind="ExternalInput").ap()
        I["ffn_down"] = dt("ffn_down", [DEPTH, F_FF, D], F32, kind="ExternalInput").ap()
        I["ple_gate"] = dt("ple_gate", [DEPTH, D, D], F32, kind="ExternalInput").ap()
        I["ple_proj"] = dt("ple_proj", [DEPTH, 256, D], F32, kind="ExternalInput").ap()
        I["norms"] = dt("norms", [128, 3, DEPTH, NCH], F32, kind="ExternalInput").ap()
        I["convw"] = dt("convw", [128, DEPTH, 4, 44], F32, kind="ExternalInput").ap()
        I["lbl"] = dt("lbl", [1, 2 * 512], F32, kind="ExternalInput").ap()
        I["hgn"] = dt("hgn", [1, 2 * 512], F32, kind="ExternalInput").ap()
        I["qkn"] = dt("qkn", [128, 2, 2], F32, kind="ExternalInput").ap()
        I["snk"] = dt("snk", [128, 2, NCH], F32, kind="ExternalInput").ap()
        I["biasT"] = dt("biasT", [128, 2 * 16 * 128], F32, kind="ExternalInput").ap()
        cf, cb = _const_arrays()
        I["cf"] = dt("cf", list(cf.shape), F32, kind="ExternalInput").ap()
        I["cb"] = dt("cb", list(cb.shape), F32, kind="ExternalInput").ap()
        self.I = I
        self.outT = dt("outT", [nseq, D, S], F32, kind="ExternalOutput").ap()
        self.wscr = dt("wscr", [120, 128, 4096], BF16, kind="Internal").ap()
        self.wimg = {}
        self.preconv_done = set()
        if self.dbg:
            self.dbg_out = {k: dt("dbg_" + k, list(shp), F32, kind="ExternalOutput").ap() for k, shp in self.dbg.items()}
        c = _consts()
        self.cf_off, self.cf_n = _offsets(F32_CONSTS, c)
        self.cb_off, self.cb_n = _offsets(BF_CONSTS, c)

        with ExitStack() as es:
            self.es = es
            fw = FW(nc, es)
            self.fw = fw
            sb = lambda name, shape, dtype: es.enter_context(nc.sbuf_tensor(name, shape, dtype))
            self.res = sb("res", [128, NCH, S], F32)
            self.res_b = [[Buf() for _ in range(NTILE)] for _ in range(NCH)]
            self.hn = sb("hn", [128, NCH, TT], BF16)
            self.hn_b = [Buf() for _ in range(NCH)]
            self.ar = sb("arena", [128, NFC, TT], BF16)
            self.ar_b = [Buf() for _ in range(NFC)]
            self.kbuf = sb("kbuf", [128, 4, S], BF16)
            self.kb_b = [[Buf() for _ in range(NTILE)] for _ in range(4)]
            self.vbuf = sb("vbuf", [128, 16, 512], BF16)
            self.vb_b = [Buf() for _ in range(16)]
            self.bias = sb("bias", [128, 2, 16, 128], BF16)
            self.bias_b = Buf()
            self.ws = [sb("ws%d" % i, [128, NCH, 512], BF16) for i in range(3)]
            self.wpool = Pool_(self.ws)
            self.stage = sb("stage", [128, 4, 512], F32)
            self.stage_b = [Buf() for _ in range(4)]
            stf = self.stage[:].rearrange("p a b -> p (a b)")
            self.upool = Pool_([stf[:, i * 520:i * 520 + 516] for i in range(3)])
            t32 = [sb("t32_%d" % i, [128, 516], F32) for i in range(7)]
            self.t32 = Pool_(t32)
            t16 = [sb("t16_%d" % i, [128, 512], BF16) for i in range(7)]
            self.t16 = Pool_(t16)
            self.cf = sb("cf_sb", [128, self.cf_n], F32)
            self.cb = sb("cb_sb", [128, self.cb_n], BF16)
            self.c_b = Buf()
            self.norms = sb("norms_sb", [128, 3, DEPTH, NCH], F32)
            self.convw = sb("convw_sb", [128, DEPTH, 4, 44], F32)
            self.qkn = sb("qkn_sb", [128, 2, 2], F32)
            self.snk = sb("snk_sb", [128, 2, NCH], F32)
            self.lb = sb("lb_sb", [128, 2, 512], F32)
            self.lb1 = sb("lb1_sb", [128, 512], F32)
            self.hgn = sb("hgn_sb", [128, 512], F32)
            self.hgn_b = Buf()
            self.small_b = Buf()
            self.lb_b = Buf()
            self.S32 = sb("S32", [128, 512], F32)
            self.S32_b = Buf()
            self.Sbf = sb("Sbf", [128, 512], BF16)
            self.Sbf_b = Buf()
            self.halo = sb("halo", [128, 2, 44, 2], F32)
            self.halo_b = [Buf(), Buf()]
            self.sm = sb("smalls", [128, 64], F32)
            self.sm_pool = Pool_([self.sm[:, i * 8:(i + 1) * 8] for i in range(8)])
            self.negrow = sb("negrow", [128, 128], BF16)
            self.sbR_b = {0: Buf(), 64: Buf()}
            self.sbrt_b = {0: [Buf(), Buf(), Buf()], 64: [Buf(), Buf(), Buf()]}
            banks = [es.enter_context(nc.psum_tensor("pb%d" % i, [128, 512], F32)) for i in range(7)]
            self.banks = Pool_(banks)
            self.pbt = es.enter_context(nc.psum_tensor("pbt", [128, 1024], BF16))
            self.pbt_b = [Buf(), Buf()]

            self.acc = None
            self.pre_rstd = None
            self.prologue()
            finals = []
            self.finals = finals
            self.load_x(0)
            for s in range(nseq):
                for li in self.layers:
                    self.layer(s, li)
            if self.dbg:
                finals += self.dbg_events
            fw.finish(finals)
        return nc

    def C(self, name, bf=True, cols=None):
        if bf:
            o = self.cb_off[name]
            n = _consts()[name].shape[1] if cols is None else cols
            return self.cb[:, o:o + n]
        o = self.cf_off[name]
        n = _consts()[name].shape[1] if cols is None else cols
        return self.cf[:, o:o + n]

    def mm(self, out, lhsT, rhs, start, stop, reads, writes, skip=False):
        nc = self.nc
        if skip:
            return self.fw.op(self.fw.pe, lambda: nc.tensor.matmul(out, lhsT, rhs, start=start, stop=stop, skip_group_check=True), reads, writes)
        return self.fw.op(self.fw.pe, lambda: nc.tensor.matmul(out, lhsT, rhs, start=start, stop=stop), reads, writes)

    def tr(self, out, in_, ident, reads, writes):
        nc = self.nc
        return self.fw.op(self.fw.pe, lambda: nc.tensor.transpose(out, in_, ident), reads, writes)

    def actf(self, out, in_, func, reads, writes, bias=0.0, scale=1.0):
        nc = self.nc
        return self.fw.op(self.fw.act, lambda: nc.scalar.activation(out=out, in_=in_, func=func, bias=bias, scale=scale), reads, writes)

    def vtt(self, out, in0, in1, op, reads, writes):
        nc = self.nc
        return self.fw.op(self.fw.dve, lambda: nc.vector.tensor_tensor(out=out, in0=in0, in1=in1, op=op), reads, writes)

    def vts(self, out, in0, s1, s2, op0, op1, reads, writes):
        nc = self.nc
        if op1 is None:
            return self.fw.op(self.fw.dve, lambda: nc.vector.tensor_scalar(out=out, in0=in0, scalar1=s1, scalar2=None, op0=op0), reads, writes)
        return self.fw.op(self.fw.dve, lambda: nc.vector.tensor_scalar(out=out, in0=in0, scalar1=s1, scalar2=s2, op0=op0, op1=op1), reads, writes)

    def vstt(self, out, in0, scalar, in1, op0, op1, reads, writes):
        nc = self.nc
        return self.fw.op(self.fw.dve, lambda: nc.vector.scalar_tensor_tensor(out=out, in0=in0, scalar=scalar, in1=in1, op0=op0, op1=op1), reads, writes)

    def vcopy(self, out, in_, reads, writes):
        nc = self.nc
        return self.fw.op(self.fw.dve, lambda: nc.vector.tensor_copy(out, in_), reads, writes)

    def vrecip(self, out, in_, reads, writes):
        nc = self.nc
        return self.fw.op(self.fw.dve, lambda: nc.vector.reciprocal(out=out, in_=in_), reads, writes)

    def pcopy(self, out, in_, reads, writes):
        nc = self.nc
        return self.fw.op(self.fw.pool, lambda: nc.gpsimd.tensor_copy(out, in_), reads, writes)

    def load_w(self, src_ap, nk=NCH, ncols=512, src2=None, bgq=False):
        nc = self.nc
        key = (src_ap.tensor.name, str(src_ap.offset), tuple(tuple(x) for x in src_ap.ap))
        slot, b = self.wpool.get()
        img = self.wimg.get(key)
        if img is None:
            idx = len(self.wimg)
            ib = Buf()
            self.wimg[key] = (idx, ib)
            if src2 is None:
                dst = slot[:, 0:nk, 0:ncols]
                src = src_ap.rearrange("(c p) n -> p c n", p=128)
                self.fw.dma_pool(lambda: nc.gpsimd.dma_start(out=dst, in_=src), reads=(), writes=[b])
            else:
                h = ncols // 2
                fns = []
                for i_, sa in enumerate((src_ap, src2)):
                    dst = slot[:, 0:nk, i_ * h:(i_ + 1) * h]
                    src = sa.rearrange("(c p) n -> p c n", p=128)
                    fns.append(lambda dst=dst, src=src: nc.gpsimd.dma_start(out=dst, in_=src))
                self.fw.dma_pool(fns, reads=(), writes=[b])
            if ncols == 512:
                simg = self.wscr[idx, :, 0:nk * 512]
                ssrc = slot[:, 0:nk, :].rearrange("p c n -> p (c n)")
                if bgq:
                    self.bg_pending.append((lambda: nc.gpsimd.dma_start(out=simg, in_=ssrc), b, ib))
                    while len(self.bg_pending) > 2:
                        f_, b_, ib_ = self.bg_pending.pop(0)
                        self.fw.dma_pool(f_, reads=[b_], writes=[ib_])
                else:
                    self.fw.dma_sp(lambda: nc.sync.dma_start(out=simg, in_=ssrc), reads=[b], writes=[ib])
            else:
                self.wimg[key] = None
                del self.wimg[key]
        else:
            idx, ib = img
            simg = self.wscr[idx, :, 0:nk * 512]
            sdst = slot[:, 0:nk, :].rearrange("p c n -> p (c n)")
            self.fw.dma_sp(lambda: nc.sync.dma_start(out=sdst, in_=simg), reads=[ib], writes=[b])
        return slot, b

    def preconvert(self, li):
        if li in self.preconv_done or not OPT_PRECONV:
            return
        self.preconv_done.add(li)
        self.bg_pending = []
        j = li // 2
        wo = self.I["ab_w_out"][j] if li % 2 == 0 else self.I["c_w_out"][j]
        for half in range(2):
            self.load_w(wo[:, half * 512:(half + 1) * 512], bgq=True)
        wup = self.I["ffn_up"][li]
        for j0 in range(0, NFC, 2):
            self.load_w(wup[:, j0 * 128:(j0 + 2) * 128], ncols=512, src2=wup[:, F_FF + j0 * 128:F_FF + (j0 + 2) * 128], bgq=True)
        while self.bg_pending:
            f_, b_, ib_ = self.bg_pending.pop(0)
            self.fw.dma_pool(f_, reads=[b_], writes=[ib_])

    def dump(self, key, sb_ap, bufs, dst=None):
        if not self.dbg or key not in self.dbg:
            return
        nc = self.nc
        d = self.dbg_out[key] if dst is None else dst
        ev = self.fw.dma_sp(lambda: nc.sync.dma_start(out=d, in_=sb_ap), reads=bufs, writes=())
        self.dbg_events.append(ev)

    def prologue(self):
        nc, fw, I = self.nc, self.fw, self.I
        self.dbg_events = []
        fw.dma_sp(lambda: nc.sync.dma_start(out=self.cf[:], in_=I["cf"]), writes=[self.c_b])
        fw.dma_pool(lambda: nc.gpsimd.dma_start(out=self.cb[:], in_=I["cb"]), writes=[self.c_b])
        fw.dma_sp(lambda: nc.sync.dma_start(out=self.norms[:], in_=I["norms"]), writes=[self.small_b])
        fw.dma_sp(lambda: nc.sync.dma_start(out=self.convw[:], in_=I["convw"]), writes=[self.small_b])
        fw.dma_sp(lambda: nc.sync.dma_start(out=self.qkn[:], in_=I["qkn"]), writes=[self.small_b])
        fw.dma_sp(lambda: nc.sync.dma_start(out=self.snk[:], in_=I["snk"]), writes=[self.small_b])
        l0, l0_b = self.t32.get()
        l1, l1_b = self.t32.get()
        fw.dma_sp(lambda: nc.sync.dma_start(out=l0[:, 0:512], in_=I["lbl"][0:1, 0:512].partition_broadcast(128)), writes=[l0_b])
        fw.dma_sp(lambda: nc.sync.dma_start(out=l1[:, 0:512], in_=I["lbl"][0:1, 512:1024].partition_broadcast(128)), writes=[l1_b])
        self.vtt(l1[:, 0:512], l1[:, 0:512], l0[:, 0:512], ALU.subtract, [l0_b, l1_b], [l1_b])
        self.actf(self.lb1[:], l1[:, 0:512], AF.Sigmoid, [l1_b], [self.small_b])
        fw.dma_pool(lambda: nc.gpsimd.dma_start(out=self.bias[:].rearrange("p a b c -> p (a b c)"), in_=I["biasT"]), writes=[self.bias_b])
        fw.op(fw.dve, lambda: nc.vector.memset(self.negrow[:], -1.0), writes=[self.c_b])
        self.vts(self.qkn[:, :, 0:1], self.qkn[:, :, 0:1], 0.125, None, ALU.mult, None, [self.small_b], [self.small_b])
        self.actf(self.snk[:], self.snk[:], AF.Exp, [self.small_b], [self.small_b])

    def load_x(self, s, T=None, pool=False):
        nc, fw = self.nc, self.fw
        for T_ in (range(NTILE) if T is None else [T]):
            for c in range(NCH):
                src = self.I["xT"][s, c * 128:(c + 1) * 128, T_ * TT:(T_ + 1) * TT]
                dst = self.res[:, c, T_ * TT:(T_ + 1) * TT]
                if pool:
                    fw.dma_pool(lambda dst=dst, src=src: nc.gpsimd.dma_start(out=dst, in_=src), writes=[self.res_b[c][T_]])
                else:
                    fw.dma_sp(lambda dst=dst, src=src: nc.sync.dma_start(out=dst, in_=src), writes=[self.res_b[c][T_]])

    def store_tile(self, s, T):
        nc, fw = self.nc, self.fw
        evs = []
        for c in range(NCH):
            dst = self.outT[s, c * 128:(c + 1) * 128, T * TT:(T + 1) * TT]
            src = self.res[:, c, T * TT:(T + 1) * TT]
            evs.append(fw.dma_pool(lambda dst=dst, src=src: nc.gpsimd.dma_start(out=dst, in_=src), reads=[self.res_b[c][T]]))
        return evs

    def store_out(self, s):
        nc, fw = self.nc, self.fw
        evs = []
        for c in range(NCH):
            dst = self.outT[s, c * 128:(c + 1) * 128, :]
            src = self.res[:, c, :]
            evs.append(fw.dma_sp(lambda dst=dst, src=src: nc.sync.dma_start(out=dst, in_=src), reads=self.res_b[c]))
        return evs

    def rmsnorm(self, T, which, li):
        t0 = T * TT
        if which == 0 and self.pre_rstd is not None and self.pre_rstd[0] == (li, T):
            _, kr, rr, r_b = self.pre_rstd
            self.pre_rstd = None
        elif self.acc is not None and self.acc["T"] == T and self.acc["n"] == NCH:
            kr, rr, r_b = self.stats_finish()
        else:
            self.stats_begin(T)
            for c in range(NCH):
                self.stats_add(c)
            kr, rr, r_b = self.stats_finish()
        for c in range(NCH):
            g = self.norms[:, which, li, c:c + 1]
            self.vstt(self.hn[:, c, :], self.res[:, c, t0:t0 + TT], g, rr, ALU.mult, ALU.mult,
                      [self.res_b[c][T], r_b, self.small_b], [self.hn_b[c]])
        self.t32.release(kr)

    def stats_begin(self, T):
        kb, (ssb, ssb_b) = self.banks.reserve()
        self.acc = dict(T=T, n=0, kb=kb, ssb=ssb, ssb_b=ssb_b, pend=None)

    def _stats_flush(self):
        a = self.acc
        if a["pend"] is not None:
            sq, sq_b, first, last = a["pend"]
            self.mm(a["ssb"][:], self.C("ones"), sq[:], first, last, [sq_b, self.c_b], [a["ssb_b"]])
            a["pend"] = None

    def stats_add(self, c):
        a = self.acc
        T = a["T"]
        t0 = T * TT
        self._stats_flush()
        sq, sq_b = self.t16.get()
        self.actf(sq[:], self.res[:, c, t0:t0 + TT], AF.Square, [self.res_b[c][T]], [sq_b])
        a["pend"] = (sq, sq_b, a["n"] == 0, a["n"] == NCH - 1)
        a["n"] += 1

    def stats_finish(self):
        a = self.acc
        self._stats_flush()
        kr, (r, r_b) = self.t32.reserve()
        rr = r[:, 0:TT]
        self.actf(rr, a["ssb"][:], AF.Ln, [a["ssb_b"]], [r_b], bias=EPS, scale=1.0 / D)
        self.actf(rr, rr, AF.Exp, [r_b], [r_b], scale=-0.5)
        self.banks.release(a["kb"])
        self.acc = None
        return kr, rr, r_b

    def stats_bg(self, key, T):
        self.stats_begin(T)
        acc = self.acc
        self.acc = None
        for c in range(NCH):
            self.acc, sv = acc, self.acc
            self.stats_add(c)
            self.acc = sv
            yield
        self.acc, sv = acc, self.acc
        kr, rr, r_b = self.stats_finish()
        self.acc = sv
        self.pre_rstd = (key, kr, rr, r_b)
        yield

    def proj_fm(self, slot, slot_b, col0, rhs_fn, nk, rhs_bufs, ncols=128):
        bank, bank_b = self.banks.get()
        for k in range(nk):
            rb_ = [rhs_bufs[k]] if len(rhs_bufs) == nk else list(rhs_bufs)
            self.mm(bank[0:ncols, :], slot[:, k, col0:col0 + ncols], rhs_fn(k), k == 0, k == nk - 1,
                    [slot_b] + rb_, [bank_b])
        return bank, bank_b

    def add_to_res(self, bank, bank_b, n, T, stats=False):
        t0 = T * TT
        self.vtt(self.res[:, n, t0:t0 + TT], bank[:], self.res[:, n, t0:t0 + TT], ALU.add,
                 [bank_b, self.res_b[n][T]], [self.res_b[n][T]])
        if stats and OPT_NORMACC:
            self.stats_add(n)

    def out_proj(self, w_ap, T):
        if OPT_NORMACC:
            self.stats_begin(T)
        for half in range(2):
            slot, slot_b = self.load_w(w_ap[:, half * 512:(half + 1) * 512])
            for nq in range(4):
                bank, bank_b = self.proj_fm(slot, slot_b, nq * 128, lambda k: self.hn[:, k, :], NCH, self.hn_b)
                self.add_to_res(bank, bank_b, half * 4 + nq, T, stats=True)

    def layer(self, s, li):
        j = li // 2
        if li % 2 == 0:
            self.even_prep(j)
        import os
        st = os.environ.get("K_STAGES", "norm,mix,outp,ffn,ple,sb,hg").split(",")
        self.st = st
        for T in range(NTILE):
            if "norm" in st:
                self.rmsnorm(T, 0, li)
            if li % 2 == 0:
                if "mix" in st:
                    self.mixer_even(s, j, T)
                if "outp" in st:
                    self.out_proj(self.I["ab_w_out"][j], T)
            else:
                if "mix" in st:
                    self.mixer_odd(s, j, T)
                if "outp" in st:
                    self.out_proj(self.I["c_w_out"][j], T)
            self.dump("res_mix_L%d" % li, self.res[:, :, T * TT:(T + 1) * TT], [self.res_b[c][T] for c in range(NCH)],
                      dst=None if not self.dbg or ("res_mix_L%d" % li) not in self.dbg else self.dbg_out["res_mix_L%d" % li][:, :, T * TT:(T + 1) * TT])
            if "ffn" in st:
                bg = None
                if "norm" in st:
                    if T + 1 < NTILE:
                        nxt = (li, T + 1)
                    else:
                        k_ = self.layers.index(li)
                        nxt = (self.layers[k_ + 1], 0) if k_ + 1 < len(self.layers) else None
                    if nxt is not None and OPT_NORMBG:
                        bg = self.stats_bg(nxt, nxt[1])
                self.ffn(s, li, T, bg)
            if "ple" in st:
                self.ple(s, li, T)
            if li == self.layers[-1]:
                self.finals += self.store_tile(s, T)
                if s + 1 < self.nseq:
                    self.load_x(s + 1, T, pool=True)
        if s == 0:
            self.dump("res_L%d" % li, self.res[:], [b for c in range(NCH) for b in self.res_b[c]])

    def ffn(self, s, li, T, bg=None):
        nc, fw = self.nc, self.fw
        t0 = T * TT
        self.rmsnorm(T, 1, li)
        cur, nxt = T % 2, (T + 1) % 2
        if T == 0:
            fw.op(fw.dve, lambda: nc.vector.memset(self.halo[:, 0, :, :], 0.0), writes=[self.halo_b[0]])
        wup = self.I["ffn_up"][li]
        groups = [(j0, 2) for j0 in range(0, NFC, 2)]
        cw = self.convw
        for (j0, nj) in groups:
            if bg is not None and j0 >= 2:
                next(bg, None)
            slot, slot_b = self.load_w(wup[:, j0 * 128:(j0 + nj) * 128], ncols=512,
                                       src2=wup[:, F_FF + j0 * 128:F_FF + (j0 + nj) * 128])
            for jj in range(nj):
                jp = j0 + jj
                ys = []
                for (col0, idx) in ((jj * 128, jp), (nj * 128 + jj * 128, NFC + jp)):
                    bank, bank_b = self.proj_fm(slot, slot_b, col0, lambda k: self.hn[:, k, :], NCH, self.hn_b)
                    u, u_b = self.upool.get()
                    y, y_b = self.t32.get()
                    self.actf(u[:, 2:2 + TT], bank[:], AF.Copy, [bank_b], [u_b])
                    self.actf(u[:, 0:2], self.halo[:, cur, idx, :], AF.Copy, [self.halo_b[cur]], [u_b])
                    self.actf(self.halo[:, nxt, idx, :], bank[:, TT - 2:TT], AF.Copy, [bank_b], [self.halo_b[nxt]])
                    self.actf(y[:, 0:TT], bank[:], AF.Identity, [bank_b, self.small_b], [y_b],
                              bias=cw[:, li, 3, idx:idx + 1], scale=cw[:, li, 2, idx:idx + 1])
                    self.vstt(y[:, 0:TT], u[:, 1:1 + TT], cw[:, li, 1, idx:idx + 1], y[:, 0:TT], ALU.mult, ALU.add,
                              [u_b, y_b, self.small_b], [y_b])
                    self.vstt(y[:, 0:TT], u[:, 0:TT], cw[:, li, 0, idx:idx + 1], y[:, 0:TT], ALU.mult, ALU.add,
                              [u_b, y_b, self.small_b], [y_b])
                    ys.append((y, y_b))
                (yg, yg_b), (yu, yu_b) = ys
                self.actf(yg[:, 0:TT], yg[:, 0:TT], AF.Silu, [yg_b], [yg_b])
                self.vtt(self.ar[:, jp, :], yg[:, 0:TT], yu[:, 0:TT], ALU.mult, [yg_b, yu_b], [self.ar_b[jp]])
        if bg is not None:
            for _ in bg:
                pass
        wd = self.I["ffn_down"][li]
        jgs = [(0, 8), (8, 8), (16, 6)]
        if OPT_NORMACC:
            self.stats_begin(T)
        for nh in range(2):
            bks = [self.banks.get() for _ in range(4)]
            for (j0, nj) in jgs:
                slot, slot_b = self.load_w(wd[j0 * 128:(j0 + nj) * 128, nh * 512:(nh + 1) * 512], nk=nj)
                for jj in range(nj):
                    jf = j0 + jj
                    for nq in range(4):
                        self.mm(bks[nq][0][:], slot[:, jj, nq * 128:(nq + 1) * 128], self.ar[:, jf, :], jf == 0, jf == NFC - 1,
                                [slot_b, self.ar_b[jf]], [bks[nq][1]])
            for nq in range(4):
                self.add_to_res(bks[nq][0], bks[nq][1], nh * 4 + nq, T, stats=True)

    def ple(self, s, li, T):
        nc, fw = self.nc, self.fw
        t0 = T * TT
        self.rmsnorm(T, 2, li)
        src = self.I["pT"][li, s, :, t0:t0 + TT].rearrange("(c p) t -> p c t", p=128)
        pbuf = self.ar[:, 20:22, :]
        pbuf_bs = [self.ar_b[20], self.ar_b[21]]
        fw.dma_pool(lambda: nc.gpsimd.dma_start(out=pbuf, in_=src), writes=pbuf_bs)
        for half in range(2):
            sg, sg_b = self.load_w(self.I["ple_gate"][li][:, half * 512:(half + 1) * 512])
            spj, spj_b = self.load_w(self.I["ple_proj"][li][:, half * 512:(half + 1) * 512], nk=2)
            for nq in range(4):
                n = half * 4 + nq
                bg, bg_b = self.proj_fm(sg, sg_b, nq * 128, lambda k: self.hn[:, k, :], NCH, self.hn_b)
                bp, bp_b = self.proj_fm(spj, spj_b, nq * 128, lambda k: self.ar[:, 20 + k, :], 2, pbuf_bs)
                g, g_b = self.t32.get()
                self.actf(g[:, 0:TT], bg[:], AF.Sigmoid, [bg_b], [g_b])
                self.vtt(g[:, 0:TT], g[:, 0:TT], bp[:], ALU.mult, [g_b, bp_b], [g_b])
                self.vtt(self.res[:, n, t0:t0 + TT], g[:, 0:TT], self.res[:, n, t0:t0 + TT], ALU.add,
                         [g_b, self.res_b[n][T]], [self.res_b[n][T]])

    def even_prep(self, j):
        nc, fw = self.nc, self.fw
        if j == 0:
            fw.op(fw.dve, lambda: nc.vector.memset(self.lb[:, 0, :], 0.0), writes=[self.lb_b])
        else:
            self.vcopy(self.lb[:, 0, :], self.lb1[:], [self.small_b], [self.lb_b])
        src = self.I["hgn"][0:1, j * 512:(j + 1) * 512].partition_broadcast(128)
        fw.dma_sp(lambda: nc.sync.dma_start(out=self.hgn[:], in_=src), writes=[self.hgn_b])
        self.vts(self.lb[:, 1, :], self.lb[:, 0, :], -1.0, 1.0, ALU.mult, ALU.add, [self.lb_b], [self.lb_b])
        fw.op(fw.dve, lambda: nc.vector.memset(self.S32[:], 0.0), writes=[self.S32_b])
        fw.op(fw.dve, lambda: nc.vector.memset(self.Sbf[:], 0.0), writes=[self.Sbf_b])

    def mixer_even(self, s, j, T):
        nc, fw = self.nc, self.fw
        t0 = T * TT
        w = self.I["ab_w_in"][j]
        hnf = lambda k: self.hn[:, k, :]
        slot, slot_b = self.load_w(w[:, 0:512])
        for m in range(4):
            bank, bank_b = self.proj_fm(slot, slot_b, m * 128, hnf, NCH, self.hn_b)
            self.actf(self.ar[:, m, :], bank[:], AF.Copy, [bank_b], [self.ar_b[m]], scale=0.125)
        slot, slot_b = self.load_w(w[:, 512:1024])
        for m in range(4):
            bank, bank_b = self.proj_fm(slot, slot_b, m * 128, hnf, NCH, self.hn_b)
            self.actf(self.kbuf[:, m, t0:t0 + TT], bank[:], AF.Copy, [bank_b], [self.kb_b[m][T]])
        def tok_block(col0, evac):
            slot, slot_b = self.load_w(w[:, col0:col0 + 512])
            for sub in range(4):
                bank, bank_b = self.banks.get()
                for k in range(NCH):
                    self.mm(bank[:], self.hn[:, k, sub * 128:(sub + 1) * 128], slot[:, k, :], k == 0, k == NCH - 1,
                            [slot_b, self.hn_b[k]], [bank_b])
                evac(sub, bank, bank_b)
        tok_block(1024, lambda sub, bank, bank_b: self.actf(self.vbuf[:, T * 4 + sub, :], bank[:], AF.Copy, [bank_b], [self.vb_b[T * 4 + sub]]))
        tok_block(1536, lambda sub, bank, bank_b: self.actf(self.ar[:, 8 + sub, :], bank[:], AF.Silu, [bank_b], [self.ar_b[8 + sub]]))
        tok_block(2048, lambda sub, bank, bank_b: self.actf(self.stage[:, sub, :], bank[:], AF.Sigmoid, [bank_b], [self.stage_b[sub]]))
        tok_block(2560, lambda sub, bank, bank_b: self.actf(self.ar[:, 12 + sub, :], bank[:], AF.Copy, [bank_b], [self.ar_b[12 + sub]]))
        tok_block(3072, lambda sub, bank, bank_b: self.actf(self.ar[:, 16 + sub, :], bank[:], AF.Silu, [bank_b], [self.ar_b[16 + sub]]))
        self.preconvert(2 * j)
        if "sb" in self.st:
            for pair in ((0, 1), (2, 3), (4, 5), (6, 7)):
                gens = [self.sb_chain(T, h) for h in pair]
                while gens:
                    for g in list(gens):
                        try:
                            next(g)
                        except StopIteration:
                            gens.remove(g)
        if "hg" in self.st and OPT_HGGEN:
            self.hgrn_tile_gen(j, T)
        elif "hg" in self.st and OPT_HGPIPE:
            fr = {0: self.hgrn_front(j, T, 0), 1: self.hgrn_front(j, T, 1)}
            outs = {}
            outs[0] = self.hgrn_mid(fr.pop(0))
            fr[2] = self.hgrn_front(j, T, 2)
            outs[1] = self.hgrn_mid(fr.pop(1))
            outs.pop(0)()
            fr[3] = self.hgrn_front(j, T, 3)
            outs[2] = self.hgrn_mid(fr.pop(2))
            outs.pop(1)()
            outs[3] = self.hgrn_mid(fr.pop(3))
            outs.pop(2)()
            outs.pop(3)()
        elif "hg" in self.st:
            prev_out = None
            for sub in range(4):
                out = self.hgrn_sub(j, T, sub)
                if prev_out is not None:
                    prev_out()
                prev_out = out
            if prev_out is not None:
                prev_out()

    def sb_chain(self, T, h):
        nc, fw = self.nc, self.fw
        hp = (h % 2) * 64
        pr = 64 - hp
        hc = h // 2
        qT = self.ar[hp:hp + 64, hc, :]
        q_b = self.ar_b[hc]
        negtri = self.C("negtri")
        sbmask = self.C("sbmask")
        onescol = self.C("ones", cols=1)
        negrow = self.negrow[pr:pr + 1, :]
        kpv, (pvb, pvb_b) = self.banks.reserve()
        Rf = self.ar[pr:pr + 1, 4:6, :].rearrange("p a b -> p (a b)").bitcast(F32)
        Rf_b = self.sbR_b[pr]
        rts = [(self.ar[pr:pr + 1, 6, :], self.sbrt_b[pr][0]), (self.ar[pr:pr + 1, 7, :], self.sbrt_b[pr][1]),
               (self.ar[pr:pr + 1, 20, :], self.sbrt_b[pr][2])]
        fw.op(fw.dve, lambda: nc.vector.memset(Rf, 0.0), writes=[Rf_b])
        blocks = [(4 * T + kl, kl * 128, True) for kl in (3, 2, 1, 0)] + [(kb, 0, False) for kb in range(4 * T - 1, -1, -1)]
        nb = len(blocks)

        def stage_a(bi):
            kb, c0, diag = blocks[bi]
            kT = self.kbuf[hp:hp + 64, hc, kb * 128:(kb + 1) * 128]
            k_b = self.kb_b[hc][kb // 4]
            kx, (X, X_b) = self.banks.reserve()
            self.mm(X[:, c0:TT], kT, qT[:, c0:TT], True, True, [k_b, q_b], [X_b])
            if OPT_LOCK:
                yield
            ke, (e, e_b) = self.t32.reserve()
            self.actf(e[:, c0:TT], X[:, c0:TT], AF.Exp, [X_b], [e_b])
            kl, (lp, lp_b) = self.t16.reserve()
            self.actf(lp[:, c0:TT], e[:, c0:TT], AF.Ln, [e_b], [lp_b], bias=1.0)
            self.t32.release(ke)
            if diag:
                self.vtt(lp[:, c0:c0 + 128], lp[:, c0:c0 + 128], sbmask, ALU.mult, [lp_b, self.c_b], [lp_b])
            rnew = None
            if bi < nb - 1 and OPT_SBEARLY:
                self.mm(pvb[pr:pr + 1, c0:TT], onescol, lp[:, c0:TT], True, True, [lp_b, self.c_b], [pvb_b], skip=True)
                self.vtt(Rf[:, c0:TT], pvb[pr:pr + 1, c0:TT], Rf[:, c0:TT], ALU.add, [pvb_b, Rf_b], [Rf_b])
                rt, rt_b = rts[bi % 3]
                self.vcopy(rt[:, c0:TT], Rf[:, c0:TT], [Rf_b], [rt_b])
                rnew = (rt, rt_b)
            if False:
                yield
            return kx, X, X_b, kl, lp, lp_b, kT, k_b, rnew

        pend = {0: (yield from stage_a(0))}
        yield
        rprev = None
        pvq = []

        def flush_pv():
            while pvq:
                (bi_, kb_, c0_, wt_, wt_b_, kw_) = pvq.pop(0)
                self.mm(pvb[hp:hp + 64, c0_:TT], self.vbuf[:, kb_, h * 64:(h + 1) * 64], wt_[:, c0_:TT], bi_ == 0, bi_ == nb - 1,
                        [self.vb_b[kb_], wt_b_], [pvb_b], skip=True)
                self.t16.release(kw_)

        for bi in range(nb):
            if bi + 1 < nb:
                pend[bi + 1] = yield from stage_a(bi + 1)
                yield
            flush_pv()
            if OPT_LOCK:
                yield
            kb, c0, diag = blocks[bi]
            kx, X, X_b, kl, lp, lp_b, kT, k_b, rnew = pend.pop(bi)
            has_r = rprev is not None
            cR = c0 + 128 if diag else 0
            use_r = has_r and cR < TT
            self.mm(X[:, c0:TT], kT, qT[:, c0:TT], True, False, [k_b, q_b], [X_b])
            if OPT_LOCK:
                yield
            self.mm(X[:, c0:TT], negtri, lp[:, c0:TT], False, not use_r, [lp_b, self.c_b], [X_b])
            if OPT_LOCK:
                yield
            if use_r:
                rt, rt_b = rprev
                self.mm(X[:, cR:TT], negrow, rt[:, cR:TT], False, True, [rt_b, self.c_b], [X_b])
            if OPT_LOCK:
                yield
            kw, (wt, wt_b) = self.t16.reserve()
            self.actf(wt[:, c0:TT], X[:, c0:TT], AF.Exp, [X_b], [wt_b])
            self.banks.release(kx)
            if diag:
                self.vtt(wt[:, c0:c0 + 128], wt[:, c0:c0 + 128], sbmask, ALU.mult, [wt_b, self.c_b], [wt_b])
            pvq.append((bi, kb, c0, wt, wt_b, kw))
            if bi < nb - 1 and not OPT_SBEARLY:
                self.mm(pvb[pr:pr + 1, c0:TT], onescol, lp[:, c0:TT], True, True, [lp_b, self.c_b], [pvb_b], skip=True)
                if OPT_LOCK:
                    yield
                self.vtt(Rf[:, c0:TT], pvb[pr:pr + 1, c0:TT], Rf[:, c0:TT], ALU.add, [pvb_b, Rf_b], [Rf_b])
                rt, rt_b = rts[bi % 3]
                self.vcopy(rt[:, c0:TT], Rf[:, c0:TT], [Rf_b], [rt_b])
                rnew = (rt, rt_b)
            rprev = rnew
            self.t16.release(kl)
            yield
        flush_pv()
        self.actf(self.hn[hp:hp + 64, hc, :], pvb[hp:hp + 64, :], AF.Copy, [pvb_b], [self.hn_b[hc]])
        self.banks.release(kpv)

    def hgrn_sub(self, j, T, sub):
        nc, fw = self.nc, self.fw
        qs, qs_b = self.ar[:, 8 + sub, :], self.ar_b[8 + sub]
        ib, ib_b = self.ar[:, 12 + sub, :], self.ar_b[12 + sub]
        gs, gs_b = self.ar[:, 16 + sub, :], self.ar_b[16 + sub]
        sg, sg_b = self.stage[:, sub, :], self.stage_b[sub]
        ident = self.C("ident")
        fA, fA_b = self.t32.get()
        f = fA[:, 0:512]
        self.vtt(f, sg, self.lb[:, 1, :], ALU.mult, [sg_b, self.lb_b], [fA_b])
        self.vtt(f, f, self.lb[:, 0, :], ALU.add, [fA_b, self.lb_b], [fA_b])
        lfB, lfB_b = self.t32.get()
        lf = lfB[:, 0:512]
        self.actf(lf, f, AF.Ln, [fA_b], [lfB_b])
        self.vts(f, f, -1.0, 1.0, ALU.mult, ALU.add, [fA_b], [fA_b])
        bd1, bd1_b = self.banks.get()
        bb, bb_b = self.banks.get()
        bd4, bd4_b = self.banks.get()
        self.mm(bd1[:], self.C("trid1", bf=False), lf, True, True, [lfB_b, self.c_b], [bd1_b])
        self.mm(bb[:], self.C("tri2", bf=False), lf, True, True, [lfB_b, self.c_b], [bb_b])
        self.mm(bd4[:], self.C("trisuf", bf=False), lf, True, True, [lfB_b, self.c_b], [bd4_b])
        bz, bz_b = self.banks.get()
        for h in range(4):
            self.mm(bz[:, 2 * h:2 * h + 2], lf[:, h * 128:(h + 1) * 128], self.C("chunkones", bf=False, cols=2), True, True,
                    [lfB_b, self.c_b], [bz_b])
        el, el_b = self.sm_pool.get()
        self.actf(el, bz[:, 0:8], AF.Exp, [bz_b], [el_b])
        E, E_b = self.t32.get()
        q1, q1_b = self.t16.get()
        self.actf(E[:, 0:512], bd1[:], AF.Exp, [bd1_b], [E_b])
        self.vtt(q1[:], qs, E[:, 0:512], ALU.mult, [qs_b, E_b], [q1_b])
        E2, E2_b = self.t32.get()
        k1, k1_b = self.t16.get()
        self.actf(E2[:, 0:512], bd1[:], AF.Exp, [bd1_b], [E2_b], scale=-1.0)
        self.vtt(k1[:], f, E2[:, 0:512], ALU.mult, [fA_b, E2_b], [k1_b])
        E3, E3_b = self.t32.get()
        q3, q3_b = self.t16.get()
        self.actf(E3[:, 0:512], bb[:], AF.Exp, [bb_b], [E3_b])
        self.vtt(q3[:], qs, E3[:, 0:512], ALU.mult, [qs_b, E3_b], [q3_b])
        E4, E4_b = self.t32.get()
        k4, k4_b = self.t16.get()
        self.actf(E4[:, 0:512], bd4[:], AF.Exp, [bd4_b], [E4_b])
        self.vtt(k4[:], f, E4[:, 0:512], ALU.mult, [fA_b, E4_b], [k4_b])
        import os
        HG = float(os.environ.get("K_HG", "9"))
        if HG < 2:
            return
        pA, pA_b = self.pbt[:, 0:512], self.pbt_b[0]
        pB, pB_b = self.pbt[:, 512:1024], self.pbt_b[0]
        for h in range(4):
            self.tr(pA[:, h * 128:(h + 1) * 128], q1[:, h * 128:(h + 1) * 128], ident, [q1_b, self.c_b], [pA_b])
        for h in range(4):
            self.tr(pB[:, h * 128:(h + 1) * 128], k1[:, h * 128:(h + 1) * 128], ident, [k1_b, self.c_b], [pB_b])
        q1T, q1T_b = self.t16.get()
        k1T, k1T_b = self.t16.get()
        if HG < 2.1:
            return
        self.vcopy(q1T[:], pA, [pA_b], [q1T_b])
        if HG < 2.2:
            return
        self.vcopy(k1T[:], pB, [pB_b], [k1T_b])
        if HG < 2.3:
            return
        for h in range(4):
            self.tr(pA[:, h * 128:(h + 1) * 128], q3[:, h * 128:(h + 1) * 128], ident, [q3_b, self.c_b], [pA_b])
        q3T, q3T_b = self.t16.get()
        self.vcopy(q3T[:], pA, [pA_b], [q3T_b])
        if HG < 3:
            return
        bs, bs_b = self.banks.get()
        for c in range(2):
            for h in range(4):
                self.mm(bs[64 * c:64 * c + 64, h * 64:(h + 1) * 64],
                        k1T[:, h * 128 + 64 * c:h * 128 + 64 * c + 64], q1T[:, h * 128 + 64 * c:h * 128 + 64 * c + 64],
                        True, True, [k1T_b, q1T_b], [bs_b])
        scm, scm_b = self.t16.get()
        self.vtt(scm[:, 0:256], bs[:, 0:256], self.C("maskS"), ALU.mult, [bs_b, self.c_b], [scm_b])
        if HG < 4:
            return
        kbo, (bo, bo_b) = self.banks.reserve()
        for c in range(2):
            pc = 64 * c
            for h in range(4):
                hs = slice(h * 128, (h + 1) * 128)
                self.mm(bo[pc:pc + 64, hs], q3T[:, h * 128 + pc:h * 128 + pc + 64], self.Sbf[:, hs], True, False,
                        [q3T_b, self.Sbf_b], [bo_b])
                self.mm(bo[pc:pc + 64, hs], scm[pc:pc + 64, h * 64:(h + 1) * 64], ib[pc:pc + 64, hs], False, True,
                        [scm_b, ib_b], [bo_b])
            bu, bu_b = self.banks.get()
            for h in range(4):
                hs = slice(h * 128, (h + 1) * 128)
                self.mm(bu[:, hs], k4[pc:pc + 64, hs], ib[pc:pc + 64, hs], True, True, [k4_b, ib_b], [bu_b])
            for h in range(4):
                hs = slice(h * 128, (h + 1) * 128)
                self.vstt(self.S32[:, hs], self.S32[:, hs], el[:, 2 * h + c:2 * h + c + 1], bu[:, hs], ALU.mult, ALU.add,
                          [self.S32_b, el_b, bu_b], [self.S32_b])
            self.actf(self.Sbf[:], self.S32[:], AF.Copy, [self.S32_b], [self.Sbf_b])
        if HG < 5:
            self.banks.release(kbo)
            return
        return lambda: self.hgrn_out(j, sub, kbo, bo, bo_b, gs, gs_b, pB, pB_b, ident)

    def hgrn_front(self, j, T, sub):
        qs, qs_b = self.ar[:, 8 + sub, :], self.ar_b[8 + sub]
        sg, sg_b = self.stage[:, sub, :], self.stage_b[sub]
        ident = self.C("ident")
        fA, fA_b = self.t32.get()
        f = fA[:, 0:512]
        self.vtt(f, sg, self.lb[:, 1, :], ALU.mult, [sg_b, self.lb_b], [fA_b])
        self.vtt(f, f, self.lb[:, 0, :], ALU.add, [fA_b, self.lb_b], [fA_b])
        lfB, lfB_b = self.t32.get()
        lf = lfB[:, 0:512]
        self.actf(lf, f, AF.Ln, [fA_b], [lfB_b])
        self.vts(f, f, -1.0, 1.0, ALU.mult, ALU.add, [fA_b], [fA_b])
        bd1, bd1_b = self.banks.get()
        bb, bb_b = self.banks.get()
        bd4, bd4_b = self.banks.get()
        self.mm(bd1[:], self.C("trid1", bf=False), lf, True, True, [lfB_b, self.c_b], [bd1_b])
        self.mm(bb[:], self.C("tri2", bf=False), lf, True, True, [lfB_b, self.c_b], [bb_b])
        self.mm(bd4[:], self.C("trisuf", bf=False), lf, True, True, [lfB_b, self.c_b], [bd4_b])
        bz, bz_b = self.banks.get()
        for h in range(4):
            self.mm(bz[:, 2 * h:2 * h + 2], lf[:, h * 128:(h + 1) * 128], self.C("chunkones", bf=False, cols=2), True, True,
                    [lfB_b, self.c_b], [bz_b])
        el, el_b = self.sm_pool.get()
        self.actf(el, bz[:, 0:8], AF.Exp, [bz_b], [el_b])
        pA, pA_b = self.pbt[:, 0:512], self.pbt_b[0]
        pB, pB_b = self.pbt[:, 512:1024], self.pbt_b[0]
        E, E_b = self.t32.get()
        kq1, (q1, q1_b) = self.t16.reserve()
        self.actf(E[:, 0:512], bd1[:], AF.Exp, [bd1_b], [E_b])
        self.vtt(q1[:], qs, E[:, 0:512], ALU.mult, [qs_b, E_b], [q1_b])
        E2, E2_b = self.t32.get()
        kk1, (k1, k1_b) = self.t16.reserve()
        self.actf(E2[:, 0:512], bd1[:], AF.Exp, [bd1_b], [E2_b], scale=-1.0)
        self.vtt(k1[:], f, E2[:, 0:512], ALU.mult, [fA_b, E2_b], [k1_b])
        for h in range(4):
            self.tr(pA[:, h * 128:(h + 1) * 128], q1[:, h * 128:(h + 1) * 128], ident, [q1_b, self.c_b], [pA_b])
        for h in range(4):
            self.tr(pB[:, h * 128:(h + 1) * 128], k1[:, h * 128:(h + 1) * 128], ident, [k1_b, self.c_b], [pB_b])
        self.t16.release(kq1)
        self.t16.release(kk1)
        kq1T, (q1T, q1T_b) = self.t16.reserve()
        kk1T, (k1T, k1T_b) = self.t16.reserve()
        self.vcopy(q1T[:], pA, [pA_b], [q1T_b])
        self.vcopy(k1T[:], pB, [pB_b], [k1T_b])
        bs, bs_b = self.banks.get()
        for c in range(2):
            for h in range(4):
                self.mm(bs[64 * c:64 * c + 64, h * 64:(h + 1) * 64],
                        k1T[:, h * 128 + 64 * c:h * 128 + 64 * c + 64], q1T[:, h * 128 + 64 * c:h * 128 + 64 * c + 64],
                        True, True, [k1T_b, q1T_b], [bs_b])
        self.t16.release(kq1T)
        self.t16.release(kk1T)
        kscm, (scm, scm_b) = self.t16.reserve()
        self.vtt(scm[:, 0:256], bs[:, 0:256], self.C("maskS"), ALU.mult, [bs_b, self.c_b], [scm_b])
        E3, E3_b = self.t32.get()
        kq3, (q3, q3_b) = self.t16.reserve()
        self.actf(E3[:, 0:512], bb[:], AF.Exp, [bb_b], [E3_b])
        self.vtt(q3[:], qs, E3[:, 0:512], ALU.mult, [qs_b, E3_b], [q3_b])
        for h in range(4):
            self.tr(pA[:, h * 128:(h + 1) * 128], q3[:, h * 128:(h + 1) * 128], ident, [q3_b, self.c_b], [pA_b])
        self.t16.release(kq3)
        kq3T, (q3T, q3T_b) = self.t16.reserve()
        self.vcopy(q3T[:], pA, [pA_b], [q3T_b])
        E4, E4_b = self.t32.get()
        kk4, (k4, k4_b) = self.t16.reserve()
        self.actf(E4[:, 0:512], bd4[:], AF.Exp, [bd4_b], [E4_b])
        self.vtt(k4[:], f, E4[:, 0:512], ALU.mult, [fA_b, E4_b], [k4_b])
        return dict(j=j, sub=sub, el=el, el_b=el_b, scm=scm, scm_b=scm_b, kscm=kscm, q3T=q3T, q3T_b=q3T_b, kq3T=kq3T,
                    k4=k4, k4_b=k4_b, kk4=kk4)

    def hgrn_mid(self, st):
        sub = st["sub"]
        ib, ib_b = self.ar[:, 12 + sub, :], self.ar_b[12 + sub]
        el, el_b, scm, scm_b, q3T, q3T_b, k4, k4_b = st["el"], st["el_b"], st["scm"], st["scm_b"], st["q3T"], st["q3T_b"], st["k4"], st["k4_b"]
        kbo, (bo, bo_b) = self.banks.reserve()
        for c in range(2):
            pc = 64 * c
            for h in range(4):
                hs = slice(h * 128, (h + 1) * 128)
                self.mm(bo[pc:pc + 64, hs], q3T[:, h * 128 + pc:h * 128 + pc + 64], self.Sbf[:, hs], True, False,
                        [q3T_b, self.Sbf_b], [bo_b])
                self.mm(bo[pc:pc + 64, hs], scm[pc:pc + 64, h * 64:(h + 1) * 64], ib[pc:pc + 64, hs], False, True,
                        [scm_b, ib_b], [bo_b])
            bu, bu_b = self.banks.get()
            for h in range(4):
                hs = slice(h * 128, (h + 1) * 128)
                self.mm(bu[:, hs], k4[pc:pc + 64, hs], ib[pc:pc + 64, hs], True, True, [k4_b, ib_b], [bu_b])
            for h in range(4):
                hs = slice(h * 128, (h + 1) * 128)
                self.vstt(self.S32[:, hs], self.S32[:, hs], el[:, 2 * h + c:2 * h + c + 1], bu[:, hs], ALU.mult, ALU.add,
                          [self.S32_b, el_b, bu_b], [self.S32_b])
            self.actf(self.Sbf[:], self.S32[:], AF.Copy, [self.S32_b], [self.Sbf_b])
        self.t16.release(st["kscm"])
        self.t16.release(st["kq3T"])
        self.t16.release(st["kk4"])
        gs, gs_b = self.ar[:, 16 + sub, :], self.ar_b[16 + sub]
        pB, pB_b = self.pbt[:, 512:1024], self.pbt_b[0]
        ident = self.C("ident")
        j = st["j"]
        return lambda: self.hgrn_out(j, sub, kbo, bo, bo_b, gs, gs_b, pB, pB_b, ident)

    def g_front(self, j, T, sub, st):
        qs, qs_b = self.ar[:, 8 + sub, :], self.ar_b[8 + sub]
        sg, sg_b = self.stage[:, sub, :], self.stage_b[sub]
        ident = self.C("ident")
        pA, pB, p_b = self.pbt[:, 0:512], self.pbt[:, 512:1024], self.pbt_b[0]
        kfA, (fA, fA_b) = self.t32.reserve()
        f = fA[:, 0:512]
        self.vtt(f, sg, self.lb[:, 1, :], ALU.mult, [sg_b, self.lb_b], [fA_b])
        self.vtt(f, f, self.lb[:, 0, :], ALU.add, [fA_b, self.lb_b], [fA_b])
        klf, (lfB, lfB_b) = self.t32.reserve()
        lf = lfB[:, 0:512]
        self.actf(lf, f, AF.Ln, [fA_b], [lfB_b])
        self.vts(f, f, -1.0, 1.0, ALU.mult, ALU.add, [fA_b], [fA_b])
        yield
        k1_, (bd1, bd1_b) = self.banks.reserve()
        k2_, (bb, bb_b) = self.banks.reserve()
        k3_, (bd4, bd4_b) = self.banks.reserve()
        self.mm(bd1[:], self.C("trid1", bf=False), lf, True, True, [lfB_b, self.c_b], [bd1_b])
        self.mm(bb[:], self.C("tri2", bf=False), lf, True, True, [lfB_b, self.c_b], [bb_b])
        self.mm(bd4[:], self.C("trisuf", bf=False), lf, True, True, [lfB_b, self.c_b], [bd4_b])
        k4_, (bz, bz_b) = self.banks.reserve()
        for h in range(4):
            self.mm(bz[:, 2 * h:2 * h + 2], lf[:, h * 128:(h + 1) * 128], self.C("chunkones", bf=False, cols=2), True, True,
                    [lfB_b, self.c_b], [bz_b])
        self.t32.release(klf)
        yield
        el, el_b = self.sm_pool.get()
        self.actf(el, bz[:, 0:8], AF.Exp, [bz_b], [el_b])
        self.banks.release(k4_)
        kE, (E, E_b) = self.t32.reserve()
        kq1, (q1, q1_b) = self.t16.reserve()
        self.actf(E[:, 0:512], bd1[:], AF.Exp, [bd1_b], [E_b])
        self.vtt(q1[:], qs, E[:, 0:512], ALU.mult, [qs_b, E_b], [q1_b])
        kk1, (k1, k1_b) = self.t16.reserve()
        self.actf(E[:, 0:512], bd1[:], AF.Exp, [bd1_b, E_b], [E_b], scale=-1.0)
        self.vtt(k1[:], f, E[:, 0:512], ALU.mult, [fA_b, E_b], [k1_b])
        self.banks.release(k1_)
        yield
        for h in range(4):
            self.tr(pA[:, h * 128:(h + 1) * 128], q1[:, h * 128:(h + 1) * 128], ident, [q1_b, self.c_b], [p_b])
        for h in range(4):
            self.tr(pB[:, h * 128:(h + 1) * 128], k1[:, h * 128:(h + 1) * 128], ident, [k1_b, self.c_b], [p_b])
        self.t16.release(kq1)
        self.t16.release(kk1)
        kq1T, (q1T, q1T_b) = self.t16.reserve()
        kk1T, (k1T, k1T_b) = self.t16.reserve()
        self.vcopy(q1T[:], pA, [p_b], [q1T_b])
        self.vcopy(k1T[:], pB, [p_b], [k1T_b])
        yield
        kbs, (bs, bs_b) = self.banks.reserve()
        for c in range(2):
            for h in range(4):
                self.mm(bs[64 * c:64 * c + 64, h * 64:(h + 1) * 64],
                        k1T[:, h * 128 + 64 * c:h * 128 + 64 * c + 64], q1T[:, h * 128 + 64 * c:h * 128 + 64 * c + 64],
                        True, True, [k1T_b, q1T_b], [bs_b])
        self.t16.release(kq1T)
        self.t16.release(kk1T)
        yield
        kscm, (scm, scm_b) = self.t16.reserve()
        self.vtt(scm[:, 0:256], bs[:, 0:256], self.C("maskS"), ALU.mult, [bs_b, self.c_b], [scm_b])
        self.banks.release(kbs)
        kq3, (q3, q3_b) = self.t16.reserve()
        self.actf(E[:, 0:512], bb[:], AF.Exp, [bb_b, E_b], [E_b])
        self.vtt(q3[:], qs, E[:, 0:512], ALU.mult, [qs_b, E_b], [q3_b])
        self.banks.release(k2_)
        yield
        for h in range(4):
            self.tr(pA[:, h * 128:(h + 1) * 128], q3[:, h * 128:(h + 1) * 128], ident, [q3_b, self.c_b], [p_b])
        self.t16.release(kq3)
        kq3T, (q3T, q3T_b) = self.t16.reserve()
        self.vcopy(q3T[:], pA, [p_b], [q3T_b])
        yield
        kk4, (k4, k4_b) = self.t16.reserve()
        self.actf(E[:, 0:512], bd4[:], AF.Exp, [bd4_b, E_b], [E_b])
        self.vtt(k4[:], f, E[:, 0:512], ALU.mult, [fA_b, E_b], [k4_b])
        self.banks.release(k3_)
        self.t32.release(kE)
        self.t32.release(kfA)
        st.update(dict(j=j, sub=sub, el=el, el_b=el_b, scm=scm, scm_b=scm_b, kscm=kscm, q3T=q3T, q3T_b=q3T_b, kq3T=kq3T,
                       k4=k4, k4_b=k4_b, kk4=kk4, front_done=True))

    def g_mid(self, st, prev):
        while prev is not None and not prev.get("mid_done"):
            yield
        sub = st["sub"]
        ib, ib_b = self.ar[:, 12 + sub, :], self.ar_b[12 + sub]
        el, el_b, scm, scm_b, q3T, q3T_b, k4, k4_b = st["el"], st["el_b"], st["scm"], st["scm_b"], st["q3T"], st["q3T_b"], st["k4"], st["k4_b"]
        kbo, (bo, bo_b) = self.banks.reserve()
        for c in range(2):
            pc = 64 * c
            for h in range(4):
                hs = slice(h * 128, (h + 1) * 128)
                self.mm(bo[pc:pc + 64, hs], q3T[:, h * 128 + pc:h * 128 + pc + 64], self.Sbf[:, hs], True, False,
                        [q3T_b, self.Sbf_b], [bo_b])
                self.mm(bo[pc:pc + 64, hs], scm[pc:pc + 64, h * 64:(h + 1) * 64], ib[pc:pc + 64, hs], False, True,
                        [scm_b, ib_b], [bo_b])
            kbu, (bu, bu_b) = self.banks.reserve()
            for h in range(4):
                hs = slice(h * 128, (h + 1) * 128)
                self.mm(bu[:, hs], k4[pc:pc + 64, hs], ib[pc:pc + 64, hs], True, True, [k4_b, ib_b], [bu_b])
            yield
            for h in range(4):
                hs = slice(h * 128, (h + 1) * 128)
                self.vstt(self.S32[:, hs], self.S32[:, hs], el[:, 2 * h + c:2 * h + c + 1], bu[:, hs], ALU.mult, ALU.add,
                          [self.S32_b, el_b, bu_b], [self.S32_b])
            self.banks.release(kbu)
            self.actf(self.Sbf[:], self.S32[:], AF.Copy, [self.S32_b], [self.Sbf_b])
            yield
        self.t16.release(st["kscm"])
        self.t16.release(st["kq3T"])
        self.t16.release(st["kk4"])
        st["kbo"], st["bo"], st["bo_b"] = kbo, bo, bo_b
        st["mid_done"] = True

    def g_out(self, st):
        nc = self.nc
        j, sub = st["j"], st["sub"]
        kbo, bo, bo_b = st["kbo"], st["bo"], st["bo_b"]
        gs, gs_b = self.ar[:, 16 + sub, :], self.ar_b[16 + sub]
        pB, p_b = self.pbt[:, 512:1024], self.pbt_b[0]
        ident = self.C("ident")
        ko1, (osb, osb_b) = self.t32.reserve()
        ko2, (osq, osq_b) = self.t32.reserve()
        o = osb[:, 0:512]
        self.actf(o, bo[:], AF.Copy, [bo_b], [osb_b])
        self.banks.release(kbo)
        yield
        self.vtt(osq[:, 0:512], o, o, ALU.mult, [osb_b], [osq_b])
        ss, ss_b = self.sm_pool.get()
        self.fw.op(self.fw.dve, lambda: nc.vector.tensor_reduce(out=ss[:, 0:4], in_=osq[:, 0:512].rearrange("p (h v) -> p h v", h=4), axis=AX.X, op=ALU.add),
                   [osq_b], [ss_b])
        self.actf(ss[:, 0:4], ss[:, 0:4], AF.Ln, [ss_b], [ss_b], bias=EPS, scale=1.0 / 128)
        self.actf(ss[:, 0:4], ss[:, 0:4], AF.Exp, [ss_b], [ss_b], scale=-0.5)
        yield
        gg = osq[:, 0:512]
        self.vtt(gg, gs, self.hgn[:], ALU.mult, [gs_b, self.hgn_b, osq_b], [osq_b])
        kyb, (yb, yb_b) = self.t16.reserve()
        for h in range(4):
            hs = slice(h * 128, (h + 1) * 128)
            self.vstt(yb[:, hs], o[:, hs], ss[:, h:h + 1], gg[:, hs], ALU.mult, ALU.mult, [osb_b, ss_b, osq_b], [yb_b])
        yield
        for h in range(4):
            self.tr(pB[:, h * 128:(h + 1) * 128], yb[:, h * 128:(h + 1) * 128], ident, [yb_b, self.c_b], [p_b])
        self.vcopy(self.hn[:, 4:8, sub * 128:(sub + 1) * 128], pB.rearrange("p (h t) -> p h t", h=4), [p_b], [self.hn_b[4 + h] for h in range(4)])
        self.t16.release(kyb)
        self.t32.release(ko1)
        self.t32.release(ko2)

    def hgrn_tile_gen(self, j, T):
        sts = [dict() for _ in range(4)]
        pending_f = list(range(4))
        pending_m = list(range(4))
        pending_o = list(range(4))
        act_f = act_m = act_o = None
        while pending_o or act_o is not None:
            if act_f is None and pending_f:
                s_ = pending_f.pop(0)
                act_f = (s_, self.g_front(j, T, s_, sts[s_]))
            if act_m is None and pending_m and sts[pending_m[0]].get("front_done"):
                s_ = pending_m.pop(0)
                act_m = (s_, self.g_mid(sts[s_], sts[s_ - 1] if s_ > 0 else None))
            if act_o is None and pending_o and sts[pending_o[0]].get("mid_done"):
                s_ = pending_o.pop(0)
                act_o = (s_, self.g_out(sts[s_]))
            for name in ("f", "m", "o"):
                cur = {"f": act_f, "m": act_m, "o": act_o}[name]
                if cur is None:
                    continue
                try:
                    next(cur[1])
                except StopIteration:
                    if name == "f":
                        act_f = None
                    elif name == "m":
                        act_m = None
                    else:
                        act_o = None

    def hgrn_out(self, j, sub, kbo, bo, bo_b, gs, gs_b, pB, pB_b, ident):
        nc = self.nc
        osb, osb_b = self.t32.get()
        o = osb[:, 0:512]
        self.actf(o, bo[:], AF.Copy, [bo_b], [osb_b])
        self.banks.release(kbo)
        osq, osq_b = self.t32.get()
        self.vtt(osq[:, 0:512], o, o, ALU.mult, [osb_b], [osq_b])
        ss, ss_b = self.sm_pool.get()
        self.fw.op(self.fw.dve, lambda: nc.vector.tensor_reduce(out=ss[:, 0:4], in_=osq[:, 0:512].rearrange("p (h v) -> p h v", h=4), axis=AX.X, op=ALU.add),
                   [osq_b], [ss_b])
        self.actf(ss[:, 0:4], ss[:, 0:4], AF.Ln, [ss_b], [ss_b], bias=EPS, scale=1.0 / 128)
        self.actf(ss[:, 0:4], ss[:, 0:4], AF.Exp, [ss_b], [ss_b], scale=-0.5)
        gg = osq[:, 0:512]
        self.vtt(gg, gs, self.hgn[:], ALU.mult, [gs_b, self.hgn_b, osq_b], [osq_b])
        yb, yb_b = self.t16.get()
        for h in range(4):
            hs = slice(h * 128, (h + 1) * 128)
            self.vstt(yb[:, hs], o[:, hs], ss[:, h:h + 1], gg[:, hs], ALU.mult, ALU.mult, [osb_b, ss_b, osq_b], [yb_b])
        for h in range(4):
            self.tr(pB[:, h * 128:(h + 1) * 128], yb[:, h * 128:(h + 1) * 128], ident, [yb_b, self.c_b], [pB_b])
        self.vcopy(self.hn[:, 4:8, sub * 128:(sub + 1) * 128], pB.rearrange("p (h t) -> p h t", h=4), [pB_b], [self.hn_b[4 + h] for h in range(4)])

    def qknorm(self, bank, bank_b, gain_ap, out_ap, out_bufs):
        sq, sq_b = self.t16.get()
        self.actf(sq[:], bank[:], AF.Square, [bank_b], [sq_b])
        b2, b2_b = self.banks.get()
        self.mm(b2[:], self.C("bones"), sq[:], True, True, [sq_b, self.c_b], [b2_b])
        r, r_b = self.t32.get()
        rr = r[:, 0:TT]
        self.actf(rr, b2[:], AF.Ln, [b2_b], [r_b], bias=EPS, scale=1.0 / 64)
        self.actf(rr, rr, AF.Exp, [r_b], [r_b], scale=-0.5)
        self.vstt(out_ap, bank[:], gain_ap, rr, ALU.mult, ALU.mult, [bank_b, r_b, self.small_b], out_bufs)

    def mixer_odd(self, s, j, T):
        nc, fw = self.nc, self.fw
        t0 = T * TT
        w = self.I["c_w_in"][j]
        hnf = lambda k: self.hn[:, k, :]
        gq = self.qkn[:, j, 0:1]
        gk = self.qkn[:, j, 1:2]
        for half in range(2):
            slot, slot_b = self.load_w(w[:, half * 512:(half + 1) * 512])
            for m in range(4):
                bank, bank_b = self.proj_fm(slot, slot_b, m * 128, hnf, NCH, self.hn_b)
                cidx = half * 4 + m
                self.qknorm(bank, bank_b, gq, self.ar[:, cidx, :], [self.ar_b[cidx]])
        slot, slot_b = self.load_w(w[:, 1024:1536])
        for m in range(2):
            bank, bank_b = self.proj_fm(slot, slot_b, m * 128, hnf, NCH, self.hn_b)
            self.qknorm(bank, bank_b, gk, self.kbuf[:, m, t0:t0 + TT], [self.kb_b[m][T]])
            bank, bank_b = self.banks.get()
            for k in range(NCH):
                self.mm(bank[0:64, :], slot[:, k, m * 128 + 64:m * 128 + 128], self.hn[:, k, :], k == 0, k == NCH - 1, [slot_b, self.hn_b[k]], [bank_b])
            for k in range(NCH):
                self.mm(bank[64:128, :], slot[:, k, m * 128:m * 128 + 64], self.hn[:, k, :], k == 0, k == NCH - 1, [slot_b, self.hn_b[k]], [bank_b])
            self.qknorm(bank, bank_b, gk, self.kbuf[:, 2 + m, t0:t0 + TT], [self.kb_b[2 + m][T]])
        for sub in range(4):
            bank, bank_b = self.banks.get()
            for k in range(NCH):
                self.mm(bank[:, 0:256], self.hn[:, k, sub * 128:(sub + 1) * 128], slot[:, k, 256:512], k == 0, k == NCH - 1,
                        [slot_b, self.hn_b[k]], [bank_b])
            self.actf(self.vbuf[:, T * 4 + sub, 0:256], bank[:, 0:256], AF.Copy, [bank_b], [self.vb_b[T * 4 + sub]])
        onesbf = self.C("ones", cols=64)
        import os
        OD = float(os.environ.get("K_ODD", "9"))
        if OD < 2:
            return
        self.preconvert(2 * j + 1)
        HPERM = [0, 2, 1, 3]
        units = [(qb, g) for qb in range(4) for g in range(4)]

        def stage_a(qb, g):
            B = 4 * T + qb
            kbs = [B - 1, B] if B > 0 else [B]
            tmps = [self.t32.get() for _ in kbs]
            for par in range(2):
                bs, bs_b = self.banks.get()
                for ki, kb in enumerate(kbs):
                    for ii in range(2):
                        i = par * 2 + ii
                        h = 4 * g + HPERM[i]
                        hp = (h % 2) * 64
                        var = 0 if (g % 2) == (h % 2) else 2
                        kc = var + g // 2
                        c0_ = ki * 256 + ii * 128
                        self.mm(bs[:, c0_:c0_ + 128], self.kbuf[hp:hp + 64, kc, kb * 128:(kb + 1) * 128],
                                self.ar[hp:hp + 64, h // 2, qb * 128:(qb + 1) * 128], True, True,
                                [self.kb_b[kc][kb // 4], self.ar_b[h // 2]], [bs_b])
                for ki, kb in enumerate(kbs):
                    ksel = 0 if kb == B - 1 else 1
                    tmp, tmp_b = tmps[ki]
                    self.vtt(tmp[:, par * 256:(par + 1) * 256], bs[:, ki * 256:(ki + 1) * 256],
                             self.bias[:, ksel, 4 * g + 2 * par:4 * g + 2 * par + 2, :].rearrange("p a b -> p (a b)"), ALU.add,
                             [bs_b, self.bias_b], [tmp_b])
            es = []
            for ki, kb in enumerate(kbs):
                tmp, tmp_b = tmps[ki]
                ke, (e, e_b) = self.t16.reserve()
                self.actf(e[:], tmp[:, 0:512], AF.Exp, [tmp_b], [e_b])
                es.append((kb, ke, e, e_b))
            return es

        nd_sets = [(self.banks.reserve(), self.banks.reserve()) for _ in range(2)]

        def stage_b(qb, g, es):
            hf = g // 2
            (kN, (nbk, nbk_b)), (kD, (dbk, dbk_b)) = nd_sets[(qb * 2 + hf) % 2]
            for i in range(4):
                h = 4 * g + HPERM[i]
                hp = (h % 2) * 64
                c = h // 2
                cs = slice((c % 4) * 128, (c % 4) * 128 + 128)
                for n_, (kb, ke, e, e_b) in enumerate(es):
                    self.mm(nbk[hp:hp + 64, cs], self.vbuf[:, kb, g * 64:(g + 1) * 64], e[:, i * 128:(i + 1) * 128],
                            n_ == 0, n_ == len(es) - 1, [self.vb_b[kb], e_b], [nbk_b])
                for n_, (kb, ke, e, e_b) in enumerate(es):
                    self.mm(dbk[hp:hp + 64, cs], onesbf, e[:, i * 128:(i + 1) * 128],
                            n_ == 0, n_ == len(es) - 1, [self.c_b, e_b], [dbk_b])
            for (kb, ke, e, e_b) in es:
                self.t16.release(ke)
            if g % 2 == 1:
                r, r_b = self.t32.get()
                for cq in range(4):
                    c = hf * 4 + cq
                    self.actf(r[:, cq * 128:(cq + 1) * 128], dbk[:, cq * 128:(cq + 1) * 128], AF.Ln, [dbk_b, self.small_b], [r_b],
                              bias=self.snk[:, j, c:c + 1])
                self.actf(r[:, 0:512], r[:, 0:512], AF.Exp, [r_b], [r_b], scale=-1.0)
                self.vtt(self.hn[:, hf * 4:(hf + 1) * 4, qb * 128:(qb + 1) * 128],
                         nbk[:].rearrange("p (c t) -> p c t", c=4), r[:, 0:512].rearrange("p (c t) -> p c t", c=4), ALU.mult,
                         [nbk_b, r_b], [self.hn_b[hf * 4 + i] for i in range(4)])

        pend = stage_a(*units[0])
        for ui, (qb, g) in enumerate(units):
            nxt = stage_a(*units[ui + 1]) if ui + 1 < len(units) else None
            stage_b(qb, g, pend)
            pend = nxt
        for (a_, b_) in nd_sets:
            self.banks.release(a_[0])
            self.banks.release(b_[0])


def _host_small(inp):
    f32 = np.float32
    norms = np.stack([inp["mix_norm"], inp["ffn_norm"], inp["ple_norm"]], 0).astype(f32)
    norms = np.ascontiguousarray(norms.reshape(3, DEPTH, NCH, 128).transpose(3, 0, 1, 2))
    cw = np.concatenate([inp["ffn_conv"].astype(f32), inp["ffn_conv_b"].astype(f32)[:, None, :]], 1)
    cw = np.ascontiguousarray(cw.reshape(DEPTH, 4, 44, 128).transpose(3, 0, 1, 2))
    lbl = np.ascontiguousarray(inp["hg_lb_logits"].astype(f32).reshape(1, 1024))
    hgn = np.ascontiguousarray(np.tile(inp["hg_out_norm"].astype(f32), (1, 4)).reshape(1, 1024))
    qkn = np.stack([np.tile(inp["q_norm"].astype(f32), (1, 2)), np.tile(inp["k_norm"].astype(f32), (1, 2))], -1)
    qkn = np.ascontiguousarray(qkn.transpose(1, 0, 2))
    snk = inp["sinks"].astype(f32).reshape(2, NCH, 2)
    snk = np.ascontiguousarray(np.repeat(snk.transpose(2, 0, 1), 64, axis=0))
    bucket, valid = _t5_bias_index()
    rb = inp["rel_bias"].astype(f32)
    bt = rb[bucket]
    bt = np.where(valid[..., None], bt, f32(NEG)).transpose(0, 1, 3, 2)
    hperm = np.array([4 * g + i for g in range(4) for i in (0, 2, 1, 3)])
    bt = bt[:, :, hperm, :]
    biasT = np.ascontiguousarray(bt.reshape(128, 2 * 16 * 128).astype(f32))
    cf, cb = _const_arrays()
    return dict(norms=norms, convw=cw, lbl=lbl, hgn=hgn, qkn=qkn, snk=snk, biasT=biasT, cf=cf, cb=cb)


_CACHE = {}


def _run(inputs, nseq, layers, ncores, dbg=None, trace=False):
    key = (nseq, tuple(layers), tuple(sorted(dbg.items())) if dbg else None)
    f32 = np.float32
    small = _host_small(inputs)
    shared = {k: np.ascontiguousarray(np.asarray(inputs[k], dtype=f32)) for k in
              ["ab_w_in", "ab_w_out", "c_w_in", "c_w_out", "ffn_up", "ffn_down", "ple_gate", "ple_proj"]}
    x = np.asarray(inputs["x"], dtype=f32)
    p = np.asarray(inputs["p"], dtype=f32)
    in_maps = []
    for ci in range(ncores):
        sl = slice(ci * nseq, (ci + 1) * nseq)
        m = dict(shared)
        m.update(small)
        m["xT"] = np.ascontiguousarray(x[sl].transpose(0, 2, 1))
        m["pT"] = np.ascontiguousarray(p[:, sl].transpose(0, 1, 3, 2))
        in_maps.append(m)
    nc = Builder(nseq, layers, dbg).build()
    res = run_bass_kernel_spmd(nc, in_maps, core_ids=list(range(ncores)))
    return res


def kernel(**inputs):
    res = _run(inputs, 2, list(range(DEPTH)), 8)
    outs = [r["outT"] for r in res.results]
    out = np.concatenate(outs, axis=0).transpose(0, 2, 1)
    return np.ascontiguousarray(out.astype(np.float32))
```

```python
import math
from contextlib import ExitStack

import numpy as np

import concourse.bass as bass
import concourse.mybir as mybir
from concourse.bass_utils import run_bass_kernel_spmd

F32 = mybir.dt.float32
BF16 = mybir.dt.bfloat16
AF = mybir.ActivationFunctionType
ALU = mybir.AluOpType
AX = mybir.AxisListType

D = 1024
S = 2048
DEPTH = 4
NCH = 8
TT = 512
NTILE = S // TT
F_FF = 2816
NFC = 22
EPS = 1e-6
NEG = -30000.0
import os as _os
_OPT = _os.environ.get("K_OPT", "normbg,normacc,lock,hggen,preconv").split(",")
OPT_SBEARLY = "sbearly" in _OPT
OPT_NORMACC = "normacc" in _OPT
OPT_NORMBG = "normbg" in _OPT
OPT_LOCK = "lock" in _OPT
OPT_HGPIPE = "hgpipe" in _OPT
OPT_HGGEN = "hggen" in _OPT
OPT_PRECONV = "preconv" in _OPT


class Buf:
    __slots__ = ("w", "r")

    def __init__(self):
        self.w = None
        self.r = {}


class Stream:
    def __init__(self, name, h, pe=False):
        self.name = name
        self.h = h
        self.pe = pe
        self.seq = 0
        self.know = {}
        self.oplist = []
        self.sem = None


class Chan:
    def __init__(self, name):
        self.name = name
        self.cnt = 0
        self.last = None
        self.sem = None


class Op:
    __slots__ = ("st", "fn", "waits", "needed", "seq", "val", "chan")


class FW:
    def __init__(self, nc, es):
        self.nc = nc
        self.es = es
        self.ops = []
        self.pe = self._mk("pe", nc.tensor, True)
        self.act = self._mk("act", nc.scalar)
        self.dve = self._mk("dve", nc.vector)
        self.pool = self._mk("pool", nc.gpsimd)
        self.sp = self._mk("sp", nc.sync)
        self.streams = [self.pe, self.act, self.dve, self.pool, self.sp]
        self.chans = []
        self.sp_ch = [self._ch("spc%d" % i) for i in range(6)]
        self.pl_ch = [self._ch("plc%d" % i) for i in range(4)]
        self._spi = 0
        self._pli = 0

    def _mk(self, name, h, pe=False):
        st = Stream(name, h, pe)
        st.sem = self.es.enter_context(self.nc.semaphore("s_" + name))
        return st

    def _ch(self, name):
        c = Chan(name)
        c.sem = self.es.enter_context(self.nc.semaphore("c_" + name))
        self.chans.append(c)
        return c

    def _need(self, st, waits, ev, raw):
        if ev is None:
            return
        s, n, kn = ev
        if s is st:
            if st.pe:
                return
        if st.know.get(s, 0) >= n:
            return
        waits.append((s, n))
        newk = dict(st.know)
        newk[s] = n
        for a, b in kn.items():
            if newk.get(a, 0) < b:
                newk[a] = b
        st.know = newk

    def op(self, st, fn, reads=(), writes=(), chan=None):
        waits = []
        for b in reads:
            self._need(st, waits, b.w, True)
        for b in writes:
            self._need(st, waits, b.w, False)
            for ev in b.r.values():
                self._need(st, waits, ev, False)
        if chan is not None:
            self._need(st, waits, chan.last, False)
        st.seq += 1
        o = Op()
        o.st = st
        o.fn = fn
        o.waits = waits
        o.needed = False
        o.seq = st.seq
        o.val = 0
        o.chan = chan
        self.ops.append(o)
        st.oplist.append(o)
        if chan is not None:
            chan.cnt += 16 * (len(fn) if isinstance(fn, (list, tuple)) else 1)
            ev = (chan, chan.cnt, st.know)
            chan.last = ev
        else:
            ev = (st, st.seq, st.know)
        for b in reads:
            b.r[ev[0]] = ev
        for b in writes:
            b.w = ev
            b.r = {}
        return ev

    def dma_sp(self, fn, reads=(), writes=()):
        ch = self.sp_ch[self._spi % len(self.sp_ch)]
        self._spi += 1
        return self.op(self.sp, fn, reads, writes, chan=ch)

    def dma_pool(self, fn, reads=(), writes=()):
        ch = self.pl_ch[self._pli % len(self.pl_ch)]
        self._pli += 1
        return self.op(self.pool, fn, reads, writes, chan=ch)

    def finish(self, final_events):
        waits = []
        for ev in final_events:
            self._need(self.sp, waits, ev, True)
        for o in self.ops:
            for (s, n) in o.waits:
                if isinstance(s, Stream):
                    s.oplist[n - 1].needed = True
        for (s, n) in waits:
            if isinstance(s, Stream):
                s.oplist[n - 1].needed = True
        for st in self.streams:
            c = 0
            for o in st.oplist:
                if o.needed:
                    c += 1
                o.val = c

        def emit_wait(st, s, n):
            if isinstance(s, Stream):
                st.h.wait_ge(s.sem, s.oplist[n - 1].val)
            else:
                st.h.wait_ge(s.sem, n)

        nw = 0
        for o in self.ops:
            for (s, n) in o.waits:
                emit_wait(o.st, s, n)
                nw += 1
            if isinstance(o.fn, (list, tuple)):
                for f_ in o.fn:
                    f_().then_inc(o.chan.sem, 16)
                continue
            ins = o.fn()
            if o.chan is not None:
                ins.then_inc(o.chan.sem, 16)
            elif o.needed:
                ins.then_inc(o.st.sem, 1)
        for (s, n) in waits:
            emit_wait(self.sp, s, n)
        self.stats = dict(n_ops=len(self.ops), n_waits=nw,
                          per_stream={st.name: len(st.oplist) for st in self.streams})


class Pool_:
    def __init__(self, tiles):
        self.items = [(t, Buf()) for t in tiles]
        self.free = list(range(len(tiles)))

    def get(self):
        k = self.free.pop(0)
        self.free.append(k)
        return self.items[k]

    def reserve(self):
        k = self.free.pop(0)
        return k, self.items[k]

    def release(self, k):
        self.free.append(k)


def _consts():
    p = np.arange(128)[:, None]
    m = np.arange(128)[None, :]
    c = {}
    c["ident"] = (p == m).astype(np.float32)
    c["ones"] = np.ones((128, 128), np.float32)
    c["bones"] = ((p // 64) == (m // 64)).astype(np.float32)
    c["negtri"] = -(p >= m).astype(np.float32)
    c["sbmask"] = (p < m).astype(np.float32)
    same = (p // 64) == (m // 64)
    tri2 = (same & (p <= m)).astype(np.float32)
    mid = (m // 64) * 64 + 31
    trimid = (same & (p <= mid)).astype(np.float32)
    c["tri2"] = tri2
    c["trid1"] = tri2 - trimid
    c["trisuf"] = (same & (p > m)).astype(np.float32)
    co = np.zeros((128, 128), np.float32)
    co[:, 0] = (np.arange(128) < 64)
    co[:, 1] = (np.arange(128) >= 64)
    c["chunkones"] = co
    mk = np.zeros((128, 256), np.float32)
    cc = np.arange(256)[None, :]
    mk[:, :] = ((p % 64) <= (cc % 64))
    c["maskS"] = mk
    return c


F32_CONSTS = ["tri2", "trid1", "trisuf", "chunkones"]
BF_CONSTS = ["ident", "ones", "bones", "negtri", "sbmask", "maskS"]


def _const_arrays():
    c = _consts()
    f = np.concatenate([c[k] for k in F32_CONSTS], axis=1)
    b = np.concatenate([c[k] for k in BF_CONSTS], axis=1)
    return np.ascontiguousarray(f), np.ascontiguousarray(b)


def _offsets(names, c):
    off = {}
    o = 0
    for k in names:
        off[k] = o
        o += c[k].shape[1]
    return off, o


def _t5_bias_index():
    W = 128
    t = np.arange(W)[None, None, :]
    s = np.arange(W)[:, None, None]
    kb = np.arange(2)[None, :, None]
    dist = t + W - (kb * W + s)
    valid = (dist >= 0) & (dist < W)
    max_exact = 16
    large = max_exact + (np.log(np.maximum(dist, max_exact) / max_exact) / math.log(128 / max_exact) * (32 - max_exact)).astype(np.int32)
    large = np.minimum(large, 31)
    bucket = np.where(dist < max_exact, np.maximum(dist, 0), large).astype(np.int32)
    return bucket, valid


class Builder:
    def __init__(self, nseq, layers, dbg=None):
        self.nseq = nseq
        self.layers = layers
        self.dbg = dbg

    def build(self):
        nc = bass.Bass("TRN2", target_bir_lowering=False)
        self.nc = nc
        nseq = self.nseq
        dt = nc.dram_tensor
        I = {}
        I["xT"] = dt("xT", [nseq, D, S], F32, kind="ExternalInput").ap()
        I["pT"] = dt("pT", [DEPTH, nseq, 256, S], F32, kind="ExternalInput").ap()
        I["ab_w_in"] = dt("ab_w_in", [2, D, 3584], F32, kind="ExternalInput").ap()
        I["ab_w_out"] = dt("ab_w_out", [2, D, D], F32, kind="ExternalInput").ap()
        I["c_w_in"] = dt("c_w_in", [2, D, 1536], F32, kind="ExternalInput").ap()
        I["c_w_out"] = dt("c_w_out", [2, D, D], F32, kind="ExternalInput").ap()
        I["ffn_up"] = dt("ffn_up", [DEPTH, D, 2 * F_FF], F32, kind="ExternalInput").ap()
        I["ffn_down"] = dt("ffn_down", [DEPTH, F_FF, D], F32, kind="ExternalInput").ap()
        I["ple_gate"] = dt("ple_gate", [DEPTH, D, D], F32, kind="ExternalInput").ap()
        I["ple_proj"] = dt("ple_proj", [DEPTH, 256, D], F32, kind="ExternalInput").ap()
        I["norms"] = dt("norms", [128, 3, DEPTH, NCH], F32, kind="ExternalInput").ap()
        I["convw"] = dt("convw", [128, DEPTH, 4, 44], F32, kind="ExternalInput").ap()
        I["lbl"] = dt("lbl", [1, 2 * 512], F32, kind="ExternalInput").ap()
        I["hgn"] = dt("hgn", [1, 2 * 512], F32, kind="ExternalInput").ap()
        I["qkn"] = dt("qkn", [128, 2, 2], F32, kind="ExternalInput").ap()
        I["snk"] = dt("snk", [128, 2, NCH], F32, kind="ExternalInput").ap()
        I["biasT"] = dt("biasT", [128, 2 * 16 * 128], F32, kind="ExternalInput").ap()
        cf, cb = _const_arrays()
        I["cf"] = dt("cf", list(cf.shape), F32, kind="ExternalInput").ap()
        I["cb"] = dt("cb", list(cb.shape), F32, kind="ExternalInput").ap()
        self.I = I
        self.outT = dt("outT", [nseq, D, S], F32, kind="ExternalOutput").ap()
        self.wscr = dt("wscr", [120, 128, 4096], BF16, kind="Internal").ap()
        self.wimg = {}
        self.preconv_done = set()
        if self.dbg:
            self.dbg_out = {k: dt("dbg_" + k, list(shp), F32, kind="ExternalOutput").ap() for k, shp in self.dbg.items()}
        c = _consts()
        self.cf_off, self.cf_n = _offsets(F32_CONSTS, c)
        self.cb_off, self.cb_n = _offsets(BF_CONSTS, c)

        with ExitStack() as es:
            self.es = es
            fw = FW(nc, es)
            self.fw = fw
            sb = lambda name, shape, dtype: es.enter_context(nc.sbuf_tensor(name, shape, dtype))
            self.res = sb("res", [128, NCH, S], F32)
            self.res_b = [[Buf() for _ in range(NTILE)] for _ in range(NCH)]
            self.hn = sb("hn", [128, NCH, TT], BF16)
            self.hn_b = [Buf() for _ in range(NCH)]
            self.ar = sb("arena", [128, NFC, TT], BF16)
            self.ar_b = [Buf() for _ in range(NFC)]
            self.kbuf = sb("kbuf", [128, 4, S], BF16)
            self.kb_b = [[Buf() for _ in range(NTILE)] for _ in range(4)]
            self.vbuf = sb("vbuf", [128, 16, 512], BF16)
            self.vb_b = [Buf() for _ in range(16)]
            self.bias = sb("bias", [128, 2, 16, 128], BF16)
            self.bias_b = Buf()
            self.ws = [sb("ws%d" % i, [128, NCH, 512], BF16) for i in range(3)]
            self.wpool = Pool_(self.ws)
            self.stage = sb("stage", [128, 4, 512], F32)
            self.stage_b = [Buf() for _ in range(4)]
            stf = self.stage[:].rearrange("p a b -> p (a b)")
            self.upool = Pool_([stf[:, i * 520:i * 520 + 516] for i in range(3)])
            t32 = [sb("t32_%d" % i, [128, 516], F32) for i in range(7)]
            self.t32 = Pool_(t32)
            t16 = [sb("t16_%d" % i, [128, 512], BF16) for i in range(7)]
            self.t16 = Pool_(t16)
            self.cf = sb("cf_sb", [128, self.cf_n], F32)
            self.cb = sb("cb_sb", [128, self.cb_n], BF16)
            self.c_b = Buf()
            self.norms = sb("norms_sb", [128, 3, DEPTH, NCH], F32)
            self.convw = sb("convw_sb", [128, DEPTH, 4, 44], F32)
            self.qkn = sb("qkn_sb", [128, 2, 2], F32)
            self.snk = sb("snk_sb", [128, 2, NCH], F32)
            self.lb = sb("lb_sb", [128, 2, 512], F32)
            self.lb1 = sb("lb1_sb", [128, 512], F32)
            self.hgn = sb("hgn_sb", [128, 512], F32)
            self.hgn_b = Buf()
            self.small_b = Buf()
            self.lb_b = Buf()
            self.S32 = sb("S32", [128, 512], F32)
            self.S32_b = Buf()
            self.Sbf = sb("Sbf", [128, 512], BF16)
            self.Sbf_b = Buf()
            self.halo = sb("halo", [128, 2, 44, 2], F32)
            self.halo_b = [Buf(), Buf()]
            self.sm = sb("smalls", [128, 64], F32)
            self.sm_pool = Pool_([self.sm[:, i * 8:(i + 1) * 8] for i in range(8)])
            self.negrow = sb("negrow", [128, 128], BF16)
            self.sbR_b = {0: Buf(), 64: Buf()}
            self.sbrt_b = {0: [Buf(), Buf(), Buf()], 64: [Buf(), Buf(), Buf()]}
            banks = [es.enter_context(nc.psum_tensor("pb%d" % i, [128, 512], F32)) for i in range(7)]
            self.banks = Pool_(banks)
            self.pbt = es.enter_context(nc.psum_tensor("pbt", [128, 1024], BF16))
            self.pbt_b = [Buf(), Buf()]

            self.acc = None
            self.pre_rstd = None
            self.prologue()
            finals = []
            self.finals = finals
            self.load_x(0)
            for s in range(nseq):
                for li in self.layers:
                    self.layer(s, li)
            if self.dbg:
                finals += self.dbg_events
            fw.finish(finals)
        return nc

    def C(self, name, bf=True, cols=None):
        if bf:
            o = self.cb_off[name]
            n = _consts()[name].shape[1] if cols is None else cols
            return self.cb[:, o:o + n]
        o = self.cf_off[name]
        n = _consts()[name].shape[1] if cols is None else cols
        return self.cf[:, o:o + n]

    def mm(self, out, lhsT, rhs, start, stop, reads, writes, skip=False):
        nc = self.nc
        if skip:
            return self.fw.op(self.fw.pe, lambda: nc.tensor.matmul(out, lhsT, rhs, start=start, stop=stop, skip_group_check=True), reads, writes)
        return self.fw.op(self.fw.pe, lambda: nc.tensor.matmul(out, lhsT, rhs, start=start, stop=stop), reads, writes)

    def tr(self, out, in_, ident, reads, writes):
        nc = self.nc
        return self.fw.op(self.fw.pe, lambda: nc.tensor.transpose(out, in_, ident), reads, writes)

    def actf(self, out, in_, func, reads, writes, bias=0.0, scale=1.0):
        nc = self.nc
        return self.fw.op(self.fw.act, lambda: nc.scalar.activation(out=out, in_=in_, func=func, bias=bias, scale=scale), reads, writes)

    def vtt(self, out, in0, in1, op, reads, writes):
        nc = self.nc
        return self.fw.op(self.fw.dve, lambda: nc.vector.tensor_tensor(out=out, in0=in0, in1=in1, op=op), reads, writes)

    def vts(self, out, in0, s1, s2, op0, op1, reads, writes):
        nc = self.nc
        if op1 is None:
            return self.fw.op(self.fw.dve, lambda: nc.vector.tensor_scalar(out=out, in0=in0, scalar1=s1, scalar2=None, op0=op0), reads, writes)
        return self.fw.op(self.fw.dve, lambda: nc.vector.tensor_scalar(out=out, in0=in0, scalar1=s1, scalar2=s2, op0=op0, op1=op1), reads, writes)

    def vstt(self, out, in0, scalar, in1, op0, op1, reads, writes):
        nc = self.nc
        return self.fw.op(self.fw.dve, lambda: nc.vector.scalar_tensor_tensor(out=out, in0=in0, scalar=scalar, in1=in1, op0=op0, op1=op1), reads, writes)

    def vcopy(self, out, in_, reads, writes):
        nc = self.nc
        return self.fw.op(self.fw.dve, lambda: nc.vector.tensor_copy(out, in_), reads, writes)

    def vrecip(self, out, in_, reads, writes):
        nc = self.nc
        return self.fw.op(self.fw.dve, lambda: nc.vector.reciprocal(out=out, in_=in_), reads, writes)

    def pcopy(self, out, in_, reads, writes):
        nc = self.nc
        return self.fw.op(self.fw.pool, lambda: nc.gpsimd.tensor_copy(out, in_), reads, writes)

    def load_w(self, src_ap, nk=NCH, ncols=512, src2=None, bgq=False):
        nc = self.nc
        key = (src_ap.tensor.name, str(src_ap.offset), tuple(tuple(x) for x in src_ap.ap))
        slot, b = self.wpool.get()
        img = self.wimg.get(key)
        if img is None:
            idx = len(self.wimg)
            ib = Buf()
            self.wimg[key] = (idx, ib)
            if src2 is None:
                dst = slot[:, 0:nk, 0:ncols]
                src = src_ap.rearrange("(c p) n -> p c n", p=128)
                self.fw.dma_pool(lambda: nc.gpsimd.dma_start(out=dst, in_=src), reads=(), writes=[b])
            else:
                h = ncols // 2
                fns = []
                for i_, sa in enumerate((src_ap, src2)):
                    dst = slot[:, 0:nk, i_ * h:(i_ + 1) * h]
                    src = sa.rearrange("(c p) n -> p c n", p=128)
                    fns.append(lambda dst=dst, src=src: nc.gpsimd.dma_start(out=dst, in_=src))
                self.fw.dma_pool(fns, reads=(), writes=[b])
            if ncols == 512:
                simg = self.wscr[idx, :, 0:nk * 512]
                ssrc = slot[:, 0:nk, :].rearrange("p c n -> p (c n)")
                if bgq:
                    self.bg_pending.append((lambda: nc.gpsimd.dma_start(out=simg, in_=ssrc), b, ib))
                    while len(self.bg_pending) > 2:
                        f_, b_, ib_ = self.bg_pending.pop(0)
                        self.fw.dma_pool(f_, reads=[b_], writes=[ib_])
                else:
                    self.fw.dma_sp(lambda: nc.sync.dma_start(out=simg, in_=ssrc), reads=[b], writes=[ib])
            else:
                self.wimg[key] = None
                del self.wimg[key]
        else:
            idx, ib = img
            simg = self.wscr[idx, :, 0:nk * 512]
            sdst = slot[:, 0:nk, :].rearrange("p c n -> p (c n)")
            self.fw.dma_sp(lambda: nc.sync.dma_start(out=sdst, in_=simg), reads=[ib], writes=[b])
        return slot, b

    def preconvert(self, li):
        if li in self.preconv_done or not OPT_PRECONV:
            return
        self.preconv_done.add(li)
        self.bg_pending = []
        j = li // 2
        wo = self.I["ab_w_out"][j] if li % 2 == 0 else self.I["c_w_out"][j]
        for half in range(2):
            self.load_w(wo[:, half * 512:(half + 1) * 512], bgq=True)
        wup = self.I["ffn_up"][li]
        for j0 in range(0, NFC, 2):
            self.load_w(wup[:, j0 * 128:(j0 + 2) * 128], ncols=512, src2=wup[:, F_FF + j0 * 128:F_FF + (j0 + 2) * 128], bgq=True)
        while self.bg_pending:
            f_, b_, ib_ = self.bg_pending.pop(0)
            self.fw.dma_pool(f_, reads=[b_], writes=[ib_])

    def dump(self, key, sb_ap, bufs, dst=None):
        if not self.dbg or key not in self.dbg:
            return
        nc = self.nc
        d = self.dbg_out[key] if dst is None else dst
        ev = self.fw.dma_sp(lambda: nc.sync.dma_start(out=d, in_=sb_ap), reads=bufs, writes=())
        self.dbg_events.append(ev)

    def prologue(self):
        nc, fw, I = self.nc, self.fw, self.I
        self.dbg_events = []
        fw.dma_sp(lambda: nc.sync.dma_start(out=self.cf[:], in_=I["cf"]), writes=[self.c_b])
        fw.dma_pool(lambda: nc.gpsimd.dma_start(out=self.cb[:], in_=I["cb"]), writes=[self.c_b])
        fw.dma_sp(lambda: nc.sync.dma_start(out=self.norms[:], in_=I["norms"]), writes=[self.small_b])
        fw.dma_sp(lambda: nc.sync.dma_start(out=self.convw[:], in_=I["convw"]), writes=[self.small_b])
        fw.dma_sp(lambda: nc.sync.dma_start(out=self.qkn[:], in_=I["qkn"]), writes=[self.small_b])
        fw.dma_sp(lambda: nc.sync.dma_start(out=self.snk[:], in_=I["snk"]), writes=[self.small_b])
        l0, l0_b = self.t32.get()
        l1, l1_b = self.t32.get()
        fw.dma_sp(lambda: nc.sync.dma_start(out=l0[:, 0:512], in_=I["lbl"][0:1, 0:512].partition_broadcast(128)), writes=[l0_b])
        fw.dma_sp(lambda: nc.sync.dma_start(out=l1[:, 0:512], in_=I["lbl"][0:1, 512:1024].partition_broadcast(128)), writes=[l1_b])
        self.vtt(l1[:, 0:512], l1[:, 0:512], l0[:, 0:512], ALU.subtract, [l0_b, l1_b], [l1_b])
        self.actf(self.lb1[:], l1[:, 0:512], AF.Sigmoid, [l1_b], [self.small_b])
        fw.dma_pool(lambda: nc.gpsimd.dma_start(out=self.bias[:].rearrange("p a b c -> p (a b c)"), in_=I["biasT"]), writes=[self.bias_b])
        fw.op(fw.dve, lambda: nc.vector.memset(self.negrow[:], -1.0), writes=[self.c_b])
        self.vts(self.qkn[:, :, 0:1], self.qkn[:, :, 0:1], 0.125, None, ALU.mult, None, [self.small_b], [self.small_b])
        self.actf(self.snk[:], self.snk[:], AF.Exp, [self.small_b], [self.small_b])

    def load_x(self, s, T=None, pool=False):
        nc, fw = self.nc, self.fw
        for T_ in (range(NTILE) if T is None else [T]):
            for c in range(NCH):
                src = self.I["xT"][s, c * 128:(c + 1) * 128, T_ * TT:(T_ + 1) * TT]
                dst = self.res[:, c, T_ * TT:(T_ + 1) * TT]
                if pool:
                    fw.dma_pool(lambda dst=dst, src=src: nc.gpsimd.dma_start(out=dst, in_=src), writes=[self.res_b[c][T_]])
                else:
                    fw.dma_sp(lambda dst=dst, src=src: nc.sync.dma_start(out=dst, in_=src), writes=[self.res_b[c][T_]])

    def store_tile(self, s, T):
        nc, fw = self.nc, self.fw
        evs = []
        for c in range(NCH):
            dst = self.outT[s, c * 128:(c + 1) * 128, T * TT:(T + 1) * TT]
            src = self.res[:, c, T * TT:(T + 1) * TT]
            evs.append(fw.dma_pool(lambda dst=dst, src=src: nc.gpsimd.dma_start(out=dst, in_=src), reads=[self.res_b[c][T]]))
        return evs

    def store_out(self, s):
        nc, fw = self.nc, self.fw
        evs = []
        for c in range(NCH):
            dst = self.outT[s, c * 128:(c + 1) * 128, :]
            src = self.res[:, c, :]
            evs.append(fw.dma_sp(lambda dst=dst, src=src: nc.sync.dma_start(out=dst, in_=src), reads=self.res_b[c]))
        return evs

    def rmsnorm(self, T, which, li):
        t0 = T * TT
        if which == 0 and self.pre_rstd is not None and self.pre_rstd[0] == (li, T):
            _, kr, rr, r_b = self.pre_rstd
            self.pre_rstd = None
        elif self.acc is not None and self.acc["T"] == T and self.acc["n"] == NCH:
            kr, rr, r_b = self.stats_finish()
        else:
            self.stats_begin(T)
            for c in range(NCH):
                self.stats_add(c)
            kr, rr, r_b = self.stats_finish()
        for c in range(NCH):
            g = self.norms[:, which, li, c:c + 1]
            self.vstt(self.hn[:, c, :], self.res[:, c, t0:t0 + TT], g, rr, ALU.mult, ALU.mult,
                      [self.res_b[c][T], r_b, self.small_b], [self.hn_b[c]])
        self.t32.release(kr)

    def stats_begin(self, T):
        kb, (ssb, ssb_b) = self.banks.reserve()
        self.acc = dict(T=T, n=0, kb=kb, ssb=ssb, ssb_b=ssb_b, pend=None)

    def _stats_flush(self):
        a = self.acc
        if a["pend"] is not None:
            sq, sq_b, first, last = a["pend"]
            self.mm(a["ssb"][:], self.C("ones"), sq[:], first, last, [sq_b, self.c_b], [a["ssb_b"]])
            a["pend"] = None

    def stats_add(self, c):
        a = self.acc
        T = a["T"]
        t0 = T * TT
        self._stats_flush()
        sq, sq_b = self.t16.get()
        self.actf(sq[:], self.res[:, c, t0:t0 + TT], AF.Square, [self.res_b[c][T]], [sq_b])
        a["pend"] = (sq, sq_b, a["n"] == 0, a["n"] == NCH - 1)
        a["n"] += 1

    def stats_finish(self):
        a = self.acc
        self._stats_flush()
        kr, (r, r_b) = self.t32.reserve()
        rr = r[:, 0:TT]
        self.actf(rr, a["ssb"][:], AF.Ln, [a["ssb_b"]], [r_b], bias=EPS, scale=1.0 / D)
        self.actf(rr, rr, AF.Exp, [r_b], [r_b], scale=-0.5)
        self.banks.release(a["kb"])
        self.acc = None
        return kr, rr, r_b

    def stats_bg(self, key, T):
        self.stats_begin(T)
        acc = self.acc
        self.acc = None
        for c in range(NCH):
            self.acc, sv = acc, self.acc
            self.stats_add(c)
            self.acc = sv
            yield
        self.acc, sv = acc, self.acc
        kr, rr, r_b = self.stats_finish()
        self.acc = sv
        self.pre_rstd = (key, kr, rr, r_b)
        yield

    def proj_fm(self, slot, slot_b, col0, rhs_fn, nk, rhs_bufs, ncols=128):
        bank, bank_b = self.banks.get()
        for k in range(nk):
            rb_ = [rhs_bufs[k]] if len(rhs_bufs) == nk else list(rhs_bufs)
            self.mm(bank[0:ncols, :], slot[:, k, col0:col0 + ncols], rhs_fn(k), k == 0, k == nk - 1,
                    [slot_b] + rb_, [bank_b])
        return bank, bank_b

    def add_to_res(self, bank, bank_b, n, T, stats=False):
        t0 = T * TT
        self.vtt(self.res[:, n, t0:t0 + TT], bank[:], self.res[:, n, t0:t0 + TT], ALU.add,
                 [bank_b, self.res_b[n][T]], [self.res_b[n][T]])
        if stats and OPT_NORMACC:
            self.stats_add(n)

    def out_proj(self, w_ap, T):
        if OPT_NORMACC:
            self.stats_begin(T)
        for half in range(2):
            slot, slot_b = self.load_w(w_ap[:, half * 512:(half + 1) * 512])
            for nq in range(4):
                bank, bank_b = self.proj_fm(slot, slot_b, nq * 128, lambda k: self.hn[:, k, :], NCH, self.hn_b)
                self.add_to_res(bank, bank_b, half * 4 + nq, T, stats=True)

    def layer(self, s, li):
        j = li // 2
        if li % 2 == 0:
            self.even_prep(j)
        import os
        st = os.environ.get("K_STAGES", "norm,mix,outp,ffn,ple,sb,hg").split(",")
        self.st = st
        for T in range(NTILE):
            if "norm" in st:
                self.rmsnorm(T, 0, li)
            if li % 2 == 0:
                if "mix" in st:
                    self.mixer_even(s, j, T)
                if "outp" in st:
                    self.out_proj(self.I["ab_w_out"][j], T)
            else:
                if "mix" in st:
                    self.mixer_odd(s, j, T)
                if "outp" in st:
                    self.out_proj(self.I["c_w_out"][j], T)
            self.dump("res_mix_L%d" % li, self.res[:, :, T * TT:(T + 1) * TT], [self.res_b[c][T] for c in range(NCH)],
                      dst=None if not self.dbg or ("res_mix_L%d" % li) not in self.dbg else self.dbg_out["res_mix_L%d" % li][:, :, T * TT:(T + 1) * TT])
            if "ffn" in st:
                bg = None
                if "norm" in st:
                    if T + 1 < NTILE:
                        nxt = (li, T + 1)
                    else:
                        k_ = self.layers.index(li)
                        nxt = (self.layers[k_ + 1], 0) if k_ + 1 < len(self.layers) else None
                    if nxt is not None and OPT_NORMBG:
                        bg = self.stats_bg(nxt, nxt[1])
                self.ffn(s, li, T, bg)
            if "ple" in st:
                self.ple(s, li, T)
            if li == self.layers[-1]:
                self.finals += self.store_tile(s, T)
                if s + 1 < self.nseq:
                    self.load_x(s + 1, T, pool=True)
        if s == 0:
            self.dump("res_L%d" % li, self.res[:], [b for c in range(NCH) for b in self.res_b[c]])

    def ffn(self, s, li, T, bg=None):
        nc, fw = self.nc, self.fw
        t0 = T * TT
        self.rmsnorm(T, 1, li)
        cur, nxt = T % 2, (T + 1) % 2
        if T == 0:
            fw.op(fw.dve, lambda: nc.vector.memset(self.halo[:, 0, :, :], 0.0), writes=[self.halo_b[0]])
        wup = self.I["ffn_up"][li]
        groups = [(j0, 2) for j0 in range(0, NFC, 2)]
        cw = self.convw
        for (j0, nj) in groups:
            if bg is not None and j0 >= 2:
                next(bg, None)
            slot, slot_b = self.load_w(wup[:, j0 * 128:(j0 + nj) * 128], ncols=512,
                                       src2=wup[:, F_FF + j0 * 128:F_FF + (j0 + nj) * 128])
            for jj in range(nj):
                jp = j0 + jj
                ys = []
                for (col0, idx) in ((jj * 128, jp), (nj * 128 + jj * 128, NFC + jp)):
                    bank, bank_b = self.proj_fm(slot, slot_b, col0, lambda k: self.hn[:, k, :], NCH, self.hn_b)
                    u, u_b = self.upool.get()
                    y, y_b = self.t32.get()
                    self.actf(u[:, 2:2 + TT], bank[:], AF.Copy, [bank_b], [u_b])
                    self.actf(u[:, 0:2], self.halo[:, cur, idx, :], AF.Copy, [self.halo_b[cur]], [u_b])
                    self.actf(self.halo[:, nxt, idx, :], bank[:, TT - 2:TT], AF.Copy, [bank_b], [self.halo_b[nxt]])
                    self.actf(y[:, 0:TT], bank[:], AF.Identity, [bank_b, self.small_b], [y_b],
                              bias=cw[:, li, 3, idx:idx + 1], scale=cw[:, li, 2, idx:idx + 1])
                    self.vstt(y[:, 0:TT], u[:, 1:1 + TT], cw[:, li, 1, idx:idx + 1], y[:, 0:TT], ALU.mult, ALU.add,
                              [u_b, y_b, self.small_b], [y_b])
                    self.vstt(y[:, 0:TT], u[:, 0:TT], cw[:, li, 0, idx:idx + 1], y[:, 0:TT], ALU.mult, ALU.add,
                              [u_b, y_b, self.small_b], [y_b])
                    ys.append((y, y_b))
                (yg, yg_b), (yu, yu_b) = ys
                self.actf(yg[:, 0:TT], yg[:, 0:TT], AF.Silu, [yg_b], [yg_b])
                self.vtt(self.ar[:, jp, :], yg[:, 0:TT], yu[:, 0:TT], ALU.mult, [yg_b, yu_b], [self.ar_b[jp]])
        if bg is not None:
            for _ in bg:
                pass
        wd = self.I["ffn_down"][li]
        jgs = [(0, 8), (8, 8), (16, 6)]
        if OPT_NORMACC:
            self.stats_begin(T)
        for nh in range(2):
            bks = [self.banks.get() for _ in range(4)]
            for (j0, nj) in jgs:
                slot, slot_b = self.load_w(wd[j0 * 128:(j0 + nj) * 128, nh * 512:(nh + 1) * 512], nk=nj)
                for jj in range(nj):
                    jf = j0 + jj
                    for nq in range(4):
                        self.mm(bks[nq][0][:], slot[:, jj, nq * 128:(nq + 1) * 128], self.ar[:, jf, :], jf == 0, jf == NFC - 1,
                                [slot_b, self.ar_b[jf]], [bks[nq][1]])
            for nq in range(4):
                self.add_to_res(bks[nq][0], bks[nq][1], nh * 4 + nq, T, stats=True)

    def ple(self, s, li, T):
        nc, fw = self.nc, self.fw
        t0 = T * TT
        self.rmsnorm(T, 2, li)
        src = self.I["pT"][li, s, :, t0:t0 + TT].rearrange("(c p) t -> p c t", p=128)
        pbuf = self.ar[:, 20:22, :]
        pbuf_bs = [self.ar_b[20], self.ar_b[21]]
        fw.dma_pool(lambda: nc.gpsimd.dma_start(out=pbuf, in_=src), writes=pbuf_bs)
        for half in range(2):
            sg, sg_b = self.load_w(self.I["ple_gate"][li][:, half * 512:(half + 1) * 512])
            spj, spj_b = self.load_w(self.I["ple_proj"][li][:, half * 512:(half + 1) * 512], nk=2)
            for nq in range(4):
                n = half * 4 + nq
                bg, bg_b = self.proj_fm(sg, sg_b, nq * 128, lambda k: self.hn[:, k, :], NCH, self.hn_b)
                bp, bp_b = self.proj_fm(spj, spj_b, nq * 128, lambda k: self.ar[:, 20 + k, :], 2, pbuf_bs)
                g, g_b = self.t32.get()
                self.actf(g[:, 0:TT], bg[:], AF.Sigmoid, [bg_b], [g_b])
                self.vtt(g[:, 0:TT], g[:, 0:TT], bp[:], ALU.mult, [g_b, bp_b], [g_b])
                self.vtt(self.res[:, n, t0:t0 + TT], g[:, 0:TT], self.res[:, n, t0:t0 + TT], ALU.add,
                         [g_b, self.res_b[n][T]], [self.res_b[n][T]])

    def even_prep(self, j):
        nc, fw = self.nc, self.fw
        if j == 0:
            fw.op(fw.dve, lambda: nc.vector.memset(self.lb[:, 0, :], 0.0), writes=[self.lb_b])
        else:
            self.vcopy(self.lb[:, 0, :], self.lb1[:], [self.small_b], [self.lb_b])
        src = self.I["hgn"][0:1, j * 512:(j + 1) * 512].partition_broadcast(128)
        fw.dma_sp(lambda: nc.sync.dma_start(out=self.hgn[:], in_=src), writes=[self.hgn_b])
        self.vts(self.lb[:, 1, :], self.lb[:, 0, :], -1.0, 1.0, ALU.mult, ALU.add, [self.lb_b], [self.lb_b])
        fw.op(fw.dve, lambda: nc.vector.memset(self.S32[:], 0.0), writes=[self.S32_b])
        fw.op(fw.dve, lambda: nc.vector.memset(self.Sbf[:], 0.0), writes=[self.Sbf_b])

    def mixer_even(self, s, j, T):
        nc, fw = self.nc, self.fw
        t0 = T * TT
        w = self.I["ab_w_in"][j]
        hnf = lambda k: self.hn[:, k, :]
        slot, slot_b = self.load_w(w[:, 0:512])
        for m in range(4):
            bank, bank_b = self.proj_fm(slot, slot_b, m * 128, hnf, NCH, self.hn_b)
            self.actf(self.ar[:, m, :], bank[:], AF.Copy, [bank_b], [self.ar_b[m]], scale=0.125)
        slot, slot_b = self.load_w(w[:, 512:1024])
        for m in range(4):
            bank, bank_b = self.proj_fm(slot, slot_b, m * 128, hnf, NCH, self.hn_b)
            self.actf(self.kbuf[:, m, t0:t0 + TT], bank[:], AF.Copy, [bank_b], [self.kb_b[m][T]])
        def tok_block(col0, evac):
            slot, slot_b = self.load_w(w[:, col0:col0 + 512])
            for sub in range(4):
                bank, bank_b = self.banks.get()
                for k in range(NCH):
                    self.mm(bank[:], self.hn[:, k, sub * 128:(sub + 1) * 128], slot[:, k, :], k == 0, k == NCH - 1,
                            [slot_b, self.hn_b[k]], [bank_b])
                evac(sub, bank, bank_b)
        tok_block(1024, lambda sub, bank, bank_b: self.actf(self.vbuf[:, T * 4 + sub, :], bank[:], AF.Copy, [bank_b], [self.vb_b[T * 4 + sub]]))
        tok_block(1536, lambda sub, bank, bank_b: self.actf(self.ar[:, 8 + sub, :], bank[:], AF.Silu, [bank_b], [self.ar_b[8 + sub]]))
        tok_block(2048, lambda sub, bank, bank_b: self.actf(self.stage[:, sub, :], bank[:], AF.Sigmoid, [bank_b], [self.stage_b[sub]]))
        tok_block(2560, lambda sub, bank, bank_b: self.actf(self.ar[:, 12 + sub, :], bank[:], AF.Copy, [bank_b], [self.ar_b[12 + sub]]))
        tok_block(3072, lambda sub, bank, bank_b: self.actf(self.ar[:, 16 + sub, :], bank[:], AF.Silu, [bank_b], [self.ar_b[16 + sub]]))
        self.preconvert(2 * j)
        if T == NTILE - 1 and (2 * j + 1) in self.layers:
            self.preconvert(2 * j + 1)
        if "sb" in self.st:
            for pair in ((0, 1), (2, 3), (4, 5), (6, 7)):
                gens = [self.sb_chain(T, h) for h in pair]
                while gens:
                    for g in list(gens):
                        try:
                            next(g)
                        except StopIteration:
                            gens.remove(g)
        if "hg" in self.st and OPT_HGGEN:
            self.hgrn_tile_gen(j, T)
        elif "hg" in self.st and OPT_HGPIPE:
            fr = {0: self.hgrn_front(j, T, 0), 1: self.hgrn_front(j, T, 1)}
            outs = {}
            outs[0] = self.hgrn_mid(fr.pop(0))
            fr[2] = self.hgrn_front(j, T, 2)
            outs[1] = self.hgrn_mid(fr.pop(1))
            outs.pop(0)()
            fr[3] = self.hgrn_front(j, T, 3)
            outs[2] = self.hgrn_mid(fr.pop(2))
            outs.pop(1)()
            outs[3] = self.hgrn_mid(fr.pop(3))
            outs.pop(2)()
            outs.pop(3)()
        elif "hg" in self.st:
            prev_out = None
            for sub in range(4):
                out = self.hgrn_sub(j, T, sub)
                if prev_out is not None:
                    prev_out()
                prev_out = out
            if prev_out is not None:
                prev_out()

    def sb_chain(self, T, h):
        nc, fw = self.nc, self.fw
        hp = (h % 2) * 64
        pr = 64 - hp
        hc = h // 2
        qT = self.ar[hp:hp + 64, hc, :]
        q_b = self.ar_b[hc]
        negtri = self.C("negtri")
        sbmask = self.C("sbmask")
        onescol = self.C("ones", cols=1)
        negrow = self.negrow[pr:pr + 1, :]
        kpv, (pvb, pvb_b) = self.banks.reserve()
        Rf = self.ar[pr:pr + 1, 4:6, :].rearrange("p a b -> p (a b)").bitcast(F32)
        Rf_b = self.sbR_b[pr]
        rts = [(self.ar[pr:pr + 1, 6, :], self.sbrt_b[pr][0]), (self.ar[pr:pr + 1, 7, :], self.sbrt_b[pr][1]),
               (self.ar[pr:pr + 1, 20, :], self.sbrt_b[pr][2])]
        fw.op(fw.dve, lambda: nc.vector.memset(Rf, 0.0), writes=[Rf_b])
        blocks = [(4 * T + kl, kl * 128, True) for kl in (3, 2, 1, 0)] + [(kb, 0, False) for kb in range(4 * T - 1, -1, -1)]
        nb = len(blocks)

        def stage_a(bi):
            kb, c0, diag = blocks[bi]
            kT = self.kbuf[hp:hp + 64, hc, kb * 128:(kb + 1) * 128]
            k_b = self.kb_b[hc][kb // 4]
            kx, (X, X_b) = self.banks.reserve()
            self.mm(X[:, c0:TT], kT, qT[:, c0:TT], True, True, [k_b, q_b], [X_b])
            if OPT_LOCK:
                yield
            ke, (e, e_b) = self.t32.reserve()
            self.actf(e[:, c0:TT], X[:, c0:TT], AF.Exp, [X_b], [e_b])
            kl, (lp, lp_b) = self.t16.reserve()
            self.actf(lp[:, c0:TT], e[:, c0:TT], AF.Ln, [e_b], [lp_b], bias=1.0)
            self.t32.release(ke)
            if diag:
                self.vtt(lp[:, c0:c0 + 128], lp[:, c0:c0 + 128], sbmask, ALU.mult, [lp_b, self.c_b], [lp_b])
            rnew = None
            if bi < nb - 1 and OPT_SBEARLY:
                self.mm(pvb[pr:pr + 1, c0:TT], onescol, lp[:, c0:TT], True, True, [lp_b, self.c_b], [pvb_b], skip=True)
                self.vtt(Rf[:, c0:TT], pvb[pr:pr + 1, c0:TT], Rf[:, c0:TT], ALU.add, [pvb_b, Rf_b], [Rf_b])
                rt, rt_b = rts[bi % 3]
                self.vcopy(rt[:, c0:TT], Rf[:, c0:TT], [Rf_b], [rt_b])
                rnew = (rt, rt_b)
            if False:
                yield
            return kx, X, X_b, kl, lp, lp_b, kT, k_b, rnew

        pend = {0: (yield from stage_a(0))}
        yield
        rprev = None
        pvq = []

        def flush_pv():
            while pvq:
                (bi_, kb_, c0_, wt_, wt_b_, kw_) = pvq.pop(0)
                self.mm(pvb[hp:hp + 64, c0_:TT], self.vbuf[:, kb_, h * 64:(h + 1) * 64], wt_[:, c0_:TT], bi_ == 0, bi_ == nb - 1,
                        [self.vb_b[kb_], wt_b_], [pvb_b], skip=True)
                self.t16.release(kw_)

        for bi in range(nb):
            if bi + 1 < nb:
                pend[bi + 1] = yield from stage_a(bi + 1)
                yield
            flush_pv()
            if OPT_LOCK:
                yield
            kb, c0, diag = blocks[bi]
            kx, X, X_b, kl, lp, lp_b, kT, k_b, rnew = pend.pop(bi)
            has_r = rprev is not None
            cR = c0 + 128 if diag else 0
            use_r = has_r and cR < TT
            self.mm(X[:, c0:TT], kT, qT[:, c0:TT], True, False, [k_b, q_b], [X_b])
            if OPT_LOCK:
                yield
            self.mm(X[:, c0:TT], negtri, lp[:, c0:TT], False, not use_r, [lp_b, self.c_b], [X_b])
            if OPT_LOCK:
                yield
            if use_r:
                rt, rt_b = rprev
                self.mm(X[:, cR:TT], negrow, rt[:, cR:TT], False, True, [rt_b, self.c_b], [X_b])
            if OPT_LOCK:
                yield
            kw, (wt, wt_b) = self.t16.reserve()
            self.actf(wt[:, c0:TT], X[:, c0:TT], AF.Exp, [X_b], [wt_b])
            self.banks.release(kx)
            if diag:
                self.vtt(wt[:, c0:c0 + 128], wt[:, c0:c0 + 128], sbmask, ALU.mult, [wt_b, self.c_b], [wt_b])
            pvq.append((bi, kb, c0, wt, wt_b, kw))
            if bi < nb - 1 and not OPT_SBEARLY:
                self.mm(pvb[pr:pr + 1, c0:TT], onescol, lp[:, c0:TT], True, True, [lp_b, self.c_b], [pvb_b], skip=True)
                if OPT_LOCK:
                    yield
                self.vtt(Rf[:, c0:TT], pvb[pr:pr + 1, c0:TT], Rf[:, c0:TT], ALU.add, [pvb_b, Rf_b], [Rf_b])
                rt, rt_b = rts[bi % 3]
                self.vcopy(rt[:, c0:TT], Rf[:, c0:TT], [Rf_b], [rt_b])
                rnew = (rt, rt_b)
            rprev = rnew
            self.t16.release(kl)
            yield
        flush_pv()
        self.actf(self.hn[hp:hp + 64, hc, :], pvb[hp:hp + 64, :], AF.Copy, [pvb_b], [self.hn_b[hc]])
        self.banks.release(kpv)

    def hgrn_sub(self, j, T, sub):
        nc, fw = self.nc, self.fw
        qs, qs_b = self.ar[:, 8 + sub, :], self.ar_b[8 + sub]
        ib, ib_b = self.ar[:, 12 + sub, :], self.ar_b[12 + sub]
        gs, gs_b = self.ar[:, 16 + sub, :], self.ar_b[16 + sub]
        sg, sg_b = self.stage[:, sub, :], self.stage_b[sub]
        ident = self.C("ident")
        fA, fA_b = self.t32.get()
        f = fA[:, 0:512]
        self.vtt(f, sg, self.lb[:, 1, :], ALU.mult, [sg_b, self.lb_b], [fA_b])
        self.vtt(f, f, self.lb[:, 0, :], ALU.add, [fA_b, self.lb_b], [fA_b])
        lfB, lfB_b = self.t32.get()
        lf = lfB[:, 0:512]
        self.actf(lf, f, AF.Ln, [fA_b], [lfB_b])
        self.vts(f, f, -1.0, 1.0, ALU.mult, ALU.add, [fA_b], [fA_b])
        bd1, bd1_b = self.banks.get()
        bb, bb_b = self.banks.get()
        bd4, bd4_b = self.banks.get()
        self.mm(bd1[:], self.C("trid1", bf=False), lf, True, True, [lfB_b, self.c_b], [bd1_b])
        self.mm(bb[:], self.C("tri2", bf=False), lf, True, True, [lfB_b, self.c_b], [bb_b])
        self.mm(bd4[:], self.C("trisuf", bf=False), lf, True, True, [lfB_b, self.c_b], [bd4_b])
        bz, bz_b = self.banks.get()
        for h in range(4):
            self.mm(bz[:, 2 * h:2 * h + 2], lf[:, h * 128:(h + 1) * 128], self.C("chunkones", bf=False, cols=2), True, True,
                    [lfB_b, self.c_b], [bz_b])
        el, el_b = self.sm_pool.get()
        self.actf(el, bz[:, 0:8], AF.Exp, [bz_b], [el_b])
        E, E_b = self.t32.get()
        q1, q1_b = self.t16.get()
        self.actf(E[:, 0:512], bd1[:], AF.Exp, [bd1_b], [E_b])
        self.vtt(q1[:], qs, E[:, 0:512], ALU.mult, [qs_b, E_b], [q1_b])
        E2, E2_b = self.t32.get()
        k1, k1_b = self.t16.get()
        self.actf(E2[:, 0:512], bd1[:], AF.Exp, [bd1_b], [E2_b], scale=-1.0)
        self.vtt(k1[:], f, E2[:, 0:512], ALU.mult, [fA_b, E2_b], [k1_b])
        E3, E3_b = self.t32.get()
        q3, q3_b = self.t16.get()
        self.actf(E3[:, 0:512], bb[:], AF.Exp, [bb_b], [E3_b])
        self.vtt(q3[:], qs, E3[:, 0:512], ALU.mult, [qs_b, E3_b], [q3_b])
        E4, E4_b = self.t32.get()
        k4, k4_b = self.t16.get()
        self.actf(E4[:, 0:512], bd4[:], AF.Exp, [bd4_b], [E4_b])
        self.vtt(k4[:], f, E4[:, 0:512], ALU.mult, [fA_b, E4_b], [k4_b])
        import os
        HG = float(os.environ.get("K_HG", "9"))
        if HG < 2:
            return
        pA, pA_b = self.pbt[:, 0:512], self.pbt_b[0]
        pB, pB_b = self.pbt[:, 512:1024], self.pbt_b[0]
        for h in range(4):
            self.tr(pA[:, h * 128:(h + 1) * 128], q1[:, h * 128:(h + 1) * 128], ident, [q1_b, self.c_b], [pA_b])
        for h in range(4):
            self.tr(pB[:, h * 128:(h + 1) * 128], k1[:, h * 128:(h + 1) * 128], ident, [k1_b, self.c_b], [pB_b])
        q1T, q1T_b = self.t16.get()
        k1T, k1T_b = self.t16.get()
        if HG < 2.1:
            return
        self.vcopy(q1T[:], pA, [pA_b], [q1T_b])
        if HG < 2.2:
            return
        self.vcopy(k1T[:], pB, [pB_b], [k1T_b])
        if HG < 2.3:
            return
        for h in range(4):
            self.tr(pA[:, h * 128:(h + 1) * 128], q3[:, h * 128:(h + 1) * 128], ident, [q3_b, self.c_b], [pA_b])
        q3T, q3T_b = self.t16.get()
        self.vcopy(q3T[:], pA, [pA_b], [q3T_b])
        if HG < 3:
            return
        bs, bs_b = self.banks.get()
        for c in range(2):
            for h in range(4):
                self.mm(bs[64 * c:64 * c + 64, h * 64:(h + 1) * 64],
                        k1T[:, h * 128 + 64 * c:h * 128 + 64 * c + 64], q1T[:, h * 128 + 64 * c:h * 128 + 64 * c + 64],
                        True, True, [k1T_b, q1T_b], [bs_b])
        scm, scm_b = self.t16.get()
        self.vtt(scm[:, 0:256], bs[:, 0:256], self.C("maskS"), ALU.mult, [bs_b, self.c_b], [scm_b])
        if HG < 4:
            return
        kbo, (bo, bo_b) = self.banks.reserve()
        for c in range(2):
            pc = 64 * c
            for h in range(4):
                hs = slice(h * 128, (h + 1) * 128)
                self.mm(bo[pc:pc + 64, hs], q3T[:, h * 128 + pc:h * 128 + pc + 64], self.Sbf[:, hs], True, False,
                        [q3T_b, self.Sbf_b], [bo_b])
                self.mm(bo[pc:pc + 64, hs], scm[pc:pc + 64, h * 64:(h + 1) * 64], ib[pc:pc + 64, hs], False, True,
                        [scm_b, ib_b], [bo_b])
            bu, bu_b = self.banks.get()
            for h in range(4):
                hs = slice(h * 128, (h + 1) * 128)
                self.mm(bu[:, hs], k4[pc:pc + 64, hs], ib[pc:pc + 64, hs], True, True, [k4_b, ib_b], [bu_b])
            for h in range(4):
                hs = slice(h * 128, (h + 1) * 128)
                self.vstt(self.S32[:, hs], self.S32[:, hs], el[:, 2 * h + c:2 * h + c + 1], bu[:, hs], ALU.mult, ALU.add,
                          [self.S32_b, el_b, bu_b], [self.S32_b])
            self.actf(self.Sbf[:], self.S32[:], AF.Copy, [self.S32_b], [self.Sbf_b])
        if HG < 5:
            self.banks.release(kbo)
            return
        return lambda: self.hgrn_out(j, sub, kbo, bo, bo_b, gs, gs_b, pB, pB_b, ident)

    def hgrn_front(self, j, T, sub):
        qs, qs_b = self.ar[:, 8 + sub, :], self.ar_b[8 + sub]
        sg, sg_b = self.stage[:, sub, :], self.stage_b[sub]
        ident = self.C("ident")
        fA, fA_b = self.t32.get()
        f = fA[:, 0:512]
        self.vtt(f, sg, self.lb[:, 1, :], ALU.mult, [sg_b, self.lb_b], [fA_b])
        self.vtt(f, f, self.lb[:, 0, :], ALU.add, [fA_b, self.lb_b], [fA_b])
        lfB, lfB_b = self.t32.get()
        lf = lfB[:, 0:512]
        self.actf(lf, f, AF.Ln, [fA_b], [lfB_b])
        self.vts(f, f, -1.0, 1.0, ALU.mult, ALU.add, [fA_b], [fA_b])
        bd1, bd1_b = self.banks.get()
        bb, bb_b = self.banks.get()
        bd4, bd4_b = self.banks.get()
        self.mm(bd1[:], self.C("trid1", bf=False), lf, True, True, [lfB_b, self.c_b], [bd1_b])
        self.mm(bb[:], self.C("tri2", bf=False), lf, True, True, [lfB_b, self.c_b], [bb_b])
        self.mm(bd4[:], self.C("trisuf", bf=False), lf, True, True, [lfB_b, self.c_b], [bd4_b])
        bz, bz_b = self.banks.get()
        for h in range(4):
            self.mm(bz[:, 2 * h:2 * h + 2], lf[:, h * 128:(h + 1) * 128], self.C("chunkones", bf=False, cols=2), True, True,
                    [lfB_b, self.c_b], [bz_b])
        el, el_b = self.sm_pool.get()
        self.actf(el, bz[:, 0:8], AF.Exp, [bz_b], [el_b])
        pA, pA_b = self.pbt[:, 0:512], self.pbt_b[0]
        pB, pB_b = self.pbt[:, 512:1024], self.pbt_b[0]
        E, E_b = self.t32.get()
        kq1, (q1, q1_b) = self.t16.reserve()
        self.actf(E[:, 0:512], bd1[:], AF.Exp, [bd1_b], [E_b])
        self.vtt(q1[:], qs, E[:, 0:512], ALU.mult, [qs_b, E_b], [q1_b])
        E2, E2_b = self.t32.get()
        kk1, (k1, k1_b) = self.t16.reserve()
        self.actf(E2[:, 0:512], bd1[:], AF.Exp, [bd1_b], [E2_b], scale=-1.0)
        self.vtt(k1[:], f, E2[:, 0:512], ALU.mult, [fA_b, E2_b], [k1_b])
        for h in range(4):
            self.tr(pA[:, h * 128:(h + 1) * 128], q1[:, h * 128:(h + 1) * 128], ident, [q1_b, self.c_b], [pA_b])
        for h in range(4):
            self.tr(pB[:, h * 128:(h + 1) * 128], k1[:, h * 128:(h + 1) * 128], ident, [k1_b, self.c_b], [pB_b])
        self.t16.release(kq1)
        self.t16.release(kk1)
        kq1T, (q1T, q1T_b) = self.t16.reserve()
        kk1T, (k1T, k1T_b) = self.t16.reserve()
        self.vcopy(q1T[:], pA, [pA_b], [q1T_b])
        self.vcopy(k1T[:], pB, [pB_b], [k1T_b])
        bs, bs_b = self.banks.get()
        for c in range(2):
            for h in range(4):
                self.mm(bs[64 * c:64 * c + 64, h * 64:(h + 1) * 64],
                        k1T[:, h * 128 + 64 * c:h * 128 + 64 * c + 64], q1T[:, h * 128 + 64 * c:h * 128 + 64 * c + 64],
                        True, True, [k1T_b, q1T_b], [bs_b])
        self.t16.release(kq1T)
        self.t16.release(kk1T)
        kscm, (scm, scm_b) = self.t16.reserve()
        self.vtt(scm[:, 0:256], bs[:, 0:256], self.C("maskS"), ALU.mult, [bs_b, self.c_b], [scm_b])
        E3, E3_b = self.t32.get()
        kq3, (q3, q3_b) = self.t16.reserve()
        self.actf(E3[:, 0:512], bb[:], AF.Exp, [bb_b], [E3_b])
        self.vtt(q3[:], qs, E3[:, 0:512], ALU.mult, [qs_b, E3_b], [q3_b])
        for h in range(4):
            self.tr(pA[:, h * 128:(h + 1) * 128], q3[:, h * 128:(h + 1) * 128], ident, [q3_b, self.c_b], [pA_b])
        self.t16.release(kq3)
        kq3T, (q3T, q3T_b) = self.t16.reserve()
        self.vcopy(q3T[:], pA, [pA_b], [q3T_b])
        E4, E4_b = self.t32.get()
        kk4, (k4, k4_b) = self.t16.reserve()
        self.actf(E4[:, 0:512], bd4[:], AF.Exp, [bd4_b], [E4_b])
        self.vtt(k4[:], f, E4[:, 0:512], ALU.mult, [fA_b, E4_b], [k4_b])
        return dict(j=j, sub=sub, el=el, el_b=el_b, scm=scm, scm_b=scm_b, kscm=kscm, q3T=q3T, q3T_b=q3T_b, kq3T=kq3T,
                    k4=k4, k4_b=k4_b, kk4=kk4)

    def hgrn_mid(self, st):
        sub = st["sub"]
        ib, ib_b = self.ar[:, 12 + sub, :], self.ar_b[12 + sub]
        el, el_b, scm, scm_b, q3T, q3T_b, k4, k4_b = st["el"], st["el_b"], st["scm"], st["scm_b"], st["q3T"], st["q3T_b"], st["k4"], st["k4_b"]
        kbo, (bo, bo_b) = self.banks.reserve()
        for c in range(2):
            pc = 64 * c
            for h in range(4):
                hs = slice(h * 128, (h + 1) * 128)
                self.mm(bo[pc:pc + 64, hs], q3T[:, h * 128 + pc:h * 128 + pc + 64], self.Sbf[:, hs], True, False,
                        [q3T_b, self.Sbf_b], [bo_b])
                self.mm(bo[pc:pc + 64, hs], scm[pc:pc + 64, h * 64:(h + 1) * 64], ib[pc:pc + 64, hs], False, True,
                        [scm_b, ib_b], [bo_b])
            bu, bu_b = self.banks.get()
            for h in range(4):
                hs = slice(h * 128, (h + 1) * 128)
                self.mm(bu[:, hs], k4[pc:pc + 64, hs], ib[pc:pc + 64, hs], True, True, [k4_b, ib_b], [bu_b])
            for h in range(4):
                hs = slice(h * 128, (h + 1) * 128)
                self.vstt(self.S32[:, hs], self.S32[:, hs], el[:, 2 * h + c:2 * h + c + 1], bu[:, hs], ALU.mult, ALU.add,
                          [self.S32_b, el_b, bu_b], [self.S32_b])
            self.actf(self.Sbf[:], self.S32[:], AF.Copy, [self.S32_b], [self.Sbf_b])
        self.t16.release(st["kscm"])
        self.t16.release(st["kq3T"])
        self.t16.release(st["kk4"])
        gs, gs_b = self.ar[:, 16 + sub, :], self.ar_b[16 + sub]
        pB, pB_b = self.pbt[:, 512:1024], self.pbt_b[0]
        ident = self.C("ident")
        j = st["j"]
        return lambda: self.hgrn_out(j, sub, kbo, bo, bo_b, gs, gs_b, pB, pB_b, ident)

    def g_front(self, j, T, sub, st):
        qs, qs_b = self.ar[:, 8 + sub, :], self.ar_b[8 + sub]
        sg, sg_b = self.stage[:, sub, :], self.stage_b[sub]
        ident = self.C("ident")
        pA, pB, p_b = self.pbt[:, 0:512], self.pbt[:, 512:1024], self.pbt_b[0]
        kfA, (fA, fA_b) = self.t32.reserve()
        f = fA[:, 0:512]
        self.vtt(f, sg, self.lb[:, 1, :], ALU.mult, [sg_b, self.lb_b], [fA_b])
        self.vtt(f, f, self.lb[:, 0, :], ALU.add, [fA_b, self.lb_b], [fA_b])
        klf, (lfB, lfB_b) = self.t32.reserve()
        lf = lfB[:, 0:512]
        self.actf(lf, f, AF.Ln, [fA_b], [lfB_b])
        self.vts(f, f, -1.0, 1.0, ALU.mult, ALU.add, [fA_b], [fA_b])
        yield
        k1_, (bd1, bd1_b) = self.banks.reserve()
        k2_, (bb, bb_b) = self.banks.reserve()
        k3_, (bd4, bd4_b) = self.banks.reserve()
        self.mm(bd1[:], self.C("trid1", bf=False), lf, True, True, [lfB_b, self.c_b], [bd1_b])
        self.mm(bb[:], self.C("tri2", bf=False), lf, True, True, [lfB_b, self.c_b], [bb_b])
        self.mm(bd4[:], self.C("trisuf", bf=False), lf, True, True, [lfB_b, self.c_b], [bd4_b])
        k4_, (bz, bz_b) = self.banks.reserve()
        for h in range(4):
            self.mm(bz[:, 2 * h:2 * h + 2], lf[:, h * 128:(h + 1) * 128], self.C("chunkones", bf=False, cols=2), True, True,
                    [lfB_b, self.c_b], [bz_b])
        self.t32.release(klf)
        yield
        el, el_b = self.sm_pool.get()
        self.actf(el, bz[:, 0:8], AF.Exp, [bz_b], [el_b])
        self.banks.release(k4_)
        kE, (E, E_b) = self.t32.reserve()
        kq1, (q1, q1_b) = self.t16.reserve()
        self.actf(E[:, 0:512], bd1[:], AF.Exp, [bd1_b], [E_b])
        self.vtt(q1[:], qs, E[:, 0:512], ALU.mult, [qs_b, E_b], [q1_b])
        kk1, (k1, k1_b) = self.t16.reserve()
        self.actf(E[:, 0:512], bd1[:], AF.Exp, [bd1_b, E_b], [E_b], scale=-1.0)
        self.vtt(k1[:], f, E[:, 0:512], ALU.mult, [fA_b, E_b], [k1_b])
        self.banks.release(k1_)
        yield
        for h in range(4):
            self.tr(pA[:, h * 128:(h + 1) * 128], q1[:, h * 128:(h + 1) * 128], ident, [q1_b, self.c_b], [p_b])
        for h in range(4):
            self.tr(pB[:, h * 128:(h + 1) * 128], k1[:, h * 128:(h + 1) * 128], ident, [k1_b, self.c_b], [p_b])
        self.t16.release(kq1)
        self.t16.release(kk1)
        kq1T, (q1T, q1T_b) = self.t16.reserve()
        kk1T, (k1T, k1T_b) = self.t16.reserve()
        self.vcopy(q1T[:], pA, [p_b], [q1T_b])
        self.vcopy(k1T[:], pB, [p_b], [k1T_b])
        yield
        kbs, (bs, bs_b) = self.banks.reserve()
        for c in range(2):
            for h in range(4):
                self.mm(bs[64 * c:64 * c + 64, h * 64:(h + 1) * 64],
                        k1T[:, h * 128 + 64 * c:h * 128 + 64 * c + 64], q1T[:, h * 128 + 64 * c:h * 128 + 64 * c + 64],
                        True, True, [k1T_b, q1T_b], [bs_b])
        self.t16.release(kq1T)
        self.t16.release(kk1T)
        yield
        kscm, (scm, scm_b) = self.t16.reserve()
        self.vtt(scm[:, 0:256], bs[:, 0:256], self.C("maskS"), ALU.mult, [bs_b, self.c_b], [scm_b])
        self.banks.release(kbs)
        kq3, (q3, q3_b) = self.t16.reserve()
        self.actf(E[:, 0:512], bb[:], AF.Exp, [bb_b, E_b], [E_b])
        self.vtt(q3[:], qs, E[:, 0:512], ALU.mult, [qs_b, E_b], [q3_b])
        self.banks.release(k2_)
        yield
        for h in range(4):
            self.tr(pA[:, h * 128:(h + 1) * 128], q3[:, h * 128:(h + 1) * 128], ident, [q3_b, self.c_b], [p_b])
        self.t16.release(kq3)
        kq3T, (q3T, q3T_b) = self.t16.reserve()
        self.vcopy(q3T[:], pA, [p_b], [q3T_b])
        yield
        kk4, (k4, k4_b) = self.t16.reserve()
        self.actf(E[:, 0:512], bd4[:], AF.Exp, [bd4_b, E_b], [E_b])
        self.vtt(k4[:], f, E[:, 0:512], ALU.mult, [fA_b, E_b], [k4_b])
        self.banks.release(k3_)
        self.t32.release(kE)
        self.t32.release(kfA)
        st.update(dict(j=j, sub=sub, el=el, el_b=el_b, scm=scm, scm_b=scm_b, kscm=kscm, q3T=q3T, q3T_b=q3T_b, kq3T=kq3T,
                       k4=k4, k4_b=k4_b, kk4=kk4, front_done=True))

    def g_mid(self, st, prev):
        while prev is not None and not prev.get("mid_done"):
            yield
        sub = st["sub"]
        ib, ib_b = self.ar[:, 12 + sub, :], self.ar_b[12 + sub]
        el, el_b, scm, scm_b, q3T, q3T_b, k4, k4_b = st["el"], st["el_b"], st["scm"], st["scm_b"], st["q3T"], st["q3T_b"], st["k4"], st["k4_b"]
        kbo, (bo, bo_b) = self.banks.reserve()
        for c in range(2):
            pc = 64 * c
            for h in range(4):
                hs = slice(h * 128, (h + 1) * 128)
                self.mm(bo[pc:pc + 64, hs], q3T[:, h * 128 + pc:h * 128 + pc + 64], self.Sbf[:, hs], True, False,
                        [q3T_b, self.Sbf_b], [bo_b])
                self.mm(bo[pc:pc + 64, hs], scm[pc:pc + 64, h * 64:(h + 1) * 64], ib[pc:pc + 64, hs], False, True,
                        [scm_b, ib_b], [bo_b])
            kbu, (bu, bu_b) = self.banks.reserve()
            for h in range(4):
                hs = slice(h * 128, (h + 1) * 128)
                self.mm(bu[:, hs], k4[pc:pc + 64, hs], ib[pc:pc + 64, hs], True, True, [k4_b, ib_b], [bu_b])
            yield
            for h in range(4):
                hs = slice(h * 128, (h + 1) * 128)
                self.vstt(self.S32[:, hs], self.S32[:, hs], el[:, 2 * h + c:2 * h + c + 1], bu[:, hs], ALU.mult, ALU.add,
                          [self.S32_b, el_b, bu_b], [self.S32_b])
            self.banks.release(kbu)
            self.actf(self.Sbf[:], self.S32[:], AF.Copy, [self.S32_b], [self.Sbf_b])
            yield
        self.t16.release(st["kscm"])
        self.t16.release(st["kq3T"])
        self.t16.release(st["kk4"])
        st["kbo"], st["bo"], st["bo_b"] = kbo, bo, bo_b
        st["mid_done"] = True

    def g_out(self, st):
        nc = self.nc
        j, sub = st["j"], st["sub"]
        kbo, bo, bo_b = st["kbo"], st["bo"], st["bo_b"]
        gs, gs_b = self.ar[:, 16 + sub, :], self.ar_b[16 + sub]
        pB, p_b = self.pbt[:, 512:1024], self.pbt_b[0]
        ident = self.C("ident")
        ko1, (osb, osb_b) = self.t32.reserve()
        ko2, (osq, osq_b) = self.t32.reserve()
        o = osb[:, 0:512]
        self.actf(o, bo[:], AF.Copy, [bo_b], [osb_b])
        self.banks.release(kbo)
        yield
        self.vtt(osq[:, 0:512], o, o, ALU.mult, [osb_b], [osq_b])
        ss, ss_b = self.sm_pool.get()
        self.fw.op(self.fw.dve, lambda: nc.vector.tensor_reduce(out=ss[:, 0:4], in_=osq[:, 0:512].rearrange("p (h v) -> p h v", h=4), axis=AX.X, op=ALU.add),
                   [osq_b], [ss_b])
        self.actf(ss[:, 0:4], ss[:, 0:4], AF.Ln, [ss_b], [ss_b], bias=EPS, scale=1.0 / 128)
        self.actf(ss[:, 0:4], ss[:, 0:4], AF.Exp, [ss_b], [ss_b], scale=-0.5)
        yield
        gg = osq[:, 0:512]
        self.vtt(gg, gs, self.hgn[:], ALU.mult, [gs_b, self.hgn_b, osq_b], [osq_b])
        kyb, (yb, yb_b) = self.t16.reserve()
        for h in range(4):
            hs = slice(h * 128, (h + 1) * 128)
            self.vstt(yb[:, hs], o[:, hs], ss[:, h:h + 1], gg[:, hs], ALU.mult, ALU.mult, [osb_b, ss_b, osq_b], [yb_b])
        yield
        for h in range(4):
            self.tr(pB[:, h * 128:(h + 1) * 128], yb[:, h * 128:(h + 1) * 128], ident, [yb_b, self.c_b], [p_b])
        self.vcopy(self.hn[:, 4:8, sub * 128:(sub + 1) * 128], pB.rearrange("p (h t) -> p h t", h=4), [p_b], [self.hn_b[4 + h] for h in range(4)])
        self.t16.release(kyb)
        self.t32.release(ko1)
        self.t32.release(ko2)

    def hgrn_tile_gen(self, j, T):
        sts = [dict() for _ in range(4)]
        pending_f = list(range(4))
        pending_m = list(range(4))
        pending_o = list(range(4))
        act_f = act_m = act_o = None
        while pending_o or act_o is not None:
            if act_f is None and pending_f:
                s_ = pending_f.pop(0)
                act_f = (s_, self.g_front(j, T, s_, sts[s_]))
            if act_m is None and pending_m and sts[pending_m[0]].get("front_done"):
                s_ = pending_m.pop(0)
                act_m = (s_, self.g_mid(sts[s_], sts[s_ - 1] if s_ > 0 else None))
            if act_o is None and pending_o and sts[pending_o[0]].get("mid_done"):
                s_ = pending_o.pop(0)
                act_o = (s_, self.g_out(sts[s_]))
            for name in ("f", "m", "o"):
                cur = {"f": act_f, "m": act_m, "o": act_o}[name]
                if cur is None:
                    continue
                try:
                    next(cur[1])
                except StopIteration:
                    if name == "f":
                        act_f = None
                    elif name == "m":
                        act_m = None
                    else:
                        act_o = None

    def hgrn_out(self, j, sub, kbo, bo, bo_b, gs, gs_b, pB, pB_b, ident):
        nc = self.nc
        osb, osb_b = self.t32.get()
        o = osb[:, 0:512]
        self.actf(o, bo[:], AF.Copy, [bo_b], [osb_b])
        self.banks.release(kbo)
        osq, osq_b = self.t32.get()
        self.vtt(osq[:, 0:512], o, o, ALU.mult, [osb_b], [osq_b])
        ss, ss_b = self.sm_pool.get()
        self.fw.op(self.fw.dve, lambda: nc.vector.tensor_reduce(out=ss[:, 0:4], in_=osq[:, 0:512].rearrange("p (h v) -> p h v", h=4), axis=AX.X, op=ALU.add),
                   [osq_b], [ss_b])
        self.actf(ss[:, 0:4], ss[:, 0:4], AF.Ln, [ss_b], [ss_b], bias=EPS, scale=1.0 / 128)
        self.actf(ss[:, 0:4], ss[:, 0:4], AF.Exp, [ss_b], [ss_b], scale=-0.5)
        gg = osq[:, 0:512]
        self.vtt(gg, gs, self.hgn[:], ALU.mult, [gs_b, self.hgn_b, osq_b], [osq_b])
        yb, yb_b = self.t16.get()
        for h in range(4):
            hs = slice(h * 128, (h + 1) * 128)
            self.vstt(yb[:, hs], o[:, hs], ss[:, h:h + 1], gg[:, hs], ALU.mult, ALU.mult, [osb_b, ss_b, osq_b], [yb_b])
        for h in range(4):
            self.tr(pB[:, h * 128:(h + 1) * 128], yb[:, h * 128:(h + 1) * 128], ident, [yb_b, self.c_b], [pB_b])
        self.vcopy(self.hn[:, 4:8, sub * 128:(sub + 1) * 128], pB.rearrange("p (h t) -> p h t", h=4), [pB_b], [self.hn_b[4 + h] for h in range(4)])

    def qknorm(self, bank, bank_b, gain_ap, out_ap, out_bufs):
        sq, sq_b = self.t16.get()
        self.actf(sq[:], bank[:], AF.Square, [bank_b], [sq_b])
        b2, b2_b = self.banks.get()
        self.mm(b2[:], self.C("bones"), sq[:], True, True, [sq_b, self.c_b], [b2_b])
        r, r_b = self.t32.get()
        rr = r[:, 0:TT]
        self.actf(rr, b2[:], AF.Ln, [b2_b], [r_b], bias=EPS, scale=1.0 / 64)
        self.actf(rr, rr, AF.Exp, [r_b], [r_b], scale=-0.5)
        self.vstt(out_ap, bank[:], gain_ap, rr, ALU.mult, ALU.mult, [bank_b, r_b, self.small_b], out_bufs)

    def mixer_odd(self, s, j, T):
        nc, fw = self.nc, self.fw
        t0 = T * TT
        w = self.I["c_w_in"][j]
        hnf = lambda k: self.hn[:, k, :]
        gq = self.qkn[:, j, 0:1]
        gk = self.qkn[:, j, 1:2]
        for half in range(2):
            slot, slot_b = self.load_w(w[:, half * 512:(half + 1) * 512])
            for m in range(4):
                bank, bank_b = self.proj_fm(slot, slot_b, m * 128, hnf, NCH, self.hn_b)
                cidx = half * 4 + m
                self.qknorm(bank, bank_b, gq, self.ar[:, cidx, :], [self.ar_b[cidx]])
        slot, slot_b = self.load_w(w[:, 1024:1536])
        for m in range(2):
            bank, bank_b = self.proj_fm(slot, slot_b, m * 128, hnf, NCH, self.hn_b)
            self.qknorm(bank, bank_b, gk, self.kbuf[:, m, t0:t0 + TT], [self.kb_b[m][T]])
            bank, bank_b = self.banks.get()
            for k in range(NCH):
                self.mm(bank[0:64, :], slot[:, k, m * 128 + 64:m * 128 + 128], self.hn[:, k, :], k == 0, k == NCH - 1, [slot_b, self.hn_b[k]], [bank_b])
            for k in range(NCH):
                self.mm(bank[64:128, :], slot[:, k, m * 128:m * 128 + 64], self.hn[:, k, :], k == 0, k == NCH - 1, [slot_b, self.hn_b[k]], [bank_b])
            self.qknorm(bank, bank_b, gk, self.kbuf[:, 2 + m, t0:t0 + TT], [self.kb_b[2 + m][T]])
        for sub in range(4):
            bank, bank_b = self.banks.get()
            for k in range(NCH):
                self.mm(bank[:, 0:256], self.hn[:, k, sub * 128:(sub + 1) * 128], slot[:, k, 256:512], k == 0, k == NCH - 1,
                        [slot_b, self.hn_b[k]], [bank_b])
            self.actf(self.vbuf[:, T * 4 + sub, 0:256], bank[:, 0:256], AF.Copy, [bank_b], [self.vb_b[T * 4 + sub]])
        onesbf = self.C("ones", cols=64)
        import os
        OD = float(os.environ.get("K_ODD", "9"))
        if OD < 2:
            return
        self.preconvert(2 * j + 1)
        HPERM = [0, 2, 1, 3]
        units = [(qb, g) for qb in range(4) for g in range(4)]

        def stage_a(qb, g):
            B = 4 * T + qb
            kbs = [B - 1, B] if B > 0 else [B]
            tmps = [self.t32.get() for _ in kbs]
            for par in range(2):
                bs, bs_b = self.banks.get()
                for ki, kb in enumerate(kbs):
                    for ii in range(2):
                        i = par * 2 + ii
                        h = 4 * g + HPERM[i]
                        hp = (h % 2) * 64
                        var = 0 if (g % 2) == (h % 2) else 2
                        kc = var + g // 2
                        c0_ = ki * 256 + ii * 128
                        self.mm(bs[:, c0_:c0_ + 128], self.kbuf[hp:hp + 64, kc, kb * 128:(kb + 1) * 128],
                                self.ar[hp:hp + 64, h // 2, qb * 128:(qb + 1) * 128], True, True,
                                [self.kb_b[kc][kb // 4], self.ar_b[h // 2]], [bs_b])
                for ki, kb in enumerate(kbs):
                    ksel = 0 if kb == B - 1 else 1
                    tmp, tmp_b = tmps[ki]
                    self.vtt(tmp[:, par * 256:(par + 1) * 256], bs[:, ki * 256:(ki + 1) * 256],
                             self.bias[:, ksel, 4 * g + 2 * par:4 * g + 2 * par + 2, :].rearrange("p a b -> p (a b)"), ALU.add,
                             [bs_b, self.bias_b], [tmp_b])
            es = []
            for ki, kb in enumerate(kbs):
                tmp, tmp_b = tmps[ki]
                ke, (e, e_b) = self.t16.reserve()
                self.actf(e[:], tmp[:, 0:512], AF.Exp, [tmp_b], [e_b])
                es.append((kb, ke, e, e_b))
            return es

        nd_sets = [(self.banks.reserve(), self.banks.reserve()) for _ in range(2)]

        def stage_b(qb, g, es):
            hf = g // 2
            (kN, (nbk, nbk_b)), (kD, (dbk, dbk_b)) = nd_sets[(qb * 2 + hf) % 2]
            for i in range(4):
                h = 4 * g + HPERM[i]
                hp = (h % 2) * 64
                c = h // 2
                cs = slice((c % 4) * 128, (c % 4) * 128 + 128)
                for n_, (kb, ke, e, e_b) in enumerate(es):
                    self.mm(nbk[hp:hp + 64, cs], self.vbuf[:, kb, g * 64:(g + 1) * 64], e[:, i * 128:(i + 1) * 128],
                            n_ == 0, n_ == len(es) - 1, [self.vb_b[kb], e_b], [nbk_b])
                for n_, (kb, ke, e, e_b) in enumerate(es):
                    self.mm(dbk[hp:hp + 64, cs], onesbf, e[:, i * 128:(i + 1) * 128],
                            n_ == 0, n_ == len(es) - 1, [self.c_b, e_b], [dbk_b])
            for (kb, ke, e, e_b) in es:
                self.t16.release(ke)
            if g % 2 == 1:
                r, r_b = self.t32.get()
                for cq in range(4):
                    c = hf * 4 + cq
                    self.actf(r[:, cq * 128:(cq + 1) * 128], dbk[:, cq * 128:(cq + 1) * 128], AF.Ln, [dbk_b, self.small_b], [r_b],
                              bias=self.snk[:, j, c:c + 1])
                self.actf(r[:, 0:512], r[:, 0:512], AF.Exp, [r_b], [r_b], scale=-1.0)
                self.vtt(self.hn[:, hf * 4:(hf + 1) * 4, qb * 128:(qb + 1) * 128],
                         nbk[:].rearrange("p (c t) -> p c t", c=4), r[:, 0:512].rearrange("p (c t) -> p c t", c=4), ALU.mult,
                         [nbk_b, r_b], [self.hn_b[hf * 4 + i] for i in range(4)])

        pend = stage_a(*units[0])
        for ui, (qb, g) in enumerate(units):
            nxt = stage_a(*units[ui + 1]) if ui + 1 < len(units) else None
            stage_b(qb, g, pend)
            pend = nxt
        for (a_, b_) in nd_sets:
            self.banks.release(a_[0])
            self.banks.release(b_[0])


def _host_small(inp):
    f32 = np.float32
    norms = np.stack([inp["mix_norm"], inp["ffn_norm"], inp["ple_norm"]], 0).astype(f32)
    norms = np.ascontiguousarray(norms.reshape(3, DEPTH, NCH, 128).transpose(3, 0, 1, 2))
    cw = np.concatenate([inp["ffn_conv"].astype(f32), inp["ffn_conv_b"].astype(f32)[:, None, :]], 1)
    cw = np.ascontiguousarray(cw.reshape(DEPTH, 4, 44, 128).transpose(3, 0, 1, 2))
    lbl = np.ascontiguousarray(inp["hg_lb_logits"].astype(f32).reshape(1, 1024))
    hgn = np.ascontiguousarray(np.tile(inp["hg_out_norm"].astype(f32), (1, 4)).reshape(1, 1024))
    qkn = np.stack([np.tile(inp["q_norm"].astype(f32), (1, 2)), np.tile(inp["k_norm"].astype(f32), (1, 2))], -1)
    qkn = np.ascontiguousarray(qkn.transpose(1, 0, 2))
    snk = inp["sinks"].astype(f32).reshape(2, NCH, 2)
    snk = np.ascontiguousarray(np.repeat(snk.transpose(2, 0, 1), 64, axis=0))
    bucket, valid = _t5_bias_index()
    rb = inp["rel_bias"].astype(f32)
    bt = rb[bucket]
    bt = np.where(valid[..., None], bt, f32(NEG)).transpose(0, 1, 3, 2)
    hperm = np.array([4 * g + i for g in range(4) for i in (0, 2, 1, 3)])
    bt = bt[:, :, hperm, :]
    biasT = np.ascontiguousarray(bt.reshape(128, 2 * 16 * 128).astype(f32))
    cf, cb = _const_arrays()
    return dict(norms=norms, convw=cw, lbl=lbl, hgn=hgn, qkn=qkn, snk=snk, biasT=biasT, cf=cf, cb=cb)


_CACHE = {}


def _run(inputs, nseq, layers, ncores, dbg=None, trace=False):
    key = (nseq, tuple(layers), tuple(sorted(dbg.items())) if dbg else None)
    f32 = np.float32
    small = _host_small(inputs)
    shared = {k: np.ascontiguousarray(np.asarray(inputs[k], dtype=f32)) for k in
              ["ab_w_in", "ab_w_out", "c_w_in", "c_w_out", "ffn_up", "ffn_down", "ple_gate", "ple_proj"]}
    x = np.asarray(inputs["x"], dtype=f32)
    p = np.asarray(inputs["p"], dtype=f32)
    in_maps = []
    for ci in range(ncores):
        sl = slice(ci * nseq, (ci + 1) * nseq)
        m = dict(shared)
        m.update(small)
        m["xT"] = np.ascontiguousarray(x[sl].transpose(0, 2, 1))
        m["pT"] = np.ascontiguousarray(p[:, sl].transpose(0, 1, 3, 2))
        in_maps.append(m)
    nc = Builder(nseq, layers, dbg).build()
    res = run_bass_kernel_spmd(nc, in_maps, core_ids=list(range(ncores)))
    return res


def kernel(**inputs):
    res = _run(inputs, 2, list(range(DEPTH)), 8)
    outs = [r["outT"] for r in res.results]
    out = np.concatenate(outs, axis=0).transpose(0, 2, 1)
    return np.ascontiguousarray(out.astype(np.float32))
```
